# Optimizing a Trainium2 kernel written in Bass

```python
import jax, jax.numpy as jnp
from jax import lax
import numpy as np

D_MODEL = 1024
BATCH = 2
SEQ = 8192
DEPTH = 2
DEC_BATCH = 32
DEC_SEQ = 32
PAST_LEN = 1024

CHUNK = 64
A_HEADS = 8
A_HEAD_DIM = 64
DA = A_HEADS * A_HEAD_DIM
W_LORA = 64
A_LORA = 64
V_LORA = 32
G_LORA = 160
RWKV_COLS = 3 * DA + W_LORA + A_LORA + G_LORA
B_HEADS = 4
B_KEY_DIM = 128
B_VAL_DIM = 128
BK = B_HEADS * B_KEY_DIM
DB = B_HEADS * B_VAL_DIM
HGRN_COLS = 2 * BK + 2 * DB
GATE_COLS = 2 * D_MODEL
P_TOTAL = RWKV_COLS + HGRN_COLS + GATE_COLS
D_FF = 4 * D_MODEL
HGRN_BLOCK = 16
RMS_EPS = 1e-6
GN_EPS = 64e-5

kernel_name = 'rwkv7_hgrn2_gated_hybrid_step'


def rmsnorm(x, w):
    xf = x.astype(jnp.float32)
    y = xf * lax.rsqrt(jnp.mean(xf * xf, axis=-1, keepdims=True) + RMS_EPS)
    return (y * w.astype(jnp.float32)).astype(x.dtype)


def rwkv7_recurrence(r, w, k, v, a, b, s0):
    f32 = jnp.float32
    xs = tuple(jnp.moveaxis(t.astype(f32), 1, 0) for t in (r, w, k, v, a, b))

    def step(s, inp):
        r_t, w_t, k_t, v_t, a_t, b_t = inp
        sa = jnp.einsum('bhvk,bhk->bhv', s, a_t)
        s = s * w_t[:, :, None, :] + sa[..., None] * b_t[:, :, None, :] + v_t[..., None] * k_t[:, :, None, :]
        return s, jnp.einsum('bhvk,bhk->bhv', s, r_t)

    s, ys = lax.scan(step, s0.astype(f32), xs)
    return jnp.moveaxis(ys, 0, 1), s


def hgrn2_chunked(q, k, v, logf, s0):
    f32 = jnp.float32
    bsz, t_len, n_h, _ = q.shape
    v_dim = v.shape[-1]
    pad = (-t_len) % HGRN_BLOCK

    def prep(t):
        t = jnp.pad(t.astype(f32), ((0, 0), (0, pad), (0, 0), (0, 0)))
        t = t.reshape(bsz, -1, HGRN_BLOCK, n_h, t.shape[-1])
        return jnp.transpose(t, (1, 0, 3, 2, 4))

    mask = jnp.tril(jnp.ones((HGRN_BLOCK, HGRN_BLOCK), dtype=bool))[:, :, None]

    def step(s, inp):
        qc, kc, vc, gc = inp
        g_cum = jnp.cumsum(gc, axis=2)
        diff = g_cum[:, :, :, None, :] - g_cum[:, :, None, :, :]
        decay = jnp.exp(jnp.where(mask, diff, -jnp.inf))
        att = jnp.einsum('bhtk,bhtsk,bhsk->bhts', qc, decay, kc)
        o = jnp.einsum('bhtk,bhkv->bhtv', qc * jnp.exp(g_cum), s) + jnp.einsum('bhts,bhsv->bhtv', att, vc)
        g_last = g_cum[:, :, -1:, :]
        s = jnp.exp(g_last[:, :, 0, :])[..., None] * s + jnp.einsum('bhsk,bhsv->bhkv', kc * jnp.exp(g_last - g_cum), vc)
        return s, o

    s, o = lax.scan(step, s0.astype(f32), (prep(q), prep(k), prep(v), prep(logf)))
    o = jnp.transpose(o, (1, 0, 3, 2, 4)).reshape(bsz, -1, n_h, v_dim)[:, :t_len]
    return o, s


def token_mixer(h, l, lb, shift_prev, s_rwkv, s_hgrn, v_first, p):
    f32 = jnp.float32
    bsz, t_len, _ = h.shape
    proj = h @ p['w_in'][l]
    rw = proj[..., :RWKV_COLS]
    hg = proj[..., RWKV_COLS:RWKV_COLS + HGRN_COLS]
    gates = jax.nn.sigmoid(proj[..., RWKV_COLS + HGRN_COLS:])

    prev = jnp.concatenate([shift_prev[:, None, :].astype(rw.dtype), rw[:, :-1]], axis=1)
    rw_mix = rw + (prev - rw) * p['rwkv_mu'][l]
    r, k, v, wl, al, gl = jnp.split(rw_mix, [DA, 2 * DA, 3 * DA, 3 * DA + W_LORA, 3 * DA + W_LORA + A_LORA], axis=-1)
    w_raw = (p['rwkv_w0'][l] + jnp.tanh(wl) @ p['rwkv_w2'][l]).astype(f32)
    decay = jnp.exp(-jnp.exp(-jax.nn.softplus(-w_raw) - 0.5))
    a = jax.nn.sigmoid((p['rwkv_a0'][l] + al @ p['rwkv_a2'][l]).astype(f32))
    g = jax.nn.sigmoid(gl) @ p['rwkv_g2'][l]
    if l == 0:
        v_first = v
    else:
        vg = jax.nn.sigmoid(p['rwkv_v0'][l - 1] + (v @ p['rwkv_vres_w1'][l - 1]) @ p['rwkv_vres_w2'][l - 1])
        v = v + (v_first - v) * vg

    def heads(t):
        return t.reshape(bsz, t_len, A_HEADS, A_HEAD_DIM).astype(f32)

    r_h, v_h, a_h, w_h = heads(r), heads(v), heads(a), heads(decay)
    kk = heads(k * p['rwkv_k_k'][l])
    kk = kk * lax.rsqrt(jnp.maximum(jnp.sum(kk * kk, axis=-1, keepdims=True), 1e-24))
    k_a = p['rwkv_k_a'][l].astype(f32).reshape(A_HEADS, A_HEAD_DIM)
    k_h = heads(k) * (1.0 + (a_h - 1.0) * k_a)
    y, s_rwkv_new = rwkv7_recurrence(r_h, w_h, k_h, v_h, -kk, kk * a_h, s_rwkv)
    g_mean = jnp.mean(y, axis=-1, keepdims=True)
    g_var = jnp.mean(jnp.square(y - g_mean), axis=-1, keepdims=True)
    y = (y - g_mean) * lax.rsqrt(g_var + GN_EPS) * p['rwkv_ln_w'][l].astype(f32).reshape(A_HEADS, A_HEAD_DIM) \
        + p['rwkv_ln_b'][l].astype(f32).reshape(A_HEADS, A_HEAD_DIM)
    y = y + jnp.sum(r_h * k_h * p['rwkv_r_k'][l].astype(f32), axis=-1, keepdims=True) * v_h
    y_a = (y.reshape(bsz, t_len, DA).astype(h.dtype) * g) @ p['w_out_a'][l]

    q, fz, i_in, og = jnp.split(hg, [BK, 2 * BK, 2 * BK + DB], axis=-1)
    f = lb + (1.0 - lb) * jax.nn.sigmoid(fz.astype(f32))

    def bheads(t, d):
        return t.reshape(bsz, t_len, B_HEADS, d)

    o, s_hgrn_new = hgrn2_chunked(bheads(jax.nn.silu(q), B_KEY_DIM), bheads(1.0 - f, B_KEY_DIM),
                                  bheads(i_in, B_VAL_DIM), bheads(jnp.log(f), B_KEY_DIM), s_hgrn)
    o = o * lax.rsqrt(jnp.mean(o * o, axis=-1, keepdims=True) + RMS_EPS)
    o = o.reshape(bsz, t_len, DB).astype(h.dtype) * p['hgrn_norm_w'][l] * jax.nn.silu(og)
    y_b = o @ p['w_out_b'][l]

    mix = (gates[..., :D_MODEL] * y_a + gates[..., D_MODEL:] * y_b) @ p['w_out'][l]
    return mix, v_first, rw[:, -1], s_rwkv_new.astype(h.dtype), s_hgrn_new.astype(h.dtype)


def trunk(x, state_shift, state_rwkv, state_hgrn, p):
    lb_soft = jax.nn.softmax(p['hgrn_lb_logits'].astype(jnp.float32), axis=0)
    lb_all = jnp.cumsum(lb_soft, axis=0) - lb_soft[0]
    v_first = None
    shifts, rwkv_states, hgrn_states = [], [], []
    for l in range(DEPTH):
        h = rmsnorm(x, p['norm_mix'][l])
        mix, v_first, sh, sa, sb = token_mixer(h, l, lb_all[l], state_shift[l], state_rwkv[l], state_hgrn[l], v_first, p)
        x = x + mix
        h = rmsnorm(x, p['norm_ffn'][l])
        x = x + jnp.square(jax.nn.relu(h @ p['w_ffn_up'][l])) @ p['w_ffn_down'][l]
        shifts.append(sh)
        rwkv_states.append(sa)
        hgrn_states.append(sb)
    y = rmsnorm(x, p['norm_final'])
    return y, jnp.stack(shifts), jnp.stack(rwkv_states), jnp.stack(hgrn_states)


def setup_inputs(seed: int = 0) -> dict:
    key = jax.random.key(seed)
    ks = iter(jax.random.split(key, 40))
    f32 = jnp.float32

    def nrm(shape, scale):
        return jax.random.normal(next(ks), shape, f32) * scale

    def gain(shape):
        return 1.0 + nrm(shape, 0.02)

    return {
        'x_prompt': nrm((BATCH, SEQ, D_MODEL), 1.0),
        'x_sample': nrm((DEC_BATCH, DEC_SEQ, D_MODEL), 1.0),
        'state_shift': nrm((DEPTH, DEC_BATCH, RWKV_COLS), 1.0),
        'state_rwkv': nrm((DEPTH, DEC_BATCH, A_HEADS, A_HEAD_DIM, A_HEAD_DIM), 0.2),
        'state_hgrn': nrm((DEPTH, DEC_BATCH, B_HEADS, B_KEY_DIM, B_VAL_DIM), 0.5),
        'norm_mix': gain((DEPTH, D_MODEL)),
        'w_in': nrm((DEPTH, D_MODEL, P_TOTAL), D_MODEL ** -0.5),
        'rwkv_mu': jax.random.uniform(next(ks), (DEPTH, RWKV_COLS), f32, 0.2, 0.8),
        'rwkv_w0': jax.random.uniform(next(ks), (DEPTH, DA), f32, -6.0, 1.0),
        'rwkv_w2': nrm((DEPTH, W_LORA, DA), 0.1 * W_LORA ** -0.5),
        'rwkv_a0': nrm((DEPTH, DA), 0.1),
        'rwkv_a2': nrm((DEPTH, A_LORA, DA), 0.1 * A_LORA ** -0.5),
        'rwkv_g2': nrm((DEPTH, G_LORA, DA), G_LORA ** -0.5),
        'rwkv_v0': 1.0 + nrm((DEPTH - 1, DA), 0.1),
        'rwkv_vres_w1': nrm((DEPTH - 1, DA, V_LORA), DA ** -0.5),
        'rwkv_vres_w2': nrm((DEPTH - 1, V_LORA, DA), 0.1 * V_LORA ** -0.5),
        'rwkv_k_k': 0.85 + nrm((DEPTH, DA), 0.02),
        'rwkv_k_a': gain((DEPTH, DA)),
        'rwkv_r_k': nrm((DEPTH, A_HEADS, A_HEAD_DIM), 0.1),
        'rwkv_ln_w': gain((DEPTH, DA)),
        'rwkv_ln_b': nrm((DEPTH, DA), 0.02),
        'hgrn_lb_logits': nrm((DEPTH, BK), 0.1),
        'hgrn_norm_w': gain((DEPTH, DB)),
        'w_out_a': nrm((DEPTH, DA, D_MODEL), DA ** -0.5),
        'w_out_b': nrm((DEPTH, DB, D_MODEL), DB ** -0.5),
        'w_out': nrm((DEPTH, D_MODEL, D_MODEL), D_MODEL ** -0.5),
        'norm_ffn': gain((DEPTH, D_MODEL)),
        'w_ffn_up': nrm((DEPTH, D_MODEL, D_FF), D_MODEL ** -0.5),
        'w_ffn_down': nrm((DEPTH, D_FF, D_MODEL), D_FF ** -0.5),
        'norm_final': gain((D_MODEL,)),
    }


def reference(x_prompt, x_sample, state_shift, state_rwkv, state_hgrn, norm_mix, w_in, rwkv_mu, rwkv_w0, rwkv_w2,
              rwkv_a0, rwkv_a2, rwkv_g2, rwkv_v0, rwkv_vres_w1, rwkv_vres_w2, rwkv_k_k, rwkv_k_a, rwkv_r_k,
              rwkv_ln_w, rwkv_ln_b, hgrn_lb_logits, hgrn_norm_w, w_out_a, w_out_b, w_out, norm_ffn, w_ffn_up,
              w_ffn_down, norm_final):
    p = dict(norm_mix=norm_mix, w_in=w_in, rwkv_mu=rwkv_mu, rwkv_w0=rwkv_w0, rwkv_w2=rwkv_w2, rwkv_a0=rwkv_a0,
             rwkv_a2=rwkv_a2, rwkv_g2=rwkv_g2, rwkv_v0=rwkv_v0, rwkv_vres_w1=rwkv_vres_w1,
             rwkv_vres_w2=rwkv_vres_w2, rwkv_k_k=rwkv_k_k, rwkv_k_a=rwkv_k_a, rwkv_r_k=rwkv_r_k,
             rwkv_ln_w=rwkv_ln_w, rwkv_ln_b=rwkv_ln_b, hgrn_lb_logits=hgrn_lb_logits, hgrn_norm_w=hgrn_norm_w,
             w_out_a=w_out_a, w_out_b=w_out_b, w_out=w_out, norm_ffn=norm_ffn, w_ffn_up=w_ffn_up,
             w_ffn_down=w_ffn_down, norm_final=norm_final)
    dt = x_prompt.dtype
    bp = x_prompt.shape[0]
    zero_shift = jnp.zeros((DEPTH, bp, RWKV_COLS), dt)
    zero_rwkv = jnp.zeros((DEPTH, bp, A_HEADS, A_HEAD_DIM, A_HEAD_DIM), dt)
    zero_hgrn = jnp.zeros((DEPTH, bp, B_HEADS, B_KEY_DIM, B_VAL_DIM), dt)
    y_prompt, shift_p, rwkv_p, hgrn_p = trunk(x_prompt, zero_shift, zero_rwkv, zero_hgrn, p)
    y_sample, shift_s, rwkv_s, hgrn_s = trunk(x_sample, state_shift, state_rwkv, state_hgrn, p)
    return (y_prompt, y_sample, shift_p, rwkv_p, hgrn_p, shift_s, rwkv_s, hgrn_s)
```

```python
import numpy as np
from contextlib import ExitStack
import concourse.bass as bass
import concourse.mybir as mybir
from concourse.bass_utils import run_bass_kernel_spmd

F32 = mybir.dt.float32
BF16 = mybir.dt.bfloat16
AF = mybir.ActivationFunctionType
ALU = mybir.AluOpType
AX = mybir.AxisListType

D = 1024
KC = 8
NCORE = 8
L = 32
NBLK = 47
RW_BLK = 15
DFF = 4096
RMS_EPS = 1e-6
GN_EPS = 64e-5
C0 = -float(np.exp(-0.5))


class Prog:
    ENGS = ["sync", "scalar", "vector", "gpsimd", "tensor"]

    def __init__(self, nc, stack):
        self.nc = nc
        self.stack = stack
        self.ops = {e: [] for e in self.ENGS}
        self.esem = {e: stack.enter_context(nc.semaphore("s_" + e)) for e in self.ENGS}
        self.ecnt = {e: 0 for e in self.ENGS}
        self.dsem = {}
        self.dcnt = {}
        self.writer = {}
        self.readers = {}
        self.waited = {e: {} for e in self.ENGS}
        self.out_tokens = []
        self.pending = {e: {} for e in self.ENGS}

    def _dsem(self, name):
        if name not in self.dsem:
            self.dsem[name] = self.stack.enter_context(self.nc.semaphore("d_" + name))
            self.dcnt[name] = 0
        return self.dsem[name]

    def _deps(self, eng, reads, writes):
        toks = []
        for k in reads:
            w = self.writer.get(k)
            if w is not None:
                toks.append(w)
        for k in writes:
            w = self.writer.get(k)
            if w is not None:
                toks.append(w)
            toks.extend(self.readers.get(k, []))
        need = {}
        for (s, sid, v) in toks:
            if sid == ("e", "tensor") and eng == "tensor":
                continue
            if sid[0] == "e" and sid[1] == eng and eng == "sync":
                continue
            if sid[0] == "d":
                v = max(v, self.dcnt[sid[1]])
            if need.get(sid, (None, -1))[1] < v:
                need[sid] = (s, v)
        for sid, (s, v) in self.pending[eng].items():
            if need.get(sid, (None, -1))[1] < v:
                need[sid] = (s, v)
        self.pending[eng] = {}
        waits = []
        for sid, (s, v) in need.items():
            if self.waited[eng].get(sid, -1) >= v:
                continue
            self.waited[eng][sid] = v
            waits.append((s, v))
        return waits

    def fence(self):
        allt = {}
        for e in self.ENGS:
            if self.ecnt[e] > 0:
                allt[("e", e)] = (self.esem[e], self.ecnt[e])
        for n, c in self.dcnt.items():
            if c > 0:
                allt[("d", n)] = (self.dsem[n], c)
        for e in self.ENGS:
            for sid, sv in allt.items():
                if sid == ("e", e):
                    continue
                self.pending[e][sid] = sv
        self.writer.clear()
        self.readers.clear()

    def _record(self, tok, reads, writes):
        for k in reads:
            self.readers.setdefault(k, []).append(tok)
        for k in writes:
            self.writer[k] = tok
            self.readers[k] = []

    def op(self, eng, fn, reads=(), writes=()):
        pk = [k for k in reads if k.startswith("psbank")]
        if pk:
            reads = [k for k in reads if not k.startswith("psbank")]
            writes = list(writes) + [k for k in pk if k not in writes]
        waits = self._deps(eng, reads, writes)
        self.ecnt[eng] += 1
        tok = (self.esem[eng], ("e", eng), self.ecnt[eng])
        self.ops[eng].append((waits, fn, self.esem[eng], 1))
        self._record(tok, reads, writes)
        return tok

    def dma(self, eng, dname, fn, reads=(), writes=(), is_output=False):
        waits = self._deps(eng, reads, writes)
        s = self._dsem(dname)
        self.dcnt[dname] += 16
        tok = (s, ("d", dname), self.dcnt[dname])
        self.ops[eng].append((waits, fn, s, 16))
        self._record(tok, reads, writes)
        if is_output:
            self.out_tokens.append(tok)
        return tok

    def emit(self, block):
        prog = self

        def mk(ename):
            def body(e):
                for (waits, fn, s, inc) in prog.ops[ename]:
                    for (ws, wv) in waits:
                        e.wait_ge(ws, wv)
                    fn(e).then_inc(s, inc)
                if ename == "gpsimd":
                    last = {}
                    for (s2, sid, v) in prog.out_tokens:
                        if last.get(sid, (None, -1))[1] < v:
                            last[sid] = (s2, v)
                    for sid, (s2, v) in last.items():
                        e.wait_ge(s2, v)
            return body
        block.sync(mk("sync"))
        block.scalar(mk("scalar"))
        block.vector(mk("vector"))
        block.gpsimd(mk("gpsimd"))
        block.tensor(mk("tensor"))


def _win_cols():
    idx = list(range(0, 1664))
    idx += list(range(1664, 1824)) + [-1] * 96
    idx += list(range(1824, 5920))
    assert len(idx) == NBLK * 128
    return np.array(idx)


def _take_cols(a, idx, axis=-1):
    a = np.moveaxis(a, axis, -1)
    out = np.zeros(a.shape[:-1] + (len(idx),), a.dtype)
    m = idx >= 0
    out[..., m] = a[..., idx[m]]
    return np.moveaxis(out, -1, axis)


def _pk(v, nchunk):
    return np.ascontiguousarray(v.reshape(nchunk, 128).T)


def _consts(rank4):
    c = {}
    c["ident"] = np.eye(128, dtype=np.float32)
    ob = np.zeros((128, 128), np.float32)
    ob[:64, :64] = 1
    ob[64:, 64:] = 1
    c["ones_bd"] = ob
    c["ones_f"] = np.ones((128, 128), np.float32)
    isel = np.zeros((128, 64), np.float32)
    isel[np.arange(128), np.arange(128) % 64] = 1
    c["isel"] = isel
    rho = np.arange(128)
    blk, pos = rho // 32, rho % 32
    same = blk[:, None] == blk[None, :]
    c["m_strict"] = (same & (pos[:, None] < pos[None, :])).astype(np.float32)
    c["m_incl"] = (same & (pos[:, None] <= pos[None, :])).astype(np.float32)
    c["m_lower"] = (same & (pos[:, None] > pos[None, :])).astype(np.float32)
    s = np.arange(32)
    c["m_att"] = (s[:, None] <= s[None, :]).astype(np.float32)
    rst = np.ones((128, 4 * 128), np.float32)
    rst[:, ::32] = 0
    c["m_reset"] = rst
    fm = np.zeros((128, 4), np.float32)
    fm[:, :3] = (np.arange(3) < rank4).astype(np.float32)[None, :]
    c["foldm"] = fm
    hm = np.zeros((128, 4), np.float32)
    if rank4 > 0:
        hm[:, rank4 - 1] = 1
    c["halom"] = hm
    return c


class Cfg:
    def __init__(self, npt=2048, w=256, nlayer=2, dbg=(), stop=None):
        self.stop = stop
        self.NPT = npt
        self.W = w
        self.NS = 4
        self.NT = npt + 4 * L
        self.NL = nlayer
        self.dbg = tuple(dbg)


PV = dict(nmix=0, nffn=8, mu=16, w0=31, a0=35, v0=39, kk=43, ka=47, rk=51, hnw=55, lbz=59, nfin=63)
NPV = 71


def build(cfg):
    nc = bass.Bass("TRN2", target_bir_lowering=False)
    NPT, W, NT, NL = cfg.NPT, cfg.W, cfg.NT, cfg.NL
    NPS = NPT // W

    def din(name, shape, dt=F32):
        return nc.dram_tensor(name, list(shape), dt, kind="ExternalInput").ap()

    def dout(name, shape, dt=F32):
        return nc.dram_tensor(name, list(shape), dt, kind="ExternalOutput").ap()

    def dint(name, shape, dt=F32):
        return nc.dram_tensor(name, list(shape), dt, kind="Internal").ap()

    xT = din("xT", [128, KC, NT])
    st_shift = din("st_shift", [NL, 128, RW_BLK, 4])
    st_rwkv = din("st_rwkv", [NL, 4, 128, 4, 64])
    st_hgrn = din("st_hgrn", [NL, 4, 128, 4, 128])
    w_in_r = din("w_in_r", [NL, NBLK, 128, KC, 128])
    w_up_r = din("w_up_r", [NL, 16, 128, KC, 256])
    w_dn_r = din("w_dn_r", [NL, 16, 128, 16, 128])
    w_oab_r = din("w_oab_r", [NL, 8, 128, 8, 128])
    w_o_r = din("w_o_r", [NL, 8, 128, 8, 128])
    w2pad = din("w2pad", [NL, 128, 512])
    a2pad = din("a2pad", [NL, 128, 512])
    g2pad = din("g2pad", [NL, 128, 2, 512])
    vw1 = din("vw1", [128, 4, 32])
    vw2 = din("vw2", [32, 512])
    pvec = din("pvec", [NL, 128, NPV])
    lnw_st = din("lnw_st", [NL, 128, 2, 64])
    lnb_st = din("lnb_st", [NL, 128, 2, 64])
    cnames = ["ident", "ones_bd", "ones_f", "isel", "m_strict", "m_incl", "m_lower", "m_att", "m_reset", "foldm", "halom"]
    cshape = dict(ident=[128, 128], ones_bd=[128, 128], ones_f=[128, 128], isel=[128, 64], m_strict=[128, 128],
                  m_incl=[128, 128], m_lower=[128, 128], m_att=[32, 32], m_reset=[128, 512], foldm=[128, 4], halom=[128, 4])
    cdram = {n: din("c_" + n, cshape[n]) for n in cnames}

    yT = dout("yT", [128, KC, NT])
    o_shift_p = dout("o_shift_p", [NL, 128, RW_BLK])
    o_rwkv_p = dout("o_rwkv_p", [NL, 128, 4, 64])
    o_hgrn_p = dout("o_hgrn_p", [NL, 128, 4, 128])
    o_shift_s = dout("o_shift_s", [NL, 128, RW_BLK, 4])
    o_rwkv_s = dout("o_rwkv_s", [NL, 4, 128, 4, 64])
    o_hgrn_s = dout("o_hgrn_s", [NL, 4, 128, 4, 128])
    dbg_out = {n: dout("dbg_" + n, shp) for (n, shp) in cfg.dbg}

    xs1 = dint("xs1", [128, KC, NT])
    vfirst_d = dint("vfirst_d", [128, 4, NT])
    cin_h = dint("cin_h", [128, 16])
    cout_h = dint("cout_h", [512, 16])
    SW = 4 * 128 + 4 * 128 + 8
    cin_s = dint("cin_s", [128, SW])
    cout_s = dint("cout_s", [512, SW])
    wb_in = dint("wb_in", [NL, NBLK, 128, KC * 128], BF16)
    wb_up = dint("wb_up", [NL, 16, 128, KC * 256], BF16)
    wb_dn = dint("wb_dn", [NL, 16, 128, 16 * 128], BF16)
    x1s = dint("x1s", [128, KC, NT])
    wb_oab = dint("wb_oab", [NL, 8, 128, 8 * 128], BF16)
    wb_o = dint("wb_o", [NL, 8, 128, 8 * 128], BF16)

    with ExitStack() as st:
        p = Prog(nc, st)

        def sb(name, shape, dt=F32):
            return st.enter_context(nc.sbuf_tensor(name, list(shape), dt))

        def X(eng, method, reads, writes, **kw):
            return p.op(eng, lambda e: getattr(e, method)(**kw), reads, writes)

        def DMA(eng, dname, out, in_, reads, writes, is_output=False, **kw):
            return p.dma(eng, dname, lambda e: e.dma_start(out=out, in_=in_, **kw), reads, writes, is_output)

        _rr = [0]

        def EW():
            _rr[0] ^= 1
            return "vector" if _rr[0] else "gpsimd"

        ps = st.enter_context(nc.psum_tensor("ps", [128, 7 * 512], F32))
        psb = st.enter_context(nc.psum_tensor("psb", [128, 1024], BF16))

        def PS(bank, off, n):
            assert off + n <= 512
            keys = ["psbank%d" % bank]
            return ps[:, bank * 512 + off: bank * 512 + off + n], keys

        def PSB(off, n):
            keys = ["psbankB"]
            return psb[:, off:off + n], keys

        cst = {}
        for n in cnames:
            cst[n] = sb("k_" + n, cshape[n])
            DMA("sync", "cst", cst[n][:], cdram[n][:], [], ["c_" + n])
        ident_b = sb("ident_b", [128, 128], BF16)
        isel_b = sb("isel_b", [128, 64], BF16)
        X("vector", "tensor_copy", ["c_ident"], ["ident_b"], out=ident_b[:], in_=cst["ident"][:])
        X("vector", "tensor_copy", ["c_isel"], ["isel_b"], out=isel_b[:], in_=cst["isel"][:])
        eps_rms = sb("eps_rms", [128, 1])
        eps_gn = sb("eps_gn", [128, 1])
        X("vector", "memset", [], ["eps_rms"], ap=eps_rms[:], constant=RMS_EPS)
        X("vector", "memset", [], ["eps_gn"], ap=eps_gn[:], constant=GN_EPS)

        def wcls(b):
            return "a" if b < RW_BLK else ("b" if 19 <= b < 27 else "c")

        def wkey(l, b):
            return "wb_in%d%s" % (l, wcls(b))

        conv_q = {l: [] for l in range(NL)}

        def conv_layer(l):
            order = list(range(0, RW_BLK)) + list(range(19, 27)) + list(range(RW_BLK, 19)) + list(range(27, NBLK))
            mk = lambda *a: (lambda: DMA(*a))
            for b in order:
                conv_q[l].append(mk("gpsimd", "cv_in%d%s" % (l, wcls(b)), wb_in[l, b], w_in_r[l, b].rearrange("p k c -> p (k c)"), [], [wkey(l, b)]))
            for g in range(8):
                conv_q[l].append(mk("gpsimd", "cv_o%d" % l, wb_oab[l, g], w_oab_r[l, g].rearrange("p k c -> p (k c)"), [], ["wb_o%d" % l]))
                conv_q[l].append(mk("gpsimd", "cv_o%d" % l, wb_o[l, g], w_o_r[l, g].rearrange("p k c -> p (k c)"), [], ["wb_o%d" % l]))
            for g in range(16):
                conv_q[l].append(mk("gpsimd", "cv_f%d" % l, wb_up[l, g], w_up_r[l, g].rearrange("p k c -> p (k c)"), [], ["wb_f%d" % l]))
                conv_q[l].append(mk("gpsimd", "cv_f%d" % l, wb_dn[l, g], w_dn_r[l, g].rearrange("p k c -> p (k c)"), [], ["wb_f%d" % l]))

        def conv_pump(l, n):
            if l < NL:
                for _ in range(n):
                    if conv_q[l]:
                        conv_q[l].pop(0)()

        for l in range(NL):
            conv_layer(l)
        conv_pump(0, RW_BLK + 8)

        pvs = sb("pvs", [128, NL, NPV])
        DMA("sync", "cstp", pvs[:], pvec.rearrange("l p n -> p l n"), [], ["pvs"])
        lnw = sb("lnw", [128, NL, 2, 64])
        lnb = sb("lnb", [128, NL, 2, 64])
        DMA("sync", "cstp", lnw[:], lnw_st.rearrange("l p g v -> p l g v"), [], ["lnw"])
        DMA("sync", "cstp", lnb[:], lnb_st.rearrange("l p g v -> p l g v"), [], ["lnb"])
        w2p_b = sb("w2p_b", [128, NL, 512], BF16)
        a2p_b = sb("a2p_b", [128, NL, 512], BF16)
        g2p_b = sb("g2p_b", [128, NL, 2, 512], BF16)
        vw1_b = sb("vw1_b", [128, 4, 32], BF16)
        vw2_b = sb("vw2_b", [32, 512], BF16)
        DMA("gpsimd", "cst2", w2p_b[:], w2pad.rearrange("l p n -> p l n"), [], ["w2p_b"])
        DMA("gpsimd", "cst2", a2p_b[:], a2pad.rearrange("l p n -> p l n"), [], ["a2p_b"])
        DMA("gpsimd", "cst2", g2p_b[:], g2pad.rearrange("l p j n -> p l j n"), [], ["g2p_b"])
        DMA("gpsimd", "cst2", vw1_b[:], vw1[:], [], ["vw1_b"])
        DMA("gpsimd", "cst2", vw2_b[:], vw2[:], [], ["vw2_b"])
        lbe = sb("lbe", [128, NL, 4])
        lbs = sb("lbs", [128, 4])
        lbv = sb("lbv", [128, NL, 4])
        oml = sb("oml", [128, NL, 4])
        X("scalar", "activation", ["pvs"], ["lbe"], out=lbe[:], in_=pvs[:, :, PV["lbz"]:PV["lbz"] + 4], func=AF.Exp)
        X("vector", "tensor_copy", ["lbe"], ["lbs"], out=lbs[:], in_=lbe[:, 0, :])
        for l in range(1, NL):
            X("vector", "tensor_tensor", ["lbe", "lbs"], ["lbs"], out=lbs[:], in0=lbs[:], in1=lbe[:, l, :], op=ALU.add)
        X("vector", "reciprocal", ["lbs"], ["lbs"], out=lbs[:], in_=lbs[:])
        for l in range(NL):
            X("vector", "tensor_tensor", ["lbe", "lbs"], ["lbe"], out=lbe[:, l, :], in0=lbe[:, l, :], in1=lbs[:], op=ALU.mult)
        X("vector", "tensor_tensor", ["lbe"], ["lbv"], out=lbv[:, 0, :], in0=lbe[:, 0, :], in1=lbe[:, 0, :], op=ALU.subtract)
        for l in range(1, NL):
            X("vector", "tensor_tensor", ["lbe", "lbv"], ["lbv"], out=lbv[:, l, :], in0=lbv[:, l - 1, :], in1=lbe[:, l, :], op=ALU.add)
        X("vector", "tensor_scalar", ["lbv"], ["oml"], out=oml[:], in0=lbv[:], scalar1=-1.0, scalar2=1.0, op0=ALU.mult, op1=ALU.add)

        def pcol(l, name, i=0, n=1):
            return pvs[:, l, PV[name] + i: PV[name] + i + n]

        def pbc(l, name, n, T):
            return pvs[:, l, PV[name]: PV[name] + n].unsqueeze(2).to_broadcast([128, n, T])

        WSL = 4
        wslot = [sb("wslot%d" % i, [128, 2, KC * 128], BF16) for i in range(WSL)]
        xt = sb("xt", [128, KC, W])
        x1 = sb("x1", [128, KC, W])
        hT = sb("hT", [128, KC, W], BF16)
        sqk_d = [sb("sqk%d" % i, [128, W]) for i in range(2)]
        rstd_d = sb("rstd", [128, W])
        RAWW = W + 4
        raw = sb("raw", [128, RW_BLK, RAWW])
        X("gpsimd", "memset", [], ["raw"], ap=raw[:], constant=0.0)
        hq = sb("hq", [128, 4, W])
        hsig = sb("hsig", [128, 4, W])
        hv_b = sb("hv_b", [128, 4, W], BF16)
        hog = sb("hog", [128, 4, W])
        big = sb("big", [128, max(16 * W, SW)])
        gates = big[:, 0:16 * W].rearrange("p (j w) -> p j w", w=W)
        yg_b = sb("yg_b", [128, 4, W], BF16)
        ob_b = sb("ob_b", [128, 4, W], BF16)
        mixin = sb("mixin", [128, KC, W], BF16)
        mtmp = sb("mtmp", [128, W])
        relu_t = [sb("relu%d" % i, [128, W]) for i in range(1)]
        _wsl = [0]

        def emit_norm(l, xbuf, xkey, N, pname, outbuf=None, outkey="hT", sqk=None, rstd=None, pfx=""):
            if outbuf is None:
                outbuf = hT
            if sqk is None:
                sqk, rstd = sqk_d, rstd_d
            nps, nkeys = PS(6, 0, N)
            for kc in range(KC):
                sq = sqk[kc % 2]
                X("scalar", "activation", [xkey], [pfx + "sqk%d" % (kc % 2)], out=sq[:, 0:N], in_=xbuf[:, kc, 0:N], func=AF.Square)
                X("tensor", "matmul", [pfx + "sqk%d" % (kc % 2), "c_ones_f"], nkeys, out=nps, lhsT=cst["ones_f"][:], rhs=sq[:, 0:N],
                  start=(kc == 0), stop=(kc == KC - 1))
            X("scalar", "activation", nkeys + ["eps_rms"], [pfx + "rstd"], out=rstd[:, 0:N], in_=nps, func=AF.Ln, scale=1.0 / D, bias=eps_rms[:])
            X("scalar", "activation", [pfx + "rstd"], [pfx + "rstd"], out=rstd[:, 0:N], in_=rstd[:, 0:N], func=AF.Exp, scale=-0.5)
            for kc in range(KC):
                X("vector", "scalar_tensor_tensor", [xkey, pfx + "rstd", "pvs"], [outkey], out=outbuf[:, kc, 0:N], in0=xbuf[:, kc, 0:N],
                  scalar=pcol(l, pname, kc), in1=rstd[:, 0:N], op0=ALU.mult, op1=ALU.mult)

        _psl = [0]

        def emit_proj(l, blocks, N, handler):
            groups = [blocks[i:i + 2] for i in range(0, len(blocks), 2)]
            for grp in groups:
                s = _wsl[0] % WSL
                _wsl[0] += 1
                if len(grp) == 2 and grp[1] == grp[0] + 1:
                    DMA("sync", "wsl%d" % s, wslot[s][:, 0:2, :], wb_in[l, grp[0]:grp[0] + 2].rearrange("j p n -> p j n"),
                        sorted(set([wkey(l, grp[0]), wkey(l, grp[1])])), ["wslot%d" % s])
                else:
                    for j, b in enumerate(grp):
                        DMA("sync", "wsl%d" % s, wslot[s][:, j, :], wb_in[l, b], [wkey(l, b)], ["wslot%d" % s])
                for j, b in enumerate(grp):
                    slot = _psl[0] % 4
                    _psl[0] += 1
                    pr, pk = PS(slot, 0, N)
                    for kc in range(KC):
                        X("tensor", "matmul", ["wslot%d" % s, "hT"], pk, out=pr, lhsT=wslot[s][:, j, kc * 128:(kc + 1) * 128],
                          rhs=hT[:, kc, 0:N], start=(kc == 0), stop=(kc == KC - 1))
                    handler(b, pr, pk)

        T = 128
        AW = 17920
        arena = sb("arena", [128, AW])
        _ao = [0]

        def aalloc(shape, dt=F32, reset=False):
            if reset:
                _ao[0] = 0
            n = int(np.prod(shape[1:]))
            n32 = n if dt == F32 else (n + 1) // 2
            a = _ao[0]
            _ao[0] += n32
            assert _ao[0] <= AW, ("arena overflow", _ao[0])
            v = arena[:, a:a + n32]
            if dt != F32:
                v = v.bitcast(dt)[:, 0:n]
            if len(shape) == 3:
                v = v.rearrange("p (a b) -> p a b", a=shape[1])
            elif len(shape) == 4:
                v = v.rearrange("p (a b c) -> p a b c", a=shape[1], b=shape[2])
            return v[0:shape[0]] if shape[0] < 128 else v

        def sbm(name, shape, dt=F32):
            return aalloc(list(shape), dt)
        mx = sbm("mx", [128, RW_BLK, T])
        lw_in = sbm("lw_in", [128, T], BF16)
        siggl = sbm("siggl", [128, 2, T], BF16)
        names4 = ["sg", "aa", "gfm", "vv", "kkn", "kh", "brec", "Gw", "tA", "tB", "Ex", "bv", "t1", "vf"]
        m4 = {n: sbm("m_" + n, [128, 4, T]) for n in names4}
        vb16 = sbm("vb16", [128, 4, T], BF16)
        t32b = sbm("t32b", [32, T], BF16)
        bdn = ["Kbd", "Bbd", "Abd", "Rbd", "KHbd", "BHbd", "Vbd"]
        bd = {n: sb(n, [128, 4, 4, 128], BF16) for n in bdn}
        for n in bdn:
            X("gpsimd", "memset", [], [n], ap=bd[n][:], constant=0.0)
        AR = sbm("AR", [128, 4, 4, 64], BF16)
        Bp = sbm("Bp", [128, 4, 4, 32], BF16)
        gL = sb("gL", [128, 4, 4])
        chn = ["X1", "X1T", "Mb", "Xa", "XaT", "Xb", "XbT"]
        chb = {n: [sbm("%s%d" % (n, g), [128, 128], BF16) for g in range(2)] for n in chn}
        chb2 = {n: [[sbm("%s_%d_%d" % (n, pp, g), [128, 128], BF16) for g in range(2)] for pp in range(2)] for n in ["Aka", "Akr", "Abr", "Ma"]}
        KT2 = [sbm("KT_%d" % pp, [128, 4, 128], BF16) for pp in range(2)]
        BT2 = [sbm("BT_%d" % pp, [128, 4, 128], BF16) for pp in range(2)]
        Vst2 = [sb("Vst_%d" % pp, [128, 2, 128], BF16) for pp in range(2)]
        for pp in range(2):
            X("gpsimd", "memset", [], ["Vst%d_0" % pp, "Vst%d_1" % pp], ap=Vst2[pp][:], constant=0.0)
        Wt = sbm("Wt", [128, 2, 128], BF16)
        Ut = sbm("Ut", [128, 2, 128], BF16)
        Sf = sb("Sf", [128, 4, 128])
        Sb = sb("Sb", [128, 4, 128], BF16)
        Yst = sbm("Yst", [128, 8, 64])
        Ysq = sbm("Ysq", [128, 8, 64])
        Yrep = sbm("Yrep", [128, 8, 2, 64])
        gst = {n: sb("gst_" + n, [128, 8]) for n in ["s1", "s2", "mean", "var"]}
        h4 = {"o": sbm("h_o", [128, 4, T])}
        for hn_, mn_ in {'f': 'sg', 'lf': 'aa', 'khh': 'kkn', 'G2': 'Gw', 'd1': 'tB', 'E2': 'Ex', 'hA': 'tA', 'o2': 'kh', 'rs': 'brec'}.items():
            h4[hn_] = m4[mn_]
        hb = {n: sbm("hb_" + n, [128, 4, T], BF16) for n in ["Qt", "Q2", "Kt", "Kh"]}
        dL = sb("dL", [128, 4, 4])
        att_b = sbm("att_b", [32, 4, 32], BF16)
        VTh = sbm("VTh", [32, 4, 128], BF16)
        KTh = sbm("KTh", [32, 4, 128], BF16)
        Shf = sb("Shf", [128, 4, 128])
        Shb = sb("Shb", [128, 4, 128], BF16)
        Dtot = sb("Dtot", [128, 4])

        def v4(ap):
            return ap.rearrange("p c (q t) -> p c q t", t=L)

        def bd_write(name, in0, in1, op, keys_r):
            for hh in range(2):
                for cc in range(2):
                    out = bd[name][hh * 64:(hh + 1) * 64, cc::2, :, 64 * cc + 32 * hh: 64 * cc + 32 * hh + 32]
                    a = v4(in0)[hh * 64:(hh + 1) * 64, cc::2]
                    if in1 is None:
                        X(EW(), "tensor_copy", keys_r, [name], out=out, in_=a)
                    else:
                        b = v4(in1)[hh * 64:(hh + 1) * 64, cc::2]
                        X(EW(), "tensor_tensor", keys_r, [name], out=out, in0=a, in1=b, op=op)

        def bc4(ap, shape):
            return ap.to_broadcast(shape)

        def rwkv_prep(l, phB, cur, prv, mxv, tok0, nvalid_blocks):
            b0, b1 = nvalid_blocks
            shp = list(cur.shape)
            mu = pvs[:, l, PV["mu"] + b0: PV["mu"] + b1]
            mu_bc = (mu.unsqueeze(2) if len(shp) == 3 else mu.unsqueeze(2).unsqueeze(3)).to_broadcast(shp)
            X("vector", "tensor_tensor", ["raw"], ["mx"], out=mxv, in0=prv, in1=cur, op=ALU.subtract)
            X("vector", "tensor_tensor", ["mx", "pvs"], ["mx"], out=mxv, in0=mxv, in1=mu_bc, op=ALU.mult)
            X("vector", "tensor_tensor", ["mx", "raw"], ["mx"], out=mxv, in0=mxv, in1=cur, op=ALU.add)
            r, k, v = mx[:, 0:4, :], mx[:, 4:8, :], mx[:, 8:12, :]
            X("scalar", "activation", ["mx"], ["lw_in"], out=lw_in[0:64, :], in_=mx[0:64, 12, :], func=AF.Tanh)
            X("scalar", "activation", ["mx"], ["lw_in"], out=lw_in[64:128, :], in_=mx[64:128, 12, :], func=AF.Copy)
            pw, kw = PS(4, 0, 512)
            pa, ka = PS(5, 0, 512)
            pg, kg = PS(6, 0, 512)
            for c in range(4):
                X("tensor", "matmul", ["lw_in", "w2p_b"], kw, out=pw[:, c * T:(c + 1) * T], lhsT=w2p_b[:, l, c * 128:(c + 1) * 128],
                  rhs=lw_in[:], start=True, stop=True)
                X("tensor", "matmul", ["lw_in", "a2p_b"], ka, out=pa[:, c * T:(c + 1) * T], lhsT=a2p_b[:, l, c * 128:(c + 1) * 128],
                  rhs=lw_in[:], start=True, stop=True)
            if phB:
                X("scalar", "activation", ["mx"], ["siggl"], out=siggl[:], in_=mx[:, 13:15, :], func=AF.Sigmoid)
                for c in range(4):
                    for j in range(2):
                        X("tensor", "matmul", ["siggl", "g2p_b"], kg, out=pg[:, c * T:(c + 1) * T],
                          lhsT=g2p_b[:, l, j, c * 128:(c + 1) * 128], rhs=siggl[:, j, :], start=(j == 0), stop=(j == 1))
            for c in range(4):
                X("scalar", "activation", kw + ["pvs"], ["sg"], out=m4["sg"][:, c, :], in_=pw[:, c * T:(c + 1) * T], func=AF.Sigmoid,
                  bias=pcol(l, "w0", c))
                X("scalar", "activation", ka + ["pvs"], ["aa"], out=m4["aa"][:, c, :], in_=pa[:, c * T:(c + 1) * T], func=AF.Sigmoid,
                  bias=pcol(l, "a0", c))
            if phB:
                X("scalar", "activation", kg, ["gfm"], out=m4["gfm"][:].rearrange("p c t -> p (c t)"), in_=pg, func=AF.Copy)
            FEED(2)
            vv = m4["vv"]
            if l == 0:
                X(EW(), "tensor_copy", ["mx"], ["vv"], out=vv[:], in_=v)
                if phB:
                    DMA("sync", "vf_st", vfirst_d[:, :, tok0:tok0 + T], vv[:], ["vv"], ["vfirst_d"])
            else:
                DMA("sync", "vf_ld", m4["vf"][:], vfirst_d[:, :, tok0:tok0 + T], ["vfirst_d"], ["vf"])
                X("gpsimd", "tensor_copy", ["mx"], ["vb16"], out=vb16[:], in_=v)
                p32, k32 = PS(4, 0, T)
                for c in range(4):
                    X("tensor", "matmul", ["vb16", "vw1_b"], k32, out=p32[0:32, :], lhsT=vw1_b[:, c, :], rhs=vb16[:, c, :],
                      start=(c == 0), stop=(c == 3))
                X("scalar", "activation", k32, ["t32b"], out=t32b[:], in_=p32[0:32, :], func=AF.Copy)
                pv_, kv_ = PS(5, 0, 512)
                for c in range(4):
                    X("tensor", "matmul", ["t32b", "vw2_b"], kv_, out=pv_[:, c * T:(c + 1) * T], lhsT=vw2_b[:, c * 128:(c + 1) * 128],
                      rhs=t32b[:], start=True, stop=True)
                for c in range(4):
                    X("scalar", "activation", kv_ + ["pvs"], ["tA"], out=m4["tA"][:, c, :], in_=pv_[:, c * T:(c + 1) * T],
                      func=AF.Sigmoid, bias=pcol(0, "v0", c))
                X("vector", "tensor_tensor", ["vf", "mx"], ["tB"], out=m4["tB"][:], in0=m4["vf"][:], in1=v, op=ALU.subtract)
                X("vector", "tensor_tensor", ["tB", "tA"], ["tB"], out=m4["tB"][:], in0=m4["tB"][:], in1=m4["tA"][:], op=ALU.mult)
                X("vector", "tensor_tensor", ["tB", "mx"], ["vv"], out=vv[:], in0=m4["tB"][:], in1=v, op=ALU.add)
            kkn, kh, brec, Gw, tA, tB, Ex = (m4[n] for n in ["kkn", "kh", "brec", "Gw", "tA", "tB", "Ex"])
            X("vector", "tensor_tensor", ["mx", "pvs"], ["kkn"], out=kkn[:], in0=k, in1=pbc(l, "kk", 4, T), op=ALU.mult)
            X("gpsimd", "tensor_tensor", ["kkn"], ["tA"], out=tA[:], in0=kkn[:], in1=kkn[:], op=ALU.mult)
            pss, kss = PS(6, 0, 512)
            for c in range(4):
                X("tensor", "matmul", ["tA", "c_ones_bd"], kss, out=pss[:, c * T:(c + 1) * T], lhsT=cst["ones_bd"][:], rhs=tA[:, c, :],
                  start=True, stop=True)
            tAf = tA[:].rearrange("p c t -> p (c t)")
            X("vector", "tensor_scalar", kss, ["tA"], out=tAf, in0=pss, scalar1=1e-24, scalar2=None, op0=ALU.max)
            X("scalar", "activation", ["tA"], ["tA"], out=tAf, in_=tAf, func=AF.Ln)
            X("scalar", "activation", ["tA"], ["tA"], out=tAf, in_=tAf, func=AF.Exp, scale=-0.5)
            X("vector", "tensor_tensor", ["kkn", "tA"], ["kkn"], out=kkn[:], in0=kkn[:], in1=tA[:], op=ALU.mult)
            FEED(2)
            X("vector", "scalar_tensor_tensor", ["aa", "pvs"], ["tB"], out=tB[:], in0=m4["aa"][:], scalar=-1.0, in1=pbc(l, "ka", 4, T),
              op0=ALU.add, op1=ALU.mult)
            X("vector", "scalar_tensor_tensor", ["tB", "mx"], ["kh"], out=kh[:], in0=tB[:], scalar=1.0, in1=k, op0=ALU.add, op1=ALU.mult)
            X("gpsimd", "tensor_tensor", ["kkn", "aa"], ["brec"], out=brec[:], in0=kkn[:], in1=m4["aa"][:], op=ALU.mult)
            sgf = m4["sg"][:].rearrange("p c t -> p (c t)")
            Gwf = Gw[:].rearrange("p c t -> p (c t)")
            X("scalar", "mul", ["sg"], ["sg"], out=sgf, in_=sgf, mul=C0)
            X("vector", "tensor_tensor_scan", ["sg", "c_m_reset"], ["Gw"], out=Gwf, data0=cst["m_reset"][:], data1=sgf, initial=0.0,
              op0=ALU.mult, op1=ALU.add)
            X("gpsimd", "tensor_tensor", ["Gw", "sg"], ["tB"], out=tB[:], in0=Gw[:], in1=m4["sg"][:], op=ALU.subtract)
            Exf = Ex[:].rearrange("p c t -> p (c t)")
            X("scalar", "activation", ["tB"], ["Ex"], out=Exf, in_=tB[:].rearrange("p c t -> p (c t)"), func=AF.Exp)
            X("vector", "scalar_tensor_tensor", ["kkn", "Ex"], ["tA"], out=tA[:], in0=kkn[:], scalar=-1.0, in1=Ex[:],
              op0=ALU.mult, op1=ALU.mult)
            bd_write("Abd", tA[:], None, None, ["tA"])
            X(EW(), "tensor_copy", ["tA"], ["AR"], out=AR[:, :, :, 0:32], in_=v4(tA[:]))
            if phB:
                X("scalar", "activation", ["Gw"], ["Ex"], out=Exf, in_=Gwf, func=AF.Exp)
                bd_write("Rbd", r, Ex[:], ALU.mult, ["mx", "Ex"])
                X(EW(), "tensor_tensor", ["mx", "Ex"], ["AR"], out=AR[:, :, :, 32:64], in0=v4(r), in1=v4(Ex[:]), op=ALU.mult)
            FEED(2)
            X("scalar", "activation", ["Gw"], ["Ex"], out=Exf, in_=Gwf, func=AF.Exp, scale=-1.0)
            bd_write("Kbd", kh[:], Ex[:], ALU.mult, ["kh", "Ex"])
            bd_write("Bbd", brec[:], Ex[:], ALU.mult, ["brec", "Ex"])
            X(EW(), "tensor_tensor", ["brec", "Ex"], ["Bp"], out=Bp[:], in0=v4(brec[:]), in1=v4(Ex[:]), op=ALU.mult)
            FEED(2)
            GL = v4(Gw[:])[:, :, :, L - 1:L]
            X("scalar", "activation", ["Gw"], ["gL"], out=gL[:].unsqueeze(3), in_=GL, func=AF.Exp)
            X("vector", "tensor_tensor", ["Gw"], ["tB"], out=v4(tB[:]), in0=GL.to_broadcast([128, 4, 4, L]), in1=v4(Gw[:]), op=ALU.subtract)
            X("scalar", "activation", ["tB"], ["Ex"], out=Exf, in_=tB[:].rearrange("p c t -> p (c t)"), func=AF.Exp)
            bd_write("KHbd", kh[:], Ex[:], ALU.mult, ["kh", "Ex"])
            bd_write("BHbd", brec[:], Ex[:], ALU.mult, ["brec", "Ex"])
            bd_write("Vbd", vv[:], None, None, ["vv"])
            if phB:
                X("vector", "tensor_tensor", ["mx", "kh"], ["tA"], out=tA[:], in0=r, in1=kh[:], op=ALU.mult)
                X("gpsimd", "tensor_tensor", ["tA", "pvs"], ["tA"], out=tA[:], in0=tA[:], in1=pbc(l, "rk", 4, T), op=ALU.mult)
                pbn, kbn = PS(4, 0, 512)
                for c in range(4):
                    X("tensor", "matmul", ["tA", "c_ones_bd"], kbn, out=pbn[:, c * T:(c + 1) * T], lhsT=cst["ones_bd"][:], rhs=tA[:, c, :],
                      start=True, stop=True)
                X("vector", "tensor_tensor", kbn + ["vv"], ["bv"], out=m4["bv"][:].rearrange("p c t -> p (c t)"), in0=pbn,
                  in1=vv[:].rearrange("p c t -> p (c t)"), op=ALU.mult)

        QS = [(4, 256), (5, 0), (6, 0)]
        _qs = [0]

        def QSLOT():
            b, o = QS[_qs[0] % 3]
            _qs[0] += 1
            return PS(b, o, 128)

        def mm_evac_copy(lhs, lk, rhs, rk, dst, dk, eng):
            pr, pk = QSLOT()
            X("tensor", "matmul", [lk, rk], pk, out=pr, lhsT=lhs, rhs=rhs, start=True, stop=True)
            if eng == "scalar":
                X("scalar", "activation", pk, [dk], out=dst, in_=pr, func=AF.Copy)
            else:
                X("vector", "tensor_copy", pk, [dk], out=dst, in_=pr)

        def mm_evac_add(lhs, lk, rhs, rk, addend, ak, dst, dk):
            pr, pk = QSLOT()
            X("tensor", "matmul", [lk, rk], pk, out=pr, lhsT=lhs, rhs=rhs, start=True, stop=True)
            X("vector", "tensor_tensor", pk + [ak], [dk], out=dst, in0=pr, in1=addend, op=ALU.add)

        def rwkv_steps(l, phB, q, par):
            NV = 64 if phB else 128
            ncol = 64 if phB else 32

            def kn(n, g):
                return "%s%d" % (n, g)

            def kp(n, g):
                return "%s%d_%d" % (n, par, g)

            def CB(n, g):
                return chb2[n][par][g]

            p1s = [PS(4, 0, 160), PS(5, 0, 160)]

            def st_stage1(g):
                p1, k1 = p1s[g]
                for (lf, rf, rk_, c0, cw) in (("Kbd", AR, "AR", 0, ncol), ("Bbd", AR, "AR", 64, ncol), ("Abd", Bp, "Bp", 128, 32)):
                    for cc in range(2):
                        c = 2 * g + cc
                        rhs = rf[:, c, q, 0:cw] if rk_ == "AR" else rf[:, c, q, :]
                        X("tensor", "matmul", [lf, rk_], k1, out=p1[:, c0:c0 + cw], lhsT=bd[lf][:, c, q, :], rhs=rhs,
                          start=(cc == 0), stop=(cc == 1))

            def st_evac1(g):
                p1, k1 = p1s[g]

                def mask_evac(dst, dkey, col, mname, eng):
                    X(eng, "tensor_tensor", k1 + ["c_" + mname], [dkey], out=dst[:].rearrange("p (b t) -> p b t", t=32),
                      in0=p1[:, col:col + 32].unsqueeze(1).to_broadcast([128, 4, 32]),
                      in1=cst[mname][:].rearrange("p (b t) -> p b t", t=32), op=ALU.mult)
                mask_evac(chb["X1"][g], kn("X1", g), 64, "m_strict", "vector")
                mask_evac(chb["X1T"][g], kn("X1T", g), 128, "m_lower", "vector")
                mask_evac(CB("Aka", g), kp("Aka", g), 0, "m_strict", "vector")
                if phB:
                    mask_evac(CB("Akr", g), kp("Akr", g), 32, "m_incl", "vector")
                    mask_evac(CB("Abr", g), kp("Abr", g), 96, "m_incl", "vector")
                X("gpsimd", "tensor_tensor", [kn("X1", g), "ident_b"], [kp("Ma", g)], out=CB("Ma", g)[:], in0=chb["X1"][g][:], in1=ident_b[:], op=ALU.add)

            def B(n, g):
                if n == "Ma":
                    return CB("Ma", g)[:], kp("Ma", g)
                return chb[n][g][:], kn(n, g)

            def inv_steps(g):
                cp = lambda lh, rh, ds, eng: (lambda: mm_evac_copy(B(lh, g)[0], B(lh, g)[1], B(rh, g)[0], B(rh, g)[1], B(ds, g)[0], B(ds, g)[1], eng))
                ad = lambda lh, rh, ds: (lambda: mm_evac_add(B(lh, g)[0], B(lh, g)[1], B(rh, g)[0], B(rh, g)[1], B(rh, g)[0], B(rh, g)[1], B(ds, g)[0], B(ds, g)[1]))
                return [cp("X1T", "X1", "Xa", "scalar"), cp("X1", "X1T", "XaT", "vector"), ad("XaT", "Ma", "Mb"),
                        cp("XaT", "Xa", "Xb", "scalar"), cp("Xa", "XaT", "XbT", "vector"), ad("XbT", "Mb", "Ma"),
                        cp("XbT", "Xb", "Xa", "scalar"), cp("Xb", "XbT", "XaT", "vector"), ad("XaT", "Ma", "Mb"),
                        cp("Xa", "XaT", "XbT", "vector"), ad("XbT", "Mb", "Ma")]

            KTp, BTp, Vstp = KT2[par], BT2[par], Vst2[par]

            def st_tokmajor(g):
                for cc in range(2):
                    c = 2 * g + cc
                    pk_, kk_ = PSB(c * 128, 128)
                    X("tensor", "transpose", ["KHbd", "ident_b"], kk_, out=pk_, in_=bd["KHbd"][:, c, q, :], identity=ident_b[:])
                    pb_, kb_ = PSB(512 + c * 128, 128)
                    X("tensor", "transpose", ["BHbd", "ident_b"], kb_, out=pb_, in_=bd["BHbd"][:, c, q, :], identity=ident_b[:])
                pv_, kv_ = PS(6, 256 + 64 * g, 64)
                for cc in range(2):
                    c = 2 * g + cc
                    X("tensor", "matmul", ["Vbd", "isel_b"], kv_, out=pv_, lhsT=bd["Vbd"][:, c, q, :], rhs=isel_b[:], start=(cc == 0), stop=(cc == 1))
                pk2, kk2 = PSB(2 * g * 128, 256)
                X("scalar", "activation", kk2, ["KT%d_%d" % (par, 2 * g), "KT%d_%d" % (par, 2 * g + 1)],
                  out=KTp[:, 2 * g:2 * g + 2, :].rearrange("p c k -> p (c k)"), in_=pk2, func=AF.Copy)
                pb2, kb2 = PSB(512 + 2 * g * 128, 256)
                X("vector", "tensor_copy", kb2, ["BT%d_%d" % (par, 2 * g), "BT%d_%d" % (par, 2 * g + 1)],
                  out=BTp[:, 2 * g:2 * g + 2, :].rearrange("p c k -> p (c k)"), in_=pb2)
                X("scalar", "activation", kv_, ["Vst%d_%d" % (par, g)], out=Vstp[:, g, 0:64], in_=pv_, func=AF.Copy)

            def st_W(g):
                pW, kW = PS(0 + g, 0, NV)
                X("tensor", "matmul", [kp("Aka", g), "Vst%d_%d" % (par, g)], kW, out=pW, lhsT=CB("Aka", g)[:], rhs=Vstp[:, g, 0:NV], start=True, stop=False)
                for cc in range(2):
                    c = 2 * g + cc
                    X("tensor", "matmul", ["Abd", "Sb%d" % c], kW, out=pW, lhsT=bd["Abd"][:, c, q, :], rhs=Sb[:, c, 0:NV], start=False, stop=(cc == 1))
                if g == 0:
                    X("scalar", "activation", kW, ["Wt%d" % g], out=Wt[:, g, 0:NV], in_=pW, func=AF.Copy)
                else:
                    X("vector", "tensor_copy", kW, ["Wt%d" % g], out=Wt[:, g, 0:NV], in_=pW)

            def st_U(g):
                pU, kU = PS(2 + g, 0, NV)
                X("tensor", "matmul", [kp("Ma", g), "Wt%d" % g], kU, out=pU, lhsT=CB("Ma", g)[:], rhs=Wt[:, g, 0:NV], start=True, stop=True)
                if g == 0:
                    X("vector", "tensor_copy", kU, ["Ut%d" % g], out=Ut[:, g, 0:NV], in_=pU)
                else:
                    X("scalar", "activation", kU, ["Ut%d" % g], out=Ut[:, g, 0:NV], in_=pU, func=AF.Copy)

            def st_Y(g):
                pY, kY = PS(0 + g, 256, 64)
                X("tensor", "matmul", [kp("Akr", g), "Vst%d_%d" % (par, g)], kY, out=pY, lhsT=CB("Akr", g)[:], rhs=Vstp[:, g, 0:64], start=True, stop=False)
                X("tensor", "matmul", [kp("Abr", g), "Ut%d" % g], kY, out=pY, lhsT=CB("Abr", g)[:], rhs=Ut[:, g, 0:64], start=False, stop=False)
                for cc in range(2):
                    c = 2 * g + cc
                    X("tensor", "matmul", ["Rbd", "Sb%d" % c], kY, out=pY, lhsT=bd["Rbd"][:, c, q, :], rhs=Sb[:, c, 0:64], start=False, stop=(cc == 1))
                X("scalar", "activation", kY, ["Yst"], out=Yst[:, g * 4 + q, :], in_=pY, func=AF.Copy)

            def st_S(c):
                g = c // 2
                pS, kS = PS(2 + (c % 2), 256 * (c // 2), NV)
                X("tensor", "matmul", ["KT%d_%d" % (par, c), "Vst%d_%d" % (par, g)], kS, out=pS, lhsT=KTp[:, c, :], rhs=Vstp[:, g, 0:NV], start=True, stop=False)
                X("tensor", "matmul", ["BT%d_%d" % (par, c), "Ut%d" % g], kS, out=pS, lhsT=BTp[:, c, :], rhs=Ut[:, g, 0:NV], start=False, stop=True)
                X("vector", "scalar_tensor_tensor", kS + ["Sf%d" % c, "gL"], ["Sf%d" % c], out=Sf[:, c, 0:NV], in0=Sf[:, c, 0:NV],
                  scalar=gL[:, c, q:q + 1], in1=pS, op0=ALU.mult, op1=ALU.add)
                X("scalar", "activation", ["Sf%d" % c], ["Sb%d" % c], out=Sb[:, c, 0:NV], in_=Sf[:, c, 0:NV], func=AF.Copy)

            mk = lambda f, a: (lambda: f(a))
            pre = [mk(st_stage1, 0), mk(st_stage1, 1), mk(st_evac1, 0), mk(st_evac1, 1), mk(st_tokmajor, 0), mk(st_tokmajor, 1)]
            i0, i1 = inv_steps(0), inv_steps(1)
            for a_, b_ in zip(i0, i1):
                pre += [a_, b_]
            chain = [mk(st_W, 0), mk(st_W, 1), mk(st_U, 0), mk(st_U, 1)]
            if phB:
                chain += [mk(st_Y, 0), mk(st_Y, 1)]
            chain += [mk(st_S, c) for c in range(4)]
            return pre, chain

        def mixer_chunks(l, phB, col0, before_chunk, after_chunk):
            pre0, _ = rwkv_steps(l, phB, 0, 0)
            for f_ in pre0:
                f_()
            for q in range(4):
                par = q % 2
                before_chunk(q)
                _, chain = rwkv_steps(l, phB, q, par)
                nxt = rwkv_steps(l, phB, q + 1, 1 - par)[0] if q < 3 else []
                hg = hgrn_chunk_parts(l, phB, q, col0)
                hg[0]()
                per = -(-len(nxt) // len(chain)) if nxt else 0
                for ci, cstep in enumerate(chain):
                    cstep()
                    if ci % 2 == 1:
                        FEED(1)
                    for _ in range(per):
                        if nxt:
                            nxt.pop(0)()
                    if ci == 1:
                        hg[1]()
                    if ci == 3:
                        hg[2]()
                while nxt:
                    nxt.pop(0)()
                after_chunk(q)

        def rwkv_post(l, col0):
            s1, s2, mean, var = (gst[n] for n in ["s1", "s2", "mean", "var"])
            X("vector", "tensor_reduce", ["Yst"], ["g_s1"], out=s1[:], in_=Yst[:], axis=AX.X, op=ALU.add)
            X("gpsimd", "tensor_tensor", ["Yst"], ["Ysq"], out=Ysq[:], in0=Yst[:], in1=Yst[:], op=ALU.mult)
            X("vector", "tensor_reduce", ["Ysq"], ["g_s2"], out=s2[:], in_=Ysq[:], axis=AX.X, op=ALU.add)
            X("vector", "tensor_scalar", ["g_s1"], ["g_mean"], out=mean[:], in0=s1[:], scalar1=1.0 / 64, scalar2=None, op0=ALU.mult)
            X("vector", "tensor_tensor", ["g_mean"], ["g_s1"], out=s1[:], in0=mean[:], in1=mean[:], op=ALU.mult)
            X("vector", "scalar_tensor_tensor", ["g_s2", "g_s1"], ["g_var"], out=var[:], in0=s2[:], scalar=1.0 / 64, in1=s1[:],
              op0=ALU.mult, op1=ALU.subtract)
            X("scalar", "activation", ["g_var", "eps_gn"], ["g_var"], out=var[:], in_=var[:], func=AF.Ln, bias=eps_gn[:])
            X("scalar", "activation", ["g_var"], ["g_var"], out=var[:], in_=var[:], func=AF.Exp, scale=-0.5)
            X("vector", "tensor_tensor", ["Yst", "g_mean"], ["Ysq"], out=Ysq[:], in0=Yst[:], in1=mean[:].unsqueeze(2).to_broadcast([128, 8, 64]),
              op=ALU.subtract)
            X("vector", "tensor_tensor", ["Ysq", "g_var"], ["Ysq"], out=Ysq[:], in0=Ysq[:], in1=var[:].unsqueeze(2).to_broadcast([128, 8, 64]),
              op=ALU.mult)
            for g in range(2):
                ys = Ysq[:, g * 4:(g + 1) * 4, :]
                X(EW(), "tensor_tensor", ["Ysq", "lnw"], ["Ysq"], out=ys, in0=ys, in1=lnw[:, l, g, :].unsqueeze(1).to_broadcast([128, 4, 64]),
                  op=ALU.mult)
                X(EW(), "tensor_tensor", ["Ysq", "lnb"], ["Yrep"], out=Yrep[:, g * 4:(g + 1) * 4, :, :],
                  in0=ys.unsqueeze(2).to_broadcast([128, 4, 2, 64]),
                  in1=lnb[:, l, g, :].unsqueeze(1).unsqueeze(1).to_broadcast([128, 4, 2, 64]), op=ALU.add)
            t1 = m4["t1"]
            for g in range(2):
                for q in range(4):
                    pT, kT = QSLOT()
                    X("tensor", "transpose", ["Yrep", "c_ident"], kT, out=pT, in_=Yrep[:, g * 4 + q, :, :].rearrange("p r v -> p (r v)"),
                      identity=cst["ident"][:])
                    for hh in range(2):
                        X("vector", "tensor_tensor", kT + ["bv"], ["t1"], out=t1[hh * 64:(hh + 1) * 64, 2 * g:2 * g + 2, q * L:(q + 1) * L],
                          in0=pT[hh * 64:(hh + 1) * 64, :].rearrange("p (c h t) -> p c h t", c=2, h=2)[:, :, hh, :],
                          in1=m4["bv"][hh * 64:(hh + 1) * 64, 2 * g:2 * g + 2, q * L:(q + 1) * L], op=ALU.add)
            X("gpsimd", "tensor_tensor", ["t1", "gfm"], ["yg_b"], out=yg_b[:, :, col0:col0 + T], in0=t1[:], in1=m4["gfm"][:], op=ALU.mult)

        def hgrn_prep(l, phB, col0):
            f, lf, khh, G2, d1, E2, hA = (h4[n] for n in ["f", "lf", "khh", "G2", "d1", "E2", "hA"])
            fl = lambda t: t[:].rearrange("p c t -> p (c t)")
            sig = hsig[:, :, col0:col0 + T]
            X("vector", "tensor_tensor", ["hsig", "oml"], ["sg"], out=f[:], in0=sig, in1=oml[:, l, :].unsqueeze(2).to_broadcast([128, 4, T]),
              op=ALU.mult)
            X("vector", "tensor_tensor", ["sg", "lbv"], ["sg"], out=f[:], in0=f[:], in1=lbv[:, l, :].unsqueeze(2).to_broadcast([128, 4, T]),
              op=ALU.add)
            X("scalar", "activation", ["sg"], ["aa"], out=fl(lf), in_=fl(f), func=AF.Ln)
            X("gpsimd", "tensor_scalar", ["sg"], ["kkn"], out=fl(khh), in0=fl(f), scalar1=-1.0, scalar2=1.0, op0=ALU.mult, op1=ALU.add)
            X("vector", "tensor_tensor_scan", ["aa", "c_m_reset"], ["Gw"], out=fl(G2), data0=cst["m_reset"][:], data1=fl(lf), initial=0.0,
              op0=ALU.mult, op1=ALU.add)
            GLv = v4(G2[:])[:, :, :, L - 1:L]
            X("scalar", "activation", ["Gw"], ["dL"], out=dL[:].unsqueeze(3), in_=GLv, func=AF.Exp)
            if phB:
                hqv = hq[:, :, col0:col0 + T]
                Gm = v4(G2[:])[:, :, :, L // 2 - 1:L // 2]
                X("vector", "tensor_tensor", ["Gw"], ["tB"], out=v4(d1[:]), in0=v4(G2[:]), in1=Gm.to_broadcast([128, 4, 4, L]), op=ALU.subtract)
                X("scalar", "activation", ["tB"], ["tA"], out=fl(hA), in_=fl(d1), func=AF.Exp)
                X("vector", "tensor_tensor", ["hq", "tA"], ["hb_Qt"], out=hb["Qt"][:], in0=hqv, in1=hA[:], op=ALU.mult)
                X("scalar", "activation", ["tB"], ["tA"], out=fl(hA), in_=fl(d1), func=AF.Exp, scale=-1.0)
                X("gpsimd", "tensor_tensor", ["kkn", "tA"], ["hb_Kt"], out=hb["Kt"][:], in0=khh[:], in1=hA[:], op=ALU.mult)
                X("scalar", "activation", ["Gw"], ["Ex"], out=fl(E2), in_=fl(G2), func=AF.Exp)
                X("vector", "tensor_tensor", ["hq", "Ex"], ["hb_Q2"], out=hb["Q2"][:], in0=hqv, in1=E2[:], op=ALU.mult)
            X("vector", "tensor_tensor", ["Gw"], ["tB"], out=v4(d1[:]), in0=GLv.to_broadcast([128, 4, 4, L]), in1=v4(G2[:]), op=ALU.subtract)
            X("scalar", "activation", ["tB"], ["tA"], out=fl(hA), in_=fl(d1), func=AF.Exp)
            X("gpsimd", "tensor_tensor", ["kkn", "tA"], ["hb_Kh"], out=hb["Kh"][:], in0=khh[:], in1=hA[:], op=ALU.mult)

        def hgrn_chunk_parts(l, phB, q, col0):
            cs = slice(q * L, (q + 1) * L)

            def part_pre():
                if phB:
                    pat, kat = PS(6, 384, 128)
                    for c in range(4):
                        X("tensor", "matmul", ["hb_Kt", "hb_Qt"], kat, out=pat[0:32, c * 32:(c + 1) * 32], lhsT=hb["Kt"][:, c, cs], rhs=hb["Qt"][:, c, cs],
                          start=True, stop=True)
                    X("vector", "tensor_tensor", kat + ["c_m_att"], ["att_b"], out=att_b[:], in0=pat[0:32, :].rearrange("p (c t) -> p c t", c=4),
                      in1=cst["m_att"][:].unsqueeze(1).to_broadcast([32, 4, 32]), op=ALU.mult)
                pvt, kvt = PSB(0, 512)
                pkt, kkt = PSB(512, 512)
                for c in range(4):
                    X("tensor", "transpose", ["hv_b", "ident_b"], kvt, out=pvt[0:32, c * 128:(c + 1) * 128],
                      in_=hv_b[:, c, col0 + q * L: col0 + (q + 1) * L], identity=ident_b[:])
                    X("tensor", "transpose", ["hb_Kh", "ident_b"], kkt, out=pkt[0:32, c * 128:(c + 1) * 128], in_=hb["Kh"][:, c, cs], identity=ident_b[:])
                X("scalar", "activation", kvt, ["VTh"], out=VTh[:].rearrange("p c v -> p (c v)"), in_=pvt[0:32, :], func=AF.Copy)
                X("vector", "tensor_copy", kkt, ["KTh"], out=KTh[:].rearrange("p c v -> p (c v)"), in_=pkt[0:32, :])

            def part_o():
                if phB:
                    po, ko = PS(4, 384, 128)
                    for c in range(4):
                        X("tensor", "matmul", ["Shb", "hb_Q2"], ko, out=po[:, c * 32:(c + 1) * 32], lhsT=Shb[:, c, :], rhs=hb["Q2"][:, c, cs], start=True, stop=False)
                        X("tensor", "matmul", ["VTh", "att_b"], ko, out=po[:, c * 32:(c + 1) * 32], lhsT=VTh[:, c, :], rhs=att_b[:, c, :], start=False, stop=True)
                    X("scalar", "activation", ko, ["h_o"], out=h4["o"][:, :, cs], in_=po.rearrange("p (c t) -> p c t", c=4), func=AF.Copy)

            def part_s():
                pss_, kss_ = PS(1, 0, 512)
                for c in range(4):
                    X("tensor", "matmul", ["KTh", "VTh"], kss_, out=pss_[:, c * 128:(c + 1) * 128], lhsT=KTh[:, c, :], rhs=VTh[:, c, :], start=True, stop=True)
                for c in range(4):
                    X("vector", "scalar_tensor_tensor", kss_ + ["Shf", "dL"], ["Shf"], out=Shf[:, c, :], in0=Shf[:, c, :], scalar=dL[:, c, q:q + 1],
                      in1=pss_[:, c * 128:(c + 1) * 128], op0=ALU.mult, op1=ALU.add)
                X("scalar", "activation", ["Shf"], ["Shb"], out=Shb[:].rearrange("p c v -> p (c v)"), in_=Shf[:].rearrange("p c v -> p (c v)"), func=AF.Copy)
                if not phB:
                    X("gpsimd", "tensor_tensor", ["Dtot", "dL"], ["Dtot"], out=Dtot[:], in0=Dtot[:], in1=dL[:, :, q], op=ALU.mult)
            return [part_pre, part_o, part_s]

        def hgrn_post(l, col0):
            o, o2, rs, hA = (h4[n] for n in ["o", "o2", "rs", "hA"])
            fl = lambda t: t[:].rearrange("p c t -> p (c t)")
            X("gpsimd", "tensor_tensor", ["h_o"], ["kh"], out=o2[:], in0=o[:], in1=o[:], op=ALU.mult)
            pn, kn_ = PS(4, 0, 512)
            for c in range(4):
                X("tensor", "matmul", ["kh", "c_ones_f"], kn_, out=pn[:, c * T:(c + 1) * T], lhsT=cst["ones_f"][:], rhs=o2[:, c, :], start=True, stop=True)
            X("scalar", "activation", kn_ + ["eps_rms"], ["brec"], out=fl(rs), in_=pn, func=AF.Ln, scale=1.0 / 128, bias=eps_rms[:])
            X("scalar", "activation", ["brec"], ["brec"], out=fl(rs), in_=fl(rs), func=AF.Exp, scale=-0.5)
            X("vector", "tensor_tensor", ["h_o", "brec"], ["h_o"], out=o[:], in0=o[:], in1=rs[:], op=ALU.mult)
            X("gpsimd", "tensor_tensor", ["hog", "pvs"], ["tA"], out=hA[:], in0=hog[:, :, col0:col0 + T], in1=pbc(l, "hnw", 4, T), op=ALU.mult)
            X("vector", "tensor_tensor", ["h_o", "tA"], ["ob_b"], out=ob_b[:, :, col0:col0 + T], in0=o[:], in1=hA[:], op=ALU.mult)

        shst = sb("shst", [128, RW_BLK, 4])
        shout = sb("shout", [128, RW_BLK, 4])
        shoutp = sb("shoutp", [128, RW_BLK])
        halo_prev = sb("halo_prev", [128, RW_BLK])
        hraw = sb("hraw", [128, 16])
        hall = sb("hall", [128, 4, 16])
        exb = sb("exb", [128, SW])
        exall = big[:, 0:SW]
        Xr = sb("Xr", [128, 4, 64])
        Xh = sb("Xh", [128, 4, 128])
        PTbd = sb("PTbd", [128, 128])
        lhsTf = sb("lhsTf", [128, 128])
        ftmp = sb("ftmp", [128, 128])
        X("vector", "memset", [], ["PTbd"], ap=PTbd[:], constant=0.0)
        X("vector", "memset", [], ["hraw"], ap=hraw[:], constant=0.0)
        groups4 = [[0, 1, 2, 3], [4, 5, 6, 7]]

        def make_handler(is_s, N):
            def handler(b, pr, pk):
                if b < 15:
                    if is_s:
                        dst = raw[:, b, 0:132].rearrange("p (s t) -> p s t", t=33)[:, :, 1:33]
                        src = pr.rearrange("p (s t) -> p s t", t=32)
                    else:
                        dst, src = raw[:, b, 1:N + 1], pr
                    X("scalar", "activation", pk, ["raw"], out=dst, in_=src, func=AF.Copy)
                elif b < 19:
                    X("scalar", "activation", pk, ["hq"], out=hq[:, b - 15, 0:N], in_=pr, func=AF.Silu)
                elif b < 23:
                    X("scalar", "activation", pk, ["hsig"], out=hsig[:, b - 19, 0:N], in_=pr, func=AF.Sigmoid)
                elif b < 27:
                    X("scalar", "activation", pk, ["hv_b"], out=hv_b[:, b - 23, 0:N], in_=pr, func=AF.Copy)
                elif b < 31:
                    X("scalar", "activation", pk, ["hog"], out=hog[:, b - 27, 0:N], in_=pr, func=AF.Silu)
                else:
                    X("scalar", "activation", pk, ["gates"], out=gates[:, b - 31, 0:N], in_=pr, func=AF.Sigmoid)
            return handler

        class Feeder:
            def __init__(self, l, blocks, N, handler):
                self.l, self.q, self.N, self.h = l, list(blocks), N, handler

            def feed(self, n=2):
                if self.q:
                    take, self.q = self.q[:n], self.q[n:]
                    emit_proj(self.l, take, self.N, self.h)

            def until(self, b):
                while self.q and self.q[0] <= b:
                    self.feed(2)

            def flush(self):
                while self.q:
                    self.feed(2)

        _feeder = [None]

        def FEED(n=2):
            if _feeder[0] is not None:
                _feeder[0].feed(n)

        def xsrc(l):
            return (xT, []) if l == 0 else (xs1, ["xs1"])

        def emit_halo(l):
            if l > 0:
                conv_pump(l, 10 ** 6)
            src, sk = xsrc(l)
            DMA("sync", "x_ld", xt[:, :, 0:1], src[:, :, NPT - 1:NPT], sk, ["xt"], allow_slow_non_contiguous=True)
            emit_norm(l, xt, "xt", 1, "nmix")

            def hh_(b, pr, pk):
                X("scalar", "activation", pk, ["hraw"], out=hraw[:, b:b + 1], in_=pr, func=AF.Copy)
            emit_proj(l, list(range(RW_BLK)), 1, hh_)
            DMA("gpsimd", "ex_h", cin_h[:, :], hraw[:], ["hraw"], ["cin_h"])
            p.op("gpsimd", lambda e: e.collective_compute("AllGather", ALU.bypass, replica_groups=groups4, ins=[cin_h[:, :]], outs=[cout_h[:, :]]),
                 ["cin_h"], ["cout_h"])
            DMA("gpsimd", "ex_h", hall[:], cout_h.rearrange("(r p) c -> p r c", p=128), ["cout_h"], ["hall"])
            X("vector", "tensor_scalar", ["hall", "c_halom"], ["halo_prev"], out=halo_prev[:], in0=hall[:, 0, 0:RW_BLK], scalar1=cst["halom"][:, 0:1],
              scalar2=None, op0=ALU.mult)
            for r in range(1, 4):
                X("vector", "scalar_tensor_tensor", ["hall", "c_halom", "halo_prev"], ["halo_prev"], out=halo_prev[:], in0=hall[:, r, 0:RW_BLK],
                  scalar=cst["halom"][:, r:r + 1], in1=halo_prev[:], op0=ALU.mult, op1=ALU.add)

        def emit_exchange(l):
            conv_pump(l, 10 ** 6)
            X("vector", "tensor_copy", ["Sf0", "Sf1", "Sf2", "Sf3"], ["exb"], out=exb[:, 0:512], in_=Sf[:].rearrange("p c v -> p (c v)"))
            X("vector", "tensor_copy", ["Shf"], ["exb"], out=exb[:, 512:1024], in_=Shf[:].rearrange("p c v -> p (c v)"))
            X("vector", "tensor_copy", ["Dtot"], ["exb"], out=exb[:, 1024:1028], in_=Dtot[:])
            X("vector", "memset", [], ["exb"], ap=exb[:, 1028:SW], constant=0.0)
            DMA("gpsimd", "ex_s", cin_s[:, :], exb[:], ["exb"], ["cin_s"])
            p.op("gpsimd", lambda e: e.collective_compute("AllGather", ALU.bypass, replica_groups=groups4, ins=[cin_s[:, :]], outs=[cout_s[:, :]]),
                 ["cin_s"], ["cout_s"])
            X("vector", "memset", [], ["Xr"], ap=Xr[:], constant=0.0)
            X("vector", "memset", [], ["Xh"], ap=Xh[:], constant=0.0)
            fm = cst["foldm"]
            for r in range(3):
                DMA("gpsimd", "ex_s", exall, cout_s[r * 128:(r + 1) * 128, :], ["cout_s"], ["gates"])
                for c in range(4):
                    for hh in range(2):
                        X("vector", "tensor_copy", ["gates"], ["PTbd"], out=PTbd[hh * 64:(hh + 1) * 64, hh * 64:(hh + 1) * 64],
                          in_=exall[hh * 64:(hh + 1) * 64, c * 128 + 64:c * 128 + 128])
                    pT, kT = QSLOT()
                    X("tensor", "transpose", ["PTbd", "c_ident"], kT, out=pT, in_=PTbd[:], identity=cst["ident"][:])
                    X("vector", "tensor_copy", kT, ["lhsTf"], out=lhsTf[:], in_=pT)
                    pm, km = QSLOT()
                    X("tensor", "matmul", ["lhsTf", "Xr"], km, out=pm[:, 0:64], lhsT=lhsTf[:], rhs=Xr[:, c, :], start=True, stop=True)
                    X("vector", "tensor_tensor", km + ["gates"], ["ftmp"], out=ftmp[:, 0:64], in0=pm[:, 0:64], in1=exall[:, c * 128:c * 128 + 64], op=ALU.add)
                    X("vector", "tensor_tensor", ["ftmp", "Xr"], ["ftmp"], out=ftmp[:, 0:64], in0=ftmp[:, 0:64], in1=Xr[:, c, :], op=ALU.subtract)
                    X("vector", "scalar_tensor_tensor", ["ftmp", "Xr", "c_foldm"], ["Xr"], out=Xr[:, c, :], in0=ftmp[:, 0:64], scalar=fm[:, r:r + 1],
                      in1=Xr[:, c, :], op0=ALU.mult, op1=ALU.add)
                for c in range(4):
                    X("vector", "scalar_tensor_tensor", ["Xh", "gates"], ["ftmp"], out=ftmp[:], in0=Xh[:, c, :], scalar=exall[:, 1024 + c:1025 + c],
                      in1=exall[:, 512 + c * 128:512 + (c + 1) * 128], op0=ALU.mult, op1=ALU.add)
                    X("vector", "tensor_tensor", ["ftmp", "Xh"], ["ftmp"], out=ftmp[:], in0=ftmp[:], in1=Xh[:, c, :], op=ALU.subtract)
                    X("vector", "scalar_tensor_tensor", ["ftmp", "Xh", "c_foldm"], ["Xh"], out=Xh[:, c, :], in0=ftmp[:], scalar=fm[:, r:r + 1],
                      in1=Xh[:, c, :], op0=ALU.mult, op1=ALU.add)

        SFK = ["Sf0", "Sf1", "Sf2", "Sf3"]
        SBK = ["Sb0", "Sb1", "Sb2", "Sb3"]

        def shadows():
            X("scalar", "activation", SFK, SBK, out=Sb[:].rearrange("p c v -> p (c v)"), in_=Sf[:].rearrange("p c v -> p (c v)"), func=AF.Copy)
            X("scalar", "activation", ["Shf"], ["Shb"], out=Shb[:].rearrange("p c v -> p (c v)"), in_=Shf[:].rearrange("p c v -> p (c v)"), func=AF.Copy)

        def init_states_A():
            X("vector", "memset", [], SFK, ap=Sf[:], constant=0.0)
            for c in range(4):
                X("vector", "tensor_copy", ["c_isel"], SFK, out=Sf[:, c, 64:128], in_=cst["isel"][:])
            X("vector", "memset", [], ["Shf"], ap=Shf[:], constant=0.0)
            X("vector", "memset", [], ["Dtot"], ap=Dtot[:], constant=1.0)
            shadows()

        def init_states_B():
            X("vector", "tensor_copy", ["Xr"], SFK, out=Sf[:, :, 0:64], in_=Xr[:])
            X("vector", "tensor_copy", ["Xh"], ["Shf"], out=Shf[:], in_=Xh[:])
            shadows()

        def layer_tile(l, phB, ti):
            is_s = (ti == NPS)
            N = 128 if is_s else W
            tok0 = NPT if is_s else ti * W
            last_prompt = (ti == NPS - 1)
            src, sk = xsrc(l)
            DMA("sync", "x_ld", xt[:, :, 0:N], src[:, :, tok0:tok0 + N], sk, ["xt"])
            emit_norm(l, xt, "xt", N, "nmix")
            if is_s:
                DMA("sync", "sh_ld", shst[:], st_shift[l], [], ["shst"])
                X("vector", "tensor_copy", ["shst"], ["raw"], out=raw[:, :, 0:132].rearrange("p b (s t) -> p b s t", t=33)[:, :, :, 0:1],
                  in_=shst[:].unsqueeze(3))
            blocks = list(range(NBLK)) if phB else (list(range(4, 13)) + list(range(19, 27)))
            fd = Feeder(l, blocks, N, make_handler(is_s, N))
            _feeder[0] = fd
            fd.until(14)
            nb = (0, RW_BLK) if phB else (4, 13)
            for j in range(N // T):
                col0 = j * T
                if is_s:
                    rv = raw[:, nb[0]:nb[1], 0:132].rearrange("p b (s t) -> p b s t", t=33)
                    cur, prv = rv[:, :, :, 1:33], rv[:, :, :, 0:32]
                    mxv = mx[:, nb[0]:nb[1], :].rearrange("p b (s t) -> p b s t", t=32)
                else:
                    cur, prv = raw[:, nb[0]:nb[1], 1 + col0:1 + col0 + T], raw[:, nb[0]:nb[1], col0:col0 + T]
                    mxv = mx[:, nb[0]:nb[1], :]
                rwkv_prep(l, phB, cur, prv, mxv, tok0 + col0, nb)
                fd.until(22)
                hgrn_prep(l, phB, col0)
                fd.until(26)
                def before_chunk(q, l=l, is_s=is_s):
                    if is_s:
                        DMA("sync", "st_ld", Sf[:, :, 0:64], st_rwkv[l, q], [], SFK)
                        DMA("sync", "st_ld", Shf[:], st_hgrn[l, q], [], ["Shf"])
                        shadows()

                def after_chunk(q, l=l, is_s=is_s):
                    if is_s:
                        DMA("gpsimd", "st_out", o_rwkv_s[l, q], Sf[:, :, 0:64], SFK, ["o_rwkv_s"], is_output=True)
                        DMA("gpsimd", "st_out", o_hgrn_s[l, q], Shf[:], ["Shf"], ["o_hgrn_s"], is_output=True)
                conv_pump(l + 1 if phB else l, 6)
                mixer_chunks(l, phB, col0, before_chunk, after_chunk)
                if phB:
                    rwkv_post(l, col0)
                    fd.until(30)
                    hgrn_post(l, col0)
            fd.flush()
            _feeder[0] = None
            if is_s:
                if phB:
                    X("vector", "tensor_copy", ["raw"], ["shout"], out=shout[:].unsqueeze(3),
                      in_=raw[:, :, 0:132].rearrange("p b (s t) -> p b s t", t=33)[:, :, :, 32:33])
                    DMA("gpsimd", "st_out", o_shift_s[l], shout[:], ["shout"], ["o_shift_s"], is_output=True)
            else:
                if phB and last_prompt:
                    X("vector", "tensor_copy", ["raw"], ["shoutp"], out=shoutp[:].unsqueeze(2), in_=raw[:, :, W:W + 1])
                    DMA("gpsimd", "st_out", o_shift_p[l], shoutp[:], ["shoutp"], ["o_shift_p"], is_output=True)
                    DMA("gpsimd", "st_out", o_rwkv_p[l], Sf[:, :, 0:64], SFK, ["o_rwkv_p"], is_output=True)
                    DMA("gpsimd", "st_out", o_hgrn_p[l], Shf[:], ["Shf"], ["o_hgrn_p"], is_output=True)
                X("vector", "tensor_copy", ["raw"], ["raw"], out=raw[:, :, 0:1], in_=raw[:, :, W:W + 1])
            if not phB:
                return
            for o8 in range(8):
                s_ = _wsl[0] % WSL
                _wsl[0] += 1
                DMA("sync", "wsl%d" % s_, wslot[s_][:, 0, :], wb_oab[l, o8], ["wb_o%d" % l], ["wslot%d" % s_])
                sa = _psl[0] % 4
                _psl[0] += 1
                pa_, ka_ = PS(sa, 0, N)
                for c in range(4):
                    X("tensor", "matmul", ["wslot%d" % s_, "yg_b"], ka_, out=pa_, lhsT=wslot[s_][:, 0, c * 128:(c + 1) * 128], rhs=yg_b[:, c, 0:N],
                      start=(c == 0), stop=(c == 3))
                sb_ = _psl[0] % 4
                _psl[0] += 1
                pb_, kb_ = PS(sb_, 0, N)
                for c in range(4):
                    X("tensor", "matmul", ["wslot%d" % s_, "ob_b"], kb_, out=pb_, lhsT=wslot[s_][:, 0, (4 + c) * 128:(5 + c) * 128], rhs=ob_b[:, c, 0:N],
                      start=(c == 0), stop=(c == 3))
                X("vector", "tensor_tensor", ka_ + ["gates"], ["mtmp"], out=mtmp[:, 0:N], in0=pa_, in1=gates[:, o8, 0:N], op=ALU.mult)
                X("vector", "tensor_tensor", kb_ + ["gates"], ["relu0"], out=relu_t[0][:, 0:N], in0=pb_, in1=gates[:, 8 + o8, 0:N], op=ALU.mult)
                X("gpsimd", "tensor_tensor", ["mtmp", "relu0"], ["mixin"], out=mixin[:, o8, 0:N], in0=mtmp[:, 0:N], in1=relu_t[0][:, 0:N], op=ALU.add)
            for o8 in range(8):
                s_ = _wsl[0] % WSL
                _wsl[0] += 1
                DMA("sync", "wsl%d" % s_, wslot[s_][:, 0, :], wb_o[l, o8], ["wb_o%d" % l], ["wslot%d" % s_])
                sm = _psl[0] % 4
                _psl[0] += 1
                pm_, km_ = PS(sm, 0, N)
                for kc in range(KC):
                    X("tensor", "matmul", ["wslot%d" % s_, "mixin"], km_, out=pm_, lhsT=wslot[s_][:, 0, kc * 128:(kc + 1) * 128], rhs=mixin[:, kc, 0:N],
                      start=(kc == 0), stop=(kc == KC - 1))
                X("vector", "tensor_tensor", km_ + ["xt"], ["x1"], out=x1[:, o8, 0:N], in0=pm_, in1=xt[:, o8, 0:N], op=ALU.add)
            DMA("gpsimd", "x1_st", x1s[:, :, tok0:tok0 + N], x1[:, :, 0:N], ["x1"], ["x1s"])

        F_x = aalloc([128, KC, 512], F32, reset=True)
        F_h = aalloc([128, KC, 512], BF16)
        F_sq = [aalloc([128, 512]) for _ in range(2)]
        F_rstd = aalloc([128, 512])
        F_relu = [aalloc([128, 512]) for _ in range(2)]
        F_act = aalloc([128, 16, 512], BF16)
        F_up = [aalloc([128, KC, 256], BF16) for _ in range(2)]
        F_dn = [aalloc([128, 16, 128], BF16) for _ in range(2)]
        _fs = [0, 0]

        def stage_F(l, tok0, N):
            DMA("sync", "f_ld", F_x[:, :, 0:N], x1s[:, :, tok0:tok0 + N], ["x1s"], ["F_x"])
            emit_norm(l, F_x, "F_x", N, "nffn", outbuf=F_h, outkey="F_h", sqk=F_sq, rstd=F_rstd, pfx="F_")
            for h in range(2):
                for fg in range(8):
                    su = _fs[0] % 2
                    _fs[0] += 1
                    DMA("sync", "up%d" % su, F_up[su][:].rearrange("p k c -> p (k c)"), wb_up[l, h * 8 + fg], ["wb_f%d" % l], ["F_up%d" % su])
                    for fb in range(2):
                        pu, ku = PS(4 + (fb % 2), 0, N)
                        for kc in range(KC):
                            X("tensor", "matmul", ["F_up%d" % su, "F_h"], ku, out=pu, lhsT=F_up[su][:, kc, fb * 128:(fb + 1) * 128], rhs=F_h[:, kc, 0:N],
                              start=(kc == 0), stop=(kc == KC - 1))
                        rt = F_relu[fb % 2]
                        X("scalar", "activation", ku, ["F_relu%d" % (fb % 2)], out=rt[:, 0:N], in_=pu, func=AF.Relu)
                        X("gpsimd", "tensor_tensor", ["F_relu%d" % (fb % 2)], ["F_act"], out=F_act[:, fg * 2 + fb, 0:N], in0=rt[:, 0:N], in1=rt[:, 0:N], op=ALU.mult)
                for o8 in range(8):
                    sd = _fs[1] % 2
                    _fs[1] += 1
                    DMA("sync", "dn%d" % sd, F_dn[sd][:].rearrange("p k c -> p (k c)"), wb_dn[l, h * 8 + o8], ["wb_f%d" % l], ["F_dn%d" % sd])
                    pd_, kd_ = PS(o8 % 4, 0, N)
                    for fc in range(16):
                        X("tensor", "matmul", ["F_dn%d" % sd, "F_act"], kd_, out=pd_, lhsT=F_dn[sd][:, fc, :], rhs=F_act[:, fc, 0:N],
                          start=(fc == 0), stop=(fc == 15))
                    X("vector", "tensor_tensor", kd_ + ["F_x"], ["F_x"], out=F_x[:, o8, 0:N], in0=pd_, in1=F_x[:, o8, 0:N], op=ALU.add)
            if l < NL - 1:
                DMA("gpsimd", "x_st", xs1[:, :, tok0:tok0 + N], F_x[:, :, 0:N], ["F_x"], ["xs1"])
            else:
                emit_norm(0, F_x, "F_x", N, "nfin", outbuf=F_x, outkey="F_x", sqk=F_sq, rstd=F_rstd, pfx="F_")
                DMA("gpsimd", "y_st", yT[:, :, tok0:tok0 + N], F_x[:, :, 0:N], ["F_x"], ["yT"], is_output=True)

        _step = [0]

        def step(fn, *a):
            _step[0] += 1
            if cfg.stop is None or _step[0] <= cfg.stop:
                fn(*a)

        def set_prev():
            X("vector", "tensor_copy", ["halo_prev"], ["raw"], out=raw[:, :, 0:1], in_=halo_prev[:].unsqueeze(2))

        for l in range(NL):
            step(emit_halo, l)
            step(set_prev)
            step(init_states_A)
            for ti in range(NPS):
                step(layer_tile, l, False, ti)
            step(emit_exchange, l)
            step(set_prev)
            step(init_states_B)
            GT = max(1, 512 // W)
            for g0 in range(0, NPS, GT):
                g1 = min(NPS, g0 + GT)
                for ti in range(g0, g1):
                    step(layer_tile, l, True, ti)
                step(p.fence)
                step(stage_F, l, g0 * W, (g1 - g0) * W)
                step(p.fence)
            step(layer_tile, l, True, NPS)
            step(p.fence)
            step(stage_F, l, NPT, 128)
            step(p.fence)

        with nc.Block() as block:
            p.emit(block)
    return nc


_NC_CACHE = {}


def _prep_shared(inp):
    f = np.float32
    NL = inp["w_in"].shape[0]
    idx = _win_cols()
    sh = {}
    w_in = _take_cols(np.asarray(inp["w_in"], f), idx)
    sh["w_in_r"] = np.ascontiguousarray(w_in.reshape(NL, KC, 128, NBLK, 128).transpose(0, 3, 2, 1, 4))
    w_up = np.asarray(inp["w_ffn_up"], f)
    sh["w_up_r"] = np.ascontiguousarray(w_up.reshape(NL, KC, 128, 16, 256).transpose(0, 3, 2, 1, 4))
    w_dn = np.asarray(inp["w_ffn_down"], f)
    sh["w_dn_r"] = np.ascontiguousarray(w_dn.reshape(NL, 2, 16, 128, 8, 128).transpose(0, 1, 4, 3, 2, 5).reshape(NL, 16, 128, 16, 128))
    woa = np.asarray(inp["w_out_a"], f).reshape(NL, 4, 128, 8, 128).transpose(0, 3, 2, 1, 4)
    wob = np.asarray(inp["w_out_b"], f).reshape(NL, 4, 128, 8, 128).transpose(0, 3, 2, 1, 4)
    sh["w_oab_r"] = np.ascontiguousarray(np.concatenate([woa, wob], axis=3))
    sh["w_o_r"] = np.ascontiguousarray(np.asarray(inp["w_out"], f).reshape(NL, 8, 128, 8, 128).transpose(0, 3, 2, 1, 4))
    w2 = np.zeros((NL, 128, 512), f)
    w2[:, 0:64] = inp["rwkv_w2"]
    a2 = np.zeros((NL, 128, 512), f)
    a2[:, 64:128] = inp["rwkv_a2"]
    g2 = np.zeros((NL, 256, 512), f)
    g2[:, 0:160] = inp["rwkv_g2"]
    sh["w2pad"], sh["a2pad"] = w2, a2
    sh["g2pad"] = np.ascontiguousarray(g2.reshape(NL, 2, 128, 512).transpose(0, 2, 1, 3))
    sh["vw1"] = np.ascontiguousarray(np.asarray(inp["rwkv_vres_w1"], f)[0].reshape(4, 128, 32).transpose(1, 0, 2))
    sh["vw2"] = np.ascontiguousarray(np.asarray(inp["rwkv_vres_w2"], f)[0])
    pv = np.zeros((NL, 128, NPV), f)
    mu = _take_cols(np.asarray(inp["rwkv_mu"], f), idx[:RW_BLK * 128])
    for l in range(NL):
        pv[l, :, PV["nmix"]:PV["nmix"] + 8] = _pk(np.asarray(inp["norm_mix"], f)[l], 8)
        pv[l, :, PV["nffn"]:PV["nffn"] + 8] = _pk(np.asarray(inp["norm_ffn"], f)[l], 8)
        pv[l, :, PV["mu"]:PV["mu"] + 15] = _pk(mu[l], 15)
        pv[l, :, PV["w0"]:PV["w0"] + 4] = _pk(np.asarray(inp["rwkv_w0"], f)[l], 4)
        pv[l, :, PV["a0"]:PV["a0"] + 4] = _pk(np.asarray(inp["rwkv_a0"], f)[l], 4)
        pv[l, :, PV["v0"]:PV["v0"] + 4] = _pk(np.asarray(inp["rwkv_v0"], f)[0], 4)
        pv[l, :, PV["kk"]:PV["kk"] + 4] = _pk(np.asarray(inp["rwkv_k_k"], f)[l], 4)
        pv[l, :, PV["ka"]:PV["ka"] + 4] = _pk(np.asarray(inp["rwkv_k_a"], f)[l], 4)
        pv[l, :, PV["rk"]:PV["rk"] + 4] = _pk(np.asarray(inp["rwkv_r_k"], f)[l].reshape(-1), 4)
        pv[l, :, PV["hnw"]:PV["hnw"] + 4] = _pk(np.asarray(inp["hgrn_norm_w"], f)[l], 4)
        pv[l, :, PV["lbz"]:PV["lbz"] + 4] = _pk(np.asarray(inp["hgrn_lb_logits"], f)[l], 4)
        pv[l, :, PV["nfin"]:PV["nfin"] + 8] = _pk(np.asarray(inp["norm_final"], f), 8)
    sh["pvec"] = pv
    for nm, key in (("lnw_st", "rwkv_ln_w"), ("lnb_st", "rwkv_ln_b")):
        a = np.asarray(inp[key], f).reshape(NL, 2, 2, 2, 64)
        a = np.broadcast_to(a[:, :, :, :, None, :], (NL, 2, 2, 2, 32, 64))
        sh[nm] = np.ascontiguousarray(a.transpose(0, 2, 3, 4, 1, 5).reshape(NL, 128, 2, 64))
    return sh


def _run(inp, npt, w, dbg=(), stop=None):
    f = np.float32
    cfg = Cfg(npt=npt, w=w, nlayer=int(inp["w_in"].shape[0]), dbg=dbg, stop=stop)
    key = (npt, w, cfg.NL, tuple(dbg), stop)
    if key not in _NC_CACHE:
        _NC_CACHE[key] = build(cfg)
    nc = _NC_CACHE[key]
    NL, NT = cfg.NL, cfg.NT
    sh = _prep_shared(inp)
    xp = np.asarray(inp["x_prompt"], f)
    xs = np.asarray(inp["x_sample"], f)
    idx = _win_cols()
    sshift = _take_cols(np.asarray(inp["state_shift"], f), idx[:RW_BLK * 128])
    srw = np.asarray(inp["state_rwkv"], f)
    shg = np.asarray(inp["state_hgrn"], f)
    in_maps = []
    for c in range(NCORE):
        b, seg = c // 4, c % 4
        xtok = np.concatenate([xp[b, seg * npt:(seg + 1) * npt], xs[4 * c:4 * c + 4].reshape(4 * L, D)], axis=0)
        m = dict(sh)
        m["xT"] = np.ascontiguousarray(xtok.reshape(NT, KC, 128).transpose(2, 1, 0))
        ss = sshift[:, 4 * c:4 * c + 4]
        m["st_shift"] = np.ascontiguousarray(ss.reshape(NL, 4, RW_BLK, 128).transpose(0, 3, 2, 1))
        r = srw[:, 4 * c:4 * c + 4].reshape(NL, 4, 4, 2, 64, 64)
        m["st_rwkv"] = np.ascontiguousarray(r.transpose(0, 1, 3, 5, 2, 4).reshape(NL, 4, 128, 4, 64))
        h = shg[:, 4 * c:4 * c + 4]
        m["st_hgrn"] = np.ascontiguousarray(h.transpose(0, 1, 3, 2, 4))
        for n, v in _consts(seg).items():
            m["c_" + n] = v
        in_maps.append(m)
    res = run_bass_kernel_spmd(nc, in_maps, core_ids=list(range(NCORE)))
    R = res.results
    B = xp.shape[0]
    y_p = np.zeros((B, 4 * npt, D), f)
    y_s = np.zeros((4 * NCORE, L, D), f)
    for c in range(NCORE):
        yt = R[c]["yT"].transpose(2, 1, 0).reshape(NT, D)
        y_p[c // 4, (c % 4) * npt:(c % 4 + 1) * npt] = yt[:npt]
        y_s[4 * c:4 * c + 4] = yt[npt:].reshape(4, L, D)

    def unshift(a):
        a = np.moveaxis(a, -2, -1)
        return a.reshape(a.shape[:-2] + (RW_BLK * 128,))[..., :1824]

    def unrw(a):
        lead = a.shape[:-3]
        a = a.reshape(lead + (2, 64, 4, 64))
        n = len(lead)
        a = a.transpose(tuple(range(n)) + (n + 2, n + 0, n + 3, n + 1))
        return a.reshape(lead + (8, 64, 64))

    def unhg(a):
        n = a.ndim - 3
        return a.transpose(tuple(range(n)) + (n + 1, n + 0, n + 2))

    lastc = [4 * bb + 3 for bb in range(B)]
    shift_p = np.stack([unshift(R[c]["o_shift_p"]) for c in lastc], axis=1)
    rwkv_p = np.stack([unrw(R[c]["o_rwkv_p"]) for c in lastc], axis=1)
    hgrn_p = np.stack([unhg(R[c]["o_hgrn_p"]) for c in lastc], axis=1)
    shift_s = np.concatenate([np.moveaxis(unshift(np.moveaxis(R[c]["o_shift_s"], -1, 1)), 1, 1) for c in range(NCORE)], axis=1)
    rwkv_s = np.concatenate([unrw(R[c]["o_rwkv_s"]) for c in range(NCORE)], axis=1)
    hgrn_s = np.concatenate([unhg(R[c]["o_hgrn_s"]) for c in range(NCORE)], axis=1)
    outs = (y_p, y_s, shift_p, rwkv_p, hgrn_p, shift_s, rwkv_s, hgrn_s)
    return tuple(np.ascontiguousarray(o, dtype=f) for o in outs), R


def kernel(**inputs):
    outs, _ = _run(inputs, 2048, 128)
    return outs
```

```python
import numpy as np
from contextlib import ExitStack
import concourse.bass as bass
import concourse.mybir as mybir
from concourse.bass_utils import run_bass_kernel_spmd

F32 = mybir.dt.float32
BF16 = mybir.dt.bfloat16
AF = mybir.ActivationFunctionType
ALU = mybir.AluOpType
AX = mybir.AxisListType

D = 1024
KC = 8
NCORE = 8
L = 32
NBLK = 47
RW_BLK = 15
DFF = 4096
RMS_EPS = 1e-6
GN_EPS = 64e-5
C0 = -float(np.exp(-0.5))


class Prog:
    ENGS = ["sync", "scalar", "vector", "gpsimd", "tensor"]

    def __init__(self, nc, stack):
        self.nc = nc
        self.stack = stack
        self.ops = {e: [] for e in self.ENGS}
        self.esem = {e: stack.enter_context(nc.semaphore("s_" + e)) for e in self.ENGS}
        self.ecnt = {e: 0 for e in self.ENGS}
        self.dsem = {}
        self.dcnt = {}
        self.writer = {}
        self.readers = {}
        self.waited = {e: {} for e in self.ENGS}
        self.out_tokens = []
        self.pending = {e: {} for e in self.ENGS}

    def _dsem(self, name):
        if name not in self.dsem:
            self.dsem[name] = self.stack.enter_context(self.nc.semaphore("d_" + name))
            self.dcnt[name] = 0
        return self.dsem[name]

    def _deps(self, eng, reads, writes):
        toks = []
        for k in reads:
            w = self.writer.get(k)
            if w is not None:
                toks.append(w)
        for k in writes:
            w = self.writer.get(k)
            if w is not None:
                toks.append(w)
            toks.extend(self.readers.get(k, []))
        need = {}
        for (s, sid, v) in toks:
            if sid == ("e", "tensor") and eng == "tensor":
                continue
            if sid[0] == "e" and sid[1] == eng and eng == "sync":
                continue
            if sid[0] == "d":
                v = max(v, self.dcnt[sid[1]])
            if need.get(sid, (None, -1))[1] < v:
                need[sid] = (s, v)
        for sid, (s, v) in self.pending[eng].items():
            if need.get(sid, (None, -1))[1] < v:
                need[sid] = (s, v)
        self.pending[eng] = {}
        waits = []
        for sid, (s, v) in need.items():
            if self.waited[eng].get(sid, -1) >= v:
                continue
            self.waited[eng][sid] = v
            waits.append((s, v))
        return waits

    def fence(self):
        allt = {}
        for e in self.ENGS:
            if self.ecnt[e] > 0:
                allt[("e", e)] = (self.esem[e], self.ecnt[e])
        for n, c in self.dcnt.items():
            if c > 0:
                allt[("d", n)] = (self.dsem[n], c)
        for e in self.ENGS:
            for sid, sv in allt.items():
                if sid == ("e", e):
                    continue
                self.pending[e][sid] = sv
        self.writer.clear()
        self.readers.clear()

    def _record(self, tok, reads, writes):
        for k in reads:
            self.readers.setdefault(k, []).append(tok)
        for k in writes:
            self.writer[k] = tok
            self.readers[k] = []

    def op(self, eng, fn, reads=(), writes=()):
        pk = [k for k in reads if k.startswith("psbank")]
        if pk:
            reads = [k for k in reads if not k.startswith("psbank")]
            writes = list(writes) + [k for k in pk if k not in writes]
        waits = self._deps(eng, reads, writes)
        self.ecnt[eng] += 1
        tok = (self.esem[eng], ("e", eng), self.ecnt[eng])
        self.ops[eng].append((waits, fn, self.esem[eng], 1))
        self._record(tok, reads, writes)
        return tok

    def dma(self, eng, dname, fn, reads=(), writes=(), is_output=False):
        waits = self._deps(eng, reads, writes)
        s = self._dsem(dname)
        self.dcnt[dname] += 16
        tok = (s, ("d", dname), self.dcnt[dname])
        self.ops[eng].append((waits, fn, s, 16))
        self._record(tok, reads, writes)
        if is_output:
            self.out_tokens.append(tok)
        return tok

    def emit(self, block):
        prog = self

        def mk(ename):
            def body(e):
                for (waits, fn, s, inc) in prog.ops[ename]:
                    for (ws, wv) in waits:
                        e.wait_ge(ws, wv)
                    fn(e).then_inc(s, inc)
                if ename == "gpsimd":
                    last = {}
                    for (s2, sid, v) in prog.out_tokens:
                        if last.get(sid, (None, -1))[1] < v:
                            last[sid] = (s2, v)
                    for sid, (s2, v) in last.items():
                        e.wait_ge(s2, v)
            return body
        block.sync(mk("sync"))
        block.scalar(mk("scalar"))
        block.vector(mk("vector"))
        block.gpsimd(mk("gpsimd"))
        block.tensor(mk("tensor"))


def _win_cols():
    idx = list(range(0, 1664))
    idx += list(range(1664, 1824)) + [-1] * 96
    idx += list(range(1824, 5920))
    assert len(idx) == NBLK * 128
    return np.array(idx)


def _take_cols(a, idx, axis=-1):
    a = np.moveaxis(a, axis, -1)
    out = np.zeros(a.shape[:-1] + (len(idx),), a.dtype)
    m = idx >= 0
    out[..., m] = a[..., idx[m]]
    return np.moveaxis(out, -1, axis)


def _pk(v, nchunk):
    return np.ascontiguousarray(v.reshape(nchunk, 128).T)


def _consts(rank4):
    c = {}
    c["ident"] = np.eye(128, dtype=np.float32)
    ob = np.zeros((128, 128), np.float32)
    ob[:64, :64] = 1
    ob[64:, 64:] = 1
    c["ones_bd"] = ob
    c["ones_f"] = np.ones((128, 128), np.float32)
    isel = np.zeros((128, 64), np.float32)
    isel[np.arange(128), np.arange(128) % 64] = 1
    c["isel"] = isel
    rho = np.arange(128)
    blk, pos = rho // 32, rho % 32
    same = blk[:, None] == blk[None, :]
    c["m_strict"] = (same & (pos[:, None] < pos[None, :])).astype(np.float32)
    c["m_incl"] = (same & (pos[:, None] <= pos[None, :])).astype(np.float32)
    c["m_lower"] = (same & (pos[:, None] > pos[None, :])).astype(np.float32)
    s = np.arange(32)
    c["m_att"] = (s[:, None] <= s[None, :]).astype(np.float32)
    rst = np.ones((128, 4 * 128), np.float32)
    rst[:, ::32] = 0
    c["m_reset"] = rst
    fm = np.zeros((128, 4), np.float32)
    fm[:, :3] = (np.arange(3) < rank4).astype(np.float32)[None, :]
    c["foldm"] = fm
    hm = np.zeros((128, 4), np.float32)
    if rank4 > 0:
        hm[:, rank4 - 1] = 1
    c["halom"] = hm
    return c


class Cfg:
    def __init__(self, npt=2048, w=256, nlayer=2, dbg=(), stop=None):
        self.stop = stop
        self.NPT = npt
        self.W = w
        self.NS = 4
        self.NT = npt + 4 * L
        self.NL = nlayer
        self.dbg = tuple(dbg)


PV = dict(nmix=0, nffn=8, mu=16, w0=31, a0=35, v0=39, kk=43, ka=47, rk=51, hnw=55, lbz=59, nfin=63)
NPV = 71


def build(cfg):
    nc = bass.Bass("TRN2", target_bir_lowering=False)
    NPT, W, NT, NL = cfg.NPT, cfg.W, cfg.NT, cfg.NL
    NPS = NPT // W

    def din(name, shape, dt=F32):
        return nc.dram_tensor(name, list(shape), dt, kind="ExternalInput").ap()

    def dout(name, shape, dt=F32):
        return nc.dram_tensor(name, list(shape), dt, kind="ExternalOutput").ap()

    def dint(name, shape, dt=F32):
        return nc.dram_tensor(name, list(shape), dt, kind="Internal").ap()

    xT = din("xT", [128, KC, NT])
    st_shift = din("st_shift", [NL, 128, RW_BLK, 4])
    st_rwkv = din("st_rwkv", [NL, 4, 128, 4, 64])
    st_hgrn = din("st_hgrn", [NL, 4, 128, 4, 128])
    w_in_r = din("w_in_r", [NL, NBLK, 128, KC, 128])
    w_up_r = din("w_up_r", [NL, 16, 128, KC, 256])
    w_dn_r = din("w_dn_r", [NL, 16, 128, 16, 128])
    w_oab_r = din("w_oab_r", [NL, 8, 128, 8, 128])
    w_o_r = din("w_o_r", [NL, 8, 128, 8, 128])
    w2pad = din("w2pad", [NL, 128, 512])
    a2pad = din("a2pad", [NL, 128, 512])
    g2pad = din("g2pad", [NL, 128, 2, 512])
    vw1 = din("vw1", [128, 4, 32])
    vw2 = din("vw2", [32, 512])
    pvec = din("pvec", [NL, 128, NPV])
    lnw_st = din("lnw_st", [NL, 128, 2, 64])
    lnb_st = din("lnb_st", [NL, 128, 2, 64])
    cnames = ["ident", "ones_bd", "ones_f", "isel", "m_strict", "m_incl", "m_lower", "m_att", "m_reset", "foldm", "halom"]
    cshape = dict(ident=[128, 128], ones_bd=[128, 128], ones_f=[128, 128], isel=[128, 64], m_strict=[128, 128],
                  m_incl=[128, 128], m_lower=[128, 128], m_att=[32, 32], m_reset=[128, 512], foldm=[128, 4], halom=[128, 4])
    cdram = {n: din("c_" + n, cshape[n]) for n in cnames}

    yT = dout("yT", [128, KC, NT])
    o_shift_p = dout("o_shift_p", [NL, 128, RW_BLK])
    o_rwkv_p = dout("o_rwkv_p", [NL, 128, 4, 64])
    o_hgrn_p = dout("o_hgrn_p", [NL, 128, 4, 128])
    o_shift_s = dout("o_shift_s", [NL, 128, RW_BLK, 4])
    o_rwkv_s = dout("o_rwkv_s", [NL, 4, 128, 4, 64])
    o_hgrn_s = dout("o_hgrn_s", [NL, 4, 128, 4, 128])
    dbg_out = {n: dout("dbg_" + n, shp) for (n, shp) in cfg.dbg}

    xs1 = dint("xs1", [128, KC, NT])
    vfirst_d = dint("vfirst_d", [128, 4, NT])
    cin_h = dint("cin_h", [128, 16])
    cout_h = dint("cout_h", [512, 16])
    SW = 4 * 128 + 4 * 128 + 8
    cin_s = dint("cin_s", [128, SW])
    cout_s = dint("cout_s", [512, SW])
    wb_in = dint("wb_in", [NL, NBLK, 128, KC * 128], BF16)
    wb_up = dint("wb_up", [NL, 16, 128, KC * 256], BF16)
    wb_dn = dint("wb_dn", [NL, 16, 128, 16 * 128], BF16)
    x1s = dint("x1s", [128, KC, NT])
    wb_oab = dint("wb_oab", [NL, 8, 128, 8 * 128], BF16)
    wb_o = dint("wb_o", [NL, 8, 128, 8 * 128], BF16)

    with ExitStack() as st:
        p = Prog(nc, st)

        def sb(name, shape, dt=F32):
            return st.enter_context(nc.sbuf_tensor(name, list(shape), dt))

        def X(eng, method, reads, writes, **kw):
            return p.op(eng, lambda e: getattr(e, method)(**kw), reads, writes)

        def DMA(eng, dname, out, in_, reads, writes, is_output=False, **kw):
            return p.dma(eng, dname, lambda e: e.dma_start(out=out, in_=in_, **kw), reads, writes, is_output)

        _rr = [0]

        def EW():
            _rr[0] ^= 1
            return "vector" if _rr[0] else "gpsimd"

        ps = st.enter_context(nc.psum_tensor("ps", [128, 7 * 512], F32))
        psb = st.enter_context(nc.psum_tensor("psb", [128, 1024], BF16))

        def PS(bank, off, n):
            assert off + n <= 512
            keys = ["psbank%d" % bank]
            return ps[:, bank * 512 + off: bank * 512 + off + n], keys

        def PSB(off, n):
            keys = ["psbankB"]
            return psb[:, off:off + n], keys

        cst = {}
        for n in cnames:
            cst[n] = sb("k_" + n, cshape[n])
            DMA("sync", "cst", cst[n][:], cdram[n][:], [], ["c_" + n])
        ident_b = sb("ident_b", [128, 128], BF16)
        isel_b = sb("isel_b", [128, 64], BF16)
        X("vector", "tensor_copy", ["c_ident"], ["ident_b"], out=ident_b[:], in_=cst["ident"][:])
        X("vector", "tensor_copy", ["c_isel"], ["isel_b"], out=isel_b[:], in_=cst["isel"][:])
        eps_rms = sb("eps_rms", [128, 1])
        eps_gn = sb("eps_gn", [128, 1])
        X("vector", "memset", [], ["eps_rms"], ap=eps_rms[:], constant=RMS_EPS)
        X("vector", "memset", [], ["eps_gn"], ap=eps_gn[:], constant=GN_EPS)

        def wcls(b):
            return "a" if b < RW_BLK else ("b" if 19 <= b < 27 else "c")

        def wkey(l, b):
            return "wb_in%d%s" % (l, wcls(b))

        conv_q = {l: [] for l in range(NL)}

        def conv_layer(l):
            order = list(range(0, RW_BLK)) + list(range(19, 27)) + list(range(RW_BLK, 19)) + list(range(27, NBLK))
            mk = lambda *a: (lambda: DMA(*a))
            for b in order:
                conv_q[l].append(mk("gpsimd", "cv_in%d%s" % (l, wcls(b)), wb_in[l, b], w_in_r[l, b].rearrange("p k c -> p (k c)"), [], [wkey(l, b)]))
            for g in range(8):
                conv_q[l].append(mk("gpsimd", "cv_o%d" % l, wb_oab[l, g], w_oab_r[l, g].rearrange("p k c -> p (k c)"), [], ["wb_o%d" % l]))
                conv_q[l].append(mk("gpsimd", "cv_o%d" % l, wb_o[l, g], w_o_r[l, g].rearrange("p k c -> p (k c)"), [], ["wb_o%d" % l]))
            for g in range(16):
                conv_q[l].append(mk("gpsimd", "cv_f%d" % l, wb_up[l, g], w_up_r[l, g].rearrange("p k c -> p (k c)"), [], ["wb_f%d" % l]))
                conv_q[l].append(mk("gpsimd", "cv_f%d" % l, wb_dn[l, g], w_dn_r[l, g].rearrange("p k c -> p (k c)"), [], ["wb_f%d" % l]))

        def conv_pump(l, n):
            if l < NL:
                for _ in range(n):
                    if conv_q[l]:
                        conv_q[l].pop(0)()

        for l in range(NL):
            conv_layer(l)
        conv_pump(0, RW_BLK + 8)

        pvs = sb("pvs", [128, NL, NPV])
        DMA("sync", "cstp", pvs[:], pvec.rearrange("l p n -> p l n"), [], ["pvs"])
        lnw = sb("lnw", [128, NL, 2, 64])
        lnb = sb("lnb", [128, NL, 2, 64])
        DMA("sync", "cstp", lnw[:], lnw_st.rearrange("l p g v -> p l g v"), [], ["lnw"])
        DMA("sync", "cstp", lnb[:], lnb_st.rearrange("l p g v -> p l g v"), [], ["lnb"])
        w2p_b = sb("w2p_b", [128, NL, 512], BF16)
        a2p_b = sb("a2p_b", [128, NL, 512], BF16)
        g2p_b = sb("g2p_b", [128, NL, 2, 512], BF16)
        vw1_b = sb("vw1_b", [128, 4, 32], BF16)
        vw2_b = sb("vw2_b", [32, 512], BF16)
        DMA("gpsimd", "cst2", w2p_b[:], w2pad.rearrange("l p n -> p l n"), [], ["w2p_b"])
        DMA("gpsimd", "cst2", a2p_b[:], a2pad.rearrange("l p n -> p l n"), [], ["a2p_b"])
        DMA("gpsimd", "cst2", g2p_b[:], g2pad.rearrange("l p j n -> p l j n"), [], ["g2p_b"])
        DMA("gpsimd", "cst2", vw1_b[:], vw1[:], [], ["vw1_b"])
        DMA("gpsimd", "cst2", vw2_b[:], vw2[:], [], ["vw2_b"])
        lbe = sb("lbe", [128, NL, 4])
        lbs = sb("lbs", [128, 4])
        lbv = sb("lbv", [128, NL, 4])
        oml = sb("oml", [128, NL, 4])
        X("scalar", "activation", ["pvs"], ["lbe"], out=lbe[:], in_=pvs[:, :, PV["lbz"]:PV["lbz"] + 4], func=AF.Exp)
        X("vector", "tensor_copy", ["lbe"], ["lbs"], out=lbs[:], in_=lbe[:, 0, :])
        for l in range(1, NL):
            X("vector", "tensor_tensor", ["lbe", "lbs"], ["lbs"], out=lbs[:], in0=lbs[:], in1=lbe[:, l, :], op=ALU.add)
        X("vector", "reciprocal", ["lbs"], ["lbs"], out=lbs[:], in_=lbs[:])
        for l in range(NL):
            X("vector", "tensor_tensor", ["lbe", "lbs"], ["lbe"], out=lbe[:, l, :], in0=lbe[:, l, :], in1=lbs[:], op=ALU.mult)
        X("vector", "tensor_tensor", ["lbe"], ["lbv"], out=lbv[:, 0, :], in0=lbe[:, 0, :], in1=lbe[:, 0, :], op=ALU.subtract)
        for l in range(1, NL):
            X("vector", "tensor_tensor", ["lbe", "lbv"], ["lbv"], out=lbv[:, l, :], in0=lbv[:, l - 1, :], in1=lbe[:, l, :], op=ALU.add)
        X("vector", "tensor_scalar", ["lbv"], ["oml"], out=oml[:], in0=lbv[:], scalar1=-1.0, scalar2=1.0, op0=ALU.mult, op1=ALU.add)

        def pcol(l, name, i=0, n=1):
            return pvs[:, l, PV[name] + i: PV[name] + i + n]

        def pbc(l, name, n, T):
            return pvs[:, l, PV[name]: PV[name] + n].unsqueeze(2).to_broadcast([128, n, T])

        WSL = 6
        wslot = [sb("wslot%d" % i, [128, 2, KC * 128], BF16) for i in range(WSL)]
        xt = sb("xt", [128, KC, W])
        x1 = sb("x1", [128, KC, W])
        hT = sb("hT", [128, KC, W], BF16)
        sqk_d = [sb("sqk%d" % i, [128, W]) for i in range(2)]
        rstd_d = sb("rstd", [128, W])
        RAWW = W + 4
        raw = sb("raw", [128, RW_BLK, RAWW])
        X("gpsimd", "memset", [], ["raw"], ap=raw[:], constant=0.0)
        hq = sb("hq", [128, 4, W])
        hsig = sb("hsig", [128, 4, W])
        hv_b = sb("hv_b", [128, 4, W], BF16)
        hog = sb("hog", [128, 4, W])
        big = sb("big", [128, max(16 * W, SW)])
        gates = big[:, 0:16 * W].rearrange("p (j w) -> p j w", w=W)
        yg_b = sb("yg_b", [128, 4, W], BF16)
        ob_b = sb("ob_b", [128, 4, W], BF16)
        mixin = sb("mixin", [128, KC, W], BF16)
        mtmp = sb("mtmp", [128, W])
        relu_t = [sb("relu%d" % i, [128, W]) for i in range(1)]
        _wsl = [0]

        def emit_norm(l, xbuf, xkey, N, pname, outbuf=None, outkey="hT", sqk=None, rstd=None, pfx=""):
            if outbuf is None:
                outbuf = hT
            if sqk is None:
                sqk, rstd = sqk_d, rstd_d
            nps, nkeys = PS(6, 0, N)
            for kc in range(KC):
                sq = sqk[kc % 2]
                X("scalar", "activation", [xkey], [pfx + "sqk%d" % (kc % 2)], out=sq[:, 0:N], in_=xbuf[:, kc, 0:N], func=AF.Square)
                X("tensor", "matmul", [pfx + "sqk%d" % (kc % 2), "c_ones_f"], nkeys, out=nps, lhsT=cst["ones_f"][:], rhs=sq[:, 0:N],
                  start=(kc == 0), stop=(kc == KC - 1))
            X("scalar", "activation", nkeys + ["eps_rms"], [pfx + "rstd"], out=rstd[:, 0:N], in_=nps, func=AF.Ln, scale=1.0 / D, bias=eps_rms[:])
            X("scalar", "activation", [pfx + "rstd"], [pfx + "rstd"], out=rstd[:, 0:N], in_=rstd[:, 0:N], func=AF.Exp, scale=-0.5)
            for kc in range(KC):
                X("vector", "scalar_tensor_tensor", [xkey, pfx + "rstd", "pvs"], [outkey], out=outbuf[:, kc, 0:N], in0=xbuf[:, kc, 0:N],
                  scalar=pcol(l, pname, kc), in1=rstd[:, 0:N], op0=ALU.mult, op1=ALU.mult)

        _psl = [0]

        def emit_proj(l, blocks, N, handler):
            groups = [blocks[i:i + 2] for i in range(0, len(blocks), 2)]
            for grp in groups:
                s = _wsl[0] % WSL
                _wsl[0] += 1
                if len(grp) == 2 and grp[1] == grp[0] + 1:
                    DMA("sync", "wsl%d" % s, wslot[s][:, 0:2, :], wb_in[l, grp[0]:grp[0] + 2].rearrange("j p n -> p j n"),
                        sorted(set([wkey(l, grp[0]), wkey(l, grp[1])])), ["wslot%d" % s])
                else:
                    for j, b in enumerate(grp):
                        DMA("sync", "wsl%d" % s, wslot[s][:, j, :], wb_in[l, b], [wkey(l, b)], ["wslot%d" % s])
                for j, b in enumerate(grp):
                    slot = _psl[0] % 4
                    _psl[0] += 1
                    pr, pk = PS(slot, 0, N)
                    for kc in range(KC):
                        X("tensor", "matmul", ["wslot%d" % s, "hT"], pk, out=pr, lhsT=wslot[s][:, j, kc * 128:(kc + 1) * 128],
                          rhs=hT[:, kc, 0:N], start=(kc == 0), stop=(kc == KC - 1))
                    handler(b, pr, pk)

        T = 128
        AW = 17920
        arena = sb("arena", [128, AW])
        _ao = [0]

        def aalloc(shape, dt=F32, reset=False):
            if reset:
                _ao[0] = 0
            n = int(np.prod(shape[1:]))
            n32 = n if dt == F32 else (n + 1) // 2
            a = _ao[0]
            _ao[0] += n32
            assert _ao[0] <= AW, ("arena overflow", _ao[0])
            v = arena[:, a:a + n32]
            if dt != F32:
                v = v.bitcast(dt)[:, 0:n]
            if len(shape) == 3:
                v = v.rearrange("p (a b) -> p a b", a=shape[1])
            elif len(shape) == 4:
                v = v.rearrange("p (a b c) -> p a b c", a=shape[1], b=shape[2])
            return v[0:shape[0]] if shape[0] < 128 else v

        def sbm(name, shape, dt=F32):
            return aalloc(list(shape), dt)
        mx = sbm("mx", [128, RW_BLK, T])
        lw_in = sbm("lw_in", [128, T], BF16)
        siggl = sbm("siggl", [128, 2, T], BF16)
        names4 = ["sg", "aa", "gfm", "vv", "kkn", "kh", "brec", "Gw", "tA", "tB", "Ex", "bv", "t1", "vf"]
        m4 = {n: sbm("m_" + n, [128, 4, T]) for n in names4}
        vb16 = sbm("vb16", [128, 4, T], BF16)
        t32b = sbm("t32b", [32, T], BF16)
        bdn = ["Kbd", "Bbd", "Abd", "Rbd", "KHbd", "BHbd", "Vbd"]
        bd = {n: sb(n, [128, 4, 4, 128], BF16) for n in bdn}
        for n in bdn:
            X("gpsimd", "memset", [], [n], ap=bd[n][:], constant=0.0)
        AR = sbm("AR", [128, 4, 4, 64], BF16)
        Bp = sbm("Bp", [128, 4, 4, 32], BF16)
        gL = sb("gL", [128, 4, 4])
        chn = ["X1", "X1T", "Mb", "Xa", "XaT", "Xb", "XbT"]
        chb = {n: [sbm("%s%d" % (n, g), [128, 128], BF16) for g in range(2)] for n in chn}
        chb2 = {n: [[sbm("%s_%d_%d" % (n, pp, g), [128, 128], BF16) for g in range(2)] for pp in range(2)] for n in ["Aka", "Akr", "Abr", "Ma"]}
        KT2 = [sbm("KT_%d" % pp, [128, 4, 128], BF16) for pp in range(2)]
        BT2 = [sbm("BT_%d" % pp, [128, 4, 128], BF16) for pp in range(2)]
        Vst2 = [sb("Vst_%d" % pp, [128, 2, 128], BF16) for pp in range(2)]
        for pp in range(2):
            X("gpsimd", "memset", [], ["Vst%d_0" % pp, "Vst%d_1" % pp], ap=Vst2[pp][:], constant=0.0)
        Wt = sbm("Wt", [128, 2, 128], BF16)
        Ut = sbm("Ut", [128, 2, 128], BF16)
        Sf = sb("Sf", [128, 4, 128])
        Sb = sb("Sb", [128, 4, 128], BF16)
        Yst = sbm("Yst", [128, 8, 64])
        Ysq = sbm("Ysq", [128, 8, 64])
        Yrep = sbm("Yrep", [128, 8, 2, 64])
        gst = {n: sb("gst_" + n, [128, 8]) for n in ["s1", "s2", "mean", "var"]}
        h4 = {"o": sbm("h_o", [128, 4, T])}
        for hn_, mn_ in {'f': 'sg', 'lf': 'aa', 'khh': 'kkn', 'G2': 'Gw', 'd1': 'tB', 'E2': 'Ex', 'hA': 'tA', 'o2': 'kh', 'rs': 'brec'}.items():
            h4[hn_] = m4[mn_]
        hb = {n: sbm("hb_" + n, [128, 4, T], BF16) for n in ["Qt", "Q2", "Kt", "Kh"]}
        dL = sb("dL", [128, 4, 4])
        att_b = sbm("att_b", [32, 4, 32], BF16)
        VTh = sbm("VTh", [32, 4, 128], BF16)
        KTh = sbm("KTh", [32, 4, 128], BF16)
        Shf = sb("Shf", [128, 4, 128])
        Shb = sb("Shb", [128, 4, 128], BF16)
        Dtot = sb("Dtot", [128, 4])

        def v4(ap):
            return ap.rearrange("p c (q t) -> p c q t", t=L)

        def bd_write(name, in0, in1, op, keys_r):
            for hh in range(2):
                for cc in range(2):
                    out = bd[name][hh * 64:(hh + 1) * 64, cc::2, :, 64 * cc + 32 * hh: 64 * cc + 32 * hh + 32]
                    a = v4(in0)[hh * 64:(hh + 1) * 64, cc::2]
                    if in1 is None:
                        X(EW(), "tensor_copy", keys_r, [name], out=out, in_=a)
                    else:
                        b = v4(in1)[hh * 64:(hh + 1) * 64, cc::2]
                        X(EW(), "tensor_tensor", keys_r, [name], out=out, in0=a, in1=b, op=op)

        def bc4(ap, shape):
            return ap.to_broadcast(shape)

        def rwkv_prep(l, phB, cur, prv, mxv, tok0, nvalid_blocks):
            b0, b1 = nvalid_blocks
            shp = list(cur.shape)
            mu = pvs[:, l, PV["mu"] + b0: PV["mu"] + b1]
            mu_bc = (mu.unsqueeze(2) if len(shp) == 3 else mu.unsqueeze(2).unsqueeze(3)).to_broadcast(shp)
            X("vector", "tensor_tensor", ["raw"], ["mx"], out=mxv, in0=prv, in1=cur, op=ALU.subtract)
            X("vector", "tensor_tensor", ["mx", "pvs"], ["mx"], out=mxv, in0=mxv, in1=mu_bc, op=ALU.mult)
            X("vector", "tensor_tensor", ["mx", "raw"], ["mx"], out=mxv, in0=mxv, in1=cur, op=ALU.add)
            r, k, v = mx[:, 0:4, :], mx[:, 4:8, :], mx[:, 8:12, :]
            X("scalar", "activation", ["mx"], ["lw_in"], out=lw_in[0:64, :], in_=mx[0:64, 12, :], func=AF.Tanh)
            X("scalar", "activation", ["mx"], ["lw_in"], out=lw_in[64:128, :], in_=mx[64:128, 12, :], func=AF.Copy)
            pw, kw = PS(4, 0, 512)
            pa, ka = PS(5, 0, 512)
            pg, kg = PS(6, 0, 512)
            for c in range(4):
                X("tensor", "matmul", ["lw_in", "w2p_b"], kw, out=pw[:, c * T:(c + 1) * T], lhsT=w2p_b[:, l, c * 128:(c + 1) * 128],
                  rhs=lw_in[:], start=True, stop=True)
                X("tensor", "matmul", ["lw_in", "a2p_b"], ka, out=pa[:, c * T:(c + 1) * T], lhsT=a2p_b[:, l, c * 128:(c + 1) * 128],
                  rhs=lw_in[:], start=True, stop=True)
            if phB:
                X("scalar", "activation", ["mx"], ["siggl"], out=siggl[:], in_=mx[:, 13:15, :], func=AF.Sigmoid)
                for c in range(4):
                    for j in range(2):
                        X("tensor", "matmul", ["siggl", "g2p_b"], kg, out=pg[:, c * T:(c + 1) * T],
                          lhsT=g2p_b[:, l, j, c * 128:(c + 1) * 128], rhs=siggl[:, j, :], start=(j == 0), stop=(j == 1))
            for c in range(4):
                X("scalar", "activation", kw + ["pvs"], ["sg"], out=m4["sg"][:, c, :], in_=pw[:, c * T:(c + 1) * T], func=AF.Sigmoid,
                  bias=pcol(l, "w0", c))
                X("scalar", "activation", ka + ["pvs"], ["aa"], out=m4["aa"][:, c, :], in_=pa[:, c * T:(c + 1) * T], func=AF.Sigmoid,
                  bias=pcol(l, "a0", c))
            if phB:
                X("scalar", "activation", kg, ["gfm"], out=m4["gfm"][:].rearrange("p c t -> p (c t)"), in_=pg, func=AF.Copy)
            FEED(2)
            vv = m4["vv"]
            if l == 0:
                X(EW(), "tensor_copy", ["mx"], ["vv"], out=vv[:], in_=v)
                if phB:
                    DMA("sync", "vf_st", vfirst_d[:, :, tok0:tok0 + T], vv[:], ["vv"], ["vfirst_d"])
            else:
                DMA("sync", "vf_ld", m4["vf"][:], vfirst_d[:, :, tok0:tok0 + T], ["vfirst_d"], ["vf"])
                X("gpsimd", "tensor_copy", ["mx"], ["vb16"], out=vb16[:], in_=v)
                p32, k32 = PS(4, 0, T)
                for c in range(4):
                    X("tensor", "matmul", ["vb16", "vw1_b"], k32, out=p32[0:32, :], lhsT=vw1_b[:, c, :], rhs=vb16[:, c, :],
                      start=(c == 0), stop=(c == 3))
                X("scalar", "activation", k32, ["t32b"], out=t32b[:], in_=p32[0:32, :], func=AF.Copy)
                pv_, kv_ = PS(5, 0, 512)
                for c in range(4):
                    X("tensor", "matmul", ["t32b", "vw2_b"], kv_, out=pv_[:, c * T:(c + 1) * T], lhsT=vw2_b[:, c * 128:(c + 1) * 128],
                      rhs=t32b[:], start=True, stop=True)
                for c in range(4):
                    X("scalar", "activation", kv_ + ["pvs"], ["tA"], out=m4["tA"][:, c, :], in_=pv_[:, c * T:(c + 1) * T],
                      func=AF.Sigmoid, bias=pcol(0, "v0", c))
                X("vector", "tensor_tensor", ["vf", "mx"], ["tB"], out=m4["tB"][:], in0=m4["vf"][:], in1=v, op=ALU.subtract)
                X("vector", "tensor_tensor", ["tB", "tA"], ["tB"], out=m4["tB"][:], in0=m4["tB"][:], in1=m4["tA"][:], op=ALU.mult)
                X("vector", "tensor_tensor", ["tB", "mx"], ["vv"], out=vv[:], in0=m4["tB"][:], in1=v, op=ALU.add)
            kkn, kh, brec, Gw, tA, tB, Ex = (m4[n] for n in ["kkn", "kh", "brec", "Gw", "tA", "tB", "Ex"])
            X("vector", "tensor_tensor", ["mx", "pvs"], ["kkn"], out=kkn[:], in0=k, in1=pbc(l, "kk", 4, T), op=ALU.mult)
            X("gpsimd", "tensor_tensor", ["kkn"], ["tA"], out=tA[:], in0=kkn[:], in1=kkn[:], op=ALU.mult)
            pss, kss = PS(6, 0, 512)
            for c in range(4):
                X("tensor", "matmul", ["tA", "c_ones_bd"], kss, out=pss[:, c * T:(c + 1) * T], lhsT=cst["ones_bd"][:], rhs=tA[:, c, :],
                  start=True, stop=True)
            tAf = tA[:].rearrange("p c t -> p (c t)")
            X("vector", "tensor_scalar", kss, ["tA"], out=tAf, in0=pss, scalar1=1e-24, scalar2=None, op0=ALU.max)
            X("scalar", "activation", ["tA"], ["tA"], out=tAf, in_=tAf, func=AF.Ln)
            X("scalar", "activation", ["tA"], ["tA"], out=tAf, in_=tAf, func=AF.Exp, scale=-0.5)
            X("vector", "tensor_tensor", ["kkn", "tA"], ["kkn"], out=kkn[:], in0=kkn[:], in1=tA[:], op=ALU.mult)
            FEED(2)
            X("vector", "scalar_tensor_tensor", ["aa", "pvs"], ["tB"], out=tB[:], in0=m4["aa"][:], scalar=-1.0, in1=pbc(l, "ka", 4, T),
              op0=ALU.add, op1=ALU.mult)
            X("vector", "scalar_tensor_tensor", ["tB", "mx"], ["kh"], out=kh[:], in0=tB[:], scalar=1.0, in1=k, op0=ALU.add, op1=ALU.mult)
            X("gpsimd", "tensor_tensor", ["kkn", "aa"], ["brec"], out=brec[:], in0=kkn[:], in1=m4["aa"][:], op=ALU.mult)
            sgf = m4["sg"][:].rearrange("p c t -> p (c t)")
            Gwf = Gw[:].rearrange("p c t -> p (c t)")
            X("scalar", "mul", ["sg"], ["sg"], out=sgf, in_=sgf, mul=C0)
            X("vector", "tensor_tensor_scan", ["sg", "c_m_reset"], ["Gw"], out=Gwf, data0=cst["m_reset"][:], data1=sgf, initial=0.0,
              op0=ALU.mult, op1=ALU.add)
            X("gpsimd", "tensor_tensor", ["Gw", "sg"], ["tB"], out=tB[:], in0=Gw[:], in1=m4["sg"][:], op=ALU.subtract)
            Exf = Ex[:].rearrange("p c t -> p (c t)")
            X("scalar", "activation", ["tB"], ["Ex"], out=Exf, in_=tB[:].rearrange("p c t -> p (c t)"), func=AF.Exp)
            X("vector", "scalar_tensor_tensor", ["kkn", "Ex"], ["tA"], out=tA[:], in0=kkn[:], scalar=-1.0, in1=Ex[:],
              op0=ALU.mult, op1=ALU.mult)
            bd_write("Abd", tA[:], None, None, ["tA"])
            X(EW(), "tensor_copy", ["tA"], ["AR"], out=AR[:, :, :, 0:32], in_=v4(tA[:]))
            if phB:
                X("scalar", "activation", ["Gw"], ["Ex"], out=Exf, in_=Gwf, func=AF.Exp)
                bd_write("Rbd", r, Ex[:], ALU.mult, ["mx", "Ex"])
                X(EW(), "tensor_tensor", ["mx", "Ex"], ["AR"], out=AR[:, :, :, 32:64], in0=v4(r), in1=v4(Ex[:]), op=ALU.mult)
            FEED(2)
            X("scalar", "activation", ["Gw"], ["Ex"], out=Exf, in_=Gwf, func=AF.Exp, scale=-1.0)
            bd_write("Kbd", kh[:], Ex[:], ALU.mult, ["kh", "Ex"])
            bd_write("Bbd", brec[:], Ex[:], ALU.mult, ["brec", "Ex"])
            X(EW(), "tensor_tensor", ["brec", "Ex"], ["Bp"], out=Bp[:], in0=v4(brec[:]), in1=v4(Ex[:]), op=ALU.mult)
            FEED(2)
            GL = v4(Gw[:])[:, :, :, L - 1:L]
            X("scalar", "activation", ["Gw"], ["gL"], out=gL[:].unsqueeze(3), in_=GL, func=AF.Exp)
            X("vector", "tensor_tensor", ["Gw"], ["tB"], out=v4(tB[:]), in0=GL.to_broadcast([128, 4, 4, L]), in1=v4(Gw[:]), op=ALU.subtract)
            X("scalar", "activation", ["tB"], ["Ex"], out=Exf, in_=tB[:].rearrange("p c t -> p (c t)"), func=AF.Exp)
            bd_write("KHbd", kh[:], Ex[:], ALU.mult, ["kh", "Ex"])
            bd_write("BHbd", brec[:], Ex[:], ALU.mult, ["brec", "Ex"])
            bd_write("Vbd", vv[:], None, None, ["vv"])
            if phB:
                X("vector", "tensor_tensor", ["mx", "kh"], ["tA"], out=tA[:], in0=r, in1=kh[:], op=ALU.mult)
                X("gpsimd", "tensor_tensor", ["tA", "pvs"], ["tA"], out=tA[:], in0=tA[:], in1=pbc(l, "rk", 4, T), op=ALU.mult)
                pbn, kbn = PS(4, 0, 512)
                for c in range(4):
                    X("tensor", "matmul", ["tA", "c_ones_bd"], kbn, out=pbn[:, c * T:(c + 1) * T], lhsT=cst["ones_bd"][:], rhs=tA[:, c, :],
                      start=True, stop=True)
                X("vector", "tensor_tensor", kbn + ["vv"], ["bv"], out=m4["bv"][:].rearrange("p c t -> p (c t)"), in0=pbn,
                  in1=vv[:].rearrange("p c t -> p (c t)"), op=ALU.mult)

        QS = [(4, 256), (5, 0), (6, 0)]
        _qs = [0]

        def QSLOT():
            b, o = QS[_qs[0] % 3]
            _qs[0] += 1
            return PS(b, o, 128)

        def mm_evac_copy(lhs, lk, rhs, rk, dst, dk, eng):
            pr, pk = QSLOT()
            X("tensor", "matmul", [lk, rk], pk, out=pr, lhsT=lhs, rhs=rhs, start=True, stop=True)
            if eng == "scalar":
                X("scalar", "activation", pk, [dk], out=dst, in_=pr, func=AF.Copy)
            else:
                X("vector", "tensor_copy", pk, [dk], out=dst, in_=pr)

        def mm_evac_add(lhs, lk, rhs, rk, addend, ak, dst, dk):
            pr, pk = QSLOT()
            X("tensor", "matmul", [lk, rk], pk, out=pr, lhsT=lhs, rhs=rhs, start=True, stop=True)
            X("vector", "tensor_tensor", pk + [ak], [dk], out=dst, in0=pr, in1=addend, op=ALU.add)

        def rwkv_steps(l, phB, q, par):
            NV = 64 if phB else 128
            ncol = 64 if phB else 32

            def kn(n, g):
                return "%s%d" % (n, g)

            def kp(n, g):
                return "%s%d_%d" % (n, par, g)

            def CB(n, g):
                return chb2[n][par][g]

            p1s = [PS(4, 0, 160), PS(5, 0, 160)]

            def st_stage1(g):
                p1, k1 = p1s[g]
                for (lf, rf, rk_, c0, cw) in (("Kbd", AR, "AR", 0, ncol), ("Bbd", AR, "AR", 64, ncol), ("Abd", Bp, "Bp", 128, 32)):
                    for cc in range(2):
                        c = 2 * g + cc
                        rhs = rf[:, c, q, 0:cw] if rk_ == "AR" else rf[:, c, q, :]
                        X("tensor", "matmul", [lf, rk_], k1, out=p1[:, c0:c0 + cw], lhsT=bd[lf][:, c, q, :], rhs=rhs,
                          start=(cc == 0), stop=(cc == 1))

            def st_evac1(g):
                p1, k1 = p1s[g]

                def mask_evac(dst, dkey, col, mname, eng):
                    X(eng, "tensor_tensor", k1 + ["c_" + mname], [dkey], out=dst[:].rearrange("p (b t) -> p b t", t=32),
                      in0=p1[:, col:col + 32].unsqueeze(1).to_broadcast([128, 4, 32]),
                      in1=cst[mname][:].rearrange("p (b t) -> p b t", t=32), op=ALU.mult)
                mask_evac(chb["X1"][g], kn("X1", g), 64, "m_strict", "vector")
                mask_evac(chb["X1T"][g], kn("X1T", g), 128, "m_lower", "vector")
                mask_evac(CB("Aka", g), kp("Aka", g), 0, "m_strict", "vector")
                if phB:
                    mask_evac(CB("Akr", g), kp("Akr", g), 32, "m_incl", "vector")
                    mask_evac(CB("Abr", g), kp("Abr", g), 96, "m_incl", "vector")
                X("gpsimd", "tensor_tensor", [kn("X1", g), "ident_b"], [kp("Ma", g)], out=CB("Ma", g)[:], in0=chb["X1"][g][:], in1=ident_b[:], op=ALU.add)

            def B(n, g):
                if n == "Ma":
                    return CB("Ma", g)[:], kp("Ma", g)
                return chb[n][g][:], kn(n, g)

            def inv_steps(g):
                cp = lambda lh, rh, ds, eng: (lambda: mm_evac_copy(B(lh, g)[0], B(lh, g)[1], B(rh, g)[0], B(rh, g)[1], B(ds, g)[0], B(ds, g)[1], eng))
                ad = lambda lh, rh, ds: (lambda: mm_evac_add(B(lh, g)[0], B(lh, g)[1], B(rh, g)[0], B(rh, g)[1], B(rh, g)[0], B(rh, g)[1], B(ds, g)[0], B(ds, g)[1]))
                return [cp("X1T", "X1", "Xa", "scalar"), cp("X1", "X1T", "XaT", "vector"), ad("XaT", "Ma", "Mb"),
                        cp("XaT", "Xa", "Xb", "scalar"), cp("Xa", "XaT", "XbT", "vector"), ad("XbT", "Mb", "Ma"),
                        cp("XbT", "Xb", "Xa", "scalar"), cp("Xb", "XbT", "XaT", "vector"), ad("XaT", "Ma", "Mb"),
                        cp("Xa", "XaT", "XbT", "vector"), ad("XbT", "Mb", "Ma")]

            KTp, BTp, Vstp = KT2[par], BT2[par], Vst2[par]

            def st_tokmajor(g):
                for cc in range(2):
                    c = 2 * g + cc
                    pk_, kk_ = PSB(c * 128, 128)
                    X("tensor", "transpose", ["KHbd", "ident_b"], kk_, out=pk_, in_=bd["KHbd"][:, c, q, :], identity=ident_b[:])
                    pb_, kb_ = PSB(512 + c * 128, 128)
                    X("tensor", "transpose", ["BHbd", "ident_b"], kb_, out=pb_, in_=bd["BHbd"][:, c, q, :], identity=ident_b[:])
                pv_, kv_ = PS(6, 256 + 64 * g, 64)
                for cc in range(2):
                    c = 2 * g + cc
                    X("tensor", "matmul", ["Vbd", "isel_b"], kv_, out=pv_, lhsT=bd["Vbd"][:, c, q, :], rhs=isel_b[:], start=(cc == 0), stop=(cc == 1))
                pk2, kk2 = PSB(2 * g * 128, 256)
                X("scalar", "activation", kk2, ["KT%d_%d" % (par, 2 * g), "KT%d_%d" % (par, 2 * g + 1)],
                  out=KTp[:, 2 * g:2 * g + 2, :].rearrange("p c k -> p (c k)"), in_=pk2, func=AF.Copy)
                pb2, kb2 = PSB(512 + 2 * g * 128, 256)
                X("vector", "tensor_copy", kb2, ["BT%d_%d" % (par, 2 * g), "BT%d_%d" % (par, 2 * g + 1)],
                  out=BTp[:, 2 * g:2 * g + 2, :].rearrange("p c k -> p (c k)"), in_=pb2)
                X("scalar", "activation", kv_, ["Vst%d_%d" % (par, g)], out=Vstp[:, g, 0:64], in_=pv_, func=AF.Copy)

            def st_W(g):
                pW, kW = PS(0 + g, 0, NV)
                X("tensor", "matmul", [kp("Aka", g), "Vst%d_%d" % (par, g)], kW, out=pW, lhsT=CB("Aka", g)[:], rhs=Vstp[:, g, 0:NV], start=True, stop=False)
                for cc in range(2):
                    c = 2 * g + cc
                    X("tensor", "matmul", ["Abd", "Sb%d" % c], kW, out=pW, lhsT=bd["Abd"][:, c, q, :], rhs=Sb[:, c, 0:NV], start=False, stop=(cc == 1))
                if g == 0:
                    X("scalar", "activation", kW, ["Wt%d" % g], out=Wt[:, g, 0:NV], in_=pW, func=AF.Copy)
                else:
                    X("vector", "tensor_copy", kW, ["Wt%d" % g], out=Wt[:, g, 0:NV], in_=pW)

            def st_U(g):
                pU, kU = PS(2 + g, 0, NV)
                X("tensor", "matmul", [kp("Ma", g), "Wt%d" % g], kU, out=pU, lhsT=CB("Ma", g)[:], rhs=Wt[:, g, 0:NV], start=True, stop=True)
                if g == 0:
                    X("vector", "tensor_copy", kU, ["Ut%d" % g], out=Ut[:, g, 0:NV], in_=pU)
                else:
                    X("scalar", "activation", kU, ["Ut%d" % g], out=Ut[:, g, 0:NV], in_=pU, func=AF.Copy)

            def st_Y(g):
                pY, kY = PS(0 + g, 256, 64)
                X("tensor", "matmul", [kp("Akr", g), "Vst%d_%d" % (par, g)], kY, out=pY, lhsT=CB("Akr", g)[:], rhs=Vstp[:, g, 0:64], start=True, stop=False)
                X("tensor", "matmul", [kp("Abr", g), "Ut%d" % g], kY, out=pY, lhsT=CB("Abr", g)[:], rhs=Ut[:, g, 0:64], start=False, stop=False)
                for cc in range(2):
                    c = 2 * g + cc
                    X("tensor", "matmul", ["Rbd", "Sb%d" % c], kY, out=pY, lhsT=bd["Rbd"][:, c, q, :], rhs=Sb[:, c, 0:64], start=False, stop=(cc == 1))
                X("scalar", "activation", kY, ["Yst"], out=Yst[:, g * 4 + q, :], in_=pY, func=AF.Copy)

            def st_S(c):
                g = c // 2
                pS, kS = PS(2 + (c % 2), 256 * (c // 2), NV)
                X("tensor", "matmul", ["KT%d_%d" % (par, c), "Vst%d_%d" % (par, g)], kS, out=pS, lhsT=KTp[:, c, :], rhs=Vstp[:, g, 0:NV], start=True, stop=False)
                X("tensor", "matmul", ["BT%d_%d" % (par, c), "Ut%d" % g], kS, out=pS, lhsT=BTp[:, c, :], rhs=Ut[:, g, 0:NV], start=False, stop=True)
                X("vector", "scalar_tensor_tensor", kS + ["Sf%d" % c, "gL"], ["Sf%d" % c], out=Sf[:, c, 0:NV], in0=Sf[:, c, 0:NV],
                  scalar=gL[:, c, q:q + 1], in1=pS, op0=ALU.mult, op1=ALU.add)
                X("scalar", "activation", ["Sf%d" % c], ["Sb%d" % c], out=Sb[:, c, 0:NV], in_=Sf[:, c, 0:NV], func=AF.Copy)

            mk = lambda f, a: (lambda: f(a))
            pre = [mk(st_stage1, 0), mk(st_stage1, 1), mk(st_evac1, 0), mk(st_evac1, 1), mk(st_tokmajor, 0), mk(st_tokmajor, 1)]
            i0, i1 = inv_steps(0), inv_steps(1)
            for a_, b_ in zip(i0, i1):
                pre += [a_, b_]
            chain = [mk(st_W, 0), mk(st_W, 1), mk(st_U, 0), mk(st_U, 1)]
            if phB:
                chain += [mk(st_Y, 0), mk(st_Y, 1)]
            chain += [mk(st_S, c) for c in range(4)]
            return pre, chain

        def mixer_chunks(l, phB, col0, before_chunk, after_chunk):
            pre0, _ = rwkv_steps(l, phB, 0, 0)
            for f_ in pre0:
                f_()
            for q in range(4):
                par = q % 2
                before_chunk(q)
                _, chain = rwkv_steps(l, phB, q, par)
                nxt = rwkv_steps(l, phB, q + 1, 1 - par)[0] if q < 3 else []
                hg = hgrn_chunk_parts(l, phB, q, col0)
                hg[0]()
                per = -(-len(nxt) // len(chain)) if nxt else 0
                for ci, cstep in enumerate(chain):
                    cstep()
                    if ci % 2 == 1:
                        FEED(1)
                    for _ in range(per):
                        if nxt:
                            nxt.pop(0)()
                    if ci == 1:
                        hg[1]()
                    if ci == 3:
                        hg[2]()
                while nxt:
                    nxt.pop(0)()
                after_chunk(q)

        def rwkv_post(l, col0):
            s1, s2, mean, var = (gst[n] for n in ["s1", "s2", "mean", "var"])
            X("vector", "tensor_reduce", ["Yst"], ["g_s1"], out=s1[:], in_=Yst[:], axis=AX.X, op=ALU.add)
            X("gpsimd", "tensor_tensor", ["Yst"], ["Ysq"], out=Ysq[:], in0=Yst[:], in1=Yst[:], op=ALU.mult)
            X("vector", "tensor_reduce", ["Ysq"], ["g_s2"], out=s2[:], in_=Ysq[:], axis=AX.X, op=ALU.add)
            X("vector", "tensor_scalar", ["g_s1"], ["g_mean"], out=mean[:], in0=s1[:], scalar1=1.0 / 64, scalar2=None, op0=ALU.mult)
            X("vector", "tensor_tensor", ["g_mean"], ["g_s1"], out=s1[:], in0=mean[:], in1=mean[:], op=ALU.mult)
            X("vector", "scalar_tensor_tensor", ["g_s2", "g_s1"], ["g_var"], out=var[:], in0=s2[:], scalar=1.0 / 64, in1=s1[:],
              op0=ALU.mult, op1=ALU.subtract)
            X("scalar", "activation", ["g_var", "eps_gn"], ["g_var"], out=var[:], in_=var[:], func=AF.Ln, bias=eps_gn[:])
            X("scalar", "activation", ["g_var"], ["g_var"], out=var[:], in_=var[:], func=AF.Exp, scale=-0.5)
            X("vector", "tensor_tensor", ["Yst", "g_mean"], ["Ysq"], out=Ysq[:], in0=Yst[:], in1=mean[:].unsqueeze(2).to_broadcast([128, 8, 64]),
              op=ALU.subtract)
            X("vector", "tensor_tensor", ["Ysq", "g_var"], ["Ysq"], out=Ysq[:], in0=Ysq[:], in1=var[:].unsqueeze(2).to_broadcast([128, 8, 64]),
              op=ALU.mult)
            for g in range(2):
                ys = Ysq[:, g * 4:(g + 1) * 4, :]
                X(EW(), "tensor_tensor", ["Ysq", "lnw"], ["Ysq"], out=ys, in0=ys, in1=lnw[:, l, g, :].unsqueeze(1).to_broadcast([128, 4, 64]),
                  op=ALU.mult)
                X(EW(), "tensor_tensor", ["Ysq", "lnb"], ["Yrep"], out=Yrep[:, g * 4:(g + 1) * 4, :, :],
                  in0=ys.unsqueeze(2).to_broadcast([128, 4, 2, 64]),
                  in1=lnb[:, l, g, :].unsqueeze(1).unsqueeze(1).to_broadcast([128, 4, 2, 64]), op=ALU.add)
            t1 = m4["t1"]
            for g in range(2):
                for q in range(4):
                    pT, kT = QSLOT()
                    X("tensor", "transpose", ["Yrep", "c_ident"], kT, out=pT, in_=Yrep[:, g * 4 + q, :, :].rearrange("p r v -> p (r v)"),
                      identity=cst["ident"][:])
                    for hh in range(2):
                        X("vector", "tensor_tensor", kT + ["bv"], ["t1"], out=t1[hh * 64:(hh + 1) * 64, 2 * g:2 * g + 2, q * L:(q + 1) * L],
                          in0=pT[hh * 64:(hh + 1) * 64, :].rearrange("p (c h t) -> p c h t", c=2, h=2)[:, :, hh, :],
                          in1=m4["bv"][hh * 64:(hh + 1) * 64, 2 * g:2 * g + 2, q * L:(q + 1) * L], op=ALU.add)
            X("gpsimd", "tensor_tensor", ["t1", "gfm"], ["yg_b"], out=yg_b[:, :, col0:col0 + T], in0=t1[:], in1=m4["gfm"][:], op=ALU.mult)

        def hgrn_prep(l, phB, col0):
            f, lf, khh, G2, d1, E2, hA = (h4[n] for n in ["f", "lf", "khh", "G2", "d1", "E2", "hA"])
            fl = lambda t: t[:].rearrange("p c t -> p (c t)")
            sig = hsig[:, :, col0:col0 + T]
            X("vector", "tensor_tensor", ["hsig", "oml"], ["sg"], out=f[:], in0=sig, in1=oml[:, l, :].unsqueeze(2).to_broadcast([128, 4, T]),
              op=ALU.mult)
            X("vector", "tensor_tensor", ["sg", "lbv"], ["sg"], out=f[:], in0=f[:], in1=lbv[:, l, :].unsqueeze(2).to_broadcast([128, 4, T]),
              op=ALU.add)
            X("scalar", "activation", ["sg"], ["aa"], out=fl(lf), in_=fl(f), func=AF.Ln)
            X("gpsimd", "tensor_scalar", ["sg"], ["kkn"], out=fl(khh), in0=fl(f), scalar1=-1.0, scalar2=1.0, op0=ALU.mult, op1=ALU.add)
            X("vector", "tensor_tensor_scan", ["aa", "c_m_reset"], ["Gw"], out=fl(G2), data0=cst["m_reset"][:], data1=fl(lf), initial=0.0,
              op0=ALU.mult, op1=ALU.add)
            GLv = v4(G2[:])[:, :, :, L - 1:L]
            X("scalar", "activation", ["Gw"], ["dL"], out=dL[:].unsqueeze(3), in_=GLv, func=AF.Exp)
            if phB:
                hqv = hq[:, :, col0:col0 + T]
                Gm = v4(G2[:])[:, :, :, L // 2 - 1:L // 2]
                X("vector", "tensor_tensor", ["Gw"], ["tB"], out=v4(d1[:]), in0=v4(G2[:]), in1=Gm.to_broadcast([128, 4, 4, L]), op=ALU.subtract)
                X("scalar", "activation", ["tB"], ["tA"], out=fl(hA), in_=fl(d1), func=AF.Exp)
                X("vector", "tensor_tensor", ["hq", "tA"], ["hb_Qt"], out=hb["Qt"][:], in0=hqv, in1=hA[:], op=ALU.mult)
                X("scalar", "activation", ["tB"], ["tA"], out=fl(hA), in_=fl(d1), func=AF.Exp, scale=-1.0)
                X("gpsimd", "tensor_tensor", ["kkn", "tA"], ["hb_Kt"], out=hb["Kt"][:], in0=khh[:], in1=hA[:], op=ALU.mult)
                X("scalar", "activation", ["Gw"], ["Ex"], out=fl(E2), in_=fl(G2), func=AF.Exp)
                X("vector", "tensor_tensor", ["hq", "Ex"], ["hb_Q2"], out=hb["Q2"][:], in0=hqv, in1=E2[:], op=ALU.mult)
            X("vector", "tensor_tensor", ["Gw"], ["tB"], out=v4(d1[:]), in0=GLv.to_broadcast([128, 4, 4, L]), in1=v4(G2[:]), op=ALU.subtract)
            X("scalar", "activation", ["tB"], ["tA"], out=fl(hA), in_=fl(d1), func=AF.Exp)
            X("gpsimd", "tensor_tensor", ["kkn", "tA"], ["hb_Kh"], out=hb["Kh"][:], in0=khh[:], in1=hA[:], op=ALU.mult)

        def hgrn_chunk_parts(l, phB, q, col0):
            cs = slice(q * L, (q + 1) * L)

            def part_pre():
                if phB:
                    pat, kat = PS(6, 384, 128)
                    for c in range(4):
                        X("tensor", "matmul", ["hb_Kt", "hb_Qt"], kat, out=pat[0:32, c * 32:(c + 1) * 32], lhsT=hb["Kt"][:, c, cs], rhs=hb["Qt"][:, c, cs],
                          start=True, stop=True)
                    X("vector", "tensor_tensor", kat + ["c_m_att"], ["att_b"], out=att_b[:], in0=pat[0:32, :].rearrange("p (c t) -> p c t", c=4),
                      in1=cst["m_att"][:].unsqueeze(1).to_broadcast([32, 4, 32]), op=ALU.mult)
                pvt, kvt = PSB(0, 512)
                pkt, kkt = PSB(512, 512)
                for c in range(4):
                    X("tensor", "transpose", ["hv_b", "ident_b"], kvt, out=pvt[0:32, c * 128:(c + 1) * 128],
                      in_=hv_b[:, c, col0 + q * L: col0 + (q + 1) * L], identity=ident_b[:])
                    X("tensor", "transpose", ["hb_Kh", "ident_b"], kkt, out=pkt[0:32, c * 128:(c + 1) * 128], in_=hb["Kh"][:, c, cs], identity=ident_b[:])
                X("scalar", "activation", kvt, ["VTh"], out=VTh[:].rearrange("p c v -> p (c v)"), in_=pvt[0:32, :], func=AF.Copy)
                X("vector", "tensor_copy", kkt, ["KTh"], out=KTh[:].rearrange("p c v -> p (c v)"), in_=pkt[0:32, :])

            def part_o():
                if phB:
                    po, ko = PS(4, 384, 128)
                    for c in range(4):
                        X("tensor", "matmul", ["Shb", "hb_Q2"], ko, out=po[:, c * 32:(c + 1) * 32], lhsT=Shb[:, c, :], rhs=hb["Q2"][:, c, cs], start=True, stop=False)
                        X("tensor", "matmul", ["VTh", "att_b"], ko, out=po[:, c * 32:(c + 1) * 32], lhsT=VTh[:, c, :], rhs=att_b[:, c, :], start=False, stop=True)
                    X("scalar", "activation", ko, ["h_o"], out=h4["o"][:, :, cs], in_=po.rearrange("p (c t) -> p c t", c=4), func=AF.Copy)

            def part_s():
                pss_, kss_ = PS(1, 0, 512)
                for c in range(4):
                    X("tensor", "matmul", ["KTh", "VTh"], kss_, out=pss_[:, c * 128:(c + 1) * 128], lhsT=KTh[:, c, :], rhs=VTh[:, c, :], start=True, stop=True)
                for c in range(4):
                    X("vector", "scalar_tensor_tensor", kss_ + ["Shf", "dL"], ["Shf"], out=Shf[:, c, :], in0=Shf[:, c, :], scalar=dL[:, c, q:q + 1],
                      in1=pss_[:, c * 128:(c + 1) * 128], op0=ALU.mult, op1=ALU.add)
                X("scalar", "activation", ["Shf"], ["Shb"], out=Shb[:].rearrange("p c v -> p (c v)"), in_=Shf[:].rearrange("p c v -> p (c v)"), func=AF.Copy)
                if not phB:
                    X("gpsimd", "tensor_tensor", ["Dtot", "dL"], ["Dtot"], out=Dtot[:], in0=Dtot[:], in1=dL[:, :, q], op=ALU.mult)
            return [part_pre, part_o, part_s]

        def hgrn_post(l, col0):
            o, o2, rs, hA = (h4[n] for n in ["o", "o2", "rs", "hA"])
            fl = lambda t: t[:].rearrange("p c t -> p (c t)")
            X("gpsimd", "tensor_tensor", ["h_o"], ["kh"], out=o2[:], in0=o[:], in1=o[:], op=ALU.mult)
            pn, kn_ = PS(4, 0, 512)
            for c in range(4):
                X("tensor", "matmul", ["kh", "c_ones_f"], kn_, out=pn[:, c * T:(c + 1) * T], lhsT=cst["ones_f"][:], rhs=o2[:, c, :], start=True, stop=True)
            X("scalar", "activation", kn_ + ["eps_rms"], ["brec"], out=fl(rs), in_=pn, func=AF.Ln, scale=1.0 / 128, bias=eps_rms[:])
            X("scalar", "activation", ["brec"], ["brec"], out=fl(rs), in_=fl(rs), func=AF.Exp, scale=-0.5)
            X("vector", "tensor_tensor", ["h_o", "brec"], ["h_o"], out=o[:], in0=o[:], in1=rs[:], op=ALU.mult)
            X("gpsimd", "tensor_tensor", ["hog", "pvs"], ["tA"], out=hA[:], in0=hog[:, :, col0:col0 + T], in1=pbc(l, "hnw", 4, T), op=ALU.mult)
            X("vector", "tensor_tensor", ["h_o", "tA"], ["ob_b"], out=ob_b[:, :, col0:col0 + T], in0=o[:], in1=hA[:], op=ALU.mult)

        shst = sb("shst", [128, RW_BLK, 4])
        shout = sb("shout", [128, RW_BLK, 4])
        shoutp = sb("shoutp", [128, RW_BLK])
        halo_prev = sb("halo_prev", [128, RW_BLK])
        hraw = sb("hraw", [128, 16])
        hall = sb("hall", [128, 4, 16])
        exb = sb("exb", [128, SW])
        exall = big[:, 0:SW]
        Xr = sb("Xr", [128, 4, 64])
        Xh = sb("Xh", [128, 4, 128])
        PTbd = sb("PTbd", [128, 128])
        lhsTf = sb("lhsTf", [128, 128])
        ftmp = sb("ftmp", [128, 128])
        X("vector", "memset", [], ["PTbd"], ap=PTbd[:], constant=0.0)
        X("vector", "memset", [], ["hraw"], ap=hraw[:], constant=0.0)
        groups4 = [[0, 1, 2, 3], [4, 5, 6, 7]]

        def make_handler(is_s, N):
            def handler(b, pr, pk):
                if b < 15:
                    if is_s:
                        dst = raw[:, b, 0:132].rearrange("p (s t) -> p s t", t=33)[:, :, 1:33]
                        src = pr.rearrange("p (s t) -> p s t", t=32)
                    else:
                        dst, src = raw[:, b, 1:N + 1], pr
                    X("scalar", "activation", pk, ["raw"], out=dst, in_=src, func=AF.Copy)
                elif b < 19:
                    X("scalar", "activation", pk, ["hq"], out=hq[:, b - 15, 0:N], in_=pr, func=AF.Silu)
                elif b < 23:
                    X("scalar", "activation", pk, ["hsig"], out=hsig[:, b - 19, 0:N], in_=pr, func=AF.Sigmoid)
                elif b < 27:
                    X("scalar", "activation", pk, ["hv_b"], out=hv_b[:, b - 23, 0:N], in_=pr, func=AF.Copy)
                elif b < 31:
                    X("scalar", "activation", pk, ["hog"], out=hog[:, b - 27, 0:N], in_=pr, func=AF.Silu)
                else:
                    X("scalar", "activation", pk, ["gates"], out=gates[:, b - 31, 0:N], in_=pr, func=AF.Sigmoid)
            return handler

        class Feeder:
            def __init__(self, l, blocks, N, handler):
                self.l, self.q, self.N, self.h = l, list(blocks), N, handler

            def feed(self, n=2):
                if self.q:
                    take, self.q = self.q[:n], self.q[n:]
                    emit_proj(self.l, take, self.N, self.h)

            def until(self, b):
                while self.q and self.q[0] <= b:
                    self.feed(2)

            def flush(self):
                while self.q:
                    self.feed(2)

        _feeder = [None]

        def FEED(n=2):
            if _feeder[0] is not None:
                _feeder[0].feed(n)

        def xsrc(l):
            return (xT, []) if l == 0 else (xs1, ["xs1"])

        def emit_halo(l):
            if l > 0:
                conv_pump(l, 10 ** 6)
            src, sk = xsrc(l)
            DMA("sync", "x_ld", xt[:, :, 0:1], src[:, :, NPT - 1:NPT], sk, ["xt"], allow_slow_non_contiguous=True)
            emit_norm(l, xt, "xt", 1, "nmix")

            def hh_(b, pr, pk):
                X("scalar", "activation", pk, ["hraw"], out=hraw[:, b:b + 1], in_=pr, func=AF.Copy)
            emit_proj(l, list(range(RW_BLK)), 1, hh_)
            DMA("gpsimd", "ex_h", cin_h[:, :], hraw[:], ["hraw"], ["cin_h"])
            p.op("gpsimd", lambda e: e.collective_compute("AllGather", ALU.bypass, replica_groups=groups4, ins=[cin_h[:, :]], outs=[cout_h[:, :]]),
                 ["cin_h"], ["cout_h"])
            DMA("gpsimd", "ex_h", hall[:], cout_h.rearrange("(r p) c -> p r c", p=128), ["cout_h"], ["hall"])
            X("vector", "tensor_scalar", ["hall", "c_halom"], ["halo_prev"], out=halo_prev[:], in0=hall[:, 0, 0:RW_BLK], scalar1=cst["halom"][:, 0:1],
              scalar2=None, op0=ALU.mult)
            for r in range(1, 4):
                X("vector", "scalar_tensor_tensor", ["hall", "c_halom", "halo_prev"], ["halo_prev"], out=halo_prev[:], in0=hall[:, r, 0:RW_BLK],
                  scalar=cst["halom"][:, r:r + 1], in1=halo_prev[:], op0=ALU.mult, op1=ALU.add)

        def emit_exchange(l):
            conv_pump(l, 10 ** 6)
            X("vector", "tensor_copy", ["Sf0", "Sf1", "Sf2", "Sf3"], ["exb"], out=exb[:, 0:512], in_=Sf[:].rearrange("p c v -> p (c v)"))
            X("vector", "tensor_copy", ["Shf"], ["exb"], out=exb[:, 512:1024], in_=Shf[:].rearrange("p c v -> p (c v)"))
            X("vector", "tensor_copy", ["Dtot"], ["exb"], out=exb[:, 1024:1028], in_=Dtot[:])
            X("vector", "memset", [], ["exb"], ap=exb[:, 1028:SW], constant=0.0)
            DMA("gpsimd", "ex_s", cin_s[:, :], exb[:], ["exb"], ["cin_s"])
            p.op("gpsimd", lambda e: e.collective_compute("AllGather", ALU.bypass, replica_groups=groups4, ins=[cin_s[:, :]], outs=[cout_s[:, :]]),
                 ["cin_s"], ["cout_s"])
            X("vector", "memset", [], ["Xr"], ap=Xr[:], constant=0.0)
            X("vector", "memset", [], ["Xh"], ap=Xh[:], constant=0.0)
            fm = cst["foldm"]
            for r in range(3):
                DMA("gpsimd", "ex_s", exall, cout_s[r * 128:(r + 1) * 128, :], ["cout_s"], ["gates"])
                for c in range(4):
                    for hh in range(2):
                        X("vector", "tensor_copy", ["gates"], ["PTbd"], out=PTbd[hh * 64:(hh + 1) * 64, hh * 64:(hh + 1) * 64],
                          in_=exall[hh * 64:(hh + 1) * 64, c * 128 + 64:c * 128 + 128])
                    pT, kT = QSLOT()
                    X("tensor", "transpose", ["PTbd", "c_ident"], kT, out=pT, in_=PTbd[:], identity=cst["ident"][:])
                    X("vector", "tensor_copy", kT, ["lhsTf"], out=lhsTf[:], in_=pT)
                    pm, km = QSLOT()
                    X("tensor", "matmul", ["lhsTf", "Xr"], km, out=pm[:, 0:64], lhsT=lhsTf[:], rhs=Xr[:, c, :], start=True, stop=True)
                    X("vector", "tensor_tensor", km + ["gates"], ["ftmp"], out=ftmp[:, 0:64], in0=pm[:, 0:64], in1=exall[:, c * 128:c * 128 + 64], op=ALU.add)
                    X("vector", "tensor_tensor", ["ftmp", "Xr"], ["ftmp"], out=ftmp[:, 0:64], in0=ftmp[:, 0:64], in1=Xr[:, c, :], op=ALU.subtract)
                    X("vector", "scalar_tensor_tensor", ["ftmp", "Xr", "c_foldm"], ["Xr"], out=Xr[:, c, :], in0=ftmp[:, 0:64], scalar=fm[:, r:r + 1],
                      in1=Xr[:, c, :], op0=ALU.mult, op1=ALU.add)
                for c in range(4):
                    X("vector", "scalar_tensor_tensor", ["Xh", "gates"], ["ftmp"], out=ftmp[:], in0=Xh[:, c, :], scalar=exall[:, 1024 + c:1025 + c],
                      in1=exall[:, 512 + c * 128:512 + (c + 1) * 128], op0=ALU.mult, op1=ALU.add)
                    X("vector", "tensor_tensor", ["ftmp", "Xh"], ["ftmp"], out=ftmp[:], in0=ftmp[:], in1=Xh[:, c, :], op=ALU.subtract)
                    X("vector", "scalar_tensor_tensor", ["ftmp", "Xh", "c_foldm"], ["Xh"], out=Xh[:, c, :], in0=ftmp[:], scalar=fm[:, r:r + 1],
                      in1=Xh[:, c, :], op0=ALU.mult, op1=ALU.add)

        SFK = ["Sf0", "Sf1", "Sf2", "Sf3"]
        SBK = ["Sb0", "Sb1", "Sb2", "Sb3"]

        def shadows():
            X("scalar", "activation", SFK, SBK, out=Sb[:].rearrange("p c v -> p (c v)"), in_=Sf[:].rearrange("p c v -> p (c v)"), func=AF.Copy)
            X("scalar", "activation", ["Shf"], ["Shb"], out=Shb[:].rearrange("p c v -> p (c v)"), in_=Shf[:].rearrange("p c v -> p (c v)"), func=AF.Copy)

        def init_states_A():
            X("vector", "memset", [], SFK, ap=Sf[:], constant=0.0)
            for c in range(4):
                X("vector", "tensor_copy", ["c_isel"], SFK, out=Sf[:, c, 64:128], in_=cst["isel"][:])
            X("vector", "memset", [], ["Shf"], ap=Shf[:], constant=0.0)
            X("vector", "memset", [], ["Dtot"], ap=Dtot[:], constant=1.0)
            shadows()

        def init_states_B():
            X("vector", "tensor_copy", ["Xr"], SFK, out=Sf[:, :, 0:64], in_=Xr[:])
            X("vector", "tensor_copy", ["Xh"], ["Shf"], out=Shf[:], in_=Xh[:])
            shadows()

        def layer_tile(l, phB, ti):
            conv_pump(l + 1 if phB else l, 8)
            is_s = (ti == NPS)
            N = 128 if is_s else W
            tok0 = NPT if is_s else ti * W
            last_prompt = (ti == NPS - 1)
            src, sk = xsrc(l)
            DMA("sync", "x_ld", xt[:, :, 0:N], src[:, :, tok0:tok0 + N], sk, ["xt"])
            emit_norm(l, xt, "xt", N, "nmix")
            if is_s:
                DMA("sync", "sh_ld", shst[:], st_shift[l], [], ["shst"])
                X("vector", "tensor_copy", ["shst"], ["raw"], out=raw[:, :, 0:132].rearrange("p b (s t) -> p b s t", t=33)[:, :, :, 0:1],
                  in_=shst[:].unsqueeze(3))
            blocks = list(range(NBLK)) if phB else (list(range(4, 13)) + list(range(19, 27)))
            fd = Feeder(l, blocks, N, make_handler(is_s, N))
            _feeder[0] = fd
            fd.until(14)
            nb = (0, RW_BLK) if phB else (4, 13)
            for j in range(N // T):
                col0 = j * T
                if is_s:
                    rv = raw[:, nb[0]:nb[1], 0:132].rearrange("p b (s t) -> p b s t", t=33)
                    cur, prv = rv[:, :, :, 1:33], rv[:, :, :, 0:32]
                    mxv = mx[:, nb[0]:nb[1], :].rearrange("p b (s t) -> p b s t", t=32)
                else:
                    cur, prv = raw[:, nb[0]:nb[1], 1 + col0:1 + col0 + T], raw[:, nb[0]:nb[1], col0:col0 + T]
                    mxv = mx[:, nb[0]:nb[1], :]
                rwkv_prep(l, phB, cur, prv, mxv, tok0 + col0, nb)
                fd.until(22)
                hgrn_prep(l, phB, col0)
                fd.until(26)
                def before_chunk(q, l=l, is_s=is_s):
                    if is_s:
                        DMA("sync", "st_ld", Sf[:, :, 0:64], st_rwkv[l, q], [], SFK)
                        DMA("sync", "st_ld", Shf[:], st_hgrn[l, q], [], ["Shf"])
                        shadows()

                def after_chunk(q, l=l, is_s=is_s):
                    if is_s:
                        DMA("gpsimd", "st_out", o_rwkv_s[l, q], Sf[:, :, 0:64], SFK, ["o_rwkv_s"], is_output=True)
                        DMA("gpsimd", "st_out", o_hgrn_s[l, q], Shf[:], ["Shf"], ["o_hgrn_s"], is_output=True)
                mixer_chunks(l, phB, col0, before_chunk, after_chunk)
                if phB:
                    rwkv_post(l, col0)
                    fd.until(30)
                    hgrn_post(l, col0)
            fd.flush()
            _feeder[0] = None
            if is_s:
                if phB:
                    X("vector", "tensor_copy", ["raw"], ["shout"], out=shout[:].unsqueeze(3),
                      in_=raw[:, :, 0:132].rearrange("p b (s t) -> p b s t", t=33)[:, :, :, 32:33])
                    DMA("gpsimd", "st_out", o_shift_s[l], shout[:], ["shout"], ["o_shift_s"], is_output=True)
            else:
                if phB and last_prompt:
                    X("vector", "tensor_copy", ["raw"], ["shoutp"], out=shoutp[:].unsqueeze(2), in_=raw[:, :, W:W + 1])
                    DMA("gpsimd", "st_out", o_shift_p[l], shoutp[:], ["shoutp"], ["o_shift_p"], is_output=True)
                    DMA("gpsimd", "st_out", o_rwkv_p[l], Sf[:, :, 0:64], SFK, ["o_rwkv_p"], is_output=True)
                    DMA("gpsimd", "st_out", o_hgrn_p[l], Shf[:], ["Shf"], ["o_hgrn_p"], is_output=True)
                X("vector", "tensor_copy", ["raw"], ["raw"], out=raw[:, :, 0:1], in_=raw[:, :, W:W + 1])
            if not phB:
                return
            for o8 in range(8):
                s_ = _wsl[0] % WSL
                _wsl[0] += 1
                DMA("sync", "wsl%d" % s_, wslot[s_][:, 0, :], wb_oab[l, o8], ["wb_o%d" % l], ["wslot%d" % s_])
                sa = _psl[0] % 4
                _psl[0] += 1
                pa_, ka_ = PS(sa, 0, N)
                for c in range(4):
                    X("tensor", "matmul", ["wslot%d" % s_, "yg_b"], ka_, out=pa_, lhsT=wslot[s_][:, 0, c * 128:(c + 1) * 128], rhs=yg_b[:, c, 0:N],
                      start=(c == 0), stop=(c == 3))
                sb_ = _psl[0] % 4
                _psl[0] += 1
                pb_, kb_ = PS(sb_, 0, N)
                for c in range(4):
                    X("tensor", "matmul", ["wslot%d" % s_, "ob_b"], kb_, out=pb_, lhsT=wslot[s_][:, 0, (4 + c) * 128:(5 + c) * 128], rhs=ob_b[:, c, 0:N],
                      start=(c == 0), stop=(c == 3))
                X("vector", "tensor_tensor", ka_ + ["gates"], ["mtmp"], out=mtmp[:, 0:N], in0=pa_, in1=gates[:, o8, 0:N], op=ALU.mult)
                X("vector", "tensor_tensor", kb_ + ["gates"], ["relu0"], out=relu_t[0][:, 0:N], in0=pb_, in1=gates[:, 8 + o8, 0:N], op=ALU.mult)
                X("gpsimd", "tensor_tensor", ["mtmp", "relu0"], ["mixin"], out=mixin[:, o8, 0:N], in0=mtmp[:, 0:N], in1=relu_t[0][:, 0:N], op=ALU.add)
            for o8 in range(8):
                s_ = _wsl[0] % WSL
                _wsl[0] += 1
                DMA("sync", "wsl%d" % s_, wslot[s_][:, 0, :], wb_o[l, o8], ["wb_o%d" % l], ["wslot%d" % s_])
                sm = _psl[0] % 4
                _psl[0] += 1
                pm_, km_ = PS(sm, 0, N)
                for kc in range(KC):
                    X("tensor", "matmul", ["wslot%d" % s_, "mixin"], km_, out=pm_, lhsT=wslot[s_][:, 0, kc * 128:(kc + 1) * 128], rhs=mixin[:, kc, 0:N],
                      start=(kc == 0), stop=(kc == KC - 1))
                X("vector", "tensor_tensor", km_ + ["xt"], ["x1"], out=x1[:, o8, 0:N], in0=pm_, in1=xt[:, o8, 0:N], op=ALU.add)
            DMA("gpsimd", "x1_st", x1s[:, :, tok0:tok0 + N], x1[:, :, 0:N], ["x1"], ["x1s"])

        F_x = aalloc([128, KC, 512], F32, reset=True)
        F_h = aalloc([128, KC, 512], BF16)
        F_sq = [aalloc([128, 512]) for _ in range(2)]
        F_rstd = aalloc([128, 512])
        F_relu = [aalloc([128, 512]) for _ in range(2)]
        F_act = aalloc([128, 16, 512], BF16)
        F_up = [aalloc([128, KC, 256], BF16) for _ in range(2)]
        F_dn = [aalloc([128, 16, 128], BF16) for _ in range(2)]
        _fs = [0, 0]

        def stage_F(l, tok0, N):
            DMA("sync", "f_ld", F_x[:, :, 0:N], x1s[:, :, tok0:tok0 + N], ["x1s"], ["F_x"])
            emit_norm(l, F_x, "F_x", N, "nffn", outbuf=F_h, outkey="F_h", sqk=F_sq, rstd=F_rstd, pfx="F_")
            for h in range(2):
                for fg in range(8):
                    su = _fs[0] % 2
                    _fs[0] += 1
                    DMA("sync", "up%d" % su, F_up[su][:].rearrange("p k c -> p (k c)"), wb_up[l, h * 8 + fg], ["wb_f%d" % l], ["F_up%d" % su])
                    for fb in range(2):
                        pu, ku = PS(4 + (fb % 2), 0, N)
                        for kc in range(KC):
                            X("tensor", "matmul", ["F_up%d" % su, "F_h"], ku, out=pu, lhsT=F_up[su][:, kc, fb * 128:(fb + 1) * 128], rhs=F_h[:, kc, 0:N],
                              start=(kc == 0), stop=(kc == KC - 1))
                        rt = F_relu[fb % 2]
                        X("scalar", "activation", ku, ["F_relu%d" % (fb % 2)], out=rt[:, 0:N], in_=pu, func=AF.Relu)
                        X("gpsimd", "tensor_tensor", ["F_relu%d" % (fb % 2)], ["F_act"], out=F_act[:, fg * 2 + fb, 0:N], in0=rt[:, 0:N], in1=rt[:, 0:N], op=ALU.mult)
                for o8 in range(8):
                    sd = _fs[1] % 2
                    _fs[1] += 1
                    DMA("sync", "dn%d" % sd, F_dn[sd][:].rearrange("p k c -> p (k c)"), wb_dn[l, h * 8 + o8], ["wb_f%d" % l], ["F_dn%d" % sd])
                    pd_, kd_ = PS(o8 % 4, 0, N)
                    for fc in range(16):
                        X("tensor", "matmul", ["F_dn%d" % sd, "F_act"], kd_, out=pd_, lhsT=F_dn[sd][:, fc, :], rhs=F_act[:, fc, 0:N],
                          start=(fc == 0), stop=(fc == 15))
                    X("vector", "tensor_tensor", kd_ + ["F_x"], ["F_x"], out=F_x[:, o8, 0:N], in0=pd_, in1=F_x[:, o8, 0:N], op=ALU.add)
            if l < NL - 1:
                DMA("gpsimd", "x_st", xs1[:, :, tok0:tok0 + N], F_x[:, :, 0:N], ["F_x"], ["xs1"])
            else:
                emit_norm(0, F_x, "F_x", N, "nfin", outbuf=F_x, outkey="F_x", sqk=F_sq, rstd=F_rstd, pfx="F_")
                DMA("gpsimd", "y_st", yT[:, :, tok0:tok0 + N], F_x[:, :, 0:N], ["F_x"], ["yT"], is_output=True)

        _step = [0]

        def step(fn, *a):
            _step[0] += 1
            if cfg.stop is None or _step[0] <= cfg.stop:
                fn(*a)

        def set_prev():
            X("vector", "tensor_copy", ["halo_prev"], ["raw"], out=raw[:, :, 0:1], in_=halo_prev[:].unsqueeze(2))

        for l in range(NL):
            step(emit_halo, l)
            step(set_prev)
            step(init_states_A)
            for ti in range(NPS):
                step(layer_tile, l, False, ti)
            step(emit_exchange, l)
            step(set_prev)
            step(init_states_B)
            GT = max(1, 512 // W)
            for g0 in range(0, NPS, GT):
                g1 = min(NPS, g0 + GT)
                for ti in range(g0, g1):
                    step(layer_tile, l, True, ti)
                step(p.fence)
                step(stage_F, l, g0 * W, (g1 - g0) * W)
                step(p.fence)
            step(layer_tile, l, True, NPS)
            step(p.fence)
            step(stage_F, l, NPT, 128)
            step(p.fence)

        with nc.Block() as block:
            p.emit(block)
    return nc


_NC_CACHE = {}


def _prep_shared(inp):
    f = np.float32
    NL = inp["w_in"].shape[0]
    idx = _win_cols()
    sh = {}
    w_in = _take_cols(np.asarray(inp["w_in"], f), idx)
    sh["w_in_r"] = np.ascontiguousarray(w_in.reshape(NL, KC, 128, NBLK, 128).transpose(0, 3, 2, 1, 4))
    w_up = np.asarray(inp["w_ffn_up"], f)
    sh["w_up_r"] = np.ascontiguousarray(w_up.reshape(NL, KC, 128, 16, 256).transpose(0, 3, 2, 1, 4))
    w_dn = np.asarray(inp["w_ffn_down"], f)
    sh["w_dn_r"] = np.ascontiguousarray(w_dn.reshape(NL, 2, 16, 128, 8, 128).transpose(0, 1, 4, 3, 2, 5).reshape(NL, 16, 128, 16, 128))
    woa = np.asarray(inp["w_out_a"], f).reshape(NL, 4, 128, 8, 128).transpose(0, 3, 2, 1, 4)
    wob = np.asarray(inp["w_out_b"], f).reshape(NL, 4, 128, 8, 128).transpose(0, 3, 2, 1, 4)
    sh["w_oab_r"] = np.ascontiguousarray(np.concatenate([woa, wob], axis=3))
    sh["w_o_r"] = np.ascontiguousarray(np.asarray(inp["w_out"], f).reshape(NL, 8, 128, 8, 128).transpose(0, 3, 2, 1, 4))
    w2 = np.zeros((NL, 128, 512), f)
    w2[:, 0:64] = inp["rwkv_w2"]
    a2 = np.zeros((NL, 128, 512), f)
    a2[:, 64:128] = inp["rwkv_a2"]
    g2 = np.zeros((NL, 256, 512), f)
    g2[:, 0:160] = inp["rwkv_g2"]
    sh["w2pad"], sh["a2pad"] = w2, a2
    sh["g2pad"] = np.ascontiguousarray(g2.reshape(NL, 2, 128, 512).transpose(0, 2, 1, 3))
    sh["vw1"] = np.ascontiguousarray(np.asarray(inp["rwkv_vres_w1"], f)[0].reshape(4, 128, 32).transpose(1, 0, 2))
    sh["vw2"] = np.ascontiguousarray(np.asarray(inp["rwkv_vres_w2"], f)[0])
    pv = np.zeros((NL, 128, NPV), f)
    mu = _take_cols(np.asarray(inp["rwkv_mu"], f), idx[:RW_BLK * 128])
    for l in range(NL):
        pv[l, :, PV["nmix"]:PV["nmix"] + 8] = _pk(np.asarray(inp["norm_mix"], f)[l], 8)
        pv[l, :, PV["nffn"]:PV["nffn"] + 8] = _pk(np.asarray(inp["norm_ffn"], f)[l], 8)
        pv[l, :, PV["mu"]:PV["mu"] + 15] = _pk(mu[l], 15)
        pv[l, :, PV["w0"]:PV["w0"] + 4] = _pk(np.asarray(inp["rwkv_w0"], f)[l], 4)
        pv[l, :, PV["a0"]:PV["a0"] + 4] = _pk(np.asarray(inp["rwkv_a0"], f)[l], 4)
        pv[l, :, PV["v0"]:PV["v0"] + 4] = _pk(np.asarray(inp["rwkv_v0"], f)[0], 4)
        pv[l, :, PV["kk"]:PV["kk"] + 4] = _pk(np.asarray(inp["rwkv_k_k"], f)[l], 4)
        pv[l, :, PV["ka"]:PV["ka"] + 4] = _pk(np.asarray(inp["rwkv_k_a"], f)[l], 4)
        pv[l, :, PV["rk"]:PV["rk"] + 4] = _pk(np.asarray(inp["rwkv_r_k"], f)[l].reshape(-1), 4)
        pv[l, :, PV["hnw"]:PV["hnw"] + 4] = _pk(np.asarray(inp["hgrn_norm_w"], f)[l], 4)
        pv[l, :, PV["lbz"]:PV["lbz"] + 4] = _pk(np.asarray(inp["hgrn_lb_logits"], f)[l], 4)
        pv[l, :, PV["nfin"]:PV["nfin"] + 8] = _pk(np.asarray(inp["norm_final"], f), 8)
    sh["pvec"] = pv
    for nm, key in (("lnw_st", "rwkv_ln_w"), ("lnb_st", "rwkv_ln_b")):
        a = np.asarray(inp[key], f).reshape(NL, 2, 2, 2, 64)
        a = np.broadcast_to(a[:, :, :, :, None, :], (NL, 2, 2, 2, 32, 64))
        sh[nm] = np.ascontiguousarray(a.transpose(0, 2, 3, 4, 1, 5).reshape(NL, 128, 2, 64))
    return sh


def _run(inp, npt, w, dbg=(), stop=None):
    f = np.float32
    cfg = Cfg(npt=npt, w=w, nlayer=int(inp["w_in"].shape[0]), dbg=dbg, stop=stop)
    key = (npt, w, cfg.NL, tuple(dbg), stop)
    if key not in _NC_CACHE:
        _NC_CACHE[key] = build(cfg)
    nc = _NC_CACHE[key]
    NL, NT = cfg.NL, cfg.NT
    sh = _prep_shared(inp)
    xp = np.asarray(inp["x_prompt"], f)
    xs = np.asarray(inp["x_sample"], f)
    idx = _win_cols()
    sshift = _take_cols(np.asarray(inp["state_shift"], f), idx[:RW_BLK * 128])
    srw = np.asarray(inp["state_rwkv"], f)
    shg = np.asarray(inp["state_hgrn"], f)
    in_maps = []
    for c in range(NCORE):
        b, seg = c // 4, c % 4
        xtok = np.concatenate([xp[b, seg * npt:(seg + 1) * npt], xs[4 * c:4 * c + 4].reshape(4 * L, D)], axis=0)
        m = dict(sh)
        m["xT"] = np.ascontiguousarray(xtok.reshape(NT, KC, 128).transpose(2, 1, 0))
        ss = sshift[:, 4 * c:4 * c + 4]
        m["st_shift"] = np.ascontiguousarray(ss.reshape(NL, 4, RW_BLK, 128).transpose(0, 3, 2, 1))
        r = srw[:, 4 * c:4 * c + 4].reshape(NL, 4, 4, 2, 64, 64)
        m["st_rwkv"] = np.ascontiguousarray(r.transpose(0, 1, 3, 5, 2, 4).reshape(NL, 4, 128, 4, 64))
        h = shg[:, 4 * c:4 * c + 4]
        m["st_hgrn"] = np.ascontiguousarray(h.transpose(0, 1, 3, 2, 4))
        for n, v in _consts(seg).items():
            m["c_" + n] = v
        in_maps.append(m)
    res = run_bass_kernel_spmd(nc, in_maps, core_ids=list(range(NCORE)))
    R = res.results
    B = xp.shape[0]
    y_p = np.zeros((B, 4 * npt, D), f)
    y_s = np.zeros((4 * NCORE, L, D), f)
    for c in range(NCORE):
        yt = R[c]["yT"].transpose(2, 1, 0).reshape(NT, D)
        y_p[c // 4, (c % 4) * npt:(c % 4 + 1) * npt] = yt[:npt]
        y_s[4 * c:4 * c + 4] = yt[npt:].reshape(4, L, D)

    def unshift(a):
        a = np.moveaxis(a, -2, -1)
        return a.reshape(a.shape[:-2] + (RW_BLK * 128,))[..., :1824]

    def unrw(a):
        lead = a.shape[:-3]
        a = a.reshape(lead + (2, 64, 4, 64))
        n = len(lead)
        a = a.transpose(tuple(range(n)) + (n + 2, n + 0, n + 3, n + 1))
        return a.reshape(lead + (8, 64, 64))

    def unhg(a):
        n = a.ndim - 3
        return a.transpose(tuple(range(n)) + (n + 1, n + 0, n + 2))

    lastc = [4 * bb + 3 for bb in range(B)]
    shift_p = np.stack([unshift(R[c]["o_shift_p"]) for c in lastc], axis=1)
    rwkv_p = np.stack([unrw(R[c]["o_rwkv_p"]) for c in lastc], axis=1)
    hgrn_p = np.stack([unhg(R[c]["o_hgrn_p"]) for c in lastc], axis=1)
    shift_s = np.concatenate([np.moveaxis(unshift(np.moveaxis(R[c]["o_shift_s"], -1, 1)), 1, 1) for c in range(NCORE)], axis=1)
    rwkv_s = np.concatenate([unrw(R[c]["o_rwkv_s"]) for c in range(NCORE)], axis=1)
    hgrn_s = np.concatenate([unhg(R[c]["o_hgrn_s"]) for c in range(NCORE)], axis=1)
    outs = (y_p, y_s, shift_p, rwkv_p, hgrn_p, shift_s, rwkv_s, hgrn_s)
    return tuple(np.ascontiguousarray(o, dtype=f) for o in outs), R


def kernel(**inputs):
    outs, _ = _run(inputs, 2048, 128)
    return outs
```

```python
import numpy as np
from contextlib import ExitStack
import concourse.bass as bass
import concourse.mybir as mybir
from concourse.bass_utils import run_bass_kernel_spmd

F32 = mybir.dt.float32
BF16 = mybir.dt.bfloat16
AF = mybir.ActivationFunctionType
ALU = mybir.AluOpType
AX = mybir.AxisListType

D = 1024
KC = 8
NCORE = 8
L = 32
NBLK = 47
RW_BLK = 15
DFF = 4096
RMS_EPS = 1e-6
GN_EPS = 64e-5
C0 = -float(np.exp(-0.5))


class Prog:
    ENGS = ["sync", "scalar", "vector", "gpsimd", "tensor"]

    def __init__(self, nc, stack):
        self.nc = nc
        self.stack = stack
        self.ops = {e: [] for e in self.ENGS}
        self.esem = {e: stack.enter_context(nc.semaphore("s_" + e)) for e in self.ENGS}
        self.ecnt = {e: 0 for e in self.ENGS}
        self.dsem = {}
        self.dcnt = {}
        self.writer = {}
        self.readers = {}
        self.waited = {e: {} for e in self.ENGS}
        self.out_tokens = []
        self.pending = {e: {} for e in self.ENGS}

    def _dsem(self, name):
        if name not in self.dsem:
            self.dsem[name] = self.stack.enter_context(self.nc.semaphore("d_" + name))
            self.dcnt[name] = 0
        return self.dsem[name]

    def _deps(self, eng, reads, writes):
        toks = []
        for k in reads:
            w = self.writer.get(k)
            if w is not None:
                toks.append(w)
        for k in writes:
            w = self.writer.get(k)
            if w is not None:
                toks.append(w)
            toks.extend(self.readers.get(k, []))
        need = {}
        for (s, sid, v) in toks:
            if sid == ("e", "tensor") and eng == "tensor":
                continue
            if sid[0] == "e" and sid[1] == eng and eng == "sync":
                continue
            if sid[0] == "d":
                v = max(v, self.dcnt[sid[1]])
            if need.get(sid, (None, -1))[1] < v:
                need[sid] = (s, v)
        for sid, (s, v) in self.pending[eng].items():
            if need.get(sid, (None, -1))[1] < v:
                need[sid] = (s, v)
        self.pending[eng] = {}
        waits = []
        for sid, (s, v) in need.items():
            if self.waited[eng].get(sid, -1) >= v:
                continue
            self.waited[eng][sid] = v
            waits.append((s, v))
        return waits

    def fence(self):
        allt = {}
        for e in self.ENGS:
            if self.ecnt[e] > 0:
                allt[("e", e)] = (self.esem[e], self.ecnt[e])
        for n, c in self.dcnt.items():
            if c > 0:
                allt[("d", n)] = (self.dsem[n], c)
        for e in self.ENGS:
            for sid, sv in allt.items():
                if sid == ("e", e):
                    continue
                self.pending[e][sid] = sv
        self.writer.clear()
        self.readers.clear()

    def _record(self, tok, reads, writes):
        for k in reads:
            self.readers.setdefault(k, []).append(tok)
        for k in writes:
            self.writer[k] = tok
            self.readers[k] = []

    def op(self, eng, fn, reads=(), writes=()):
        pk = [k for k in reads if k.startswith("psbank")]
        if pk:
            reads = [k for k in reads if not k.startswith("psbank")]
            writes = list(writes) + [k for k in pk if k not in writes]
        waits = self._deps(eng, reads, writes)
        self.ecnt[eng] += 1
        tok = (self.esem[eng], ("e", eng), self.ecnt[eng])
        self.ops[eng].append((waits, fn, self.esem[eng], 1))
        self._record(tok, reads, writes)
        return tok

    def dma(self, eng, dname, fn, reads=(), writes=(), is_output=False):
        waits = self._deps(eng, reads, writes)
        s = self._dsem(dname)
        self.dcnt[dname] += 16
        tok = (s, ("d", dname), self.dcnt[dname])
        self.ops[eng].append((waits, fn, s, 16))
        self._record(tok, reads, writes)
        if is_output:
            self.out_tokens.append(tok)
        return tok

    def emit(self, block):
        prog = self

        def mk(ename):
            def body(e):
                for (waits, fn, s, inc) in prog.ops[ename]:
                    for (ws, wv) in waits:
                        e.wait_ge(ws, wv)
                    fn(e).then_inc(s, inc)
                if ename == "gpsimd":
                    last = {}
                    for (s2, sid, v) in prog.out_tokens:
                        if last.get(sid, (None, -1))[1] < v:
                            last[sid] = (s2, v)
                    for sid, (s2, v) in last.items():
                        e.wait_ge(s2, v)
            return body
        block.sync(mk("sync"))
        block.scalar(mk("scalar"))
        block.vector(mk("vector"))
        block.gpsimd(mk("gpsimd"))
        block.tensor(mk("tensor"))


def _win_cols():
    idx = list(range(0, 1664))
    idx += list(range(1664, 1824)) + [-1] * 96
    idx += list(range(1824, 5920))
    assert len(idx) == NBLK * 128
    return np.array(idx)


def _take_cols(a, idx, axis=-1):
    a = np.moveaxis(a, axis, -1)
    out = np.zeros(a.shape[:-1] + (len(idx),), a.dtype)
    m = idx >= 0
    out[..., m] = a[..., idx[m]]
    return np.moveaxis(out, -1, axis)


def _pk(v, nchunk):
    return np.ascontiguousarray(v.reshape(nchunk, 128).T)


def _consts(rank4):
    c = {}
    c["ident"] = np.eye(128, dtype=np.float32)
    ob = np.zeros((128, 128), np.float32)
    ob[:64, :64] = 1
    ob[64:, 64:] = 1
    c["ones_bd"] = ob
    c["ones_f"] = np.ones((128, 128), np.float32)
    isel = np.zeros((128, 64), np.float32)
    isel[np.arange(128), np.arange(128) % 64] = 1
    c["isel"] = isel
    rho = np.arange(128)
    blk, pos = rho // 32, rho % 32
    same = blk[:, None] == blk[None, :]
    c["m_strict"] = (same & (pos[:, None] < pos[None, :])).astype(np.float32)
    c["m_incl"] = (same & (pos[:, None] <= pos[None, :])).astype(np.float32)
    c["m_lower"] = (same & (pos[:, None] > pos[None, :])).astype(np.float32)
    s = np.arange(32)
    c["m_att"] = (s[:, None] <= s[None, :]).astype(np.float32)
    rst = np.ones((128, 4 * 128), np.float32)
    rst[:, ::32] = 0
    c["m_reset"] = rst
    fm = np.zeros((128, 4), np.float32)
    fm[:, :3] = (np.arange(3) < rank4).astype(np.float32)[None, :]
    c["foldm"] = fm
    hm = np.zeros((128, 4), np.float32)
    if rank4 > 0:
        hm[:, rank4 - 1] = 1
    c["halom"] = hm
    return c


class Cfg:
    def __init__(self, npt=2048, w=256, nlayer=2, dbg=(), stop=None):
        self.stop = stop
        self.NPT = npt
        self.W = w
        self.NS = 4
        self.NT = npt + 4 * L
        self.NL = nlayer
        self.dbg = tuple(dbg)


PV = dict(nmix=0, nffn=8, mu=16, w0=31, a0=35, v0=39, kk=43, ka=47, rk=51, hnw=55, lbz=59, nfin=63)
NPV = 71


def build(cfg):
    nc = bass.Bass("TRN2", target_bir_lowering=False)
    NPT, W, NT, NL = cfg.NPT, cfg.W, cfg.NT, cfg.NL
    NPS = NPT // W

    def din(name, shape, dt=F32):
        return nc.dram_tensor(name, list(shape), dt, kind="ExternalInput").ap()

    def dout(name, shape, dt=F32):
        return nc.dram_tensor(name, list(shape), dt, kind="ExternalOutput").ap()

    def dint(name, shape, dt=F32):
        return nc.dram_tensor(name, list(shape), dt, kind="Internal").ap()

    xT = din("xT", [128, KC, NT])
    st_shift = din("st_shift", [NL, 128, RW_BLK, 4])
    st_rwkv = din("st_rwkv", [NL, 4, 128, 4, 64])
    st_hgrn = din("st_hgrn", [NL, 4, 128, 4, 128])
    w_in_r = din("w_in_r", [NL, NBLK, 128, KC, 128])
    w_up_r = din("w_up_r", [NL, 16, 128, KC, 256])
    w_dn_r = din("w_dn_r", [NL, 16, 128, 16, 128])
    w_oab_r = din("w_oab_r", [NL, 8, 128, 8, 128])
    w_o_r = din("w_o_r", [NL, 8, 128, 8, 128])
    w2pad = din("w2pad", [NL, 128, 512])
    a2pad = din("a2pad", [NL, 128, 512])
    g2pad = din("g2pad", [NL, 128, 2, 512])
    vw1 = din("vw1", [128, 4, 32])
    vw2 = din("vw2", [32, 512])
    pvec = din("pvec", [NL, 128, NPV])
    lnw_st = din("lnw_st", [NL, 128, 2, 64])
    lnb_st = din("lnb_st", [NL, 128, 2, 64])
    cnames = ["ident", "ones_bd", "ones_f", "isel", "m_strict", "m_incl", "m_lower", "m_att", "m_reset", "foldm", "halom"]
    cshape = dict(ident=[128, 128], ones_bd=[128, 128], ones_f=[128, 128], isel=[128, 64], m_strict=[128, 128],
                  m_incl=[128, 128], m_lower=[128, 128], m_att=[32, 32], m_reset=[128, 512], foldm=[128, 4], halom=[128, 4])
    cdram = {n: din("c_" + n, cshape[n]) for n in cnames}

    yT = dout("yT", [128, KC, NT])
    o_shift_p = dout("o_shift_p", [NL, 128, RW_BLK])
    o_rwkv_p = dout("o_rwkv_p", [NL, 128, 4, 64])
    o_hgrn_p = dout("o_hgrn_p", [NL, 128, 4, 128])
    o_shift_s = dout("o_shift_s", [NL, 128, RW_BLK, 4])
    o_rwkv_s = dout("o_rwkv_s", [NL, 4, 128, 4, 64])
    o_hgrn_s = dout("o_hgrn_s", [NL, 4, 128, 4, 128])
    dbg_out = {n: dout("dbg_" + n, shp) for (n, shp) in cfg.dbg}

    xs1 = dint("xs1", [128, KC, NT])
    vfirst_d = dint("vfirst_d", [128, 4, NT])
    cin_h = dint("cin_h", [128, 16])
    cout_h = dint("cout_h", [512, 16])
    SW = 4 * 128 + 4 * 128 + 8
    cin_s = dint("cin_s", [128, SW])
    cout_s = dint("cout_s", [512, SW])
    wb_in = dint("wb_in", [NL, NBLK, 128, KC * 128], BF16)
    wb_up = dint("wb_up", [NL, 16, 128, KC * 256], BF16)
    wb_dn = dint("wb_dn", [NL, 16, 128, 16 * 128], BF16)
    x1s = dint("x1s", [128, KC, NT])
    wb_oab = dint("wb_oab", [NL, 8, 128, 8 * 128], BF16)
    wb_o = dint("wb_o", [NL, 8, 128, 8 * 128], BF16)

    with ExitStack() as st:
        p = Prog(nc, st)

        def sb(name, shape, dt=F32):
            return st.enter_context(nc.sbuf_tensor(name, list(shape), dt))

        def X(eng, method, reads, writes, **kw):
            return p.op(eng, lambda e: getattr(e, method)(**kw), reads, writes)

        def DMA(eng, dname, out, in_, reads, writes, is_output=False, **kw):
            return p.dma(eng, dname, lambda e: e.dma_start(out=out, in_=in_, **kw), reads, writes, is_output)

        _rr = [0]

        def EW():
            _rr[0] ^= 1
            return "vector" if _rr[0] else "gpsimd"

        ps = st.enter_context(nc.psum_tensor("ps", [128, 7 * 512], F32))
        psb = st.enter_context(nc.psum_tensor("psb", [128, 1024], BF16))

        def PS(bank, off, n):
            assert off + n <= 512
            keys = ["psbank%d" % bank]
            return ps[:, bank * 512 + off: bank * 512 + off + n], keys

        def PSB(off, n):
            keys = ["psbankB"]
            return psb[:, off:off + n], keys

        cst = {}
        for n in cnames:
            cst[n] = sb("k_" + n, cshape[n])
            DMA("sync", "cst", cst[n][:], cdram[n][:], [], ["c_" + n])
        ident_b = sb("ident_b", [128, 128], BF16)
        isel_b = sb("isel_b", [128, 64], BF16)
        X("vector", "tensor_copy", ["c_ident"], ["ident_b"], out=ident_b[:], in_=cst["ident"][:])
        X("vector", "tensor_copy", ["c_isel"], ["isel_b"], out=isel_b[:], in_=cst["isel"][:])
        eps_rms = sb("eps_rms", [128, 1])
        eps_gn = sb("eps_gn", [128, 1])
        X("vector", "memset", [], ["eps_rms"], ap=eps_rms[:], constant=RMS_EPS)
        X("vector", "memset", [], ["eps_gn"], ap=eps_gn[:], constant=GN_EPS)

        def wcls(b):
            return "a" if b < RW_BLK else ("b" if 19 <= b < 27 else "c")

        def wkey(l, b):
            return "wb_in%d%s" % (l, wcls(b))

        conv_q = {l: [] for l in range(NL)}

        def conv_layer(l):
            order = list(range(0, RW_BLK)) + list(range(19, 27)) + list(range(RW_BLK, 19)) + list(range(27, NBLK))
            mk = lambda *a: (lambda: DMA(*a))
            for b in order:
                conv_q[l].append(mk("gpsimd", "cv_in%d%s" % (l, wcls(b)), wb_in[l, b], w_in_r[l, b].rearrange("p k c -> p (k c)"), [], [wkey(l, b)]))
            for g in range(8):
                conv_q[l].append(mk("gpsimd", "cv_o%d" % l, wb_oab[l, g], w_oab_r[l, g].rearrange("p k c -> p (k c)"), [], ["wb_o%d" % l]))
                conv_q[l].append(mk("gpsimd", "cv_o%d" % l, wb_o[l, g], w_o_r[l, g].rearrange("p k c -> p (k c)"), [], ["wb_o%d" % l]))
            for g in range(16):
                conv_q[l].append(mk("gpsimd", "cv_f%d" % l, wb_up[l, g], w_up_r[l, g].rearrange("p k c -> p (k c)"), [], ["wb_f%d" % l]))
                conv_q[l].append(mk("gpsimd", "cv_f%d" % l, wb_dn[l, g], w_dn_r[l, g].rearrange("p k c -> p (k c)"), [], ["wb_f%d" % l]))

        def conv_pump(l, n):
            if l < NL:
                for _ in range(n):
                    if conv_q[l]:
                        conv_q[l].pop(0)()

        for l in range(NL):
            conv_layer(l)
        conv_pump(0, RW_BLK + 8)

        pvs = sb("pvs", [128, NL, NPV])
        DMA("sync", "cstp", pvs[:], pvec.rearrange("l p n -> p l n"), [], ["pvs"])
        lnw = sb("lnw", [128, NL, 2, 64])
        lnb = sb("lnb", [128, NL, 2, 64])
        DMA("sync", "cstp", lnw[:], lnw_st.rearrange("l p g v -> p l g v"), [], ["lnw"])
        DMA("sync", "cstp", lnb[:], lnb_st.rearrange("l p g v -> p l g v"), [], ["lnb"])
        w2p_b = sb("w2p_b", [128, NL, 512], BF16)
        a2p_b = sb("a2p_b", [128, NL, 512], BF16)
        g2p_b = sb("g2p_b", [128, NL, 2, 512], BF16)
        vw1_b = sb("vw1_b", [128, 4, 32], BF16)
        vw2_b = sb("vw2_b", [32, 512], BF16)
        DMA("gpsimd", "cst2", w2p_b[:], w2pad.rearrange("l p n -> p l n"), [], ["w2p_b"])
        DMA("gpsimd", "cst2", a2p_b[:], a2pad.rearrange("l p n -> p l n"), [], ["a2p_b"])
        DMA("gpsimd", "cst2", g2p_b[:], g2pad.rearrange("l p j n -> p l j n"), [], ["g2p_b"])
        DMA("gpsimd", "cst2", vw1_b[:], vw1[:], [], ["vw1_b"])
        DMA("gpsimd", "cst2", vw2_b[:], vw2[:], [], ["vw2_b"])
        lbe = sb("lbe", [128, NL, 4])
        lbs = sb("lbs", [128, 4])
        lbv = sb("lbv", [128, NL, 4])
        oml = sb("oml", [128, NL, 4])
        X("scalar", "activation", ["pvs"], ["lbe"], out=lbe[:], in_=pvs[:, :, PV["lbz"]:PV["lbz"] + 4], func=AF.Exp)
        X("vector", "tensor_copy", ["lbe"], ["lbs"], out=lbs[:], in_=lbe[:, 0, :])
        for l in range(1, NL):
            X("vector", "tensor_tensor", ["lbe", "lbs"], ["lbs"], out=lbs[:], in0=lbs[:], in1=lbe[:, l, :], op=ALU.add)
        X("vector", "reciprocal", ["lbs"], ["lbs"], out=lbs[:], in_=lbs[:])
        for l in range(NL):
            X("vector", "tensor_tensor", ["lbe", "lbs"], ["lbe"], out=lbe[:, l, :], in0=lbe[:, l, :], in1=lbs[:], op=ALU.mult)
        X("vector", "tensor_tensor", ["lbe"], ["lbv"], out=lbv[:, 0, :], in0=lbe[:, 0, :], in1=lbe[:, 0, :], op=ALU.subtract)
        for l in range(1, NL):
            X("vector", "tensor_tensor", ["lbe", "lbv"], ["lbv"], out=lbv[:, l, :], in0=lbv[:, l - 1, :], in1=lbe[:, l, :], op=ALU.add)
        X("vector", "tensor_scalar", ["lbv"], ["oml"], out=oml[:], in0=lbv[:], scalar1=-1.0, scalar2=1.0, op0=ALU.mult, op1=ALU.add)

        def pcol(l, name, i=0, n=1):
            return pvs[:, l, PV[name] + i: PV[name] + i + n]

        def pbc(l, name, n, T):
            return pvs[:, l, PV[name]: PV[name] + n].unsqueeze(2).to_broadcast([128, n, T])

        WSL = 8
        wslot = [sb("wslot%d" % i, [128, 2, KC * 128], BF16) for i in range(WSL)]
        xt = sb("xt", [128, KC, W])
        x1 = sb("x1", [128, KC, W])
        hT = sb("hT", [128, KC, W], BF16)
        sqk_d = [sb("sqk%d" % i, [128, W]) for i in range(2)]
        rstd_d = sb("rstd", [128, W])
        RAWW = W + 4
        raw = sb("raw", [128, RW_BLK, RAWW])
        X("gpsimd", "memset", [], ["raw"], ap=raw[:], constant=0.0)
        hq = sb("hq", [128, 4, W])
        hsig = sb("hsig", [128, 4, W])
        hv_b = sb("hv_b", [128, 4, W], BF16)
        hog = sb("hog", [128, 4, W])
        big = sb("big", [128, max(16 * W, SW)])
        gates = big[:, 0:16 * W].rearrange("p (j w) -> p j w", w=W)
        yg_b = sb("yg_b", [128, 4, W], BF16)
        ob_b = sb("ob_b", [128, 4, W], BF16)
        mixin = sb("mixin", [128, KC, W], BF16)
        mtmp = sb("mtmp", [128, W])
        relu_t = [sb("relu%d" % i, [128, W]) for i in range(1)]
        _wsl = [0]

        def emit_norm(l, xbuf, xkey, N, pname, outbuf=None, outkey="hT", sqk=None, rstd=None, pfx=""):
            if outbuf is None:
                outbuf = hT
            if sqk is None:
                sqk, rstd = sqk_d, rstd_d
            nps, nkeys = PS(6, 0, N)
            for kc in range(KC):
                sq = sqk[kc % 2]
                X("scalar", "activation", [xkey], [pfx + "sqk%d" % (kc % 2)], out=sq[:, 0:N], in_=xbuf[:, kc, 0:N], func=AF.Square)
                X("tensor", "matmul", [pfx + "sqk%d" % (kc % 2), "c_ones_f"], nkeys, out=nps, lhsT=cst["ones_f"][:], rhs=sq[:, 0:N],
                  start=(kc == 0), stop=(kc == KC - 1))
            X("scalar", "activation", nkeys + ["eps_rms"], [pfx + "rstd"], out=rstd[:, 0:N], in_=nps, func=AF.Ln, scale=1.0 / D, bias=eps_rms[:])
            X("scalar", "activation", [pfx + "rstd"], [pfx + "rstd"], out=rstd[:, 0:N], in_=rstd[:, 0:N], func=AF.Exp, scale=-0.5)
            for kc in range(KC):
                X("vector", "scalar_tensor_tensor", [xkey, pfx + "rstd", "pvs"], [outkey], out=outbuf[:, kc, 0:N], in0=xbuf[:, kc, 0:N],
                  scalar=pcol(l, pname, kc), in1=rstd[:, 0:N], op0=ALU.mult, op1=ALU.mult)

        _psl = [0]

        def emit_proj(l, blocks, N, handler):
            groups = [blocks[i:i + 2] for i in range(0, len(blocks), 2)]
            for grp in groups:
                s = _wsl[0] % WSL
                _wsl[0] += 1
                if len(grp) == 2 and grp[1] == grp[0] + 1:
                    DMA("sync", "wsl%d" % s, wslot[s][:, 0:2, :], wb_in[l, grp[0]:grp[0] + 2].rearrange("j p n -> p j n"),
                        sorted(set([wkey(l, grp[0]), wkey(l, grp[1])])), ["wslot%d" % s])
                else:
                    for j, b in enumerate(grp):
                        DMA("sync", "wsl%d" % s, wslot[s][:, j, :], wb_in[l, b], [wkey(l, b)], ["wslot%d" % s])
                for j, b in enumerate(grp):
                    slot = _psl[0] % 4
                    _psl[0] += 1
                    pr, pk = PS(slot, 0, N)
                    for kc in range(KC):
                        X("tensor", "matmul", ["wslot%d" % s, "hT"], pk, out=pr, lhsT=wslot[s][:, j, kc * 128:(kc + 1) * 128],
                          rhs=hT[:, kc, 0:N], start=(kc == 0), stop=(kc == KC - 1))
                    handler(b, pr, pk)

        T = 128
        AW = 17920
        arena = sb("arena", [128, AW])
        _ao = [0]

        def aalloc(shape, dt=F32, reset=False):
            if reset:
                _ao[0] = 0
            n = int(np.prod(shape[1:]))
            n32 = n if dt == F32 else (n + 1) // 2
            a = _ao[0]
            _ao[0] += n32
            assert _ao[0] <= AW, ("arena overflow", _ao[0])
            v = arena[:, a:a + n32]
            if dt != F32:
                v = v.bitcast(dt)[:, 0:n]
            if len(shape) == 3:
                v = v.rearrange("p (a b) -> p a b", a=shape[1])
            elif len(shape) == 4:
                v = v.rearrange("p (a b c) -> p a b c", a=shape[1], b=shape[2])
            return v[0:shape[0]] if shape[0] < 128 else v

        def sbm(name, shape, dt=F32):
            return aalloc(list(shape), dt)
        mx = sbm("mx", [128, RW_BLK, T])
        lw_in = sbm("lw_in", [128, T], BF16)
        siggl = sbm("siggl", [128, 2, T], BF16)
        names4 = ["sg", "aa", "gfm", "vv", "kkn", "kh", "brec", "Gw", "tA", "tB", "Ex", "bv", "t1", "vf"]
        m4 = {n: sbm("m_" + n, [128, 4, T]) for n in names4}
        vb16 = sbm("vb16", [128, 4, T], BF16)
        t32b = sbm("t32b", [32, T], BF16)
        bdn = ["Kbd", "Bbd", "Abd", "Rbd", "KHbd", "BHbd", "Vbd"]
        bd = {n: sb(n, [128, 4, 4, 128], BF16) for n in bdn}
        for n in bdn:
            X("gpsimd", "memset", [], [n], ap=bd[n][:], constant=0.0)
        AR = sbm("AR", [128, 4, 4, 64], BF16)
        Bp = sbm("Bp", [128, 4, 4, 32], BF16)
        gL = sb("gL", [128, 4, 4])
        chn = ["X1", "X1T", "Mb", "Xa", "XaT", "Xb", "XbT"]
        chb = {n: [sbm("%s%d" % (n, g), [128, 128], BF16) for g in range(2)] for n in chn}
        chb2 = {n: [[sbm("%s_%d_%d" % (n, pp, g), [128, 128], BF16) for g in range(2)] for pp in range(2)] for n in ["Aka", "Akr", "Abr", "Ma"]}
        KT2 = [sbm("KT_%d" % pp, [128, 4, 128], BF16) for pp in range(2)]
        BT2 = [sbm("BT_%d" % pp, [128, 4, 128], BF16) for pp in range(2)]
        Vst2 = [sb("Vst_%d" % pp, [128, 2, 128], BF16) for pp in range(2)]
        for pp in range(2):
            X("gpsimd", "memset", [], ["Vst%d_0" % pp, "Vst%d_1" % pp], ap=Vst2[pp][:], constant=0.0)
        Wt = sbm("Wt", [128, 2, 128], BF16)
        Ut = sbm("Ut", [128, 2, 128], BF16)
        Sf = sb("Sf", [128, 4, 128])
        Sb = sb("Sb", [128, 4, 128], BF16)
        Yst = sbm("Yst", [128, 8, 64])
        Ysq = sbm("Ysq", [128, 8, 64])
        Yrep = sbm("Yrep", [128, 8, 2, 64])
        gst = {n: sb("gst_" + n, [128, 8]) for n in ["s1", "s2", "mean", "var"]}
        h4 = {"o": sbm("h_o", [128, 4, T])}
        for hn_, mn_ in {'f': 'sg', 'lf': 'aa', 'khh': 'kkn', 'G2': 'Gw', 'd1': 'tB', 'E2': 'Ex', 'hA': 'tA', 'o2': 'kh', 'rs': 'brec'}.items():
            h4[hn_] = m4[mn_]
        hb = {n: sbm("hb_" + n, [128, 4, T], BF16) for n in ["Qt", "Q2", "Kt", "Kh"]}
        dL = sb("dL", [128, 4, 4])
        att_b = sbm("att_b", [32, 4, 32], BF16)
        VTh = sbm("VTh", [32, 4, 128], BF16)
        KTh = sbm("KTh", [32, 4, 128], BF16)
        Shf = sb("Shf", [128, 4, 128])
        Shb = sb("Shb", [128, 4, 128], BF16)
        Dtot = sb("Dtot", [128, 4])

        def v4(ap):
            return ap.rearrange("p c (q t) -> p c q t", t=L)

        def bd_write(name, in0, in1, op, keys_r):
            for hh in range(2):
                for cc in range(2):
                    out = bd[name][hh * 64:(hh + 1) * 64, cc::2, :, 64 * cc + 32 * hh: 64 * cc + 32 * hh + 32]
                    a = v4(in0)[hh * 64:(hh + 1) * 64, cc::2]
                    if in1 is None:
                        X(EW(), "tensor_copy", keys_r, [name], out=out, in_=a)
                    else:
                        b = v4(in1)[hh * 64:(hh + 1) * 64, cc::2]
                        X(EW(), "tensor_tensor", keys_r, [name], out=out, in0=a, in1=b, op=op)

        def bc4(ap, shape):
            return ap.to_broadcast(shape)

        def rwkv_prep(l, phB, cur, prv, mxv, tok0, nvalid_blocks):
            b0, b1 = nvalid_blocks
            shp = list(cur.shape)
            mu = pvs[:, l, PV["mu"] + b0: PV["mu"] + b1]
            mu_bc = (mu.unsqueeze(2) if len(shp) == 3 else mu.unsqueeze(2).unsqueeze(3)).to_broadcast(shp)
            X("vector", "tensor_tensor", ["raw"], ["mx"], out=mxv, in0=prv, in1=cur, op=ALU.subtract)
            X("vector", "tensor_tensor", ["mx", "pvs"], ["mx"], out=mxv, in0=mxv, in1=mu_bc, op=ALU.mult)
            X("vector", "tensor_tensor", ["mx", "raw"], ["mx"], out=mxv, in0=mxv, in1=cur, op=ALU.add)
            r, k, v = mx[:, 0:4, :], mx[:, 4:8, :], mx[:, 8:12, :]
            X("scalar", "activation", ["mx"], ["lw_in"], out=lw_in[0:64, :], in_=mx[0:64, 12, :], func=AF.Tanh)
            X("scalar", "activation", ["mx"], ["lw_in"], out=lw_in[64:128, :], in_=mx[64:128, 12, :], func=AF.Copy)
            pw, kw = PS(4, 0, 512)
            pa, ka = PS(5, 0, 512)
            pg, kg = PS(6, 0, 512)
            for c in range(4):
                X("tensor", "matmul", ["lw_in", "w2p_b"], kw, out=pw[:, c * T:(c + 1) * T], lhsT=w2p_b[:, l, c * 128:(c + 1) * 128],
                  rhs=lw_in[:], start=True, stop=True)
                X("tensor", "matmul", ["lw_in", "a2p_b"], ka, out=pa[:, c * T:(c + 1) * T], lhsT=a2p_b[:, l, c * 128:(c + 1) * 128],
                  rhs=lw_in[:], start=True, stop=True)
            if phB:
                X("scalar", "activation", ["mx"], ["siggl"], out=siggl[:], in_=mx[:, 13:15, :], func=AF.Sigmoid)
                for c in range(4):
                    for j in range(2):
                        X("tensor", "matmul", ["siggl", "g2p_b"], kg, out=pg[:, c * T:(c + 1) * T],
                          lhsT=g2p_b[:, l, j, c * 128:(c + 1) * 128], rhs=siggl[:, j, :], start=(j == 0), stop=(j == 1))
            for c in range(4):
                X("scalar", "activation", kw + ["pvs"], ["sg"], out=m4["sg"][:, c, :], in_=pw[:, c * T:(c + 1) * T], func=AF.Sigmoid,
                  bias=pcol(l, "w0", c))
                X("scalar", "activation", ka + ["pvs"], ["aa"], out=m4["aa"][:, c, :], in_=pa[:, c * T:(c + 1) * T], func=AF.Sigmoid,
                  bias=pcol(l, "a0", c))
            if phB:
                X("scalar", "activation", kg, ["gfm"], out=m4["gfm"][:].rearrange("p c t -> p (c t)"), in_=pg, func=AF.Copy)
            FEED(2)
            vv = m4["vv"]
            if l == 0:
                X(EW(), "tensor_copy", ["mx"], ["vv"], out=vv[:], in_=v)
                if phB:
                    DMA("sync", "vf_st", vfirst_d[:, :, tok0:tok0 + T], vv[:], ["vv"], ["vfirst_d"])
            else:
                DMA("sync", "vf_ld", m4["vf"][:], vfirst_d[:, :, tok0:tok0 + T], ["vfirst_d"], ["vf"])
                X("gpsimd", "tensor_copy", ["mx"], ["vb16"], out=vb16[:], in_=v)
                p32, k32 = PS(4, 0, T)
                for c in range(4):
                    X("tensor", "matmul", ["vb16", "vw1_b"], k32, out=p32[0:32, :], lhsT=vw1_b[:, c, :], rhs=vb16[:, c, :],
                      start=(c == 0), stop=(c == 3))
                X("scalar", "activation", k32, ["t32b"], out=t32b[:], in_=p32[0:32, :], func=AF.Copy)
                pv_, kv_ = PS(5, 0, 512)
                for c in range(4):
                    X("tensor", "matmul", ["t32b", "vw2_b"], kv_, out=pv_[:, c * T:(c + 1) * T], lhsT=vw2_b[:, c * 128:(c + 1) * 128],
                      rhs=t32b[:], start=True, stop=True)
                for c in range(4):
                    X("scalar", "activation", kv_ + ["pvs"], ["tA"], out=m4["tA"][:, c, :], in_=pv_[:, c * T:(c + 1) * T],
                      func=AF.Sigmoid, bias=pcol(0, "v0", c))
                X("vector", "tensor_tensor", ["vf", "mx"], ["tB"], out=m4["tB"][:], in0=m4["vf"][:], in1=v, op=ALU.subtract)
                X("vector", "tensor_tensor", ["tB", "tA"], ["tB"], out=m4["tB"][:], in0=m4["tB"][:], in1=m4["tA"][:], op=ALU.mult)
                X("vector", "tensor_tensor", ["tB", "mx"], ["vv"], out=vv[:], in0=m4["tB"][:], in1=v, op=ALU.add)
            kkn, kh, brec, Gw, tA, tB, Ex = (m4[n] for n in ["kkn", "kh", "brec", "Gw", "tA", "tB", "Ex"])
            X("vector", "tensor_tensor", ["mx", "pvs"], ["kkn"], out=kkn[:], in0=k, in1=pbc(l, "kk", 4, T), op=ALU.mult)
            X("gpsimd", "tensor_tensor", ["kkn"], ["tA"], out=tA[:], in0=kkn[:], in1=kkn[:], op=ALU.mult)
            pss, kss = PS(6, 0, 512)
            for c in range(4):
                X("tensor", "matmul", ["tA", "c_ones_bd"], kss, out=pss[:, c * T:(c + 1) * T], lhsT=cst["ones_bd"][:], rhs=tA[:, c, :],
                  start=True, stop=True)
            tAf = tA[:].rearrange("p c t -> p (c t)")
            X("vector", "tensor_scalar", kss, ["tA"], out=tAf, in0=pss, scalar1=1e-24, scalar2=None, op0=ALU.max)
            X("scalar", "activation", ["tA"], ["tA"], out=tAf, in_=tAf, func=AF.Ln)
            X("scalar", "activation", ["tA"], ["tA"], out=tAf, in_=tAf, func=AF.Exp, scale=-0.5)
            X("vector", "tensor_tensor", ["kkn", "tA"], ["kkn"], out=kkn[:], in0=kkn[:], in1=tA[:], op=ALU.mult)
            FEED(2)
            X("vector", "scalar_tensor_tensor", ["aa", "pvs"], ["tB"], out=tB[:], in0=m4["aa"][:], scalar=-1.0, in1=pbc(l, "ka", 4, T),
              op0=ALU.add, op1=ALU.mult)
            X("vector", "scalar_tensor_tensor", ["tB", "mx"], ["kh"], out=kh[:], in0=tB[:], scalar=1.0, in1=k, op0=ALU.add, op1=ALU.mult)
            X("gpsimd", "tensor_tensor", ["kkn", "aa"], ["brec"], out=brec[:], in0=kkn[:], in1=m4["aa"][:], op=ALU.mult)
            sgf = m4["sg"][:].rearrange("p c t -> p (c t)")
            Gwf = Gw[:].rearrange("p c t -> p (c t)")
            X("scalar", "mul", ["sg"], ["sg"], out=sgf, in_=sgf, mul=C0)
            X("vector", "tensor_tensor_scan", ["sg", "c_m_reset"], ["Gw"], out=Gwf, data0=cst["m_reset"][:], data1=sgf, initial=0.0,
              op0=ALU.mult, op1=ALU.add)
            X("gpsimd", "tensor_tensor", ["Gw", "sg"], ["tB"], out=tB[:], in0=Gw[:], in1=m4["sg"][:], op=ALU.subtract)
            Exf = Ex[:].rearrange("p c t -> p (c t)")
            X("scalar", "activation", ["tB"], ["Ex"], out=Exf, in_=tB[:].rearrange("p c t -> p (c t)"), func=AF.Exp)
            X("vector", "scalar_tensor_tensor", ["kkn", "Ex"], ["tA"], out=tA[:], in0=kkn[:], scalar=-1.0, in1=Ex[:],
              op0=ALU.mult, op1=ALU.mult)
            bd_write("Abd", tA[:], None, None, ["tA"])
            X(EW(), "tensor_copy", ["tA"], ["AR"], out=AR[:, :, :, 0:32], in_=v4(tA[:]))
            if phB:
                X("scalar", "activation", ["Gw"], ["Ex"], out=Exf, in_=Gwf, func=AF.Exp)
                bd_write("Rbd", r, Ex[:], ALU.mult, ["mx", "Ex"])
                X(EW(), "tensor_tensor", ["mx", "Ex"], ["AR"], out=AR[:, :, :, 32:64], in0=v4(r), in1=v4(Ex[:]), op=ALU.mult)
            FEED(2)
            X("scalar", "activation", ["Gw"], ["Ex"], out=Exf, in_=Gwf, func=AF.Exp, scale=-1.0)
            bd_write("Kbd", kh[:], Ex[:], ALU.mult, ["kh", "Ex"])
            bd_write("Bbd", brec[:], Ex[:], ALU.mult, ["brec", "Ex"])
            X(EW(), "tensor_tensor", ["brec", "Ex"], ["Bp"], out=Bp[:], in0=v4(brec[:]), in1=v4(Ex[:]), op=ALU.mult)
            FEED(2)
            GL = v4(Gw[:])[:, :, :, L - 1:L]
            X("scalar", "activation", ["Gw"], ["gL"], out=gL[:].unsqueeze(3), in_=GL, func=AF.Exp)
            X("vector", "tensor_tensor", ["Gw"], ["tB"], out=v4(tB[:]), in0=GL.to_broadcast([128, 4, 4, L]), in1=v4(Gw[:]), op=ALU.subtract)
            X("scalar", "activation", ["tB"], ["Ex"], out=Exf, in_=tB[:].rearrange("p c t -> p (c t)"), func=AF.Exp)
            bd_write("KHbd", kh[:], Ex[:], ALU.mult, ["kh", "Ex"])
            bd_write("BHbd", brec[:], Ex[:], ALU.mult, ["brec", "Ex"])
            bd_write("Vbd", vv[:], None, None, ["vv"])
            if phB:
                X("vector", "tensor_tensor", ["mx", "kh"], ["tA"], out=tA[:], in0=r, in1=kh[:], op=ALU.mult)
                X("gpsimd", "tensor_tensor", ["tA", "pvs"], ["tA"], out=tA[:], in0=tA[:], in1=pbc(l, "rk", 4, T), op=ALU.mult)
                pbn, kbn = PS(4, 0, 512)
                for c in range(4):
                    X("tensor", "matmul", ["tA", "c_ones_bd"], kbn, out=pbn[:, c * T:(c + 1) * T], lhsT=cst["ones_bd"][:], rhs=tA[:, c, :],
                      start=True, stop=True)
                X("vector", "tensor_tensor", kbn + ["vv"], ["bv"], out=m4["bv"][:].rearrange("p c t -> p (c t)"), in0=pbn,
                  in1=vv[:].rearrange("p c t -> p (c t)"), op=ALU.mult)

        QS = [(4, 256), (5, 0), (6, 0)]
        _qs = [0]

        def QSLOT():
            b, o = QS[_qs[0] % 3]
            _qs[0] += 1
            return PS(b, o, 128)

        def mm_evac_copy(lhs, lk, rhs, rk, dst, dk, eng):
            pr, pk = QSLOT()
            X("tensor", "matmul", [lk, rk], pk, out=pr, lhsT=lhs, rhs=rhs, start=True, stop=True)
            if eng == "scalar":
                X("scalar", "activation", pk, [dk], out=dst, in_=pr, func=AF.Copy)
            else:
                X("vector", "tensor_copy", pk, [dk], out=dst, in_=pr)

        def mm_evac_add(lhs, lk, rhs, rk, addend, ak, dst, dk):
            pr, pk = QSLOT()
            X("tensor", "matmul", [lk, rk], pk, out=pr, lhsT=lhs, rhs=rhs, start=True, stop=True)
            X("vector", "tensor_tensor", pk + [ak], [dk], out=dst, in0=pr, in1=addend, op=ALU.add)

        def rwkv_steps(l, phB, q, par):
            NV = 64 if phB else 128
            ncol = 64 if phB else 32

            def kn(n, g):
                return "%s%d" % (n, g)

            def kp(n, g):
                return "%s%d_%d" % (n, par, g)

            def CB(n, g):
                return chb2[n][par][g]

            p1s = [PS(4, 0, 160), PS(5, 0, 160)]

            def st_stage1(g):
                p1, k1 = p1s[g]
                for (lf, rf, rk_, c0, cw) in (("Kbd", AR, "AR", 0, ncol), ("Bbd", AR, "AR", 64, ncol), ("Abd", Bp, "Bp", 128, 32)):
                    for cc in range(2):
                        c = 2 * g + cc
                        rhs = rf[:, c, q, 0:cw] if rk_ == "AR" else rf[:, c, q, :]
                        X("tensor", "matmul", [lf, rk_], k1, out=p1[:, c0:c0 + cw], lhsT=bd[lf][:, c, q, :], rhs=rhs,
                          start=(cc == 0), stop=(cc == 1))

            def st_evac1(g):
                p1, k1 = p1s[g]

                def mask_evac(dst, dkey, col, mname, eng):
                    X(eng, "tensor_tensor", k1 + ["c_" + mname], [dkey], out=dst[:].rearrange("p (b t) -> p b t", t=32),
                      in0=p1[:, col:col + 32].unsqueeze(1).to_broadcast([128, 4, 32]),
                      in1=cst[mname][:].rearrange("p (b t) -> p b t", t=32), op=ALU.mult)
                mask_evac(chb["X1"][g], kn("X1", g), 64, "m_strict", "vector")
                mask_evac(chb["X1T"][g], kn("X1T", g), 128, "m_lower", "vector")
                mask_evac(CB("Aka", g), kp("Aka", g), 0, "m_strict", "vector")
                if phB:
                    mask_evac(CB("Akr", g), kp("Akr", g), 32, "m_incl", "vector")
                    mask_evac(CB("Abr", g), kp("Abr", g), 96, "m_incl", "vector")
                X("gpsimd", "tensor_tensor", [kn("X1", g), "ident_b"], [kp("Ma", g)], out=CB("Ma", g)[:], in0=chb["X1"][g][:], in1=ident_b[:], op=ALU.add)

            def B(n, g):
                if n == "Ma":
                    return CB("Ma", g)[:], kp("Ma", g)
                return chb[n][g][:], kn(n, g)

            def inv_steps(g):
                cp = lambda lh, rh, ds, eng: (lambda: mm_evac_copy(B(lh, g)[0], B(lh, g)[1], B(rh, g)[0], B(rh, g)[1], B(ds, g)[0], B(ds, g)[1], eng))
                ad = lambda lh, rh, ds: (lambda: mm_evac_add(B(lh, g)[0], B(lh, g)[1], B(rh, g)[0], B(rh, g)[1], B(rh, g)[0], B(rh, g)[1], B(ds, g)[0], B(ds, g)[1]))
                return [cp("X1T", "X1", "Xa", "scalar"), cp("X1", "X1T", "XaT", "vector"), ad("XaT", "Ma", "Mb"),
                        cp("XaT", "Xa", "Xb", "scalar"), cp("Xa", "XaT", "XbT", "vector"), ad("XbT", "Mb", "Ma"),
                        cp("XbT", "Xb", "Xa", "scalar"), cp("Xb", "XbT", "XaT", "vector"), ad("XaT", "Ma", "Mb"),
                        cp("Xa", "XaT", "XbT", "vector"), ad("XbT", "Mb", "Ma")]

            KTp, BTp, Vstp = KT2[par], BT2[par], Vst2[par]

            def st_tokmajor(g):
                for cc in range(2):
                    c = 2 * g + cc
                    pk_, kk_ = PSB(c * 128, 128)
                    X("tensor", "transpose", ["KHbd", "ident_b"], kk_, out=pk_, in_=bd["KHbd"][:, c, q, :], identity=ident_b[:])
                    pb_, kb_ = PSB(512 + c * 128, 128)
                    X("tensor", "transpose", ["BHbd", "ident_b"], kb_, out=pb_, in_=bd["BHbd"][:, c, q, :], identity=ident_b[:])
                pv_, kv_ = PS(6, 256 + 64 * g, 64)
                for cc in range(2):
                    c = 2 * g + cc
                    X("tensor", "matmul", ["Vbd", "isel_b"], kv_, out=pv_, lhsT=bd["Vbd"][:, c, q, :], rhs=isel_b[:], start=(cc == 0), stop=(cc == 1))
                pk2, kk2 = PSB(2 * g * 128, 256)
                X("scalar", "activation", kk2, ["KT%d_%d" % (par, 2 * g), "KT%d_%d" % (par, 2 * g + 1)],
                  out=KTp[:, 2 * g:2 * g + 2, :].rearrange("p c k -> p (c k)"), in_=pk2, func=AF.Copy)
                pb2, kb2 = PSB(512 + 2 * g * 128, 256)
                X("vector", "tensor_copy", kb2, ["BT%d_%d" % (par, 2 * g), "BT%d_%d" % (par, 2 * g + 1)],
                  out=BTp[:, 2 * g:2 * g + 2, :].rearrange("p c k -> p (c k)"), in_=pb2)
                X("scalar", "activation", kv_, ["Vst%d_%d" % (par, g)], out=Vstp[:, g, 0:64], in_=pv_, func=AF.Copy)

            def st_W(g):
                pW, kW = PS(0 + g, 0, NV)
                X("tensor", "matmul", [kp("Aka", g), "Vst%d_%d" % (par, g)], kW, out=pW, lhsT=CB("Aka", g)[:], rhs=Vstp[:, g, 0:NV], start=True, stop=False)
                for cc in range(2):
                    c = 2 * g + cc
                    X("tensor", "matmul", ["Abd", "Sb%d" % c], kW, out=pW, lhsT=bd["Abd"][:, c, q, :], rhs=Sb[:, c, 0:NV], start=False, stop=(cc == 1))
                if g == 0:
                    X("scalar", "activation", kW, ["Wt%d" % g], out=Wt[:, g, 0:NV], in_=pW, func=AF.Copy)
                else:
                    X("vector", "tensor_copy", kW, ["Wt%d" % g], out=Wt[:, g, 0:NV], in_=pW)

            def st_U(g):
                pU, kU = PS(2 + g, 0, NV)
                X("tensor", "matmul", [kp("Ma", g), "Wt%d" % g], kU, out=pU, lhsT=CB("Ma", g)[:], rhs=Wt[:, g, 0:NV], start=True, stop=True)
                if g == 0:
                    X("vector", "tensor_copy", kU, ["Ut%d" % g], out=Ut[:, g, 0:NV], in_=pU)
                else:
                    X("scalar", "activation", kU, ["Ut%d" % g], out=Ut[:, g, 0:NV], in_=pU, func=AF.Copy)

            def st_Y(g):
                pY, kY = PS(0 + g, 256, 64)
                X("tensor", "matmul", [kp("Akr", g), "Vst%d_%d" % (par, g)], kY, out=pY, lhsT=CB("Akr", g)[:], rhs=Vstp[:, g, 0:64], start=True, stop=False)
                X("tensor", "matmul", [kp("Abr", g), "Ut%d" % g], kY, out=pY, lhsT=CB("Abr", g)[:], rhs=Ut[:, g, 0:64], start=False, stop=False)
                for cc in range(2):
                    c = 2 * g + cc
                    X("tensor", "matmul", ["Rbd", "Sb%d" % c], kY, out=pY, lhsT=bd["Rbd"][:, c, q, :], rhs=Sb[:, c, 0:64], start=False, stop=(cc == 1))
                X("scalar", "activation", kY, ["Yst"], out=Yst[:, g * 4 + q, :], in_=pY, func=AF.Copy)

            def st_S(c):
                g = c // 2
                pS, kS = PS(2 + (c % 2), 256 * (c // 2), NV)
                X("tensor", "matmul", ["KT%d_%d" % (par, c), "Vst%d_%d" % (par, g)], kS, out=pS, lhsT=KTp[:, c, :], rhs=Vstp[:, g, 0:NV], start=True, stop=False)
                X("tensor", "matmul", ["BT%d_%d" % (par, c), "Ut%d" % g], kS, out=pS, lhsT=BTp[:, c, :], rhs=Ut[:, g, 0:NV], start=False, stop=True)
                X("vector", "scalar_tensor_tensor", kS + ["Sf%d" % c, "gL"], ["Sf%d" % c], out=Sf[:, c, 0:NV], in0=Sf[:, c, 0:NV],
                  scalar=gL[:, c, q:q + 1], in1=pS, op0=ALU.mult, op1=ALU.add)
                X("scalar", "activation", ["Sf%d" % c], ["Sb%d" % c], out=Sb[:, c, 0:NV], in_=Sf[:, c, 0:NV], func=AF.Copy)

            mk = lambda f, a: (lambda: f(a))
            pre = [mk(st_stage1, 0), mk(st_stage1, 1), mk(st_evac1, 0), mk(st_evac1, 1), mk(st_tokmajor, 0), mk(st_tokmajor, 1)]
            i0, i1 = inv_steps(0), inv_steps(1)
            for a_, b_ in zip(i0, i1):
                pre += [a_, b_]
            chain = [mk(st_W, 0), mk(st_W, 1), mk(st_U, 0), mk(st_U, 1)]
            if phB:
                chain += [mk(st_Y, 0), mk(st_Y, 1)]
            chain += [mk(st_S, c) for c in range(4)]
            return pre, chain

        def mixer_chunks(l, phB, col0, before_chunk, after_chunk):
            pre0, _ = rwkv_steps(l, phB, 0, 0)
            for f_ in pre0:
                f_()
            for q in range(4):
                par = q % 2
                before_chunk(q)
                _, chain = rwkv_steps(l, phB, q, par)
                nxt = rwkv_steps(l, phB, q + 1, 1 - par)[0] if q < 3 else []
                hg = hgrn_chunk_parts(l, phB, q, col0)
                hg[0]()
                per = -(-len(nxt) // len(chain)) if nxt else 0
                for ci, cstep in enumerate(chain):
                    cstep()
                    if ci % 2 == 1:
                        FEED(1)
                    for _ in range(per):
                        if nxt:
                            nxt.pop(0)()
                    if ci == 1:
                        hg[1]()
                    if ci == 3:
                        hg[2]()
                while nxt:
                    nxt.pop(0)()
                after_chunk(q)

        def rwkv_post(l, col0):
            s1, s2, mean, var = (gst[n] for n in ["s1", "s2", "mean", "var"])
            X("vector", "tensor_reduce", ["Yst"], ["g_s1"], out=s1[:], in_=Yst[:], axis=AX.X, op=ALU.add)
            X("gpsimd", "tensor_tensor", ["Yst"], ["Ysq"], out=Ysq[:], in0=Yst[:], in1=Yst[:], op=ALU.mult)
            X("vector", "tensor_reduce", ["Ysq"], ["g_s2"], out=s2[:], in_=Ysq[:], axis=AX.X, op=ALU.add)
            X("vector", "tensor_scalar", ["g_s1"], ["g_mean"], out=mean[:], in0=s1[:], scalar1=1.0 / 64, scalar2=None, op0=ALU.mult)
            X("vector", "tensor_tensor", ["g_mean"], ["g_s1"], out=s1[:], in0=mean[:], in1=mean[:], op=ALU.mult)
            X("vector", "scalar_tensor_tensor", ["g_s2", "g_s1"], ["g_var"], out=var[:], in0=s2[:], scalar=1.0 / 64, in1=s1[:],
              op0=ALU.mult, op1=ALU.subtract)
            X("scalar", "activation", ["g_var", "eps_gn"], ["g_var"], out=var[:], in_=var[:], func=AF.Ln, bias=eps_gn[:])
            X("scalar", "activation", ["g_var"], ["g_var"], out=var[:], in_=var[:], func=AF.Exp, scale=-0.5)
            X("vector", "tensor_tensor", ["Yst", "g_mean"], ["Ysq"], out=Ysq[:], in0=Yst[:], in1=mean[:].unsqueeze(2).to_broadcast([128, 8, 64]),
              op=ALU.subtract)
            X("vector", "tensor_tensor", ["Ysq", "g_var"], ["Ysq"], out=Ysq[:], in0=Ysq[:], in1=var[:].unsqueeze(2).to_broadcast([128, 8, 64]),
              op=ALU.mult)
            for g in range(2):
                ys = Ysq[:, g * 4:(g + 1) * 4, :]
                X(EW(), "tensor_tensor", ["Ysq", "lnw"], ["Ysq"], out=ys, in0=ys, in1=lnw[:, l, g, :].unsqueeze(1).to_broadcast([128, 4, 64]),
                  op=ALU.mult)
                X(EW(), "tensor_tensor", ["Ysq", "lnb"], ["Yrep"], out=Yrep[:, g * 4:(g + 1) * 4, :, :],
                  in0=ys.unsqueeze(2).to_broadcast([128, 4, 2, 64]),
                  in1=lnb[:, l, g, :].unsqueeze(1).unsqueeze(1).to_broadcast([128, 4, 2, 64]), op=ALU.add)
            t1 = m4["t1"]
            for g in range(2):
                for q in range(4):
                    pT, kT = QSLOT()
                    X("tensor", "transpose", ["Yrep", "c_ident"], kT, out=pT, in_=Yrep[:, g * 4 + q, :, :].rearrange("p r v -> p (r v)"),
                      identity=cst["ident"][:])
                    for hh in range(2):
                        X("vector", "tensor_tensor", kT + ["bv"], ["t1"], out=t1[hh * 64:(hh + 1) * 64, 2 * g:2 * g + 2, q * L:(q + 1) * L],
                          in0=pT[hh * 64:(hh + 1) * 64, :].rearrange("p (c h t) -> p c h t", c=2, h=2)[:, :, hh, :],
                          in1=m4["bv"][hh * 64:(hh + 1) * 64, 2 * g:2 * g + 2, q * L:(q + 1) * L], op=ALU.add)
            X("gpsimd", "tensor_tensor", ["t1", "gfm"], ["yg_b"], out=yg_b[:, :, col0:col0 + T], in0=t1[:], in1=m4["gfm"][:], op=ALU.mult)

        def hgrn_prep(l, phB, col0):
            f, lf, khh, G2, d1, E2, hA = (h4[n] for n in ["f", "lf", "khh", "G2", "d1", "E2", "hA"])
            fl = lambda t: t[:].rearrange("p c t -> p (c t)")
            sig = hsig[:, :, col0:col0 + T]
            X("vector", "tensor_tensor", ["hsig", "oml"], ["sg"], out=f[:], in0=sig, in1=oml[:, l, :].unsqueeze(2).to_broadcast([128, 4, T]),
              op=ALU.mult)
            X("vector", "tensor_tensor", ["sg", "lbv"], ["sg"], out=f[:], in0=f[:], in1=lbv[:, l, :].unsqueeze(2).to_broadcast([128, 4, T]),
              op=ALU.add)
            X("scalar", "activation", ["sg"], ["aa"], out=fl(lf), in_=fl(f), func=AF.Ln)
            X("gpsimd", "tensor_scalar", ["sg"], ["kkn"], out=fl(khh), in0=fl(f), scalar1=-1.0, scalar2=1.0, op0=ALU.mult, op1=ALU.add)
            X("vector", "tensor_tensor_scan", ["aa", "c_m_reset"], ["Gw"], out=fl(G2), data0=cst["m_reset"][:], data1=fl(lf), initial=0.0,
              op0=ALU.mult, op1=ALU.add)
            GLv = v4(G2[:])[:, :, :, L - 1:L]
            X("scalar", "activation", ["Gw"], ["dL"], out=dL[:].unsqueeze(3), in_=GLv, func=AF.Exp)
            if phB:
                hqv = hq[:, :, col0:col0 + T]
                Gm = v4(G2[:])[:, :, :, L // 2 - 1:L // 2]
                X("vector", "tensor_tensor", ["Gw"], ["tB"], out=v4(d1[:]), in0=v4(G2[:]), in1=Gm.to_broadcast([128, 4, 4, L]), op=ALU.subtract)
                X("scalar", "activation", ["tB"], ["tA"], out=fl(hA), in_=fl(d1), func=AF.Exp)
                X("vector", "tensor_tensor", ["hq", "tA"], ["hb_Qt"], out=hb["Qt"][:], in0=hqv, in1=hA[:], op=ALU.mult)
                X("scalar", "activation", ["tB"], ["tA"], out=fl(hA), in_=fl(d1), func=AF.Exp, scale=-1.0)
                X("gpsimd", "tensor_tensor", ["kkn", "tA"], ["hb_Kt"], out=hb["Kt"][:], in0=khh[:], in1=hA[:], op=ALU.mult)
                X("scalar", "activation", ["Gw"], ["Ex"], out=fl(E2), in_=fl(G2), func=AF.Exp)
                X("vector", "tensor_tensor", ["hq", "Ex"], ["hb_Q2"], out=hb["Q2"][:], in0=hqv, in1=E2[:], op=ALU.mult)
            X("vector", "tensor_tensor", ["Gw"], ["tB"], out=v4(d1[:]), in0=GLv.to_broadcast([128, 4, 4, L]), in1=v4(G2[:]), op=ALU.subtract)
            X("scalar", "activation", ["tB"], ["tA"], out=fl(hA), in_=fl(d1), func=AF.Exp)
            X("gpsimd", "tensor_tensor", ["kkn", "tA"], ["hb_Kh"], out=hb["Kh"][:], in0=khh[:], in1=hA[:], op=ALU.mult)

        def hgrn_chunk_parts(l, phB, q, col0):
            cs = slice(q * L, (q + 1) * L)

            def part_pre():
                if phB:
                    pat, kat = PS(6, 384, 128)
                    for c in range(4):
                        X("tensor", "matmul", ["hb_Kt", "hb_Qt"], kat, out=pat[0:32, c * 32:(c + 1) * 32], lhsT=hb["Kt"][:, c, cs], rhs=hb["Qt"][:, c, cs],
                          start=True, stop=True)
                    X("vector", "tensor_tensor", kat + ["c_m_att"], ["att_b"], out=att_b[:], in0=pat[0:32, :].rearrange("p (c t) -> p c t", c=4),
                      in1=cst["m_att"][:].unsqueeze(1).to_broadcast([32, 4, 32]), op=ALU.mult)
                pvt, kvt = PSB(0, 512)
                pkt, kkt = PSB(512, 512)
                for c in range(4):
                    X("tensor", "transpose", ["hv_b", "ident_b"], kvt, out=pvt[0:32, c * 128:(c + 1) * 128],
                      in_=hv_b[:, c, col0 + q * L: col0 + (q + 1) * L], identity=ident_b[:])
                    X("tensor", "transpose", ["hb_Kh", "ident_b"], kkt, out=pkt[0:32, c * 128:(c + 1) * 128], in_=hb["Kh"][:, c, cs], identity=ident_b[:])
                X("scalar", "activation", kvt, ["VTh"], out=VTh[:].rearrange("p c v -> p (c v)"), in_=pvt[0:32, :], func=AF.Copy)
                X("vector", "tensor_copy", kkt, ["KTh"], out=KTh[:].rearrange("p c v -> p (c v)"), in_=pkt[0:32, :])

            def part_o():
                if phB:
                    po, ko = PS(4, 384, 128)
                    for c in range(4):
                        X("tensor", "matmul", ["Shb", "hb_Q2"], ko, out=po[:, c * 32:(c + 1) * 32], lhsT=Shb[:, c, :], rhs=hb["Q2"][:, c, cs], start=True, stop=False)
                        X("tensor", "matmul", ["VTh", "att_b"], ko, out=po[:, c * 32:(c + 1) * 32], lhsT=VTh[:, c, :], rhs=att_b[:, c, :], start=False, stop=True)
                    X("scalar", "activation", ko, ["h_o"], out=h4["o"][:, :, cs], in_=po.rearrange("p (c t) -> p c t", c=4), func=AF.Copy)

            def part_s():
                pss_, kss_ = PS(1, 0, 512)
                for c in range(4):
                    X("tensor", "matmul", ["KTh", "VTh"], kss_, out=pss_[:, c * 128:(c + 1) * 128], lhsT=KTh[:, c, :], rhs=VTh[:, c, :], start=True, stop=True)
                for c in range(4):
                    X("vector", "scalar_tensor_tensor", kss_ + ["Shf", "dL"], ["Shf"], out=Shf[:, c, :], in0=Shf[:, c, :], scalar=dL[:, c, q:q + 1],
                      in1=pss_[:, c * 128:(c + 1) * 128], op0=ALU.mult, op1=ALU.add)
                X("scalar", "activation", ["Shf"], ["Shb"], out=Shb[:].rearrange("p c v -> p (c v)"), in_=Shf[:].rearrange("p c v -> p (c v)"), func=AF.Copy)
                if not phB:
                    X("gpsimd", "tensor_tensor", ["Dtot", "dL"], ["Dtot"], out=Dtot[:], in0=Dtot[:], in1=dL[:, :, q], op=ALU.mult)
            return [part_pre, part_o, part_s]

        def hgrn_post(l, col0):
            o, o2, rs, hA = (h4[n] for n in ["o", "o2", "rs", "hA"])
            fl = lambda t: t[:].rearrange("p c t -> p (c t)")
            X("gpsimd", "tensor_tensor", ["h_o"], ["kh"], out=o2[:], in0=o[:], in1=o[:], op=ALU.mult)
            pn, kn_ = PS(4, 0, 512)
            for c in range(4):
                X("tensor", "matmul", ["kh", "c_ones_f"], kn_, out=pn[:, c * T:(c + 1) * T], lhsT=cst["ones_f"][:], rhs=o2[:, c, :], start=True, stop=True)
            X("scalar", "activation", kn_ + ["eps_rms"], ["brec"], out=fl(rs), in_=pn, func=AF.Ln, scale=1.0 / 128, bias=eps_rms[:])
            X("scalar", "activation", ["brec"], ["brec"], out=fl(rs), in_=fl(rs), func=AF.Exp, scale=-0.5)
            X("vector", "tensor_tensor", ["h_o", "brec"], ["h_o"], out=o[:], in0=o[:], in1=rs[:], op=ALU.mult)
            X("gpsimd", "tensor_tensor", ["hog", "pvs"], ["tA"], out=hA[:], in0=hog[:, :, col0:col0 + T], in1=pbc(l, "hnw", 4, T), op=ALU.mult)
            X("vector", "tensor_tensor", ["h_o", "tA"], ["ob_b"], out=ob_b[:, :, col0:col0 + T], in0=o[:], in1=hA[:], op=ALU.mult)

        shst = sb("shst", [128, RW_BLK, 4])
        shout = sb("shout", [128, RW_BLK, 4])
        shoutp = sb("shoutp", [128, RW_BLK])
        halo_prev = sb("halo_prev", [128, RW_BLK])
        hraw = sb("hraw", [128, 16])
        hall = sb("hall", [128, 4, 16])
        exb = sb("exb", [128, SW])
        exall = big[:, 0:SW]
        Xr = sb("Xr", [128, 4, 64])
        Xh = sb("Xh", [128, 4, 128])
        PTbd = sb("PTbd", [128, 128])
        lhsTf = sb("lhsTf", [128, 128])
        ftmp = sb("ftmp", [128, 128])
        X("vector", "memset", [], ["PTbd"], ap=PTbd[:], constant=0.0)
        X("vector", "memset", [], ["hraw"], ap=hraw[:], constant=0.0)
        groups4 = [[0, 1, 2, 3], [4, 5, 6, 7]]

        def make_handler(is_s, N):
            def handler(b, pr, pk):
                if b < 15:
                    if is_s:
                        dst = raw[:, b, 0:132].rearrange("p (s t) -> p s t", t=33)[:, :, 1:33]
                        src = pr.rearrange("p (s t) -> p s t", t=32)
                    else:
                        dst, src = raw[:, b, 1:N + 1], pr
                    X("scalar", "activation", pk, ["raw"], out=dst, in_=src, func=AF.Copy)
                elif b < 19:
                    X("scalar", "activation", pk, ["hq"], out=hq[:, b - 15, 0:N], in_=pr, func=AF.Silu)
                elif b < 23:
                    X("scalar", "activation", pk, ["hsig"], out=hsig[:, b - 19, 0:N], in_=pr, func=AF.Sigmoid)
                elif b < 27:
                    X("scalar", "activation", pk, ["hv_b"], out=hv_b[:, b - 23, 0:N], in_=pr, func=AF.Copy)
                elif b < 31:
                    X("scalar", "activation", pk, ["hog"], out=hog[:, b - 27, 0:N], in_=pr, func=AF.Silu)
                else:
                    X("scalar", "activation", pk, ["gates"], out=gates[:, b - 31, 0:N], in_=pr, func=AF.Sigmoid)
            return handler

        class Feeder:
            def __init__(self, l, blocks, N, handler):
                self.l, self.q, self.N, self.h = l, list(blocks), N, handler

            def feed(self, n=2):
                if self.q:
                    take, self.q = self.q[:n], self.q[n:]
                    emit_proj(self.l, take, self.N, self.h)

            def until(self, b):
                while self.q and self.q[0] <= b:
                    self.feed(2)

            def flush(self):
                while self.q:
                    self.feed(2)

        _feeder = [None]

        def FEED(n=2):
            if _feeder[0] is not None:
                _feeder[0].feed(n)

        def xsrc(l):
            return (xT, []) if l == 0 else (xs1, ["xs1"])

        def emit_halo(l):
            if l > 0:
                conv_pump(l, 10 ** 6)
            src, sk = xsrc(l)
            DMA("sync", "x_ld", xt[:, :, 0:1], src[:, :, NPT - 1:NPT], sk, ["xt"], allow_slow_non_contiguous=True)
            emit_norm(l, xt, "xt", 1, "nmix")

            def hh_(b, pr, pk):
                X("scalar", "activation", pk, ["hraw"], out=hraw[:, b:b + 1], in_=pr, func=AF.Copy)
            emit_proj(l, list(range(RW_BLK)), 1, hh_)
            DMA("gpsimd", "ex_h", cin_h[:, :], hraw[:], ["hraw"], ["cin_h"])
            p.op("gpsimd", lambda e: e.collective_compute("AllGather", ALU.bypass, replica_groups=groups4, ins=[cin_h[:, :]], outs=[cout_h[:, :]]),
                 ["cin_h"], ["cout_h"])
            DMA("gpsimd", "ex_h", hall[:], cout_h.rearrange("(r p) c -> p r c", p=128), ["cout_h"], ["hall"])
            X("vector", "tensor_scalar", ["hall", "c_halom"], ["halo_prev"], out=halo_prev[:], in0=hall[:, 0, 0:RW_BLK], scalar1=cst["halom"][:, 0:1],
              scalar2=None, op0=ALU.mult)
            for r in range(1, 4):
                X("vector", "scalar_tensor_tensor", ["hall", "c_halom", "halo_prev"], ["halo_prev"], out=halo_prev[:], in0=hall[:, r, 0:RW_BLK],
                  scalar=cst["halom"][:, r:r + 1], in1=halo_prev[:], op0=ALU.mult, op1=ALU.add)

        def emit_exchange(l):
            conv_pump(l, 10 ** 6)
            X("vector", "tensor_copy", ["Sf0", "Sf1", "Sf2", "Sf3"], ["exb"], out=exb[:, 0:512], in_=Sf[:].rearrange("p c v -> p (c v)"))
            X("vector", "tensor_copy", ["Shf"], ["exb"], out=exb[:, 512:1024], in_=Shf[:].rearrange("p c v -> p (c v)"))
            X("vector", "tensor_copy", ["Dtot"], ["exb"], out=exb[:, 1024:1028], in_=Dtot[:])
            X("vector", "memset", [], ["exb"], ap=exb[:, 1028:SW], constant=0.0)
            DMA("gpsimd", "ex_s", cin_s[:, :], exb[:], ["exb"], ["cin_s"])
            p.op("gpsimd", lambda e: e.collective_compute("AllGather", ALU.bypass, replica_groups=groups4, ins=[cin_s[:, :]], outs=[cout_s[:, :]]),
                 ["cin_s"], ["cout_s"])
            X("vector", "memset", [], ["Xr"], ap=Xr[:], constant=0.0)
            X("vector", "memset", [], ["Xh"], ap=Xh[:], constant=0.0)
            fm = cst["foldm"]
            for r in range(3):
                DMA("gpsimd", "ex_s", exall, cout_s[r * 128:(r + 1) * 128, :], ["cout_s"], ["gates"])
                for c in range(4):
                    for hh in range(2):
                        X("vector", "tensor_copy", ["gates"], ["PTbd"], out=PTbd[hh * 64:(hh + 1) * 64, hh * 64:(hh + 1) * 64],
                          in_=exall[hh * 64:(hh + 1) * 64, c * 128 + 64:c * 128 + 128])
                    pT, kT = QSLOT()
                    X("tensor", "transpose", ["PTbd", "c_ident"], kT, out=pT, in_=PTbd[:], identity=cst["ident"][:])
                    X("vector", "tensor_copy", kT, ["lhsTf"], out=lhsTf[:], in_=pT)
                    pm, km = QSLOT()
                    X("tensor", "matmul", ["lhsTf", "Xr"], km, out=pm[:, 0:64], lhsT=lhsTf[:], rhs=Xr[:, c, :], start=True, stop=True)
                    X("vector", "tensor_tensor", km + ["gates"], ["ftmp"], out=ftmp[:, 0:64], in0=pm[:, 0:64], in1=exall[:, c * 128:c * 128 + 64], op=ALU.add)
                    X("vector", "tensor_tensor", ["ftmp", "Xr"], ["ftmp"], out=ftmp[:, 0:64], in0=ftmp[:, 0:64], in1=Xr[:, c, :], op=ALU.subtract)
                    X("vector", "scalar_tensor_tensor", ["ftmp", "Xr", "c_foldm"], ["Xr"], out=Xr[:, c, :], in0=ftmp[:, 0:64], scalar=fm[:, r:r + 1],
                      in1=Xr[:, c, :], op0=ALU.mult, op1=ALU.add)
                for c in range(4):
                    X("vector", "scalar_tensor_tensor", ["Xh", "gates"], ["ftmp"], out=ftmp[:], in0=Xh[:, c, :], scalar=exall[:, 1024 + c:1025 + c],
                      in1=exall[:, 512 + c * 128:512 + (c + 1) * 128], op0=ALU.mult, op1=ALU.add)
                    X("vector", "tensor_tensor", ["ftmp", "Xh"], ["ftmp"], out=ftmp[:], in0=ftmp[:], in1=Xh[:, c, :], op=ALU.subtract)
                    X("vector", "scalar_tensor_tensor", ["ftmp", "Xh", "c_foldm"], ["Xh"], out=Xh[:, c, :], in0=ftmp[:], scalar=fm[:, r:r + 1],
                      in1=Xh[:, c, :], op0=ALU.mult, op1=ALU.add)

        SFK = ["Sf0", "Sf1", "Sf2", "Sf3"]
        SBK = ["Sb0", "Sb1", "Sb2", "Sb3"]

        def shadows():
            X("scalar", "activation", SFK, SBK, out=Sb[:].rearrange("p c v -> p (c v)"), in_=Sf[:].rearrange("p c v -> p (c v)"), func=AF.Copy)
            X("scalar", "activation", ["Shf"], ["Shb"], out=Shb[:].rearrange("p c v -> p (c v)"), in_=Shf[:].rearrange("p c v -> p (c v)"), func=AF.Copy)

        def init_states_A():
            X("vector", "memset", [], SFK, ap=Sf[:], constant=0.0)
            for c in range(4):
                X("vector", "tensor_copy", ["c_isel"], SFK, out=Sf[:, c, 64:128], in_=cst["isel"][:])
            X("vector", "memset", [], ["Shf"], ap=Shf[:], constant=0.0)
            X("vector", "memset", [], ["Dtot"], ap=Dtot[:], constant=1.0)
            shadows()

        def init_states_B():
            X("vector", "tensor_copy", ["Xr"], SFK, out=Sf[:, :, 0:64], in_=Xr[:])
            X("vector", "tensor_copy", ["Xh"], ["Shf"], out=Shf[:], in_=Xh[:])
            shadows()

        def layer_tile(l, phB, ti):
            conv_pump(l + 1 if phB else l, 8)
            is_s = (ti == NPS)
            N = 128 if is_s else W
            tok0 = NPT if is_s else ti * W
            last_prompt = (ti == NPS - 1)
            src, sk = xsrc(l)
            DMA("sync", "x_ld", xt[:, :, 0:N], src[:, :, tok0:tok0 + N], sk, ["xt"])
            emit_norm(l, xt, "xt", N, "nmix")
            if is_s:
                DMA("sync", "sh_ld", shst[:], st_shift[l], [], ["shst"])
                X("vector", "tensor_copy", ["shst"], ["raw"], out=raw[:, :, 0:132].rearrange("p b (s t) -> p b s t", t=33)[:, :, :, 0:1],
                  in_=shst[:].unsqueeze(3))
            blocks = list(range(NBLK)) if phB else (list(range(4, 13)) + list(range(19, 27)))
            fd = Feeder(l, blocks, N, make_handler(is_s, N))
            _feeder[0] = fd
            fd.until(14)
            nb = (0, RW_BLK) if phB else (4, 13)
            for j in range(N // T):
                col0 = j * T
                if is_s:
                    rv = raw[:, nb[0]:nb[1], 0:132].rearrange("p b (s t) -> p b s t", t=33)
                    cur, prv = rv[:, :, :, 1:33], rv[:, :, :, 0:32]
                    mxv = mx[:, nb[0]:nb[1], :].rearrange("p b (s t) -> p b s t", t=32)
                else:
                    cur, prv = raw[:, nb[0]:nb[1], 1 + col0:1 + col0 + T], raw[:, nb[0]:nb[1], col0:col0 + T]
                    mxv = mx[:, nb[0]:nb[1], :]
                rwkv_prep(l, phB, cur, prv, mxv, tok0 + col0, nb)
                fd.until(22)
                hgrn_prep(l, phB, col0)
                fd.until(26)
                def before_chunk(q, l=l, is_s=is_s):
                    if is_s:
                        DMA("sync", "st_ld", Sf[:, :, 0:64], st_rwkv[l, q], [], SFK)
                        DMA("sync", "st_ld", Shf[:], st_hgrn[l, q], [], ["Shf"])
                        shadows()

                def after_chunk(q, l=l, is_s=is_s):
                    if is_s:
                        DMA("gpsimd", "st_out", o_rwkv_s[l, q], Sf[:, :, 0:64], SFK, ["o_rwkv_s"], is_output=True)
                        DMA("gpsimd", "st_out", o_hgrn_s[l, q], Shf[:], ["Shf"], ["o_hgrn_s"], is_output=True)
                mixer_chunks(l, phB, col0, before_chunk, after_chunk)
                if phB:
                    rwkv_post(l, col0)
                    fd.until(30)
                    hgrn_post(l, col0)
            fd.flush()
            _feeder[0] = None
            if is_s:
                if phB:
                    X("vector", "tensor_copy", ["raw"], ["shout"], out=shout[:].unsqueeze(3),
                      in_=raw[:, :, 0:132].rearrange("p b (s t) -> p b s t", t=33)[:, :, :, 32:33])
                    DMA("gpsimd", "st_out", o_shift_s[l], shout[:], ["shout"], ["o_shift_s"], is_output=True)
            else:
                if phB and last_prompt:
                    X("vector", "tensor_copy", ["raw"], ["shoutp"], out=shoutp[:].unsqueeze(2), in_=raw[:, :, W:W + 1])
                    DMA("gpsimd", "st_out", o_shift_p[l], shoutp[:], ["shoutp"], ["o_shift_p"], is_output=True)
                    DMA("gpsimd", "st_out", o_rwkv_p[l], Sf[:, :, 0:64], SFK, ["o_rwkv_p"], is_output=True)
                    DMA("gpsimd", "st_out", o_hgrn_p[l], Shf[:], ["Shf"], ["o_hgrn_p"], is_output=True)
                X("vector", "tensor_copy", ["raw"], ["raw"], out=raw[:, :, 0:1], in_=raw[:, :, W:W + 1])
            if not phB:
                return
            for o8 in range(8):
                s_ = _wsl[0] % WSL
                _wsl[0] += 1
                DMA("sync", "wsl%d" % s_, wslot[s_][:, 0, :], wb_oab[l, o8], ["wb_o%d" % l], ["wslot%d" % s_])
                sa = _psl[0] % 4
                _psl[0] += 1
                pa_, ka_ = PS(sa, 0, N)
                for c in range(4):
                    X("tensor", "matmul", ["wslot%d" % s_, "yg_b"], ka_, out=pa_, lhsT=wslot[s_][:, 0, c * 128:(c + 1) * 128], rhs=yg_b[:, c, 0:N],
                      start=(c == 0), stop=(c == 3))
                sb_ = _psl[0] % 4
                _psl[0] += 1
                pb_, kb_ = PS(sb_, 0, N)
                for c in range(4):
                    X("tensor", "matmul", ["wslot%d" % s_, "ob_b"], kb_, out=pb_, lhsT=wslot[s_][:, 0, (4 + c) * 128:(5 + c) * 128], rhs=ob_b[:, c, 0:N],
                      start=(c == 0), stop=(c == 3))
                X("vector", "tensor_tensor", ka_ + ["gates"], ["mtmp"], out=mtmp[:, 0:N], in0=pa_, in1=gates[:, o8, 0:N], op=ALU.mult)
                X("vector", "tensor_tensor", kb_ + ["gates"], ["relu0"], out=relu_t[0][:, 0:N], in0=pb_, in1=gates[:, 8 + o8, 0:N], op=ALU.mult)
                X("gpsimd", "tensor_tensor", ["mtmp", "relu0"], ["mixin"], out=mixin[:, o8, 0:N], in0=mtmp[:, 0:N], in1=relu_t[0][:, 0:N], op=ALU.add)
            for o8 in range(8):
                s_ = _wsl[0] % WSL
                _wsl[0] += 1
                DMA("sync", "wsl%d" % s_, wslot[s_][:, 0, :], wb_o[l, o8], ["wb_o%d" % l], ["wslot%d" % s_])
                sm = _psl[0] % 4
                _psl[0] += 1
                pm_, km_ = PS(sm, 0, N)
                for kc in range(KC):
                    X("tensor", "matmul", ["wslot%d" % s_, "mixin"], km_, out=pm_, lhsT=wslot[s_][:, 0, kc * 128:(kc + 1) * 128], rhs=mixin[:, kc, 0:N],
                      start=(kc == 0), stop=(kc == KC - 1))
                X("vector", "tensor_tensor", km_ + ["xt"], ["x1"], out=x1[:, o8, 0:N], in0=pm_, in1=xt[:, o8, 0:N], op=ALU.add)
            DMA("gpsimd", "x1_st", x1s[:, :, tok0:tok0 + N], x1[:, :, 0:N], ["x1"], ["x1s"])

        F_x = aalloc([128, KC, 512], F32, reset=True)
        F_h = aalloc([128, KC, 512], BF16)
        F_sq = [aalloc([128, 512]) for _ in range(2)]
        F_rstd = aalloc([128, 512])
        F_relu = [aalloc([128, 512]) for _ in range(2)]
        F_act = aalloc([128, 16, 512], BF16)
        F_up = [aalloc([128, KC, 256], BF16) for _ in range(2)]
        F_dn = [aalloc([128, 16, 128], BF16) for _ in range(2)]
        _fs = [0, 0]

        def stage_F(l, tok0, N):
            DMA("sync", "f_ld", F_x[:, :, 0:N], x1s[:, :, tok0:tok0 + N], ["x1s"], ["F_x"])
            emit_norm(l, F_x, "F_x", N, "nffn", outbuf=F_h, outkey="F_h", sqk=F_sq, rstd=F_rstd, pfx="F_")
            for h in range(2):
                for fg in range(8):
                    su = _fs[0] % 2
                    _fs[0] += 1
                    DMA("sync", "up%d" % su, F_up[su][:].rearrange("p k c -> p (k c)"), wb_up[l, h * 8 + fg], ["wb_f%d" % l], ["F_up%d" % su])
                    for fb in range(2):
                        pu, ku = PS(4 + (fb % 2), 0, N)
                        for kc in range(KC):
                            X("tensor", "matmul", ["F_up%d" % su, "F_h"], ku, out=pu, lhsT=F_up[su][:, kc, fb * 128:(fb + 1) * 128], rhs=F_h[:, kc, 0:N],
                              start=(kc == 0), stop=(kc == KC - 1))
                        rt = F_relu[fb % 2]
                        X("scalar", "activation", ku, ["F_relu%d" % (fb % 2)], out=rt[:, 0:N], in_=pu, func=AF.Relu)
                        X("gpsimd", "tensor_tensor", ["F_relu%d" % (fb % 2)], ["F_act"], out=F_act[:, fg * 2 + fb, 0:N], in0=rt[:, 0:N], in1=rt[:, 0:N], op=ALU.mult)
                for o8 in range(8):
                    sd = _fs[1] % 2
                    _fs[1] += 1
                    DMA("sync", "dn%d" % sd, F_dn[sd][:].rearrange("p k c -> p (k c)"), wb_dn[l, h * 8 + o8], ["wb_f%d" % l], ["F_dn%d" % sd])
                    pd_, kd_ = PS(o8 % 4, 0, N)
                    for fc in range(16):
                        X("tensor", "matmul", ["F_dn%d" % sd, "F_act"], kd_, out=pd_, lhsT=F_dn[sd][:, fc, :], rhs=F_act[:, fc, 0:N],
                          start=(fc == 0), stop=(fc == 15))
                    X("vector", "tensor_tensor", kd_ + ["F_x"], ["F_x"], out=F_x[:, o8, 0:N], in0=pd_, in1=F_x[:, o8, 0:N], op=ALU.add)
            if l < NL - 1:
                DMA("gpsimd", "x_st", xs1[:, :, tok0:tok0 + N], F_x[:, :, 0:N], ["F_x"], ["xs1"])
            else:
                emit_norm(0, F_x, "F_x", N, "nfin", outbuf=F_x, outkey="F_x", sqk=F_sq, rstd=F_rstd, pfx="F_")
                DMA("gpsimd", "y_st", yT[:, :, tok0:tok0 + N], F_x[:, :, 0:N], ["F_x"], ["yT"], is_output=True)

        _step = [0]

        def step(fn, *a):
            _step[0] += 1
            if cfg.stop is None or _step[0] <= cfg.stop:
                fn(*a)

        def set_prev():
            X("vector", "tensor_copy", ["halo_prev"], ["raw"], out=raw[:, :, 0:1], in_=halo_prev[:].unsqueeze(2))

        for l in range(NL):
            step(emit_halo, l)
            step(set_prev)
            step(init_states_A)
            for ti in range(NPS):
                step(layer_tile, l, False, ti)
            step(emit_exchange, l)
            step(set_prev)
            step(init_states_B)
            GT = max(1, 512 // W)
            for g0 in range(0, NPS, GT):
                g1 = min(NPS, g0 + GT)
                for ti in range(g0, g1):
                    step(layer_tile, l, True, ti)
                step(p.fence)
                step(stage_F, l, g0 * W, (g1 - g0) * W)
                step(p.fence)
            step(layer_tile, l, True, NPS)
            step(p.fence)
            step(stage_F, l, NPT, 128)
            step(p.fence)

        with nc.Block() as block:
            p.emit(block)
    return nc


_NC_CACHE = {}


def _prep_shared(inp):
    f = np.float32
    NL = inp["w_in"].shape[0]
    idx = _win_cols()
    sh = {}
    w_in = _take_cols(np.asarray(inp["w_in"], f), idx)
    sh["w_in_r"] = np.ascontiguousarray(w_in.reshape(NL, KC, 128, NBLK, 128).transpose(0, 3, 2, 1, 4))
    w_up = np.asarray(inp["w_ffn_up"], f)
    sh["w_up_r"] = np.ascontiguousarray(w_up.reshape(NL, KC, 128, 16, 256).transpose(0, 3, 2, 1, 4))
    w_dn = np.asarray(inp["w_ffn_down"], f)
    sh["w_dn_r"] = np.ascontiguousarray(w_dn.reshape(NL, 2, 16, 128, 8, 128).transpose(0, 1, 4, 3, 2, 5).reshape(NL, 16, 128, 16, 128))
    woa = np.asarray(inp["w_out_a"], f).reshape(NL, 4, 128, 8, 128).transpose(0, 3, 2, 1, 4)
    wob = np.asarray(inp["w_out_b"], f).reshape(NL, 4, 128, 8, 128).transpose(0, 3, 2, 1, 4)
    sh["w_oab_r"] = np.ascontiguousarray(np.concatenate([woa, wob], axis=3))
    sh["w_o_r"] = np.ascontiguousarray(np.asarray(inp["w_out"], f).reshape(NL, 8, 128, 8, 128).transpose(0, 3, 2, 1, 4))
    w2 = np.zeros((NL, 128, 512), f)
    w2[:, 0:64] = inp["rwkv_w2"]
    a2 = np.zeros((NL, 128, 512), f)
    a2[:, 64:128] = inp["rwkv_a2"]
    g2 = np.zeros((NL, 256, 512), f)
    g2[:, 0:160] = inp["rwkv_g2"]
    sh["w2pad"], sh["a2pad"] = w2, a2
    sh["g2pad"] = np.ascontiguousarray(g2.reshape(NL, 2, 128, 512).transpose(0, 2, 1, 3))
    sh["vw1"] = np.ascontiguousarray(np.asarray(inp["rwkv_vres_w1"], f)[0].reshape(4, 128, 32).transpose(1, 0, 2))
    sh["vw2"] = np.ascontiguousarray(np.asarray(inp["rwkv_vres_w2"], f)[0])
    pv = np.zeros((NL, 128, NPV), f)
    mu = _take_cols(np.asarray(inp["rwkv_mu"], f), idx[:RW_BLK * 128])
    for l in range(NL):
        pv[l, :, PV["nmix"]:PV["nmix"] + 8] = _pk(np.asarray(inp["norm_mix"], f)[l], 8)
        pv[l, :, PV["nffn"]:PV["nffn"] + 8] = _pk(np.asarray(inp["norm_ffn"], f)[l], 8)
        pv[l, :, PV["mu"]:PV["mu"] + 15] = _pk(mu[l], 15)
        pv[l, :, PV["w0"]:PV["w0"] + 4] = _pk(np.asarray(inp["rwkv_w0"], f)[l], 4)
        pv[l, :, PV["a0"]:PV["a0"] + 4] = _pk(np.asarray(inp["rwkv_a0"], f)[l], 4)
        pv[l, :, PV["v0"]:PV["v0"] + 4] = _pk(np.asarray(inp["rwkv_v0"], f)[0], 4)
        pv[l, :, PV["kk"]:PV["kk"] + 4] = _pk(np.asarray(inp["rwkv_k_k"], f)[l], 4)
        pv[l, :, PV["ka"]:PV["ka"] + 4] = _pk(np.asarray(inp["rwkv_k_a"], f)[l], 4)
        pv[l, :, PV["rk"]:PV["rk"] + 4] = _pk(np.asarray(inp["rwkv_r_k"], f)[l].reshape(-1), 4)
        pv[l, :, PV["hnw"]:PV["hnw"] + 4] = _pk(np.asarray(inp["hgrn_norm_w"], f)[l], 4)
        pv[l, :, PV["lbz"]:PV["lbz"] + 4] = _pk(np.asarray(inp["hgrn_lb_logits"], f)[l], 4)
        pv[l, :, PV["nfin"]:PV["nfin"] + 8] = _pk(np.asarray(inp["norm_final"], f), 8)
    sh["pvec"] = pv
    for nm, key in (("lnw_st", "rwkv_ln_w"), ("lnb_st", "rwkv_ln_b")):
        a = np.asarray(inp[key], f).reshape(NL, 2, 2, 2, 64)
        a = np.broadcast_to(a[:, :, :, :, None, :], (NL, 2, 2, 2, 32, 64))
        sh[nm] = np.ascontiguousarray(a.transpose(0, 2, 3, 4, 1, 5).reshape(NL, 128, 2, 64))
    return sh


def _run(inp, npt, w, dbg=(), stop=None):
    f = np.float32
    cfg = Cfg(npt=npt, w=w, nlayer=int(inp["w_in"].shape[0]), dbg=dbg, stop=stop)
    key = (npt, w, cfg.NL, tuple(dbg), stop)
    if key not in _NC_CACHE:
        _NC_CACHE[key] = build(cfg)
    nc = _NC_CACHE[key]
    NL, NT = cfg.NL, cfg.NT
    sh = _prep_shared(inp)
    xp = np.asarray(inp["x_prompt"], f)
    xs = np.asarray(inp["x_sample"], f)
    idx = _win_cols()
    sshift = _take_cols(np.asarray(inp["state_shift"], f), idx[:RW_BLK * 128])
    srw = np.asarray(inp["state_rwkv"], f)
    shg = np.asarray(inp["state_hgrn"], f)
    in_maps = []
    for c in range(NCORE):
        b, seg = c // 4, c % 4
        xtok = np.concatenate([xp[b, seg * npt:(seg + 1) * npt], xs[4 * c:4 * c + 4].reshape(4 * L, D)], axis=0)
        m = dict(sh)
        m["xT"] = np.ascontiguousarray(xtok.reshape(NT, KC, 128).transpose(2, 1, 0))
        ss = sshift[:, 4 * c:4 * c + 4]
        m["st_shift"] = np.ascontiguousarray(ss.reshape(NL, 4, RW_BLK, 128).transpose(0, 3, 2, 1))
        r = srw[:, 4 * c:4 * c + 4].reshape(NL, 4, 4, 2, 64, 64)
        m["st_rwkv"] = np.ascontiguousarray(r.transpose(0, 1, 3, 5, 2, 4).reshape(NL, 4, 128, 4, 64))
        h = shg[:, 4 * c:4 * c + 4]
        m["st_hgrn"] = np.ascontiguousarray(h.transpose(0, 1, 3, 2, 4))
        for n, v in _consts(seg).items():
            m["c_" + n] = v
        in_maps.append(m)
    res = run_bass_kernel_spmd(nc, in_maps, core_ids=list(range(NCORE)))
    R = res.results
    B = xp.shape[0]
    y_p = np.zeros((B, 4 * npt, D), f)
    y_s = np.zeros((4 * NCORE, L, D), f)
    for c in range(NCORE):
        yt = R[c]["yT"].transpose(2, 1, 0).reshape(NT, D)
        y_p[c // 4, (c % 4) * npt:(c % 4 + 1) * npt] = yt[:npt]
        y_s[4 * c:4 * c + 4] = yt[npt:].reshape(4, L, D)

    def unshift(a):
        a = np.moveaxis(a, -2, -1)
        return a.reshape(a.shape[:-2] + (RW_BLK * 128,))[..., :1824]

    def unrw(a):
        lead = a.shape[:-3]
        a = a.reshape(lead + (2, 64, 4, 64))
        n = len(lead)
        a = a.transpose(tuple(range(n)) + (n + 2, n + 0, n + 3, n + 1))
        return a.reshape(lead + (8, 64, 64))

    def unhg(a):
        n = a.ndim - 3
        return a.transpose(tuple(range(n)) + (n + 1, n + 0, n + 2))

    lastc = [4 * bb + 3 for bb in range(B)]
    shift_p = np.stack([unshift(R[c]["o_shift_p"]) for c in lastc], axis=1)
    rwkv_p = np.stack([unrw(R[c]["o_rwkv_p"]) for c in lastc], axis=1)
    hgrn_p = np.stack([unhg(R[c]["o_hgrn_p"]) for c in lastc], axis=1)
    shift_s = np.concatenate([np.moveaxis(unshift(np.moveaxis(R[c]["o_shift_s"], -1, 1)), 1, 1) for c in range(NCORE)], axis=1)
    rwkv_s = np.concatenate([unrw(R[c]["o_rwkv_s"]) for c in range(NCORE)], axis=1)
    hgrn_s = np.concatenate([unhg(R[c]["o_hgrn_s"]) for c in range(NCORE)], axis=1)
    outs = (y_p, y_s, shift_p, rwkv_p, hgrn_p, shift_s, rwkv_s, hgrn_s)
    return tuple(np.ascontiguousarray(o, dtype=f) for o in outs), R


def kernel(**inputs):
    outs, _ = _run(inputs, 2048, 128)
    return outs
```

```python
import numpy as np
from contextlib import ExitStack
import concourse.bass as bass
import concourse.mybir as mybir
from concourse.bass_utils import run_bass_kernel_spmd

F32 = mybir.dt.float32
BF16 = mybir.dt.bfloat16
AF = mybir.ActivationFunctionType
ALU = mybir.AluOpType
AX = mybir.AxisListType

D = 1024
KC = 8
NCORE = 8
L = 32
NBLK = 47
RW_BLK = 15
DFF = 4096
RMS_EPS = 1e-6
GN_EPS = 64e-5
C0 = -float(np.exp(-0.5))


class Prog:
    ENGS = ["sync", "scalar", "vector", "gpsimd", "tensor"]

    def __init__(self, nc, stack):
        self.nc = nc
        self.stack = stack
        self.ops = {e: [] for e in self.ENGS}
        self.esem = {e: stack.enter_context(nc.semaphore("s_" + e)) for e in self.ENGS}
        self.ecnt = {e: 0 for e in self.ENGS}
        self.dsem = {}
        self.dcnt = {}
        self.writer = {}
        self.readers = {}
        self.waited = {e: {} for e in self.ENGS}
        self.out_tokens = []
        self.pending = {e: {} for e in self.ENGS}

    def _dsem(self, name):
        if name not in self.dsem:
            self.dsem[name] = self.stack.enter_context(self.nc.semaphore("d_" + name))
            self.dcnt[name] = 0
        return self.dsem[name]

    def _deps(self, eng, reads, writes):
        toks = []
        for k in reads:
            w = self.writer.get(k)
            if w is not None:
                toks.append(w)
        for k in writes:
            w = self.writer.get(k)
            if w is not None:
                toks.append(w)
            toks.extend(self.readers.get(k, []))
        need = {}
        for (s, sid, v) in toks:
            if sid == ("e", "tensor") and eng == "tensor":
                continue
            if sid[0] == "e" and sid[1] == eng and eng == "sync":
                continue
            if sid[0] == "d":
                v = max(v, self.dcnt[sid[1]])
            if need.get(sid, (None, -1))[1] < v:
                need[sid] = (s, v)
        for sid, (s, v) in self.pending[eng].items():
            if need.get(sid, (None, -1))[1] < v:
                need[sid] = (s, v)
        self.pending[eng] = {}
        waits = []
        for sid, (s, v) in need.items():
            if self.waited[eng].get(sid, -1) >= v:
                continue
            self.waited[eng][sid] = v
            waits.append((s, v))
        return waits

    def fence(self):
        allt = {}
        for e in self.ENGS:
            if self.ecnt[e] > 0:
                allt[("e", e)] = (self.esem[e], self.ecnt[e])
        for n, c in self.dcnt.items():
            if c > 0:
                allt[("d", n)] = (self.dsem[n], c)
        for e in self.ENGS:
            for sid, sv in allt.items():
                if sid == ("e", e):
                    continue
                self.pending[e][sid] = sv
        self.writer.clear()
        self.readers.clear()

    def _record(self, tok, reads, writes):
        for k in reads:
            self.readers.setdefault(k, []).append(tok)
        for k in writes:
            self.writer[k] = tok
            self.readers[k] = []

    def op(self, eng, fn, reads=(), writes=()):
        pk = [k for k in reads if k.startswith("psbank")]
        if pk:
            reads = [k for k in reads if not k.startswith("psbank")]
            writes = list(writes) + [k for k in pk if k not in writes]
        waits = self._deps(eng, reads, writes)
        self.ecnt[eng] += 1
        tok = (self.esem[eng], ("e", eng), self.ecnt[eng])
        self.ops[eng].append((waits, fn, self.esem[eng], 1))
        self._record(tok, reads, writes)
        return tok

    def dma(self, eng, dname, fn, reads=(), writes=(), is_output=False):
        waits = self._deps(eng, reads, writes)
        s = self._dsem(dname)
        self.dcnt[dname] += 16
        tok = (s, ("d", dname), self.dcnt[dname])
        self.ops[eng].append((waits, fn, s, 16))
        self._record(tok, reads, writes)
        if is_output:
            self.out_tokens.append(tok)
        return tok

    def emit(self, block):
        prog = self

        def mk(ename):
            def body(e):
                for (waits, fn, s, inc) in prog.ops[ename]:
                    for (ws, wv) in waits:
                        e.wait_ge(ws, wv)
                    fn(e).then_inc(s, inc)
                if ename == "gpsimd":
                    last = {}
                    for (s2, sid, v) in prog.out_tokens:
                        if last.get(sid, (None, -1))[1] < v:
                            last[sid] = (s2, v)
                    for sid, (s2, v) in last.items():
                        e.wait_ge(s2, v)
            return body
        block.sync(mk("sync"))
        block.scalar(mk("scalar"))
        block.vector(mk("vector"))
        block.gpsimd(mk("gpsimd"))
        block.tensor(mk("tensor"))


def _win_cols():
    idx = list(range(0, 1664))
    idx += list(range(1664, 1824)) + [-1] * 96
    idx += list(range(1824, 5920))
    assert len(idx) == NBLK * 128
    return np.array(idx)


def _take_cols(a, idx, axis=-1):
    a = np.moveaxis(a, axis, -1)
    out = np.zeros(a.shape[:-1] + (len(idx),), a.dtype)
    m = idx >= 0
    out[..., m] = a[..., idx[m]]
    return np.moveaxis(out, -1, axis)


def _pk(v, nchunk):
    return np.ascontiguousarray(v.reshape(nchunk, 128).T)


def _consts(rank4):
    c = {}
    c["ident"] = np.eye(128, dtype=np.float32)
    ob = np.zeros((128, 128), np.float32)
    ob[:64, :64] = 1
    ob[64:, 64:] = 1
    c["ones_bd"] = ob
    c["ones_f"] = np.ones((128, 128), np.float32)
    isel = np.zeros((128, 64), np.float32)
    isel[np.arange(128), np.arange(128) % 64] = 1
    c["isel"] = isel
    rho = np.arange(128)
    blk, pos = rho // 32, rho % 32
    same = blk[:, None] == blk[None, :]
    c["m_strict"] = (same & (pos[:, None] < pos[None, :])).astype(np.float32)
    c["m_incl"] = (same & (pos[:, None] <= pos[None, :])).astype(np.float32)
    c["m_lower"] = (same & (pos[:, None] > pos[None, :])).astype(np.float32)
    s = np.arange(32)
    c["m_att"] = (s[:, None] <= s[None, :]).astype(np.float32)
    rst = np.ones((128, 4 * 128), np.float32)
    rst[:, ::32] = 0
    c["m_reset"] = rst
    fm = np.zeros((128, 4), np.float32)
    fm[:, :3] = (np.arange(3) < rank4).astype(np.float32)[None, :]
    c["foldm"] = fm
    hm = np.zeros((128, 4), np.float32)
    if rank4 > 0:
        hm[:, rank4 - 1] = 1
    c["halom"] = hm
    return c


class Cfg:
    def __init__(self, npt=2048, w=256, nlayer=2, dbg=(), stop=None):
        self.stop = stop
        self.NPT = npt
        self.W = w
        self.NS = 4
        self.NT = npt + 4 * L
        self.NL = nlayer
        self.dbg = tuple(dbg)


PV = dict(nmix=0, nffn=8, mu=16, w0=31, a0=35, v0=39, kk=43, ka=47, rk=51, hnw=55, lbz=59, nfin=63)
NPV = 71


def build(cfg):
    nc = bass.Bass("TRN2", target_bir_lowering=False)
    NPT, W, NT, NL = cfg.NPT, cfg.W, cfg.NT, cfg.NL
    NPS = NPT // W

    def din(name, shape, dt=F32):
        return nc.dram_tensor(name, list(shape), dt, kind="ExternalInput").ap()

    def dout(name, shape, dt=F32):
        return nc.dram_tensor(name, list(shape), dt, kind="ExternalOutput").ap()

    def dint(name, shape, dt=F32):
        return nc.dram_tensor(name, list(shape), dt, kind="Internal").ap()

    xT = din("xT", [128, KC, NT])
    st_shift = din("st_shift", [NL, 128, RW_BLK, 4])
    st_rwkv = din("st_rwkv", [NL, 4, 128, 4, 64])
    st_hgrn = din("st_hgrn", [NL, 4, 128, 4, 128])
    w_in_r = din("w_in_r", [NL, NBLK, 128, KC, 128])
    w_up_r = din("w_up_r", [NL, 16, 128, KC, 256])
    w_dn_r = din("w_dn_r", [NL, 16, 128, 16, 128])
    w_oab_r = din("w_oab_r", [NL, 8, 128, 8, 128])
    w_o_r = din("w_o_r", [NL, 8, 128, 8, 128])
    w2pad = din("w2pad", [NL, 128, 512])
    a2pad = din("a2pad", [NL, 128, 512])
    g2pad = din("g2pad", [NL, 128, 2, 512])
    vw1 = din("vw1", [128, 4, 32])
    vw2 = din("vw2", [32, 512])
    pvec = din("pvec", [NL, 128, NPV])
    lnw_st = din("lnw_st", [NL, 128, 2, 64])
    lnb_st = din("lnb_st", [NL, 128, 2, 64])
    cnames = ["ident", "ones_bd", "ones_f", "isel", "m_strict", "m_incl", "m_lower", "m_att", "m_reset", "foldm", "halom"]
    cshape = dict(ident=[128, 128], ones_bd=[128, 128], ones_f=[128, 128], isel=[128, 64], m_strict=[128, 128],
                  m_incl=[128, 128], m_lower=[128, 128], m_att=[32, 32], m_reset=[128, 512], foldm=[128, 4], halom=[128, 4])
    cdram = {n: din("c_" + n, cshape[n]) for n in cnames}

    yT = dout("yT", [128, KC, NT])
    o_shift_p = dout("o_shift_p", [NL, 128, RW_BLK])
    o_rwkv_p = dout("o_rwkv_p", [NL, 128, 4, 64])
    o_hgrn_p = dout("o_hgrn_p", [NL, 128, 4, 128])
    o_shift_s = dout("o_shift_s", [NL, 128, RW_BLK, 4])
    o_rwkv_s = dout("o_rwkv_s", [NL, 4, 128, 4, 64])
    o_hgrn_s = dout("o_hgrn_s", [NL, 4, 128, 4, 128])
    dbg_out = {n: dout("dbg_" + n, shp) for (n, shp) in cfg.dbg}

    xs1 = dint("xs1", [128, KC, NT])
    vfirst_d = dint("vfirst_d", [128, 4, NT])
    cin_h = dint("cin_h", [128, 16])
    cout_h = dint("cout_h", [512, 16])
    SW = 4 * 128 + 4 * 128 + 8
    cin_s = dint("cin_s", [128, SW])
    cout_s = dint("cout_s", [512, SW])
    wb_in = dint("wb_in", [NL, NBLK, 128, KC * 128], BF16)
    wb_up = dint("wb_up", [NL, 16, 128, KC * 256], BF16)
    wb_dn = dint("wb_dn", [NL, 16, 128, 16 * 128], BF16)
    x1s = dint("x1s", [128, KC, NT])
    wb_oab = dint("wb_oab", [NL, 8, 128, 8 * 128], BF16)
    wb_o = dint("wb_o", [NL, 8, 128, 8 * 128], BF16)

    with ExitStack() as st:
        p = Prog(nc, st)

        def sb(name, shape, dt=F32):
            return st.enter_context(nc.sbuf_tensor(name, list(shape), dt))

        def X(eng, method, reads, writes, **kw):
            return p.op(eng, lambda e: getattr(e, method)(**kw), reads, writes)

        def DMA(eng, dname, out, in_, reads, writes, is_output=False, **kw):
            return p.dma(eng, dname, lambda e: e.dma_start(out=out, in_=in_, **kw), reads, writes, is_output)

        _rr = [0]

        def EW():
            _rr[0] ^= 1
            return "vector" if _rr[0] else "gpsimd"

        ps = st.enter_context(nc.psum_tensor("ps", [128, 7 * 512], F32))
        psb = st.enter_context(nc.psum_tensor("psb", [128, 1024], BF16))

        def PS(bank, off, n):
            assert off + n <= 512
            keys = ["psbank%d" % bank]
            return ps[:, bank * 512 + off: bank * 512 + off + n], keys

        def PSB(off, n):
            keys = ["psbankB"]
            return psb[:, off:off + n], keys

        cst = {}
        for n in cnames:
            cst[n] = sb("k_" + n, cshape[n])
            DMA("sync", "cst", cst[n][:], cdram[n][:], [], ["c_" + n])
        ident_b = sb("ident_b", [128, 128], BF16)
        isel_b = sb("isel_b", [128, 64], BF16)
        X("vector", "tensor_copy", ["c_ident"], ["ident_b"], out=ident_b[:], in_=cst["ident"][:])
        X("vector", "tensor_copy", ["c_isel"], ["isel_b"], out=isel_b[:], in_=cst["isel"][:])
        eps_rms = sb("eps_rms", [128, 1])
        eps_gn = sb("eps_gn", [128, 1])
        X("vector", "memset", [], ["eps_rms"], ap=eps_rms[:], constant=RMS_EPS)
        X("vector", "memset", [], ["eps_gn"], ap=eps_gn[:], constant=GN_EPS)

        def wcls(b):
            return "a" if b < RW_BLK else ("b" if 19 <= b < 27 else "c")

        def wkey(l, b):
            return "wb_in%d%s" % (l, wcls(b))

        conv_q = {l: [] for l in range(NL)}

        def conv_layer(l):
            order = list(range(0, RW_BLK)) + list(range(19, 27)) + list(range(RW_BLK, 19)) + list(range(27, NBLK))
            mk = lambda *a: (lambda: DMA(*a))
            for b in order:
                conv_q[l].append(mk("gpsimd", "cv_in%d%s" % (l, wcls(b)), wb_in[l, b], w_in_r[l, b].rearrange("p k c -> p (k c)"), [], [wkey(l, b)]))
            for g in range(8):
                conv_q[l].append(mk("gpsimd", "cv_o%d" % l, wb_oab[l, g], w_oab_r[l, g].rearrange("p k c -> p (k c)"), [], ["wb_o%d" % l]))
                conv_q[l].append(mk("gpsimd", "cv_o%d" % l, wb_o[l, g], w_o_r[l, g].rearrange("p k c -> p (k c)"), [], ["wb_o%d" % l]))
            for g in range(16):
                conv_q[l].append(mk("gpsimd", "cv_f%d" % l, wb_up[l, g], w_up_r[l, g].rearrange("p k c -> p (k c)"), [], ["wb_f%d" % l]))
                conv_q[l].append(mk("gpsimd", "cv_f%d" % l, wb_dn[l, g], w_dn_r[l, g].rearrange("p k c -> p (k c)"), [], ["wb_f%d" % l]))

        def conv_pump(l, n):
            if l < NL:
                for _ in range(n):
                    if conv_q[l]:
                        conv_q[l].pop(0)()

        for l in range(NL):
            conv_layer(l)
        conv_pump(0, RW_BLK + 8)

        pvs = sb("pvs", [128, NL, NPV])
        DMA("sync", "cstp", pvs[:], pvec.rearrange("l p n -> p l n"), [], ["pvs"])
        lnw = sb("lnw", [128, NL, 2, 64])
        lnb = sb("lnb", [128, NL, 2, 64])
        DMA("sync", "cstp", lnw[:], lnw_st.rearrange("l p g v -> p l g v"), [], ["lnw"])
        DMA("sync", "cstp", lnb[:], lnb_st.rearrange("l p g v -> p l g v"), [], ["lnb"])
        w2p_b = sb("w2p_b", [128, NL, 512], BF16)
        a2p_b = sb("a2p_b", [128, NL, 512], BF16)
        g2p_b = sb("g2p_b", [128, NL, 2, 512], BF16)
        vw1_b = sb("vw1_b", [128, 4, 32], BF16)
        vw2_b = sb("vw2_b", [32, 512], BF16)
        DMA("gpsimd", "cst2", w2p_b[:], w2pad.rearrange("l p n -> p l n"), [], ["w2p_b"])
        DMA("gpsimd", "cst2", a2p_b[:], a2pad.rearrange("l p n -> p l n"), [], ["a2p_b"])
        DMA("gpsimd", "cst2", g2p_b[:], g2pad.rearrange("l p j n -> p l j n"), [], ["g2p_b"])
        DMA("gpsimd", "cst2", vw1_b[:], vw1[:], [], ["vw1_b"])
        DMA("gpsimd", "cst2", vw2_b[:], vw2[:], [], ["vw2_b"])
        lbe = sb("lbe", [128, NL, 4])
        lbs = sb("lbs", [128, 4])
        lbv = sb("lbv", [128, NL, 4])
        oml = sb("oml", [128, NL, 4])
        X("scalar", "activation", ["pvs"], ["lbe"], out=lbe[:], in_=pvs[:, :, PV["lbz"]:PV["lbz"] + 4], func=AF.Exp)
        X("vector", "tensor_copy", ["lbe"], ["lbs"], out=lbs[:], in_=lbe[:, 0, :])
        for l in range(1, NL):
            X("vector", "tensor_tensor", ["lbe", "lbs"], ["lbs"], out=lbs[:], in0=lbs[:], in1=lbe[:, l, :], op=ALU.add)
        X("vector", "reciprocal", ["lbs"], ["lbs"], out=lbs[:], in_=lbs[:])
        for l in range(NL):
            X("vector", "tensor_tensor", ["lbe", "lbs"], ["lbe"], out=lbe[:, l, :], in0=lbe[:, l, :], in1=lbs[:], op=ALU.mult)
        X("vector", "tensor_tensor", ["lbe"], ["lbv"], out=lbv[:, 0, :], in0=lbe[:, 0, :], in1=lbe[:, 0, :], op=ALU.subtract)
        for l in range(1, NL):
            X("vector", "tensor_tensor", ["lbe", "lbv"], ["lbv"], out=lbv[:, l, :], in0=lbv[:, l - 1, :], in1=lbe[:, l, :], op=ALU.add)
        X("vector", "tensor_scalar", ["lbv"], ["oml"], out=oml[:], in0=lbv[:], scalar1=-1.0, scalar2=1.0, op0=ALU.mult, op1=ALU.add)

        def pcol(l, name, i=0, n=1):
            return pvs[:, l, PV[name] + i: PV[name] + i + n]

        def pbc(l, name, n, T):
            return pvs[:, l, PV[name]: PV[name] + n].unsqueeze(2).to_broadcast([128, n, T])

        WSL = 8
        wslot = [sb("wslot%d" % i, [128, 2, KC * 128], BF16) for i in range(WSL)]
        xt = sb("xt", [128, KC, W])
        x1 = sb("x1", [128, KC, W])
        hT = sb("hT", [128, KC, W], BF16)
        sqk_d = [sb("sqk%d" % i, [128, W]) for i in range(2)]
        rstd_d = sb("rstd", [128, W])
        RAWW = W + 4
        raw = sb("raw", [128, RW_BLK, RAWW])
        X("gpsimd", "memset", [], ["raw"], ap=raw[:], constant=0.0)
        hq = sb("hq", [128, 4, W])
        hsig = sb("hsig", [128, 4, W])
        hv_b = sb("hv_b", [128, 4, W], BF16)
        hog = sb("hog", [128, 4, W])
        big = sb("big", [128, max(16 * W, SW)])
        gates = big[:, 0:16 * W].rearrange("p (j w) -> p j w", w=W)
        yg_b = sb("yg_b", [128, 4, W], BF16)
        ob_b = sb("ob_b", [128, 4, W], BF16)
        mixin = sb("mixin", [128, KC, W], BF16)
        mtmp = sb("mtmp", [128, W])
        relu_t = [sb("relu%d" % i, [128, W]) for i in range(1)]
        _wsl = [0]

        def emit_norm(l, xbuf, xkey, N, pname, outbuf=None, outkey="hT", sqk=None, rstd=None, pfx=""):
            if outbuf is None:
                outbuf = hT
            if sqk is None:
                sqk, rstd = sqk_d, rstd_d
            nps, nkeys = PS(6, 0, N)
            for kc in range(KC):
                sq = sqk[kc % 2]
                X("scalar", "activation", [xkey], [pfx + "sqk%d" % (kc % 2)], out=sq[:, 0:N], in_=xbuf[:, kc, 0:N], func=AF.Square)
                X("tensor", "matmul", [pfx + "sqk%d" % (kc % 2), "c_ones_f"], nkeys, out=nps, lhsT=cst["ones_f"][:], rhs=sq[:, 0:N],
                  start=(kc == 0), stop=(kc == KC - 1))
            X("scalar", "activation", nkeys + ["eps_rms"], [pfx + "rstd"], out=rstd[:, 0:N], in_=nps, func=AF.Ln, scale=1.0 / D, bias=eps_rms[:])
            X("scalar", "activation", [pfx + "rstd"], [pfx + "rstd"], out=rstd[:, 0:N], in_=rstd[:, 0:N], func=AF.Exp, scale=-0.5)
            for kc in range(KC):
                X("vector", "scalar_tensor_tensor", [xkey, pfx + "rstd", "pvs"], [outkey], out=outbuf[:, kc, 0:N], in0=xbuf[:, kc, 0:N],
                  scalar=pcol(l, pname, kc), in1=rstd[:, 0:N], op0=ALU.mult, op1=ALU.mult)

        _psl = [0]

        def emit_proj(l, blocks, N, handler):
            groups = [blocks[i:i + 2] for i in range(0, len(blocks), 2)]
            for grp in groups:
                s = _wsl[0] % WSL
                _wsl[0] += 1
                if len(grp) == 2 and grp[1] == grp[0] + 1:
                    DMA("sync", "wsl%d" % s, wslot[s][:, 0:2, :], wb_in[l, grp[0]:grp[0] + 2].rearrange("j p n -> p j n"),
                        sorted(set([wkey(l, grp[0]), wkey(l, grp[1])])), ["wslot%d" % s])
                else:
                    for j, b in enumerate(grp):
                        DMA("sync", "wsl%d" % s, wslot[s][:, j, :], wb_in[l, b], [wkey(l, b)], ["wslot%d" % s])
                for j, b in enumerate(grp):
                    slot = _psl[0] % 4
                    _psl[0] += 1
                    pr, pk = PS(slot, 0, N)
                    for kc in range(KC):
                        X("tensor", "matmul", ["wslot%d" % s, "hT"], pk, out=pr, lhsT=wslot[s][:, j, kc * 128:(kc + 1) * 128],
                          rhs=hT[:, kc, 0:N], start=(kc == 0), stop=(kc == KC - 1))
                    handler(b, pr, pk)

        T = 128
        AW = 17920
        arena = sb("arena", [128, AW])
        _ao = [0]

        def aalloc(shape, dt=F32, reset=False):
            if reset:
                _ao[0] = 0
            n = int(np.prod(shape[1:]))
            n32 = n if dt == F32 else (n + 1) // 2
            a = _ao[0]
            _ao[0] += n32
            assert _ao[0] <= AW, ("arena overflow", _ao[0])
            v = arena[:, a:a + n32]
            if dt != F32:
                v = v.bitcast(dt)[:, 0:n]
            if len(shape) == 3:
                v = v.rearrange("p (a b) -> p a b", a=shape[1])
            elif len(shape) == 4:
                v = v.rearrange("p (a b c) -> p a b c", a=shape[1], b=shape[2])
            return v[0:shape[0]] if shape[0] < 128 else v

        def sbm(name, shape, dt=F32):
            return aalloc(list(shape), dt)
        mx = sbm("mx", [128, RW_BLK, T])
        lw_in = sbm("lw_in", [128, T], BF16)
        siggl = sbm("siggl", [128, 2, T], BF16)
        names4 = ["sg", "aa", "gfm", "vv", "kkn", "kh", "brec", "Gw", "tA", "tB", "Ex", "bv", "t1", "vf"]
        m4 = {n: sbm("m_" + n, [128, 4, T]) for n in names4}
        vb16 = sbm("vb16", [128, 4, T], BF16)
        t32b = sbm("t32b", [32, T], BF16)
        bdn = ["Kbd", "Bbd", "Abd", "Rbd", "KHbd", "BHbd", "Vbd"]
        bd = {n: sb(n, [128, 4, 4, 128], BF16) for n in bdn}
        for n in bdn:
            X("gpsimd", "memset", [], [n], ap=bd[n][:], constant=0.0)
        AR = sbm("AR", [128, 4, 4, 64], BF16)
        Bp = sbm("Bp", [128, 4, 4, 32], BF16)
        gL = sb("gL", [128, 4, 4])
        chn = ["X1", "X1T", "Mb", "Xa", "XaT", "Xb", "XbT"]
        chb = {n: [sbm("%s%d" % (n, g), [128, 128], BF16) for g in range(2)] for n in chn}
        chb2 = {n: [[sbm("%s_%d_%d" % (n, pp, g), [128, 128], BF16) for g in range(2)] for pp in range(2)] for n in ["Aka", "Akr", "Abr", "Ma"]}
        KT2 = [sbm("KT_%d" % pp, [128, 4, 128], BF16) for pp in range(2)]
        BT2 = [sbm("BT_%d" % pp, [128, 4, 128], BF16) for pp in range(2)]
        Vst2 = [sb("Vst_%d" % pp, [128, 2, 128], BF16) for pp in range(2)]
        for pp in range(2):
            X("gpsimd", "memset", [], ["Vst%d_0" % pp, "Vst%d_1" % pp], ap=Vst2[pp][:], constant=0.0)
        Wt = sbm("Wt", [128, 2, 128], BF16)
        Ut = sbm("Ut", [128, 2, 128], BF16)
        Sf = sb("Sf", [128, 4, 128])
        Sb = sb("Sb", [128, 4, 128], BF16)
        Yst = sbm("Yst", [128, 8, 64])
        Ysq = sbm("Ysq", [128, 8, 64])
        Yrep = sbm("Yrep", [128, 8, 2, 64])
        gst = {n: sb("gst_" + n, [128, 8]) for n in ["s1", "s2", "mean", "var"]}
        h4 = {"o": sbm("h_o", [128, 4, T])}
        for hn_, mn_ in {'f': 'sg', 'lf': 'aa', 'khh': 'kkn', 'G2': 'Gw', 'd1': 'tB', 'E2': 'Ex', 'hA': 'tA', 'o2': 'kh', 'rs': 'brec'}.items():
            h4[hn_] = m4[mn_]
        hb = {n: sbm("hb_" + n, [128, 4, T], BF16) for n in ["Qt", "Q2", "Kt", "Kh"]}
        dL = sb("dL", [128, 4, 4])
        att_b = sbm("att_b", [32, 4, 32], BF16)
        VTh = sbm("VTh", [32, 4, 128], BF16)
        KTh = sbm("KTh", [32, 4, 128], BF16)
        Shf = sb("Shf", [128, 4, 128])
        Shb = sb("Shb", [128, 4, 128], BF16)
        Dtot = sb("Dtot", [128, 4])

        def v4(ap):
            return ap.rearrange("p c (q t) -> p c q t", t=L)

        def bd_write(name, in0, in1, op, keys_r):
            for hh in range(2):
                for cc in range(2):
                    out = bd[name][hh * 64:(hh + 1) * 64, cc::2, :, 64 * cc + 32 * hh: 64 * cc + 32 * hh + 32]
                    a = v4(in0)[hh * 64:(hh + 1) * 64, cc::2]
                    if in1 is None:
                        X(EW(), "tensor_copy", keys_r, [name], out=out, in_=a)
                    else:
                        b = v4(in1)[hh * 64:(hh + 1) * 64, cc::2]
                        X(EW(), "tensor_tensor", keys_r, [name], out=out, in0=a, in1=b, op=op)

        def bc4(ap, shape):
            return ap.to_broadcast(shape)

        def rwkv_prep(l, phB, cur, prv, mxv, tok0, nvalid_blocks):
            b0, b1 = nvalid_blocks
            shp = list(cur.shape)
            mu = pvs[:, l, PV["mu"] + b0: PV["mu"] + b1]
            mu_bc = (mu.unsqueeze(2) if len(shp) == 3 else mu.unsqueeze(2).unsqueeze(3)).to_broadcast(shp)
            X("vector", "tensor_tensor", ["raw"], ["mx"], out=mxv, in0=prv, in1=cur, op=ALU.subtract)
            X("vector", "tensor_tensor", ["mx", "pvs"], ["mx"], out=mxv, in0=mxv, in1=mu_bc, op=ALU.mult)
            X("vector", "tensor_tensor", ["mx", "raw"], ["mx"], out=mxv, in0=mxv, in1=cur, op=ALU.add)
            r, k, v = mx[:, 0:4, :], mx[:, 4:8, :], mx[:, 8:12, :]
            X("scalar", "activation", ["mx"], ["lw_in"], out=lw_in[0:64, :], in_=mx[0:64, 12, :], func=AF.Tanh)
            X("scalar", "activation", ["mx"], ["lw_in"], out=lw_in[64:128, :], in_=mx[64:128, 12, :], func=AF.Copy)
            pw, kw = PS(4, 0, 512)
            pa, ka = PS(5, 0, 512)
            pg, kg = PS(6, 0, 512)
            for c in range(4):
                X("tensor", "matmul", ["lw_in", "w2p_b"], kw, out=pw[:, c * T:(c + 1) * T], lhsT=w2p_b[:, l, c * 128:(c + 1) * 128],
                  rhs=lw_in[:], start=True, stop=True)
                X("tensor", "matmul", ["lw_in", "a2p_b"], ka, out=pa[:, c * T:(c + 1) * T], lhsT=a2p_b[:, l, c * 128:(c + 1) * 128],
                  rhs=lw_in[:], start=True, stop=True)
            if phB:
                X("scalar", "activation", ["mx"], ["siggl"], out=siggl[:], in_=mx[:, 13:15, :], func=AF.Sigmoid)
                for c in range(4):
                    for j in range(2):
                        X("tensor", "matmul", ["siggl", "g2p_b"], kg, out=pg[:, c * T:(c + 1) * T],
                          lhsT=g2p_b[:, l, j, c * 128:(c + 1) * 128], rhs=siggl[:, j, :], start=(j == 0), stop=(j == 1))
            for c in range(4):
                X("scalar", "activation", kw + ["pvs"], ["sg"], out=m4["sg"][:, c, :], in_=pw[:, c * T:(c + 1) * T], func=AF.Sigmoid,
                  bias=pcol(l, "w0", c))
                X("scalar", "activation", ka + ["pvs"], ["aa"], out=m4["aa"][:, c, :], in_=pa[:, c * T:(c + 1) * T], func=AF.Sigmoid,
                  bias=pcol(l, "a0", c))
            if phB:
                X("scalar", "activation", kg, ["gfm"], out=m4["gfm"][:].rearrange("p c t -> p (c t)"), in_=pg, func=AF.Copy)
            FEED(2)
            vv = m4["vv"]
            if l == 0:
                X(EW(), "tensor_copy", ["mx"], ["vv"], out=vv[:], in_=v)
                if phB:
                    DMA("sync", "vf_st", vfirst_d[:, :, tok0:tok0 + T], vv[:], ["vv"], ["vfirst_d"])
            else:
                DMA("sync", "vf_ld", m4["vf"][:], vfirst_d[:, :, tok0:tok0 + T], ["vfirst_d"], ["vf"])
                X("gpsimd", "tensor_copy", ["mx"], ["vb16"], out=vb16[:], in_=v)
                p32, k32 = PS(4, 0, T)
                for c in range(4):
                    X("tensor", "matmul", ["vb16", "vw1_b"], k32, out=p32[0:32, :], lhsT=vw1_b[:, c, :], rhs=vb16[:, c, :],
                      start=(c == 0), stop=(c == 3))
                X("scalar", "activation", k32, ["t32b"], out=t32b[:], in_=p32[0:32, :], func=AF.Copy)
                pv_, kv_ = PS(5, 0, 512)
                for c in range(4):
                    X("tensor", "matmul", ["t32b", "vw2_b"], kv_, out=pv_[:, c * T:(c + 1) * T], lhsT=vw2_b[:, c * 128:(c + 1) * 128],
                      rhs=t32b[:], start=True, stop=True)
                for c in range(4):
                    X("scalar", "activation", kv_ + ["pvs"], ["tA"], out=m4["tA"][:, c, :], in_=pv_[:, c * T:(c + 1) * T],
                      func=AF.Sigmoid, bias=pcol(0, "v0", c))
                X("vector", "tensor_tensor", ["vf", "mx"], ["tB"], out=m4["tB"][:], in0=m4["vf"][:], in1=v, op=ALU.subtract)
                X("vector", "tensor_tensor", ["tB", "tA"], ["tB"], out=m4["tB"][:], in0=m4["tB"][:], in1=m4["tA"][:], op=ALU.mult)
                X("vector", "tensor_tensor", ["tB", "mx"], ["vv"], out=vv[:], in0=m4["tB"][:], in1=v, op=ALU.add)
            kkn, kh, brec, Gw, tA, tB, Ex = (m4[n] for n in ["kkn", "kh", "brec", "Gw", "tA", "tB", "Ex"])
            X("vector", "tensor_tensor", ["mx", "pvs"], ["kkn"], out=kkn[:], in0=k, in1=pbc(l, "kk", 4, T), op=ALU.mult)
            X("gpsimd", "tensor_tensor", ["kkn"], ["tA"], out=tA[:], in0=kkn[:], in1=kkn[:], op=ALU.mult)
            pss, kss = PS(6, 0, 512)
            for c in range(4):
                X("tensor", "matmul", ["tA", "c_ones_bd"], kss, out=pss[:, c * T:(c + 1) * T], lhsT=cst["ones_bd"][:], rhs=tA[:, c, :],
                  start=True, stop=True)
            tAf = tA[:].rearrange("p c t -> p (c t)")
            X("vector", "tensor_scalar", kss, ["tA"], out=tAf, in0=pss, scalar1=1e-24, scalar2=None, op0=ALU.max)
            X("scalar", "activation", ["tA"], ["tA"], out=tAf, in_=tAf, func=AF.Ln)
            X("scalar", "activation", ["tA"], ["tA"], out=tAf, in_=tAf, func=AF.Exp, scale=-0.5)
            X("vector", "tensor_tensor", ["kkn", "tA"], ["kkn"], out=kkn[:], in0=kkn[:], in1=tA[:], op=ALU.mult)
            FEED(2)
            X("vector", "scalar_tensor_tensor", ["aa", "pvs"], ["tB"], out=tB[:], in0=m4["aa"][:], scalar=-1.0, in1=pbc(l, "ka", 4, T),
              op0=ALU.add, op1=ALU.mult)
            X("vector", "scalar_tensor_tensor", ["tB", "mx"], ["kh"], out=kh[:], in0=tB[:], scalar=1.0, in1=k, op0=ALU.add, op1=ALU.mult)
            X("gpsimd", "tensor_tensor", ["kkn", "aa"], ["brec"], out=brec[:], in0=kkn[:], in1=m4["aa"][:], op=ALU.mult)
            sgf = m4["sg"][:].rearrange("p c t -> p (c t)")
            Gwf = Gw[:].rearrange("p c t -> p (c t)")
            X("scalar", "mul", ["sg"], ["sg"], out=sgf, in_=sgf, mul=C0)
            X("vector", "tensor_tensor_scan", ["sg", "c_m_reset"], ["Gw"], out=Gwf, data0=cst["m_reset"][:], data1=sgf, initial=0.0,
              op0=ALU.mult, op1=ALU.add)
            X("gpsimd", "tensor_tensor", ["Gw", "sg"], ["tB"], out=tB[:], in0=Gw[:], in1=m4["sg"][:], op=ALU.subtract)
            Exf = Ex[:].rearrange("p c t -> p (c t)")
            X("scalar", "activation", ["tB"], ["Ex"], out=Exf, in_=tB[:].rearrange("p c t -> p (c t)"), func=AF.Exp)
            X("vector", "scalar_tensor_tensor", ["kkn", "Ex"], ["tA"], out=tA[:], in0=kkn[:], scalar=-1.0, in1=Ex[:],
              op0=ALU.mult, op1=ALU.mult)
            bd_write("Abd", tA[:], None, None, ["tA"])
            X(EW(), "tensor_copy", ["tA"], ["AR"], out=AR[:, :, :, 0:32], in_=v4(tA[:]))
            if phB:
                X("scalar", "activation", ["Gw"], ["Ex"], out=Exf, in_=Gwf, func=AF.Exp)
                bd_write("Rbd", r, Ex[:], ALU.mult, ["mx", "Ex"])
                X(EW(), "tensor_tensor", ["mx", "Ex"], ["AR"], out=AR[:, :, :, 32:64], in0=v4(r), in1=v4(Ex[:]), op=ALU.mult)
            FEED(2)
            X("scalar", "activation", ["Gw"], ["Ex"], out=Exf, in_=Gwf, func=AF.Exp, scale=-1.0)
            bd_write("Kbd", kh[:], Ex[:], ALU.mult, ["kh", "Ex"])
            bd_write("Bbd", brec[:], Ex[:], ALU.mult, ["brec", "Ex"])
            X(EW(), "tensor_tensor", ["brec", "Ex"], ["Bp"], out=Bp[:], in0=v4(brec[:]), in1=v4(Ex[:]), op=ALU.mult)
            FEED(2)
            GL = v4(Gw[:])[:, :, :, L - 1:L]
            X("scalar", "activation", ["Gw"], ["gL"], out=gL[:].unsqueeze(3), in_=GL, func=AF.Exp)
            X("vector", "tensor_tensor", ["Gw"], ["tB"], out=v4(tB[:]), in0=GL.to_broadcast([128, 4, 4, L]), in1=v4(Gw[:]), op=ALU.subtract)
            X("scalar", "activation", ["tB"], ["Ex"], out=Exf, in_=tB[:].rearrange("p c t -> p (c t)"), func=AF.Exp)
            bd_write("KHbd", kh[:], Ex[:], ALU.mult, ["kh", "Ex"])
            bd_write("BHbd", brec[:], Ex[:], ALU.mult, ["brec", "Ex"])
            bd_write("Vbd", vv[:], None, None, ["vv"])
            if phB:
                X("vector", "tensor_tensor", ["mx", "kh"], ["tA"], out=tA[:], in0=r, in1=kh[:], op=ALU.mult)
                X("gpsimd", "tensor_tensor", ["tA", "pvs"], ["tA"], out=tA[:], in0=tA[:], in1=pbc(l, "rk", 4, T), op=ALU.mult)
                pbn, kbn = PS(4, 0, 512)
                for c in range(4):
                    X("tensor", "matmul", ["tA", "c_ones_bd"], kbn, out=pbn[:, c * T:(c + 1) * T], lhsT=cst["ones_bd"][:], rhs=tA[:, c, :],
                      start=True, stop=True)
                X("vector", "tensor_tensor", kbn + ["vv"], ["bv"], out=m4["bv"][:].rearrange("p c t -> p (c t)"), in0=pbn,
                  in1=vv[:].rearrange("p c t -> p (c t)"), op=ALU.mult)

        QS = [(4, 256), (5, 0), (6, 0)]
        _qs = [0]

        def QSLOT():
            b, o = QS[_qs[0] % 3]
            _qs[0] += 1
            return PS(b, o, 128)

        def mm_evac_copy(lhs, lk, rhs, rk, dst, dk, eng):
            pr, pk = QSLOT()
            X("tensor", "matmul", [lk, rk], pk, out=pr, lhsT=lhs, rhs=rhs, start=True, stop=True)
            if eng == "scalar":
                X("scalar", "activation", pk, [dk], out=dst, in_=pr, func=AF.Copy)
            else:
                X("vector", "tensor_copy", pk, [dk], out=dst, in_=pr)

        def mm_evac_add(lhs, lk, rhs, rk, addend, ak, dst, dk):
            pr, pk = QSLOT()
            X("tensor", "matmul", [lk, rk], pk, out=pr, lhsT=lhs, rhs=rhs, start=True, stop=True)
            X("vector", "tensor_tensor", pk + [ak], [dk], out=dst, in0=pr, in1=addend, op=ALU.add)

        def rwkv_steps(l, phB, q, par):
            NV = 64 if phB else 128
            ncol = 64 if phB else 32

            def kn(n, g):
                return "%s%d" % (n, g)

            def kp(n, g):
                return "%s%d_%d" % (n, par, g)

            def CB(n, g):
                return chb2[n][par][g]

            p1s = [PS(4, 0, 160), PS(5, 0, 160)]

            def st_stage1(g):
                p1, k1 = p1s[g]
                for (lf, rf, rk_, c0, cw) in (("Kbd", AR, "AR", 0, ncol), ("Bbd", AR, "AR", 64, ncol), ("Abd", Bp, "Bp", 128, 32)):
                    for cc in range(2):
                        c = 2 * g + cc
                        rhs = rf[:, c, q, 0:cw] if rk_ == "AR" else rf[:, c, q, :]
                        X("tensor", "matmul", [lf, rk_], k1, out=p1[:, c0:c0 + cw], lhsT=bd[lf][:, c, q, :], rhs=rhs,
                          start=(cc == 0), stop=(cc == 1))

            def st_evac1(g):
                p1, k1 = p1s[g]

                def mask_evac(dst, dkey, col, mname, eng):
                    X(eng, "tensor_tensor", k1 + ["c_" + mname], [dkey], out=dst[:].rearrange("p (b t) -> p b t", t=32),
                      in0=p1[:, col:col + 32].unsqueeze(1).to_broadcast([128, 4, 32]),
                      in1=cst[mname][:].rearrange("p (b t) -> p b t", t=32), op=ALU.mult)
                mask_evac(chb["X1"][g], kn("X1", g), 64, "m_strict", "vector")
                mask_evac(chb["X1T"][g], kn("X1T", g), 128, "m_lower", "vector")
                mask_evac(CB("Aka", g), kp("Aka", g), 0, "m_strict", "vector")
                if phB:
                    mask_evac(CB("Akr", g), kp("Akr", g), 32, "m_incl", "vector")
                    mask_evac(CB("Abr", g), kp("Abr", g), 96, "m_incl", "vector")
                X("gpsimd", "tensor_tensor", [kn("X1", g), "ident_b"], [kp("Ma", g)], out=CB("Ma", g)[:], in0=chb["X1"][g][:], in1=ident_b[:], op=ALU.add)

            def B(n, g):
                if n == "Ma":
                    return CB("Ma", g)[:], kp("Ma", g)
                return chb[n][g][:], kn(n, g)

            def inv_steps(g):
                cp = lambda lh, rh, ds, eng: (lambda: mm_evac_copy(B(lh, g)[0], B(lh, g)[1], B(rh, g)[0], B(rh, g)[1], B(ds, g)[0], B(ds, g)[1], eng))
                ad = lambda lh, rh, ds: (lambda: mm_evac_add(B(lh, g)[0], B(lh, g)[1], B(rh, g)[0], B(rh, g)[1], B(rh, g)[0], B(rh, g)[1], B(ds, g)[0], B(ds, g)[1]))
                return [cp("X1T", "X1", "Xa", "scalar"), cp("X1", "X1T", "XaT", "vector"), ad("XaT", "Ma", "Mb"),
                        cp("XaT", "Xa", "Xb", "scalar"), cp("Xa", "XaT", "XbT", "vector"), ad("XbT", "Mb", "Ma"),
                        cp("XbT", "Xb", "Xa", "scalar"), cp("Xb", "XbT", "XaT", "vector"), ad("XaT", "Ma", "Mb"),
                        cp("Xa", "XaT", "XbT", "vector"), ad("XbT", "Mb", "Ma")]

            KTp, BTp, Vstp = KT2[par], BT2[par], Vst2[par]

            def st_tokmajor(g):
                for cc in range(2):
                    c = 2 * g + cc
                    pk_, kk_ = PSB(c * 128, 128)
                    X("tensor", "transpose", ["KHbd", "ident_b"], kk_, out=pk_, in_=bd["KHbd"][:, c, q, :], identity=ident_b[:])
                    pb_, kb_ = PSB(512 + c * 128, 128)
                    X("tensor", "transpose", ["BHbd", "ident_b"], kb_, out=pb_, in_=bd["BHbd"][:, c, q, :], identity=ident_b[:])
                pv_, kv_ = PS(6, 256 + 64 * g, 64)
                for cc in range(2):
                    c = 2 * g + cc
                    X("tensor", "matmul", ["Vbd", "isel_b"], kv_, out=pv_, lhsT=bd["Vbd"][:, c, q, :], rhs=isel_b[:], start=(cc == 0), stop=(cc == 1))
                pk2, kk2 = PSB(2 * g * 128, 256)
                X("scalar", "activation", kk2, ["KT%d_%d" % (par, 2 * g), "KT%d_%d" % (par, 2 * g + 1)],
                  out=KTp[:, 2 * g:2 * g + 2, :].rearrange("p c k -> p (c k)"), in_=pk2, func=AF.Copy)
                pb2, kb2 = PSB(512 + 2 * g * 128, 256)
                X("vector", "tensor_copy", kb2, ["BT%d_%d" % (par, 2 * g), "BT%d_%d" % (par, 2 * g + 1)],
                  out=BTp[:, 2 * g:2 * g + 2, :].rearrange("p c k -> p (c k)"), in_=pb2)
                X("scalar", "activation", kv_, ["Vst%d_%d" % (par, g)], out=Vstp[:, g, 0:64], in_=pv_, func=AF.Copy)

            def st_W(g):
                pW, kW = PS(0 + g, 0, NV)
                X("tensor", "matmul", [kp("Aka", g), "Vst%d_%d" % (par, g)], kW, out=pW, lhsT=CB("Aka", g)[:], rhs=Vstp[:, g, 0:NV], start=True, stop=False)
                for cc in range(2):
                    c = 2 * g + cc
                    X("tensor", "matmul", ["Abd", "Sb%d" % c], kW, out=pW, lhsT=bd["Abd"][:, c, q, :], rhs=Sb[:, c, 0:NV], start=False, stop=(cc == 1))
                if g == 0:
                    X("scalar", "activation", kW, ["Wt%d" % g], out=Wt[:, g, 0:NV], in_=pW, func=AF.Copy)
                else:
                    X("vector", "tensor_copy", kW, ["Wt%d" % g], out=Wt[:, g, 0:NV], in_=pW)

            def st_U(g):
                pU, kU = PS(2 + g, 0, NV)
                X("tensor", "matmul", [kp("Ma", g), "Wt%d" % g], kU, out=pU, lhsT=CB("Ma", g)[:], rhs=Wt[:, g, 0:NV], start=True, stop=True)
                if g == 0:
                    X("vector", "tensor_copy", kU, ["Ut%d" % g], out=Ut[:, g, 0:NV], in_=pU)
                else:
                    X("scalar", "activation", kU, ["Ut%d" % g], out=Ut[:, g, 0:NV], in_=pU, func=AF.Copy)

            def st_Y(g):
                pY, kY = PS(0 + g, 256, 64)
                X("tensor", "matmul", [kp("Akr", g), "Vst%d_%d" % (par, g)], kY, out=pY, lhsT=CB("Akr", g)[:], rhs=Vstp[:, g, 0:64], start=True, stop=False)
                X("tensor", "matmul", [kp("Abr", g), "Ut%d" % g], kY, out=pY, lhsT=CB("Abr", g)[:], rhs=Ut[:, g, 0:64], start=False, stop=False)
                for cc in range(2):
                    c = 2 * g + cc
                    X("tensor", "matmul", ["Rbd", "Sb%d" % c], kY, out=pY, lhsT=bd["Rbd"][:, c, q, :], rhs=Sb[:, c, 0:64], start=False, stop=(cc == 1))
                X("scalar", "activation", kY, ["Yst"], out=Yst[:, g * 4 + q, :], in_=pY, func=AF.Copy)

            def st_S(c):
                g = c // 2
                pS, kS = PS(2 + (c % 2), 256 * (c // 2), NV)
                X("tensor", "matmul", ["KT%d_%d" % (par, c), "Vst%d_%d" % (par, g)], kS, out=pS, lhsT=KTp[:, c, :], rhs=Vstp[:, g, 0:NV], start=True, stop=False)
                X("tensor", "matmul", ["BT%d_%d" % (par, c), "Ut%d" % g], kS, out=pS, lhsT=BTp[:, c, :], rhs=Ut[:, g, 0:NV], start=False, stop=True)
                X("vector", "scalar_tensor_tensor", kS + ["Sf%d" % c, "gL"], ["Sf%d" % c], out=Sf[:, c, 0:NV], in0=Sf[:, c, 0:NV],
                  scalar=gL[:, c, q:q + 1], in1=pS, op0=ALU.mult, op1=ALU.add)
                X("scalar", "activation", ["Sf%d" % c], ["Sb%d" % c], out=Sb[:, c, 0:NV], in_=Sf[:, c, 0:NV], func=AF.Copy)

            mk = lambda f, a: (lambda: f(a))
            pre = [mk(st_stage1, 0), mk(st_stage1, 1), mk(st_evac1, 0), mk(st_evac1, 1), mk(st_tokmajor, 0), mk(st_tokmajor, 1)]
            i0, i1 = inv_steps(0), inv_steps(1)
            for a_, b_ in zip(i0, i1):
                pre += [a_, b_]
            chain = [mk(st_W, 0), mk(st_W, 1), mk(st_U, 0), mk(st_U, 1)]
            if phB:
                chain += [mk(st_Y, 0), mk(st_Y, 1)]
            chain += [mk(st_S, c) for c in range(4)]
            return pre, chain

        def mixer_chunks(l, phB, col0, before_chunk, after_chunk):
            pre0, _ = rwkv_steps(l, phB, 0, 0)
            for f_ in pre0:
                f_()
            for q in range(4):
                par = q % 2
                before_chunk(q)
                _, chain = rwkv_steps(l, phB, q, par)
                nxt = rwkv_steps(l, phB, q + 1, 1 - par)[0] if q < 3 else []
                hg = hgrn_chunk_parts(l, phB, q, col0)
                hg[0]()
                per = -(-len(nxt) // len(chain)) if nxt else 0
                for ci, cstep in enumerate(chain):
                    cstep()
                    if ci % 2 == 1:
                        FEED(1)
                    for _ in range(per):
                        if nxt:
                            nxt.pop(0)()
                    if ci == 1:
                        hg[1]()
                    if ci == 3:
                        hg[2]()
                while nxt:
                    nxt.pop(0)()
                after_chunk(q)

        def rwkv_post(l, col0):
            s1, s2, mean, var = (gst[n] for n in ["s1", "s2", "mean", "var"])
            X("vector", "tensor_reduce", ["Yst"], ["g_s1"], out=s1[:], in_=Yst[:], axis=AX.X, op=ALU.add)
            X("gpsimd", "tensor_tensor", ["Yst"], ["Ysq"], out=Ysq[:], in0=Yst[:], in1=Yst[:], op=ALU.mult)
            X("vector", "tensor_reduce", ["Ysq"], ["g_s2"], out=s2[:], in_=Ysq[:], axis=AX.X, op=ALU.add)
            X("vector", "tensor_scalar", ["g_s1"], ["g_mean"], out=mean[:], in0=s1[:], scalar1=1.0 / 64, scalar2=None, op0=ALU.mult)
            X("vector", "tensor_tensor", ["g_mean"], ["g_s1"], out=s1[:], in0=mean[:], in1=mean[:], op=ALU.mult)
            X("vector", "scalar_tensor_tensor", ["g_s2", "g_s1"], ["g_var"], out=var[:], in0=s2[:], scalar=1.0 / 64, in1=s1[:],
              op0=ALU.mult, op1=ALU.subtract)
            X("scalar", "activation", ["g_var", "eps_gn"], ["g_var"], out=var[:], in_=var[:], func=AF.Ln, bias=eps_gn[:])
            X("scalar", "activation", ["g_var"], ["g_var"], out=var[:], in_=var[:], func=AF.Exp, scale=-0.5)
            X("vector", "tensor_tensor", ["Yst", "g_mean"], ["Ysq"], out=Ysq[:], in0=Yst[:], in1=mean[:].unsqueeze(2).to_broadcast([128, 8, 64]),
              op=ALU.subtract)
            X("vector", "tensor_tensor", ["Ysq", "g_var"], ["Ysq"], out=Ysq[:], in0=Ysq[:], in1=var[:].unsqueeze(2).to_broadcast([128, 8, 64]),
              op=ALU.mult)
            for g in range(2):
                ys = Ysq[:, g * 4:(g + 1) * 4, :]
                X(EW(), "tensor_tensor", ["Ysq", "lnw"], ["Ysq"], out=ys, in0=ys, in1=lnw[:, l, g, :].unsqueeze(1).to_broadcast([128, 4, 64]),
                  op=ALU.mult)
                X(EW(), "tensor_tensor", ["Ysq", "lnb"], ["Yrep"], out=Yrep[:, g * 4:(g + 1) * 4, :, :],
                  in0=ys.unsqueeze(2).to_broadcast([128, 4, 2, 64]),
                  in1=lnb[:, l, g, :].unsqueeze(1).unsqueeze(1).to_broadcast([128, 4, 2, 64]), op=ALU.add)
            t1 = m4["t1"]
            for g in range(2):
                for q in range(4):
                    pT, kT = QSLOT()
                    X("tensor", "transpose", ["Yrep", "c_ident"], kT, out=pT, in_=Yrep[:, g * 4 + q, :, :].rearrange("p r v -> p (r v)"),
                      identity=cst["ident"][:])
                    for hh in range(2):
                        X("vector", "tensor_tensor", kT + ["bv"], ["t1"], out=t1[hh * 64:(hh + 1) * 64, 2 * g:2 * g + 2, q * L:(q + 1) * L],
                          in0=pT[hh * 64:(hh + 1) * 64, :].rearrange("p (c h t) -> p c h t", c=2, h=2)[:, :, hh, :],
                          in1=m4["bv"][hh * 64:(hh + 1) * 64, 2 * g:2 * g + 2, q * L:(q + 1) * L], op=ALU.add)
            X("gpsimd", "tensor_tensor", ["t1", "gfm"], ["yg_b"], out=yg_b[:, :, col0:col0 + T], in0=t1[:], in1=m4["gfm"][:], op=ALU.mult)

        def hgrn_prep(l, phB, col0):
            f, lf, khh, G2, d1, E2, hA = (h4[n] for n in ["f", "lf", "khh", "G2", "d1", "E2", "hA"])
            fl = lambda t: t[:].rearrange("p c t -> p (c t)")
            sig = hsig[:, :, col0:col0 + T]
            X("vector", "tensor_tensor", ["hsig", "oml"], ["sg"], out=f[:], in0=sig, in1=oml[:, l, :].unsqueeze(2).to_broadcast([128, 4, T]),
              op=ALU.mult)
            X("vector", "tensor_tensor", ["sg", "lbv"], ["sg"], out=f[:], in0=f[:], in1=lbv[:, l, :].unsqueeze(2).to_broadcast([128, 4, T]),
              op=ALU.add)
            X("scalar", "activation", ["sg"], ["aa"], out=fl(lf), in_=fl(f), func=AF.Ln)
            X("gpsimd", "tensor_scalar", ["sg"], ["kkn"], out=fl(khh), in0=fl(f), scalar1=-1.0, scalar2=1.0, op0=ALU.mult, op1=ALU.add)
            X("vector", "tensor_tensor_scan", ["aa", "c_m_reset"], ["Gw"], out=fl(G2), data0=cst["m_reset"][:], data1=fl(lf), initial=0.0,
              op0=ALU.mult, op1=ALU.add)
            GLv = v4(G2[:])[:, :, :, L - 1:L]
            X("scalar", "activation", ["Gw"], ["dL"], out=dL[:].unsqueeze(3), in_=GLv, func=AF.Exp)
            if phB:
                hqv = hq[:, :, col0:col0 + T]
                Gm = v4(G2[:])[:, :, :, L // 2 - 1:L // 2]
                X("vector", "tensor_tensor", ["Gw"], ["tB"], out=v4(d1[:]), in0=v4(G2[:]), in1=Gm.to_broadcast([128, 4, 4, L]), op=ALU.subtract)
                X("scalar", "activation", ["tB"], ["tA"], out=fl(hA), in_=fl(d1), func=AF.Exp)
                X("vector", "tensor_tensor", ["hq", "tA"], ["hb_Qt"], out=hb["Qt"][:], in0=hqv, in1=hA[:], op=ALU.mult)
                X("scalar", "activation", ["tB"], ["tA"], out=fl(hA), in_=fl(d1), func=AF.Exp, scale=-1.0)
                X("gpsimd", "tensor_tensor", ["kkn", "tA"], ["hb_Kt"], out=hb["Kt"][:], in0=khh[:], in1=hA[:], op=ALU.mult)
                X("scalar", "activation", ["Gw"], ["Ex"], out=fl(E2), in_=fl(G2), func=AF.Exp)
                X("vector", "tensor_tensor", ["hq", "Ex"], ["hb_Q2"], out=hb["Q2"][:], in0=hqv, in1=E2[:], op=ALU.mult)
            X("vector", "tensor_tensor", ["Gw"], ["tB"], out=v4(d1[:]), in0=GLv.to_broadcast([128, 4, 4, L]), in1=v4(G2[:]), op=ALU.subtract)
            X("scalar", "activation", ["tB"], ["tA"], out=fl(hA), in_=fl(d1), func=AF.Exp)
            X("gpsimd", "tensor_tensor", ["kkn", "tA"], ["hb_Kh"], out=hb["Kh"][:], in0=khh[:], in1=hA[:], op=ALU.mult)

        def hgrn_chunk_parts(l, phB, q, col0):
            cs = slice(q * L, (q + 1) * L)

            def part_pre():
                if phB:
                    pat, kat = PS(6, 384, 128)
                    for c in range(4):
                        X("tensor", "matmul", ["hb_Kt", "hb_Qt"], kat, out=pat[0:32, c * 32:(c + 1) * 32], lhsT=hb["Kt"][:, c, cs], rhs=hb["Qt"][:, c, cs],
                          start=True, stop=True)
                    X("vector", "tensor_tensor", kat + ["c_m_att"], ["att_b"], out=att_b[:], in0=pat[0:32, :].rearrange("p (c t) -> p c t", c=4),
                      in1=cst["m_att"][:].unsqueeze(1).to_broadcast([32, 4, 32]), op=ALU.mult)
                pvt, kvt = PSB(0, 512)
                pkt, kkt = PSB(512, 512)
                for c in range(4):
                    X("tensor", "transpose", ["hv_b", "ident_b"], kvt, out=pvt[0:32, c * 128:(c + 1) * 128],
                      in_=hv_b[:, c, col0 + q * L: col0 + (q + 1) * L], identity=ident_b[:])
                    X("tensor", "transpose", ["hb_Kh", "ident_b"], kkt, out=pkt[0:32, c * 128:(c + 1) * 128], in_=hb["Kh"][:, c, cs], identity=ident_b[:])
                X("scalar", "activation", kvt, ["VTh"], out=VTh[:].rearrange("p c v -> p (c v)"), in_=pvt[0:32, :], func=AF.Copy)
                X("vector", "tensor_copy", kkt, ["KTh"], out=KTh[:].rearrange("p c v -> p (c v)"), in_=pkt[0:32, :])

            def part_o():
                if phB:
                    po, ko = PS(4, 384, 128)
                    for c in range(4):
                        X("tensor", "matmul", ["Shb", "hb_Q2"], ko, out=po[:, c * 32:(c + 1) * 32], lhsT=Shb[:, c, :], rhs=hb["Q2"][:, c, cs], start=True, stop=False)
                        X("tensor", "matmul", ["VTh", "att_b"], ko, out=po[:, c * 32:(c + 1) * 32], lhsT=VTh[:, c, :], rhs=att_b[:, c, :], start=False, stop=True)
                    X("scalar", "activation", ko, ["h_o"], out=h4["o"][:, :, cs], in_=po.rearrange("p (c t) -> p c t", c=4), func=AF.Copy)

            def part_s():
                pss_, kss_ = PS(1, 0, 512)
                for c in range(4):
                    X("tensor", "matmul", ["KTh", "VTh"], kss_, out=pss_[:, c * 128:(c + 1) * 128], lhsT=KTh[:, c, :], rhs=VTh[:, c, :], start=True, stop=True)
                for c in range(4):
                    X("vector", "scalar_tensor_tensor", kss_ + ["Shf", "dL"], ["Shf"], out=Shf[:, c, :], in0=Shf[:, c, :], scalar=dL[:, c, q:q + 1],
                      in1=pss_[:, c * 128:(c + 1) * 128], op0=ALU.mult, op1=ALU.add)
                X("scalar", "activation", ["Shf"], ["Shb"], out=Shb[:].rearrange("p c v -> p (c v)"), in_=Shf[:].rearrange("p c v -> p (c v)"), func=AF.Copy)
                if not phB:
                    X("gpsimd", "tensor_tensor", ["Dtot", "dL"], ["Dtot"], out=Dtot[:], in0=Dtot[:], in1=dL[:, :, q], op=ALU.mult)
            return [part_pre, part_o, part_s]

        def hgrn_post(l, col0):
            o, o2, rs, hA = (h4[n] for n in ["o", "o2", "rs", "hA"])
            fl = lambda t: t[:].rearrange("p c t -> p (c t)")
            X("gpsimd", "tensor_tensor", ["h_o"], ["kh"], out=o2[:], in0=o[:], in1=o[:], op=ALU.mult)
            pn, kn_ = PS(4, 0, 512)
            for c in range(4):
                X("tensor", "matmul", ["kh", "c_ones_f"], kn_, out=pn[:, c * T:(c + 1) * T], lhsT=cst["ones_f"][:], rhs=o2[:, c, :], start=True, stop=True)
            X("scalar", "activation", kn_ + ["eps_rms"], ["brec"], out=fl(rs), in_=pn, func=AF.Ln, scale=1.0 / 128, bias=eps_rms[:])
            X("scalar", "activation", ["brec"], ["brec"], out=fl(rs), in_=fl(rs), func=AF.Exp, scale=-0.5)
            X("vector", "tensor_tensor", ["h_o", "brec"], ["h_o"], out=o[:], in0=o[:], in1=rs[:], op=ALU.mult)
            X("gpsimd", "tensor_tensor", ["hog", "pvs"], ["tA"], out=hA[:], in0=hog[:, :, col0:col0 + T], in1=pbc(l, "hnw", 4, T), op=ALU.mult)
            X("vector", "tensor_tensor", ["h_o", "tA"], ["ob_b"], out=ob_b[:, :, col0:col0 + T], in0=o[:], in1=hA[:], op=ALU.mult)

        shst = sb("shst", [128, RW_BLK, 4])
        shout = sb("shout", [128, RW_BLK, 4])
        shoutp = sb("shoutp", [128, RW_BLK])
        halo_prev = sb("halo_prev", [128, RW_BLK])
        hraw = sb("hraw", [128, 16])
        hall = sb("hall", [128, 4, 16])
        exb = sb("exb", [128, SW])
        exall = big[:, 0:SW]
        Xr = sb("Xr", [128, 4, 64])
        Xh = sb("Xh", [128, 4, 128])
        PTbd = sb("PTbd", [128, 128])
        lhsTf = sb("lhsTf", [128, 128])
        ftmp = sb("ftmp", [128, 128])
        X("vector", "memset", [], ["PTbd"], ap=PTbd[:], constant=0.0)
        X("vector", "memset", [], ["hraw"], ap=hraw[:], constant=0.0)
        groups4 = [[0, 1, 2, 3], [4, 5, 6, 7]]

        def make_handler(is_s, N):
            def handler(b, pr, pk):
                if b < 15:
                    if is_s:
                        dst = raw[:, b, 0:132].rearrange("p (s t) -> p s t", t=33)[:, :, 1:33]
                        src = pr.rearrange("p (s t) -> p s t", t=32)
                    else:
                        dst, src = raw[:, b, 1:N + 1], pr
                    X("scalar", "activation", pk, ["raw"], out=dst, in_=src, func=AF.Copy)
                elif b < 19:
                    X("scalar", "activation", pk, ["hq"], out=hq[:, b - 15, 0:N], in_=pr, func=AF.Silu)
                elif b < 23:
                    X("scalar", "activation", pk, ["hsig"], out=hsig[:, b - 19, 0:N], in_=pr, func=AF.Sigmoid)
                elif b < 27:
                    X("scalar", "activation", pk, ["hv_b"], out=hv_b[:, b - 23, 0:N], in_=pr, func=AF.Copy)
                elif b < 31:
                    X("scalar", "activation", pk, ["hog"], out=hog[:, b - 27, 0:N], in_=pr, func=AF.Silu)
                else:
                    X("scalar", "activation", pk, ["gates"], out=gates[:, b - 31, 0:N], in_=pr, func=AF.Sigmoid)
            return handler

        class Feeder:
            def __init__(self, l, blocks, N, handler):
                self.l, self.q, self.N, self.h = l, list(blocks), N, handler

            def feed(self, n=2):
                if self.q:
                    take, self.q = self.q[:n], self.q[n:]
                    emit_proj(self.l, take, self.N, self.h)

            def until(self, b):
                while self.q and self.q[0] <= b:
                    self.feed(2)

            def flush(self):
                while self.q:
                    self.feed(2)

        _feeder = [None]

        def FEED(n=2):
            if _feeder[0] is not None:
                _feeder[0].feed(n)

        def xsrc(l):
            return (xT, []) if l == 0 else (xs1, ["xs1"])

        def emit_halo(l):
            if l > 0:
                conv_pump(l, 10 ** 6)
            src, sk = xsrc(l)
            DMA("sync", "x_ld", xt[:, :, 0:1], src[:, :, NPT - 1:NPT], sk, ["xt"], allow_slow_non_contiguous=True)
            emit_norm(l, xt, "xt", 1, "nmix")

            def hh_(b, pr, pk):
                X("scalar", "activation", pk, ["hraw"], out=hraw[:, b:b + 1], in_=pr, func=AF.Copy)
            emit_proj(l, list(range(RW_BLK)), 1, hh_)
            DMA("gpsimd", "ex_h", cin_h[:, :], hraw[:], ["hraw"], ["cin_h"])
            p.op("gpsimd", lambda e: e.collective_compute("AllGather", ALU.bypass, replica_groups=groups4, ins=[cin_h[:, :]], outs=[cout_h[:, :]]),
                 ["cin_h"], ["cout_h"])
            DMA("gpsimd", "ex_h", hall[:], cout_h.rearrange("(r p) c -> p r c", p=128), ["cout_h"], ["hall"])
            X("vector", "tensor_scalar", ["hall", "c_halom"], ["halo_prev"], out=halo_prev[:], in0=hall[:, 0, 0:RW_BLK], scalar1=cst["halom"][:, 0:1],
              scalar2=None, op0=ALU.mult)
            for r in range(1, 4):
                X("vector", "scalar_tensor_tensor", ["hall", "c_halom", "halo_prev"], ["halo_prev"], out=halo_prev[:], in0=hall[:, r, 0:RW_BLK],
                  scalar=cst["halom"][:, r:r + 1], in1=halo_prev[:], op0=ALU.mult, op1=ALU.add)

        def emit_exchange(l):
            conv_pump(l, 10 ** 6)
            X("vector", "tensor_copy", ["Sf0", "Sf1", "Sf2", "Sf3"], ["exb"], out=exb[:, 0:512], in_=Sf[:].rearrange("p c v -> p (c v)"))
            X("vector", "tensor_copy", ["Shf"], ["exb"], out=exb[:, 512:1024], in_=Shf[:].rearrange("p c v -> p (c v)"))
            X("vector", "tensor_copy", ["Dtot"], ["exb"], out=exb[:, 1024:1028], in_=Dtot[:])
            X("vector", "memset", [], ["exb"], ap=exb[:, 1028:SW], constant=0.0)
            DMA("gpsimd", "ex_s", cin_s[:, :], exb[:], ["exb"], ["cin_s"])
            p.op("gpsimd", lambda e: e.collective_compute("AllGather", ALU.bypass, replica_groups=groups4, ins=[cin_s[:, :]], outs=[cout_s[:, :]]),
                 ["cin_s"], ["cout_s"])
            X("vector", "memset", [], ["Xr"], ap=Xr[:], constant=0.0)
            X("vector", "memset", [], ["Xh"], ap=Xh[:], constant=0.0)
            fm = cst["foldm"]
            for r in range(3):
                DMA("gpsimd", "ex_s", exall, cout_s[r * 128:(r + 1) * 128, :], ["cout_s"], ["gates"])
                for c in range(4):
                    for hh in range(2):
                        X("vector", "tensor_copy", ["gates"], ["PTbd"], out=PTbd[hh * 64:(hh + 1) * 64, hh * 64:(hh + 1) * 64],
                          in_=exall[hh * 64:(hh + 1) * 64, c * 128 + 64:c * 128 + 128])
                    pT, kT = QSLOT()
                    X("tensor", "transpose", ["PTbd", "c_ident"], kT, out=pT, in_=PTbd[:], identity=cst["ident"][:])
                    X("vector", "tensor_copy", kT, ["lhsTf"], out=lhsTf[:], in_=pT)
                    pm, km = QSLOT()
                    X("tensor", "matmul", ["lhsTf", "Xr"], km, out=pm[:, 0:64], lhsT=lhsTf[:], rhs=Xr[:, c, :], start=True, stop=True)
                    X("vector", "tensor_tensor", km + ["gates"], ["ftmp"], out=ftmp[:, 0:64], in0=pm[:, 0:64], in1=exall[:, c * 128:c * 128 + 64], op=ALU.add)
                    X("vector", "tensor_tensor", ["ftmp", "Xr"], ["ftmp"], out=ftmp[:, 0:64], in0=ftmp[:, 0:64], in1=Xr[:, c, :], op=ALU.subtract)
                    X("vector", "scalar_tensor_tensor", ["ftmp", "Xr", "c_foldm"], ["Xr"], out=Xr[:, c, :], in0=ftmp[:, 0:64], scalar=fm[:, r:r + 1],
                      in1=Xr[:, c, :], op0=ALU.mult, op1=ALU.add)
                for c in range(4):
                    X("vector", "scalar_tensor_tensor", ["Xh", "gates"], ["ftmp"], out=ftmp[:], in0=Xh[:, c, :], scalar=exall[:, 1024 + c:1025 + c],
                      in1=exall[:, 512 + c * 128:512 + (c + 1) * 128], op0=ALU.mult, op1=ALU.add)
                    X("vector", "tensor_tensor", ["ftmp", "Xh"], ["ftmp"], out=ftmp[:], in0=ftmp[:], in1=Xh[:, c, :], op=ALU.subtract)
                    X("vector", "scalar_tensor_tensor", ["ftmp", "Xh", "c_foldm"], ["Xh"], out=Xh[:, c, :], in0=ftmp[:], scalar=fm[:, r:r + 1],
                      in1=Xh[:, c, :], op0=ALU.mult, op1=ALU.add)

        SFK = ["Sf0", "Sf1", "Sf2", "Sf3"]
        SBK = ["Sb0", "Sb1", "Sb2", "Sb3"]

        def shadows():
            X("scalar", "activation", SFK, SBK, out=Sb[:].rearrange("p c v -> p (c v)"), in_=Sf[:].rearrange("p c v -> p (c v)"), func=AF.Copy)
            X("scalar", "activation", ["Shf"], ["Shb"], out=Shb[:].rearrange("p c v -> p (c v)"), in_=Shf[:].rearrange("p c v -> p (c v)"), func=AF.Copy)

        def init_states_A():
            X("vector", "memset", [], SFK, ap=Sf[:], constant=0.0)
            for c in range(4):
                X("vector", "tensor_copy", ["c_isel"], SFK, out=Sf[:, c, 64:128], in_=cst["isel"][:])
            X("vector", "memset", [], ["Shf"], ap=Shf[:], constant=0.0)
            X("vector", "memset", [], ["Dtot"], ap=Dtot[:], constant=1.0)
            shadows()

        def init_states_B():
            X("vector", "tensor_copy", ["Xr"], SFK, out=Sf[:, :, 0:64], in_=Xr[:])
            X("vector", "tensor_copy", ["Xh"], ["Shf"], out=Shf[:], in_=Xh[:])
            shadows()

        def layer_tile(l, phB, ti):
            conv_pump(l + 1 if phB else l, 8)
            is_s = (ti == NPS)
            N = 128 if is_s else W
            tok0 = NPT if is_s else ti * W
            last_prompt = (ti == NPS - 1)
            src, sk = xsrc(l)
            DMA("sync", "x_ld", xt[:, :, 0:N], src[:, :, tok0:tok0 + N], sk, ["xt"])
            emit_norm(l, xt, "xt", N, "nmix")
            if is_s:
                DMA("sync", "sh_ld", shst[:], st_shift[l], [], ["shst"])
                X("vector", "tensor_copy", ["shst"], ["raw"], out=raw[:, :, 0:132].rearrange("p b (s t) -> p b s t", t=33)[:, :, :, 0:1],
                  in_=shst[:].unsqueeze(3))
            blocks = list(range(NBLK)) if phB else (list(range(4, 13)) + list(range(19, 27)))
            fd = Feeder(l, blocks, N, make_handler(is_s, N))
            _feeder[0] = fd
            fd.until(14)
            nb = (0, RW_BLK) if phB else (4, 13)
            for j in range(N // T):
                col0 = j * T
                if is_s:
                    rv = raw[:, nb[0]:nb[1], 0:132].rearrange("p b (s t) -> p b s t", t=33)
                    cur, prv = rv[:, :, :, 1:33], rv[:, :, :, 0:32]
                    mxv = mx[:, nb[0]:nb[1], :].rearrange("p b (s t) -> p b s t", t=32)
                else:
                    cur, prv = raw[:, nb[0]:nb[1], 1 + col0:1 + col0 + T], raw[:, nb[0]:nb[1], col0:col0 + T]
                    mxv = mx[:, nb[0]:nb[1], :]
                rwkv_prep(l, phB, cur, prv, mxv, tok0 + col0, nb)
                fd.until(22)
                hgrn_prep(l, phB, col0)
                fd.until(26)
                def before_chunk(q, l=l, is_s=is_s):
                    if is_s:
                        DMA("sync", "st_ld", Sf[:, :, 0:64], st_rwkv[l, q], [], SFK)
                        DMA("sync", "st_ld", Shf[:], st_hgrn[l, q], [], ["Shf"])
                        shadows()

                def after_chunk(q, l=l, is_s=is_s):
                    if is_s:
                        DMA("gpsimd", "st_out", o_rwkv_s[l, q], Sf[:, :, 0:64], SFK, ["o_rwkv_s"], is_output=True)
                        DMA("gpsimd", "st_out", o_hgrn_s[l, q], Shf[:], ["Shf"], ["o_hgrn_s"], is_output=True)
                mixer_chunks(l, phB, col0, before_chunk, after_chunk)
                if phB:
                    rwkv_post(l, col0)
                    fd.until(30)
                    hgrn_post(l, col0)
            fd.flush()
            _feeder[0] = None
            if is_s:
                if phB:
                    X("vector", "tensor_copy", ["raw"], ["shout"], out=shout[:].unsqueeze(3),
                      in_=raw[:, :, 0:132].rearrange("p b (s t) -> p b s t", t=33)[:, :, :, 32:33])
                    DMA("gpsimd", "st_out", o_shift_s[l], shout[:], ["shout"], ["o_shift_s"], is_output=True)
            else:
                if phB and last_prompt:
                    X("vector", "tensor_copy", ["raw"], ["shoutp"], out=shoutp[:].unsqueeze(2), in_=raw[:, :, W:W + 1])
                    DMA("gpsimd", "st_out", o_shift_p[l], shoutp[:], ["shoutp"], ["o_shift_p"], is_output=True)
                    DMA("gpsimd", "st_out", o_rwkv_p[l], Sf[:, :, 0:64], SFK, ["o_rwkv_p"], is_output=True)
                    DMA("gpsimd", "st_out", o_hgrn_p[l], Shf[:], ["Shf"], ["o_hgrn_p"], is_output=True)
                X("vector", "tensor_copy", ["raw"], ["raw"], out=raw[:, :, 0:1], in_=raw[:, :, W:W + 1])
            if not phB:
                return
            for o8 in range(8):
                s_ = _wsl[0] % WSL
                _wsl[0] += 1
                DMA("sync", "wsl%d" % s_, wslot[s_][:, 0, :], wb_oab[l, o8], ["wb_o%d" % l], ["wslot%d" % s_])
                sa = _psl[0] % 4
                _psl[0] += 1
                pa_, ka_ = PS(sa, 0, N)
                for c in range(4):
                    X("tensor", "matmul", ["wslot%d" % s_, "yg_b"], ka_, out=pa_, lhsT=wslot[s_][:, 0, c * 128:(c + 1) * 128], rhs=yg_b[:, c, 0:N],
                      start=(c == 0), stop=(c == 3))
                sb_ = _psl[0] % 4
                _psl[0] += 1
                pb_, kb_ = PS(sb_, 0, N)
                for c in range(4):
                    X("tensor", "matmul", ["wslot%d" % s_, "ob_b"], kb_, out=pb_, lhsT=wslot[s_][:, 0, (4 + c) * 128:(5 + c) * 128], rhs=ob_b[:, c, 0:N],
                      start=(c == 0), stop=(c == 3))
                X("vector", "tensor_tensor", ka_ + ["gates"], ["mtmp"], out=mtmp[:, 0:N], in0=pa_, in1=gates[:, o8, 0:N], op=ALU.mult)
                X("vector", "tensor_tensor", kb_ + ["gates"], ["relu0"], out=relu_t[0][:, 0:N], in0=pb_, in1=gates[:, 8 + o8, 0:N], op=ALU.mult)
                X("gpsimd", "tensor_tensor", ["mtmp", "relu0"], ["mixin"], out=mixin[:, o8, 0:N], in0=mtmp[:, 0:N], in1=relu_t[0][:, 0:N], op=ALU.add)
            for o8 in range(8):
                s_ = _wsl[0] % WSL
                _wsl[0] += 1
                DMA("sync", "wsl%d" % s_, wslot[s_][:, 0, :], wb_o[l, o8], ["wb_o%d" % l], ["wslot%d" % s_])
                sm = _psl[0] % 4
                _psl[0] += 1
                pm_, km_ = PS(sm, 0, N)
                for kc in range(KC):
                    X("tensor", "matmul", ["wslot%d" % s_, "mixin"], km_, out=pm_, lhsT=wslot[s_][:, 0, kc * 128:(kc + 1) * 128], rhs=mixin[:, kc, 0:N],
                      start=(kc == 0), stop=(kc == KC - 1))
                X("vector", "tensor_tensor", km_ + ["xt"], ["x1"], out=x1[:, o8, 0:N], in0=pm_, in1=xt[:, o8, 0:N], op=ALU.add)
            DMA("gpsimd", "x1_st", x1s[:, :, tok0:tok0 + N], x1[:, :, 0:N], ["x1"], ["x1s"])

        F_x = aalloc([128, KC, 512], F32, reset=True)
        F_h = aalloc([128, KC, 512], BF16)
        F_sq = [aalloc([128, 512]) for _ in range(2)]
        F_rstd = aalloc([128, 512])
        F_relu = F_sq
        F_act = aalloc([128, 16, 512], BF16)
        F_up = [aalloc([128, KC, 256], BF16) for _ in range(3)]
        F_dn = [aalloc([128, 16, 128], BF16) for _ in range(3)]
        _fs = [0, 0]

        def stage_F(l, tok0, N):
            DMA("sync", "f_ld", F_x[:, :, 0:N], x1s[:, :, tok0:tok0 + N], ["x1s"], ["F_x"])
            emit_norm(l, F_x, "F_x", N, "nffn", outbuf=F_h, outkey="F_h", sqk=F_sq, rstd=F_rstd, pfx="F_")
            for h in range(2):
                for fg in range(8):
                    su = _fs[0] % 3
                    _fs[0] += 1
                    DMA("sync", "up%d" % su, F_up[su][:].rearrange("p k c -> p (k c)"), wb_up[l, h * 8 + fg], ["wb_f%d" % l], ["F_up%d" % su])
                    for fb in range(2):
                        pu, ku = PS(4 + (fb % 2), 0, N)
                        for kc in range(KC):
                            X("tensor", "matmul", ["F_up%d" % su, "F_h"], ku, out=pu, lhsT=F_up[su][:, kc, fb * 128:(fb + 1) * 128], rhs=F_h[:, kc, 0:N],
                              start=(kc == 0), stop=(kc == KC - 1))
                        rt = F_relu[fb % 2]
                        X("scalar", "activation", ku, ["F_sqk%d" % (fb % 2)], out=rt[:, 0:N], in_=pu, func=AF.Relu)
                        X("gpsimd", "tensor_tensor", ["F_sqk%d" % (fb % 2)], ["F_act"], out=F_act[:, fg * 2 + fb, 0:N], in0=rt[:, 0:N], in1=rt[:, 0:N], op=ALU.mult)
                for o8 in range(8):
                    sd = _fs[1] % 3
                    _fs[1] += 1
                    DMA("sync", "dn%d" % sd, F_dn[sd][:].rearrange("p k c -> p (k c)"), wb_dn[l, h * 8 + o8], ["wb_f%d" % l], ["F_dn%d" % sd])
                    pd_, kd_ = PS(o8 % 4, 0, N)
                    for fc in range(16):
                        X("tensor", "matmul", ["F_dn%d" % sd, "F_act"], kd_, out=pd_, lhsT=F_dn[sd][:, fc, :], rhs=F_act[:, fc, 0:N],
                          start=(fc == 0), stop=(fc == 15))
                    X("vector", "tensor_tensor", kd_ + ["F_x"], ["F_x"], out=F_x[:, o8, 0:N], in0=pd_, in1=F_x[:, o8, 0:N], op=ALU.add)
            if l < NL - 1:
                DMA("gpsimd", "x_st", xs1[:, :, tok0:tok0 + N], F_x[:, :, 0:N], ["F_x"], ["xs1"])
            else:
                emit_norm(0, F_x, "F_x", N, "nfin", outbuf=F_x, outkey="F_x", sqk=F_sq, rstd=F_rstd, pfx="F_")
                DMA("gpsimd", "y_st", yT[:, :, tok0:tok0 + N], F_x[:, :, 0:N], ["F_x"], ["yT"], is_output=True)

        _step = [0]

        def step(fn, *a):
            _step[0] += 1
            if cfg.stop is None or _step[0] <= cfg.stop:
                fn(*a)

        def set_prev():
            X("vector", "tensor_copy", ["halo_prev"], ["raw"], out=raw[:, :, 0:1], in_=halo_prev[:].unsqueeze(2))

        for l in range(NL):
            step(emit_halo, l)
            step(set_prev)
            step(init_states_A)
            for ti in range(NPS):
                step(layer_tile, l, False, ti)
            step(emit_exchange, l)
            step(set_prev)
            step(init_states_B)
            GT = max(1, 512 // W)
            for g0 in range(0, NPS, GT):
                g1 = min(NPS, g0 + GT)
                for ti in range(g0, g1):
                    step(layer_tile, l, True, ti)
                step(p.fence)
                step(stage_F, l, g0 * W, (g1 - g0) * W)
                step(p.fence)
            step(layer_tile, l, True, NPS)
            step(p.fence)
            step(stage_F, l, NPT, 128)
            step(p.fence)

        with nc.Block() as block:
            p.emit(block)
    return nc


_NC_CACHE = {}


def _prep_shared(inp):
    f = np.float32
    NL = inp["w_in"].shape[0]
    idx = _win_cols()
    sh = {}
    w_in = _take_cols(np.asarray(inp["w_in"], f), idx)
    sh["w_in_r"] = np.ascontiguousarray(w_in.reshape(NL, KC, 128, NBLK, 128).transpose(0, 3, 2, 1, 4))
    w_up = np.asarray(inp["w_ffn_up"], f)
    sh["w_up_r"] = np.ascontiguousarray(w_up.reshape(NL, KC, 128, 16, 256).transpose(0, 3, 2, 1, 4))
    w_dn = np.asarray(inp["w_ffn_down"], f)
    sh["w_dn_r"] = np.ascontiguousarray(w_dn.reshape(NL, 2, 16, 128, 8, 128).transpose(0, 1, 4, 3, 2, 5).reshape(NL, 16, 128, 16, 128))
    woa = np.asarray(inp["w_out_a"], f).reshape(NL, 4, 128, 8, 128).transpose(0, 3, 2, 1, 4)
    wob = np.asarray(inp["w_out_b"], f).reshape(NL, 4, 128, 8, 128).transpose(0, 3, 2, 1, 4)
    sh["w_oab_r"] = np.ascontiguousarray(np.concatenate([woa, wob], axis=3))
    sh["w_o_r"] = np.ascontiguousarray(np.asarray(inp["w_out"], f).reshape(NL, 8, 128, 8, 128).transpose(0, 3, 2, 1, 4))
    w2 = np.zeros((NL, 128, 512), f)
    w2[:, 0:64] = inp["rwkv_w2"]
    a2 = np.zeros((NL, 128, 512), f)
    a2[:, 64:128] = inp["rwkv_a2"]
    g2 = np.zeros((NL, 256, 512), f)
    g2[:, 0:160] = inp["rwkv_g2"]
    sh["w2pad"], sh["a2pad"] = w2, a2
    sh["g2pad"] = np.ascontiguousarray(g2.reshape(NL, 2, 128, 512).transpose(0, 2, 1, 3))
    sh["vw1"] = np.ascontiguousarray(np.asarray(inp["rwkv_vres_w1"], f)[0].reshape(4, 128, 32).transpose(1, 0, 2))
    sh["vw2"] = np.ascontiguousarray(np.asarray(inp["rwkv_vres_w2"], f)[0])
    pv = np.zeros((NL, 128, NPV), f)
    mu = _take_cols(np.asarray(inp["rwkv_mu"], f), idx[:RW_BLK * 128])
    for l in range(NL):
        pv[l, :, PV["nmix"]:PV["nmix"] + 8] = _pk(np.asarray(inp["norm_mix"], f)[l], 8)
        pv[l, :, PV["nffn"]:PV["nffn"] + 8] = _pk(np.asarray(inp["norm_ffn"], f)[l], 8)
        pv[l, :, PV["mu"]:PV["mu"] + 15] = _pk(mu[l], 15)
        pv[l, :, PV["w0"]:PV["w0"] + 4] = _pk(np.asarray(inp["rwkv_w0"], f)[l], 4)
        pv[l, :, PV["a0"]:PV["a0"] + 4] = _pk(np.asarray(inp["rwkv_a0"], f)[l], 4)
        pv[l, :, PV["v0"]:PV["v0"] + 4] = _pk(np.asarray(inp["rwkv_v0"], f)[0], 4)
        pv[l, :, PV["kk"]:PV["kk"] + 4] = _pk(np.asarray(inp["rwkv_k_k"], f)[l], 4)
        pv[l, :, PV["ka"]:PV["ka"] + 4] = _pk(np.asarray(inp["rwkv_k_a"], f)[l], 4)
        pv[l, :, PV["rk"]:PV["rk"] + 4] = _pk(np.asarray(inp["rwkv_r_k"], f)[l].reshape(-1), 4)
        pv[l, :, PV["hnw"]:PV["hnw"] + 4] = _pk(np.asarray(inp["hgrn_norm_w"], f)[l], 4)
        pv[l, :, PV["lbz"]:PV["lbz"] + 4] = _pk(np.asarray(inp["hgrn_lb_logits"], f)[l], 4)
        pv[l, :, PV["nfin"]:PV["nfin"] + 8] = _pk(np.asarray(inp["norm_final"], f), 8)
    sh["pvec"] = pv
    for nm, key in (("lnw_st", "rwkv_ln_w"), ("lnb_st", "rwkv_ln_b")):
        a = np.asarray(inp[key], f).reshape(NL, 2, 2, 2, 64)
        a = np.broadcast_to(a[:, :, :, :, None, :], (NL, 2, 2, 2, 32, 64))
        sh[nm] = np.ascontiguousarray(a.transpose(0, 2, 3, 4, 1, 5).reshape(NL, 128, 2, 64))
    return sh


def _run(inp, npt, w, dbg=(), stop=None):
    f = np.float32
    cfg = Cfg(npt=npt, w=w, nlayer=int(inp["w_in"].shape[0]), dbg=dbg, stop=stop)
    key = (npt, w, cfg.NL, tuple(dbg), stop)
    if key not in _NC_CACHE:
        _NC_CACHE[key] = build(cfg)
    nc = _NC_CACHE[key]
    NL, NT = cfg.NL, cfg.NT
    sh = _prep_shared(inp)
    xp = np.asarray(inp["x_prompt"], f)
    xs = np.asarray(inp["x_sample"], f)
    idx = _win_cols()
    sshift = _take_cols(np.asarray(inp["state_shift"], f), idx[:RW_BLK * 128])
    srw = np.asarray(inp["state_rwkv"], f)
    shg = np.asarray(inp["state_hgrn"], f)
    in_maps = []
    for c in range(NCORE):
        b, seg = c // 4, c % 4
        xtok = np.concatenate([xp[b, seg * npt:(seg + 1) * npt], xs[4 * c:4 * c + 4].reshape(4 * L, D)], axis=0)
        m = dict(sh)
        m["xT"] = np.ascontiguousarray(xtok.reshape(NT, KC, 128).transpose(2, 1, 0))
        ss = sshift[:, 4 * c:4 * c + 4]
        m["st_shift"] = np.ascontiguousarray(ss.reshape(NL, 4, RW_BLK, 128).transpose(0, 3, 2, 1))
        r = srw[:, 4 * c:4 * c + 4].reshape(NL, 4, 4, 2, 64, 64)
        m["st_rwkv"] = np.ascontiguousarray(r.transpose(0, 1, 3, 5, 2, 4).reshape(NL, 4, 128, 4, 64))
        h = shg[:, 4 * c:4 * c + 4]
        m["st_hgrn"] = np.ascontiguousarray(h.transpose(0, 1, 3, 2, 4))
        for n, v in _consts(seg).items():
            m["c_" + n] = v
        in_maps.append(m)
    res = run_bass_kernel_spmd(nc, in_maps, core_ids=list(range(NCORE)))
    R = res.results
    B = xp.shape[0]
    y_p = np.zeros((B, 4 * npt, D), f)
    y_s = np.zeros((4 * NCORE, L, D), f)
    for c in range(NCORE):
        yt = R[c]["yT"].transpose(2, 1, 0).reshape(NT, D)
        y_p[c // 4, (c % 4) * npt:(c % 4 + 1) * npt] = yt[:npt]
        y_s[4 * c:4 * c + 4] = yt[npt:].reshape(4, L, D)

    def unshift(a):
        a = np.moveaxis(a, -2, -1)
        return a.reshape(a.shape[:-2] + (RW_BLK * 128,))[..., :1824]

    def unrw(a):
        lead = a.shape[:-3]
        a = a.reshape(lead + (2, 64, 4, 64))
        n = len(lead)
        a = a.transpose(tuple(range(n)) + (n + 2, n + 0, n + 3, n + 1))
        return a.reshape(lead + (8, 64, 64))

    def unhg(a):
        n = a.ndim - 3
        return a.transpose(tuple(range(n)) + (n + 1, n + 0, n + 2))

    lastc = [4 * bb + 3 for bb in range(B)]
    shift_p = np.stack([unshift(R[c]["o_shift_p"]) for c in lastc], axis=1)
    rwkv_p = np.stack([unrw(R[c]["o_rwkv_p"]) for c in lastc], axis=1)
    hgrn_p = np.stack([unhg(R[c]["o_hgrn_p"]) for c in lastc], axis=1)
    shift_s = np.concatenate([np.moveaxis(unshift(np.moveaxis(R[c]["o_shift_s"], -1, 1)), 1, 1) for c in range(NCORE)], axis=1)
    rwkv_s = np.concatenate([unrw(R[c]["o_rwkv_s"]) for c in range(NCORE)], axis=1)
    hgrn_s = np.concatenate([unhg(R[c]["o_hgrn_s"]) for c in range(NCORE)], axis=1)
    outs = (y_p, y_s, shift_p, rwkv_p, hgrn_p, shift_s, rwkv_s, hgrn_s)
    return tuple(np.ascontiguousarray(o, dtype=f) for o in outs), R


def kernel(**inputs):
    outs, _ = _run(inputs, 2048, 128)
    return outs
```

```python
import numpy as np
from contextlib import ExitStack
import concourse.bass as bass
import concourse.mybir as mybir
from concourse.bass_utils import run_bass_kernel_spmd

F32 = mybir.dt.float32
BF16 = mybir.dt.bfloat16
AF = mybir.ActivationFunctionType
ALU = mybir.AluOpType
AX = mybir.AxisListType

D = 1024
KC = 8
NCORE = 8
L = 32
NBLK = 47
RW_BLK = 15
DFF = 4096
RMS_EPS = 1e-6
GN_EPS = 64e-5
C0 = -float(np.exp(-0.5))


class Prog:
    ENGS = ["sync", "scalar", "vector", "gpsimd", "tensor"]

    def __init__(self, nc, stack):
        self.nc = nc
        self.stack = stack
        self.ops = {e: [] for e in self.ENGS}
        self.esem = {e: stack.enter_context(nc.semaphore("s_" + e)) for e in self.ENGS}
        self.ecnt = {e: 0 for e in self.ENGS}
        self.dsem = {}
        self.dcnt = {}
        self.writer = {}
        self.readers = {}
        self.waited = {e: {} for e in self.ENGS}
        self.out_tokens = []
        self.pending = {e: {} for e in self.ENGS}

    def _dsem(self, name):
        if name not in self.dsem:
            self.dsem[name] = self.stack.enter_context(self.nc.semaphore("d_" + name))
            self.dcnt[name] = 0
        return self.dsem[name]

    def _deps(self, eng, reads, writes):
        toks = []
        for k in reads:
            w = self.writer.get(k)
            if w is not None:
                toks.append(w)
        for k in writes:
            w = self.writer.get(k)
            if w is not None:
                toks.append(w)
            toks.extend(self.readers.get(k, []))
        need = {}
        for (s, sid, v) in toks:
            if sid == ("e", "tensor") and eng == "tensor":
                continue
            if sid[0] == "e" and sid[1] == eng and eng == "sync":
                continue
            if sid[0] == "d":
                v = max(v, self.dcnt[sid[1]])
            if need.get(sid, (None, -1))[1] < v:
                need[sid] = (s, v)
        for sid, (s, v) in self.pending[eng].items():
            if need.get(sid, (None, -1))[1] < v:
                need[sid] = (s, v)
        self.pending[eng] = {}
        waits = []
        for sid, (s, v) in need.items():
            if self.waited[eng].get(sid, -1) >= v:
                continue
            self.waited[eng][sid] = v
            waits.append((s, v))
        return waits

    def fence(self):
        allt = {}
        for e in self.ENGS:
            if self.ecnt[e] > 0:
                allt[("e", e)] = (self.esem[e], self.ecnt[e])
        for n, c in self.dcnt.items():
            if c > 0:
                allt[("d", n)] = (self.dsem[n], c)
        for e in self.ENGS:
            for sid, sv in allt.items():
                if sid == ("e", e):
                    continue
                self.pending[e][sid] = sv
        self.writer.clear()
        self.readers.clear()

    def _record(self, tok, reads, writes):
        for k in reads:
            self.readers.setdefault(k, []).append(tok)
        for k in writes:
            self.writer[k] = tok
            self.readers[k] = []

    def op(self, eng, fn, reads=(), writes=()):
        pk = [k for k in reads if k.startswith("psbank")]
        if pk:
            reads = [k for k in reads if not k.startswith("psbank")]
            writes = list(writes) + [k for k in pk if k not in writes]
        waits = self._deps(eng, reads, writes)
        self.ecnt[eng] += 1
        tok = (self.esem[eng], ("e", eng), self.ecnt[eng])
        self.ops[eng].append((waits, fn, self.esem[eng], 1))
        self._record(tok, reads, writes)
        return tok

    def dma(self, eng, dname, fn, reads=(), writes=(), is_output=False):
        waits = self._deps(eng, reads, writes)
        s = self._dsem(dname)
        self.dcnt[dname] += 16
        tok = (s, ("d", dname), self.dcnt[dname])
        self.ops[eng].append((waits, fn, s, 16))
        self._record(tok, reads, writes)
        if is_output:
            self.out_tokens.append(tok)
        return tok

    def emit(self, block):
        prog = self

        def mk(ename):
            def body(e):
                for (waits, fn, s, inc) in prog.ops[ename]:
                    for (ws, wv) in waits:
                        e.wait_ge(ws, wv)
                    fn(e).then_inc(s, inc)
                if ename == "gpsimd":
                    last = {}
                    for (s2, sid, v) in prog.out_tokens:
                        if last.get(sid, (None, -1))[1] < v:
                            last[sid] = (s2, v)
                    for sid, (s2, v) in last.items():
                        e.wait_ge(s2, v)
            return body
        block.sync(mk("sync"))
        block.scalar(mk("scalar"))
        block.vector(mk("vector"))
        block.gpsimd(mk("gpsimd"))
        block.tensor(mk("tensor"))


def _win_cols():
    idx = list(range(0, 1664))
    idx += list(range(1664, 1824)) + [-1] * 96
    idx += list(range(1824, 5920))
    assert len(idx) == NBLK * 128
    return np.array(idx)


def _take_cols(a, idx, axis=-1):
    a = np.moveaxis(a, axis, -1)
    out = np.zeros(a.shape[:-1] + (len(idx),), a.dtype)
    m = idx >= 0
    out[..., m] = a[..., idx[m]]
    return np.moveaxis(out, -1, axis)


def _pk(v, nchunk):
    return np.ascontiguousarray(v.reshape(nchunk, 128).T)


def _consts(rank4):
    c = {}
    c["ident"] = np.eye(128, dtype=np.float32)
    ob = np.zeros((128, 128), np.float32)
    ob[:64, :64] = 1
    ob[64:, 64:] = 1
    c["ones_bd"] = ob
    c["ones_f"] = np.ones((128, 128), np.float32)
    isel = np.zeros((128, 64), np.float32)
    isel[np.arange(128), np.arange(128) % 64] = 1
    c["isel"] = isel
    rho = np.arange(128)
    blk, pos = rho // 32, rho % 32
    same = blk[:, None] == blk[None, :]
    c["m_strict"] = (same & (pos[:, None] < pos[None, :])).astype(np.float32)
    c["m_incl"] = (same & (pos[:, None] <= pos[None, :])).astype(np.float32)
    c["m_lower"] = (same & (pos[:, None] > pos[None, :])).astype(np.float32)
    s = np.arange(32)
    c["m_att"] = (s[:, None] <= s[None, :]).astype(np.float32)
    rst = np.ones((128, 4 * 128), np.float32)
    rst[:, ::32] = 0
    c["m_reset"] = rst
    fm = np.zeros((128, 4), np.float32)
    fm[:, :3] = (np.arange(3) < rank4).astype(np.float32)[None, :]
    c["foldm"] = fm
    hm = np.zeros((128, 4), np.float32)
    if rank4 > 0:
        hm[:, rank4 - 1] = 1
    c["halom"] = hm
    return c


class Cfg:
    def __init__(self, npt=2048, w=256, nlayer=2, dbg=(), stop=None):
        self.stop = stop
        self.NPT = npt
        self.W = w
        self.NS = 4
        self.NT = npt + 4 * L
        self.NL = nlayer
        self.dbg = tuple(dbg)


PV = dict(nmix=0, nffn=8, mu=16, w0=31, a0=35, v0=39, kk=43, ka=47, rk=51, hnw=55, lbz=59, nfin=63)
NPV = 71


def build(cfg):
    nc = bass.Bass("TRN2", target_bir_lowering=False)
    NPT, W, NT, NL = cfg.NPT, cfg.W, cfg.NT, cfg.NL
    NPS = NPT // W

    def din(name, shape, dt=F32):
        return nc.dram_tensor(name, list(shape), dt, kind="ExternalInput").ap()

    def dout(name, shape, dt=F32):
        return nc.dram_tensor(name, list(shape), dt, kind="ExternalOutput").ap()

    def dint(name, shape, dt=F32):
        return nc.dram_tensor(name, list(shape), dt, kind="Internal").ap()

    xT = din("xT", [128, KC, NT])
    st_shift = din("st_shift", [NL, 128, RW_BLK, 4])
    st_rwkv = din("st_rwkv", [NL, 4, 128, 4, 64])
    st_hgrn = din("st_hgrn", [NL, 4, 128, 4, 128])
    w_in_r = din("w_in_r", [NL, NBLK, 128, KC, 128])
    w_up_r = din("w_up_r", [NL, 16, 128, KC, 256])
    w_dn_r = din("w_dn_r", [NL, 16, 128, 16, 128])
    w_oab_r = din("w_oab_r", [NL, 8, 128, 8, 128])
    w_o_r = din("w_o_r", [NL, 8, 128, 8, 128])
    w2pad = din("w2pad", [NL, 128, 512])
    a2pad = din("a2pad", [NL, 128, 512])
    g2pad = din("g2pad", [NL, 128, 2, 512])
    vw1 = din("vw1", [128, 4, 32])
    vw2 = din("vw2", [32, 512])
    pvec = din("pvec", [NL, 128, NPV])
    lnw_st = din("lnw_st", [NL, 128, 2, 64])
    lnb_st = din("lnb_st", [NL, 128, 2, 64])
    cnames = ["ident", "ones_bd", "ones_f", "isel", "m_strict", "m_incl", "m_lower", "m_att", "m_reset", "foldm", "halom"]
    cshape = dict(ident=[128, 128], ones_bd=[128, 128], ones_f=[128, 128], isel=[128, 64], m_strict=[128, 128],
                  m_incl=[128, 128], m_lower=[128, 128], m_att=[32, 32], m_reset=[128, 512], foldm=[128, 4], halom=[128, 4])
    cdram = {n: din("c_" + n, cshape[n]) for n in cnames}

    yT = dout("yT", [128, KC, NT])
    o_shift_p = dout("o_shift_p", [NL, 128, RW_BLK])
    o_rwkv_p = dout("o_rwkv_p", [NL, 128, 4, 64])
    o_hgrn_p = dout("o_hgrn_p", [NL, 128, 4, 128])
    o_shift_s = dout("o_shift_s", [NL, 128, RW_BLK, 4])
    o_rwkv_s = dout("o_rwkv_s", [NL, 4, 128, 4, 64])
    o_hgrn_s = dout("o_hgrn_s", [NL, 4, 128, 4, 128])
    dbg_out = {n: dout("dbg_" + n, shp) for (n, shp) in cfg.dbg}

    xs1 = dint("xs1", [128, KC, NT])
    vfirst_d = dint("vfirst_d", [128, 4, NT])
    cin_h = dint("cin_h", [128, 16])
    cout_h = dint("cout_h", [512, 16])
    SW = 4 * 128 + 4 * 128 + 8
    cin_s = dint("cin_s", [128, SW])
    cout_s = dint("cout_s", [512, SW])
    wb_in = dint("wb_in", [NL, NBLK, 128, KC * 128], BF16)
    wb_up = dint("wb_up", [NL, 16, 128, KC * 256], BF16)
    wb_dn = dint("wb_dn", [NL, 16, 128, 16 * 128], BF16)
    x1s = dint("x1s", [128, KC, NT])
    wb_oab = dint("wb_oab", [NL, 8, 128, 8 * 128], BF16)
    wb_o = dint("wb_o", [NL, 8, 128, 8 * 128], BF16)

    with ExitStack() as st:
        p = Prog(nc, st)

        def sb(name, shape, dt=F32):
            return st.enter_context(nc.sbuf_tensor(name, list(shape), dt))

        def X(eng, method, reads, writes, **kw):
            return p.op(eng, lambda e: getattr(e, method)(**kw), reads, writes)

        def DMA(eng, dname, out, in_, reads, writes, is_output=False, **kw):
            return p.dma(eng, dname, lambda e: e.dma_start(out=out, in_=in_, **kw), reads, writes, is_output)

        _rr = [0]

        def EW():
            _rr[0] ^= 1
            return "vector" if _rr[0] else "gpsimd"

        ps = st.enter_context(nc.psum_tensor("ps", [128, 7 * 512], F32))
        psb = st.enter_context(nc.psum_tensor("psb", [128, 1024], BF16))

        def PS(bank, off, n):
            assert off + n <= 512
            keys = ["psbank%d" % bank]
            return ps[:, bank * 512 + off: bank * 512 + off + n], keys

        def PSB(off, n):
            keys = ["psbankB"]
            return psb[:, off:off + n], keys

        cst = {}
        for n in cnames:
            cst[n] = sb("k_" + n, cshape[n])
            DMA("sync", "cst", cst[n][:], cdram[n][:], [], ["c_" + n])
        ident_b = sb("ident_b", [128, 128], BF16)
        isel_b = sb("isel_b", [128, 64], BF16)
        X("vector", "tensor_copy", ["c_ident"], ["ident_b"], out=ident_b[:], in_=cst["ident"][:])
        X("vector", "tensor_copy", ["c_isel"], ["isel_b"], out=isel_b[:], in_=cst["isel"][:])
        eps_rms = sb("eps_rms", [128, 1])
        eps_gn = sb("eps_gn", [128, 1])
        X("vector", "memset", [], ["eps_rms"], ap=eps_rms[:], constant=RMS_EPS)
        X("vector", "memset", [], ["eps_gn"], ap=eps_gn[:], constant=GN_EPS)

        def wcls(b):
            return "a" if b < RW_BLK else ("b" if 19 <= b < 27 else "c")

        def wkey(l, b):
            return "wb_in%d%s" % (l, wcls(b))

        conv_q = {l: [] for l in range(NL)}

        def conv_layer(l):
            order = list(range(0, RW_BLK)) + list(range(19, 27)) + list(range(RW_BLK, 19)) + list(range(27, NBLK))
            mk = lambda *a: (lambda: DMA(*a))
            for b in order:
                conv_q[l].append(mk("gpsimd", "cv_in%d%s" % (l, wcls(b)), wb_in[l, b], w_in_r[l, b].rearrange("p k c -> p (k c)"), [], [wkey(l, b)]))
            for g in range(8):
                conv_q[l].append(mk("gpsimd", "cv_o%d" % l, wb_oab[l, g], w_oab_r[l, g].rearrange("p k c -> p (k c)"), [], ["wb_o%d" % l]))
                conv_q[l].append(mk("gpsimd", "cv_o%d" % l, wb_o[l, g], w_o_r[l, g].rearrange("p k c -> p (k c)"), [], ["wb_o%d" % l]))
            for g in range(16):
                conv_q[l].append(mk("gpsimd", "cv_f%d" % l, wb_up[l, g], w_up_r[l, g].rearrange("p k c -> p (k c)"), [], ["wb_f%d" % l]))
                conv_q[l].append(mk("gpsimd", "cv_f%d" % l, wb_dn[l, g], w_dn_r[l, g].rearrange("p k c -> p (k c)"), [], ["wb_f%d" % l]))

        def conv_pump(l, n):
            if l < NL:
                for _ in range(n):
                    if conv_q[l]:
                        conv_q[l].pop(0)()

        for l in range(NL):
            conv_layer(l)
        conv_pump(0, RW_BLK)

        pvs = sb("pvs", [128, NL, NPV])
        DMA("sync", "cstp", pvs[:], pvec.rearrange("l p n -> p l n"), [], ["pvs"])
        lnw = sb("lnw", [128, NL, 2, 64])
        lnb = sb("lnb", [128, NL, 2, 64])
        DMA("sync", "cstp", lnw[:], lnw_st.rearrange("l p g v -> p l g v"), [], ["lnw"])
        DMA("sync", "cstp", lnb[:], lnb_st.rearrange("l p g v -> p l g v"), [], ["lnb"])
        w2p_b = sb("w2p_b", [128, NL, 512], BF16)
        a2p_b = sb("a2p_b", [128, NL, 512], BF16)
        g2p_b = sb("g2p_b", [128, NL, 2, 512], BF16)
        vw1_b = sb("vw1_b", [128, 4, 32], BF16)
        vw2_b = sb("vw2_b", [32, 512], BF16)
        DMA("gpsimd", "cst2", w2p_b[:], w2pad.rearrange("l p n -> p l n"), [], ["w2p_b"])
        DMA("gpsimd", "cst2", a2p_b[:], a2pad.rearrange("l p n -> p l n"), [], ["a2p_b"])
        DMA("gpsimd", "cst2", g2p_b[:], g2pad.rearrange("l p j n -> p l j n"), [], ["g2p_b"])
        DMA("gpsimd", "cst2", vw1_b[:], vw1[:], [], ["vw1_b"])
        DMA("gpsimd", "cst2", vw2_b[:], vw2[:], [], ["vw2_b"])
        lbe = sb("lbe", [128, NL, 4])
        lbs = sb("lbs", [128, 4])
        lbv = sb("lbv", [128, NL, 4])
        oml = sb("oml", [128, NL, 4])
        X("scalar", "activation", ["pvs"], ["lbe"], out=lbe[:], in_=pvs[:, :, PV["lbz"]:PV["lbz"] + 4], func=AF.Exp)
        X("vector", "tensor_copy", ["lbe"], ["lbs"], out=lbs[:], in_=lbe[:, 0, :])
        for l in range(1, NL):
            X("vector", "tensor_tensor", ["lbe", "lbs"], ["lbs"], out=lbs[:], in0=lbs[:], in1=lbe[:, l, :], op=ALU.add)
        X("vector", "reciprocal", ["lbs"], ["lbs"], out=lbs[:], in_=lbs[:])
        for l in range(NL):
            X("vector", "tensor_tensor", ["lbe", "lbs"], ["lbe"], out=lbe[:, l, :], in0=lbe[:, l, :], in1=lbs[:], op=ALU.mult)
        X("vector", "tensor_tensor", ["lbe"], ["lbv"], out=lbv[:, 0, :], in0=lbe[:, 0, :], in1=lbe[:, 0, :], op=ALU.subtract)
        for l in range(1, NL):
            X("vector", "tensor_tensor", ["lbe", "lbv"], ["lbv"], out=lbv[:, l, :], in0=lbv[:, l - 1, :], in1=lbe[:, l, :], op=ALU.add)
        X("vector", "tensor_scalar", ["lbv"], ["oml"], out=oml[:], in0=lbv[:], scalar1=-1.0, scalar2=1.0, op0=ALU.mult, op1=ALU.add)

        def pcol(l, name, i=0, n=1):
            return pvs[:, l, PV[name] + i: PV[name] + i + n]

        def pbc(l, name, n, T):
            return pvs[:, l, PV[name]: PV[name] + n].unsqueeze(2).to_broadcast([128, n, T])

        WSL = 8
        wslot = [sb("wslot%d" % i, [128, 2, KC * 128], BF16) for i in range(WSL)]
        xt = sb("xt", [128, KC, W])
        x1 = sb("x1", [128, KC, W])
        hT = sb("hT", [128, KC, W], BF16)
        sqk_d = [sb("sqk%d" % i, [128, W]) for i in range(2)]
        rstd_d = sb("rstd", [128, W])
        RAWW = W + 4
        raw = sb("raw", [128, RW_BLK, RAWW])
        X("gpsimd", "memset", [], ["raw"], ap=raw[:], constant=0.0)
        hq = sb("hq", [128, 4, W])
        hsig = sb("hsig", [128, 4, W])
        hv_b = sb("hv_b", [128, 4, W], BF16)
        hog = sb("hog", [128, 4, W])
        big = sb("big", [128, max(16 * W, SW)])
        gates = big[:, 0:16 * W].rearrange("p (j w) -> p j w", w=W)
        yg_b = sb("yg_b", [128, 4, W], BF16)
        ob_b = sb("ob_b", [128, 4, W], BF16)
        mixin = sb("mixin", [128, KC, W], BF16)
        mtmp = sb("mtmp", [128, W])
        relu_t = [sb("relu%d" % i, [128, W]) for i in range(1)]
        _wsl = [0]

        def emit_norm(l, xbuf, xkey, N, pname, outbuf=None, outkey="hT", sqk=None, rstd=None, pfx=""):
            if outbuf is None:
                outbuf = hT
            if sqk is None:
                sqk, rstd = sqk_d, rstd_d
            nps, nkeys = PS(6, 0, N)
            for kc in range(KC):
                sq = sqk[kc % 2]
                X("scalar", "activation", [xkey], [pfx + "sqk%d" % (kc % 2)], out=sq[:, 0:N], in_=xbuf[:, kc, 0:N], func=AF.Square)
                X("tensor", "matmul", [pfx + "sqk%d" % (kc % 2), "c_ones_f"], nkeys, out=nps, lhsT=cst["ones_f"][:], rhs=sq[:, 0:N],
                  start=(kc == 0), stop=(kc == KC - 1))
            X("scalar", "activation", nkeys + ["eps_rms"], [pfx + "rstd"], out=rstd[:, 0:N], in_=nps, func=AF.Ln, scale=1.0 / D, bias=eps_rms[:])
            X("scalar", "activation", [pfx + "rstd"], [pfx + "rstd"], out=rstd[:, 0:N], in_=rstd[:, 0:N], func=AF.Exp, scale=-0.5)
            for kc in range(KC):
                X("vector", "scalar_tensor_tensor", [xkey, pfx + "rstd", "pvs"], [outkey], out=outbuf[:, kc, 0:N], in0=xbuf[:, kc, 0:N],
                  scalar=pcol(l, pname, kc), in1=rstd[:, 0:N], op0=ALU.mult, op1=ALU.mult)

        _psl = [0]

        def emit_proj(l, blocks, N, handler):
            groups = [blocks[i:i + 2] for i in range(0, len(blocks), 2)]
            for grp in groups:
                s = _wsl[0] % WSL
                _wsl[0] += 1
                if len(grp) == 2 and grp[1] == grp[0] + 1:
                    DMA("sync", "wsl%d" % s, wslot[s][:, 0:2, :], wb_in[l, grp[0]:grp[0] + 2].rearrange("j p n -> p j n"),
                        sorted(set([wkey(l, grp[0]), wkey(l, grp[1])])), ["wslot%d" % s])
                else:
                    for j, b in enumerate(grp):
                        DMA("sync", "wsl%d" % s, wslot[s][:, j, :], wb_in[l, b], [wkey(l, b)], ["wslot%d" % s])
                for j, b in enumerate(grp):
                    slot = _psl[0] % 4
                    _psl[0] += 1
                    pr, pk = PS(slot, 0, N)
                    for kc in range(KC):
                        X("tensor", "matmul", ["wslot%d" % s, "hT"], pk, out=pr, lhsT=wslot[s][:, j, kc * 128:(kc + 1) * 128],
                          rhs=hT[:, kc, 0:N], start=(kc == 0), stop=(kc == KC - 1))
                    handler(b, pr, pk)

        T = 128
        AW = 17920
        arena = sb("arena", [128, AW])
        _ao = [0]

        def aalloc(shape, dt=F32, reset=False):
            if reset:
                _ao[0] = 0
            n = int(np.prod(shape[1:]))
            n32 = n if dt == F32 else (n + 1) // 2
            a = _ao[0]
            _ao[0] += n32
            assert _ao[0] <= AW, ("arena overflow", _ao[0])
            v = arena[:, a:a + n32]
            if dt != F32:
                v = v.bitcast(dt)[:, 0:n]
            if len(shape) == 3:
                v = v.rearrange("p (a b) -> p a b", a=shape[1])
            elif len(shape) == 4:
                v = v.rearrange("p (a b c) -> p a b c", a=shape[1], b=shape[2])
            return v[0:shape[0]] if shape[0] < 128 else v

        def sbm(name, shape, dt=F32):
            return aalloc(list(shape), dt)
        mx = sbm("mx", [128, RW_BLK, T])
        lw_in = sbm("lw_in", [128, T], BF16)
        siggl = sbm("siggl", [128, 2, T], BF16)
        names4 = ["sg", "aa", "gfm", "vv", "kkn", "kh", "brec", "Gw", "tA", "tB", "Ex", "bv", "t1", "vf"]
        m4 = {n: sbm("m_" + n, [128, 4, T]) for n in names4}
        vb16 = sbm("vb16", [128, 4, T], BF16)
        t32b = sbm("t32b", [32, T], BF16)
        bdn = ["Kbd", "Bbd", "Abd", "Rbd", "KHbd", "BHbd", "Vbd"]
        bd = {n: sb(n, [128, 4, 4, 128], BF16) for n in bdn}
        for n in bdn:
            X("gpsimd", "memset", [], [n], ap=bd[n][:], constant=0.0)
        AR = sbm("AR", [128, 4, 4, 64], BF16)
        Bp = sbm("Bp", [128, 4, 4, 32], BF16)
        gL = sb("gL", [128, 4, 4])
        chn = ["X1", "X1T", "Mb", "Xa", "XaT", "Xb", "XbT"]
        chb = {n: [sbm("%s%d" % (n, g), [128, 128], BF16) for g in range(2)] for n in chn}
        chb2 = {n: [[sbm("%s_%d_%d" % (n, pp, g), [128, 128], BF16) for g in range(2)] for pp in range(2)] for n in ["Aka", "Akr", "Abr", "Ma"]}
        KT2 = [sbm("KT_%d" % pp, [128, 4, 128], BF16) for pp in range(2)]
        BT2 = [sbm("BT_%d" % pp, [128, 4, 128], BF16) for pp in range(2)]
        Vst2 = [sb("Vst_%d" % pp, [128, 2, 128], BF16) for pp in range(2)]
        for pp in range(2):
            X("gpsimd", "memset", [], ["Vst%d_0" % pp, "Vst%d_1" % pp], ap=Vst2[pp][:], constant=0.0)
        Wt = sbm("Wt", [128, 2, 128], BF16)
        Ut = sbm("Ut", [128, 2, 128], BF16)
        Sf = sb("Sf", [128, 4, 128])
        Sb = sb("Sb", [128, 4, 128], BF16)
        Yst = sbm("Yst", [128, 8, 64])
        Ysq = sbm("Ysq", [128, 8, 64])
        Yrep = sbm("Yrep", [128, 8, 2, 64])
        gst = {n: sb("gst_" + n, [128, 8]) for n in ["s1", "s2", "mean", "var"]}
        h4 = {"o": sbm("h_o", [128, 4, T])}
        for hn_, mn_ in {'f': 'sg', 'lf': 'aa', 'khh': 'kkn', 'G2': 'Gw', 'd1': 'tB', 'E2': 'Ex', 'hA': 'tA', 'o2': 'kh', 'rs': 'brec'}.items():
            h4[hn_] = m4[mn_]
        hb = {n: sbm("hb_" + n, [128, 4, T], BF16) for n in ["Qt", "Q2", "Kt", "Kh"]}
        dL = sb("dL", [128, 4, 4])
        att_b = sbm("att_b", [32, 4, 32], BF16)
        VTh = sbm("VTh", [32, 4, 128], BF16)
        KTh = sbm("KTh", [32, 4, 128], BF16)
        Shf = sb("Shf", [128, 4, 128])
        Shb = sb("Shb", [128, 4, 128], BF16)
        Dtot = sb("Dtot", [128, 4])

        def v4(ap):
            return ap.rearrange("p c (q t) -> p c q t", t=L)

        def bd_write(name, in0, in1, op, keys_r):
            for hh in range(2):
                for cc in range(2):
                    out = bd[name][hh * 64:(hh + 1) * 64, cc::2, :, 64 * cc + 32 * hh: 64 * cc + 32 * hh + 32]
                    a = v4(in0)[hh * 64:(hh + 1) * 64, cc::2]
                    if in1 is None:
                        X(EW(), "tensor_copy", keys_r, [name], out=out, in_=a)
                    else:
                        b = v4(in1)[hh * 64:(hh + 1) * 64, cc::2]
                        X(EW(), "tensor_tensor", keys_r, [name], out=out, in0=a, in1=b, op=op)

        def bc4(ap, shape):
            return ap.to_broadcast(shape)

        def rwkv_prep(l, phB, cur, prv, mxv, tok0, nvalid_blocks):
            b0, b1 = nvalid_blocks
            shp = list(cur.shape)
            mu = pvs[:, l, PV["mu"] + b0: PV["mu"] + b1]
            mu_bc = (mu.unsqueeze(2) if len(shp) == 3 else mu.unsqueeze(2).unsqueeze(3)).to_broadcast(shp)
            X("vector", "tensor_tensor", ["raw"], ["mx"], out=mxv, in0=prv, in1=cur, op=ALU.subtract)
            X("vector", "tensor_tensor", ["mx", "pvs"], ["mx"], out=mxv, in0=mxv, in1=mu_bc, op=ALU.mult)
            X("vector", "tensor_tensor", ["mx", "raw"], ["mx"], out=mxv, in0=mxv, in1=cur, op=ALU.add)
            r, k, v = mx[:, 0:4, :], mx[:, 4:8, :], mx[:, 8:12, :]
            X("scalar", "activation", ["mx"], ["lw_in"], out=lw_in[0:64, :], in_=mx[0:64, 12, :], func=AF.Tanh)
            X("scalar", "activation", ["mx"], ["lw_in"], out=lw_in[64:128, :], in_=mx[64:128, 12, :], func=AF.Copy)
            pw, kw = PS(4, 0, 512)
            pa, ka = PS(5, 0, 512)
            pg, kg = PS(6, 0, 512)
            for c in range(4):
                X("tensor", "matmul", ["lw_in", "w2p_b"], kw, out=pw[:, c * T:(c + 1) * T], lhsT=w2p_b[:, l, c * 128:(c + 1) * 128],
                  rhs=lw_in[:], start=True, stop=True)
                X("tensor", "matmul", ["lw_in", "a2p_b"], ka, out=pa[:, c * T:(c + 1) * T], lhsT=a2p_b[:, l, c * 128:(c + 1) * 128],
                  rhs=lw_in[:], start=True, stop=True)
            if phB:
                X("scalar", "activation", ["mx"], ["siggl"], out=siggl[:], in_=mx[:, 13:15, :], func=AF.Sigmoid)
                for c in range(4):
                    for j in range(2):
                        X("tensor", "matmul", ["siggl", "g2p_b"], kg, out=pg[:, c * T:(c + 1) * T],
                          lhsT=g2p_b[:, l, j, c * 128:(c + 1) * 128], rhs=siggl[:, j, :], start=(j == 0), stop=(j == 1))
            for c in range(4):
                X("scalar", "activation", kw + ["pvs"], ["sg"], out=m4["sg"][:, c, :], in_=pw[:, c * T:(c + 1) * T], func=AF.Sigmoid,
                  bias=pcol(l, "w0", c))
                X("scalar", "activation", ka + ["pvs"], ["aa"], out=m4["aa"][:, c, :], in_=pa[:, c * T:(c + 1) * T], func=AF.Sigmoid,
                  bias=pcol(l, "a0", c))
            if phB:
                X("scalar", "activation", kg, ["gfm"], out=m4["gfm"][:].rearrange("p c t -> p (c t)"), in_=pg, func=AF.Copy)
            FEED(2)
            vv = m4["vv"]
            if l == 0:
                X(EW(), "tensor_copy", ["mx"], ["vv"], out=vv[:], in_=v)
                if phB:
                    DMA("sync", "vf_st", vfirst_d[:, :, tok0:tok0 + T], vv[:], ["vv"], ["vfirst_d"])
            else:
                DMA("sync", "vf_ld", m4["vf"][:], vfirst_d[:, :, tok0:tok0 + T], ["vfirst_d"], ["vf"])
                X("gpsimd", "tensor_copy", ["mx"], ["vb16"], out=vb16[:], in_=v)
                p32, k32 = PS(4, 0, T)
                for c in range(4):
                    X("tensor", "matmul", ["vb16", "vw1_b"], k32, out=p32[0:32, :], lhsT=vw1_b[:, c, :], rhs=vb16[:, c, :],
                      start=(c == 0), stop=(c == 3))
                X("scalar", "activation", k32, ["t32b"], out=t32b[:], in_=p32[0:32, :], func=AF.Copy)
                pv_, kv_ = PS(5, 0, 512)
                for c in range(4):
                    X("tensor", "matmul", ["t32b", "vw2_b"], kv_, out=pv_[:, c * T:(c + 1) * T], lhsT=vw2_b[:, c * 128:(c + 1) * 128],
                      rhs=t32b[:], start=True, stop=True)
                for c in range(4):
                    X("scalar", "activation", kv_ + ["pvs"], ["tA"], out=m4["tA"][:, c, :], in_=pv_[:, c * T:(c + 1) * T],
                      func=AF.Sigmoid, bias=pcol(0, "v0", c))
                X("vector", "tensor_tensor", ["vf", "mx"], ["tB"], out=m4["tB"][:], in0=m4["vf"][:], in1=v, op=ALU.subtract)
                X("vector", "tensor_tensor", ["tB", "tA"], ["tB"], out=m4["tB"][:], in0=m4["tB"][:], in1=m4["tA"][:], op=ALU.mult)
                X("vector", "tensor_tensor", ["tB", "mx"], ["vv"], out=vv[:], in0=m4["tB"][:], in1=v, op=ALU.add)
            kkn, kh, brec, Gw, tA, tB, Ex = (m4[n] for n in ["kkn", "kh", "brec", "Gw", "tA", "tB", "Ex"])
            X("vector", "tensor_tensor", ["mx", "pvs"], ["kkn"], out=kkn[:], in0=k, in1=pbc(l, "kk", 4, T), op=ALU.mult)
            X("gpsimd", "tensor_tensor", ["kkn"], ["tA"], out=tA[:], in0=kkn[:], in1=kkn[:], op=ALU.mult)
            pss, kss = PS(6, 0, 512)
            for c in range(4):
                X("tensor", "matmul", ["tA", "c_ones_bd"], kss, out=pss[:, c * T:(c + 1) * T], lhsT=cst["ones_bd"][:], rhs=tA[:, c, :],
                  start=True, stop=True)
            tAf = tA[:].rearrange("p c t -> p (c t)")
            X("vector", "tensor_scalar", kss, ["tA"], out=tAf, in0=pss, scalar1=1e-24, scalar2=None, op0=ALU.max)
            X("scalar", "activation", ["tA"], ["tA"], out=tAf, in_=tAf, func=AF.Ln)
            X("scalar", "activation", ["tA"], ["tA"], out=tAf, in_=tAf, func=AF.Exp, scale=-0.5)
            X("vector", "tensor_tensor", ["kkn", "tA"], ["kkn"], out=kkn[:], in0=kkn[:], in1=tA[:], op=ALU.mult)
            FEED(2)
            X("vector", "scalar_tensor_tensor", ["aa", "pvs"], ["tB"], out=tB[:], in0=m4["aa"][:], scalar=-1.0, in1=pbc(l, "ka", 4, T),
              op0=ALU.add, op1=ALU.mult)
            X("vector", "scalar_tensor_tensor", ["tB", "mx"], ["kh"], out=kh[:], in0=tB[:], scalar=1.0, in1=k, op0=ALU.add, op1=ALU.mult)
            X("gpsimd", "tensor_tensor", ["kkn", "aa"], ["brec"], out=brec[:], in0=kkn[:], in1=m4["aa"][:], op=ALU.mult)
            sgf = m4["sg"][:].rearrange("p c t -> p (c t)")
            Gwf = Gw[:].rearrange("p c t -> p (c t)")
            X("scalar", "mul", ["sg"], ["sg"], out=sgf, in_=sgf, mul=C0)
            X("vector", "tensor_tensor_scan", ["sg", "c_m_reset"], ["Gw"], out=Gwf, data0=cst["m_reset"][:], data1=sgf, initial=0.0,
              op0=ALU.mult, op1=ALU.add)
            X("gpsimd", "tensor_tensor", ["Gw", "sg"], ["tB"], out=tB[:], in0=Gw[:], in1=m4["sg"][:], op=ALU.subtract)
            Exf = Ex[:].rearrange("p c t -> p (c t)")
            X("scalar", "activation", ["tB"], ["Ex"], out=Exf, in_=tB[:].rearrange("p c t -> p (c t)"), func=AF.Exp)
            X("vector", "scalar_tensor_tensor", ["kkn", "Ex"], ["tA"], out=tA[:], in0=kkn[:], scalar=-1.0, in1=Ex[:],
              op0=ALU.mult, op1=ALU.mult)
            bd_write("Abd", tA[:], None, None, ["tA"])
            X(EW(), "tensor_copy", ["tA"], ["AR"], out=AR[:, :, :, 0:32], in_=v4(tA[:]))
            if phB:
                X("scalar", "activation", ["Gw"], ["Ex"], out=Exf, in_=Gwf, func=AF.Exp)
                bd_write("Rbd", r, Ex[:], ALU.mult, ["mx", "Ex"])
                X(EW(), "tensor_tensor", ["mx", "Ex"], ["AR"], out=AR[:, :, :, 32:64], in0=v4(r), in1=v4(Ex[:]), op=ALU.mult)
            FEED(2)
            X("scalar", "activation", ["Gw"], ["Ex"], out=Exf, in_=Gwf, func=AF.Exp, scale=-1.0)
            bd_write("Kbd", kh[:], Ex[:], ALU.mult, ["kh", "Ex"])
            bd_write("Bbd", brec[:], Ex[:], ALU.mult, ["brec", "Ex"])
            X(EW(), "tensor_tensor", ["brec", "Ex"], ["Bp"], out=Bp[:], in0=v4(brec[:]), in1=v4(Ex[:]), op=ALU.mult)
            FEED(2)
            GL = v4(Gw[:])[:, :, :, L - 1:L]
            X("scalar", "activation", ["Gw"], ["gL"], out=gL[:].unsqueeze(3), in_=GL, func=AF.Exp)
            X("vector", "tensor_tensor", ["Gw"], ["tB"], out=v4(tB[:]), in0=GL.to_broadcast([128, 4, 4, L]), in1=v4(Gw[:]), op=ALU.subtract)
            X("scalar", "activation", ["tB"], ["Ex"], out=Exf, in_=tB[:].rearrange("p c t -> p (c t)"), func=AF.Exp)
            bd_write("KHbd", kh[:], Ex[:], ALU.mult, ["kh", "Ex"])
            bd_write("BHbd", brec[:], Ex[:], ALU.mult, ["brec", "Ex"])
            bd_write("Vbd", vv[:], None, None, ["vv"])
            if phB:
                X("vector", "tensor_tensor", ["mx", "kh"], ["tA"], out=tA[:], in0=r, in1=kh[:], op=ALU.mult)
                X("gpsimd", "tensor_tensor", ["tA", "pvs"], ["tA"], out=tA[:], in0=tA[:], in1=pbc(l, "rk", 4, T), op=ALU.mult)
                pbn, kbn = PS(4, 0, 512)
                for c in range(4):
                    X("tensor", "matmul", ["tA", "c_ones_bd"], kbn, out=pbn[:, c * T:(c + 1) * T], lhsT=cst["ones_bd"][:], rhs=tA[:, c, :],
                      start=True, stop=True)
                X("vector", "tensor_tensor", kbn + ["vv"], ["bv"], out=m4["bv"][:].rearrange("p c t -> p (c t)"), in0=pbn,
                  in1=vv[:].rearrange("p c t -> p (c t)"), op=ALU.mult)

        QS = [(4, 256), (5, 0), (6, 0)]
        _qs = [0]

        def QSLOT():
            b, o = QS[_qs[0] % 3]
            _qs[0] += 1
            return PS(b, o, 128)

        def mm_evac_copy(lhs, lk, rhs, rk, dst, dk, eng):
            pr, pk = QSLOT()
            X("tensor", "matmul", [lk, rk], pk, out=pr, lhsT=lhs, rhs=rhs, start=True, stop=True)
            if eng == "scalar":
                X("scalar", "activation", pk, [dk], out=dst, in_=pr, func=AF.Copy)
            else:
                X("vector", "tensor_copy", pk, [dk], out=dst, in_=pr)

        def mm_evac_add(lhs, lk, rhs, rk, addend, ak, dst, dk):
            pr, pk = QSLOT()
            X("tensor", "matmul", [lk, rk], pk, out=pr, lhsT=lhs, rhs=rhs, start=True, stop=True)
            X("vector", "tensor_tensor", pk + [ak], [dk], out=dst, in0=pr, in1=addend, op=ALU.add)

        def rwkv_steps(l, phB, q, par):
            NV = 64 if phB else 128
            ncol = 64 if phB else 32

            def kn(n, g):
                return "%s%d" % (n, g)

            def kp(n, g):
                return "%s%d_%d" % (n, par, g)

            def CB(n, g):
                return chb2[n][par][g]

            p1s = [PS(4, 0, 160), PS(5, 0, 160)]

            def st_stage1(g):
                p1, k1 = p1s[g]
                for (lf, rf, rk_, c0, cw) in (("Kbd", AR, "AR", 0, ncol), ("Bbd", AR, "AR", 64, ncol), ("Abd", Bp, "Bp", 128, 32)):
                    for cc in range(2):
                        c = 2 * g + cc
                        rhs = rf[:, c, q, 0:cw] if rk_ == "AR" else rf[:, c, q, :]
                        X("tensor", "matmul", [lf, rk_], k1, out=p1[:, c0:c0 + cw], lhsT=bd[lf][:, c, q, :], rhs=rhs,
                          start=(cc == 0), stop=(cc == 1))

            def st_evac1(g):
                p1, k1 = p1s[g]

                def mask_evac(dst, dkey, col, mname, eng):
                    X(eng, "tensor_tensor", k1 + ["c_" + mname], [dkey], out=dst[:].rearrange("p (b t) -> p b t", t=32),
                      in0=p1[:, col:col + 32].unsqueeze(1).to_broadcast([128, 4, 32]),
                      in1=cst[mname][:].rearrange("p (b t) -> p b t", t=32), op=ALU.mult)
                mask_evac(chb["X1"][g], kn("X1", g), 64, "m_strict", "vector")
                mask_evac(chb["X1T"][g], kn("X1T", g), 128, "m_lower", "vector")
                mask_evac(CB("Aka", g), kp("Aka", g), 0, "m_strict", "vector")
                if phB:
                    mask_evac(CB("Akr", g), kp("Akr", g), 32, "m_incl", "vector")
                    mask_evac(CB("Abr", g), kp("Abr", g), 96, "m_incl", "vector")
                X("gpsimd", "tensor_tensor", [kn("X1", g), "ident_b"], [kp("Ma", g)], out=CB("Ma", g)[:], in0=chb["X1"][g][:], in1=ident_b[:], op=ALU.add)

            def B(n, g):
                if n == "Ma":
                    return CB("Ma", g)[:], kp("Ma", g)
                return chb[n][g][:], kn(n, g)

            def inv_steps(g):
                cp = lambda lh, rh, ds, eng: (lambda: mm_evac_copy(B(lh, g)[0], B(lh, g)[1], B(rh, g)[0], B(rh, g)[1], B(ds, g)[0], B(ds, g)[1], eng))
                ad = lambda lh, rh, ds: (lambda: mm_evac_add(B(lh, g)[0], B(lh, g)[1], B(rh, g)[0], B(rh, g)[1], B(rh, g)[0], B(rh, g)[1], B(ds, g)[0], B(ds, g)[1]))
                return [cp("X1T", "X1", "Xa", "scalar"), cp("X1", "X1T", "XaT", "vector"), ad("XaT", "Ma", "Mb"),
                        cp("XaT", "Xa", "Xb", "scalar"), cp("Xa", "XaT", "XbT", "vector"), ad("XbT", "Mb", "Ma"),
                        cp("XbT", "Xb", "Xa", "scalar"), cp("Xb", "XbT", "XaT", "vector"), ad("XaT", "Ma", "Mb"),
                        cp("Xa", "XaT", "XbT", "vector"), ad("XbT", "Mb", "Ma")]

            KTp, BTp, Vstp = KT2[par], BT2[par], Vst2[par]

            def st_tokmajor(g):
                for cc in range(2):
                    c = 2 * g + cc
                    pk_, kk_ = PSB(c * 128, 128)
                    X("tensor", "transpose", ["KHbd", "ident_b"], kk_, out=pk_, in_=bd["KHbd"][:, c, q, :], identity=ident_b[:])
                    pb_, kb_ = PSB(512 + c * 128, 128)
                    X("tensor", "transpose", ["BHbd", "ident_b"], kb_, out=pb_, in_=bd["BHbd"][:, c, q, :], identity=ident_b[:])
                pv_, kv_ = PS(6, 256 + 64 * g, 64)
                for cc in range(2):
                    c = 2 * g + cc
                    X("tensor", "matmul", ["Vbd", "isel_b"], kv_, out=pv_, lhsT=bd["Vbd"][:, c, q, :], rhs=isel_b[:], start=(cc == 0), stop=(cc == 1))
                pk2, kk2 = PSB(2 * g * 128, 256)
                X("scalar", "activation", kk2, ["KT%d_%d" % (par, 2 * g), "KT%d_%d" % (par, 2 * g + 1)],
                  out=KTp[:, 2 * g:2 * g + 2, :].rearrange("p c k -> p (c k)"), in_=pk2, func=AF.Copy)
                pb2, kb2 = PSB(512 + 2 * g * 128, 256)
                X("vector", "tensor_copy", kb2, ["BT%d_%d" % (par, 2 * g), "BT%d_%d" % (par, 2 * g + 1)],
                  out=BTp[:, 2 * g:2 * g + 2, :].rearrange("p c k -> p (c k)"), in_=pb2)
                X("scalar", "activation", kv_, ["Vst%d_%d" % (par, g)], out=Vstp[:, g, 0:64], in_=pv_, func=AF.Copy)

            def st_W(g):
                pW, kW = PS(0 + g, 0, NV)
                X("tensor", "matmul", [kp("Aka", g), "Vst%d_%d" % (par, g)], kW, out=pW, lhsT=CB("Aka", g)[:], rhs=Vstp[:, g, 0:NV], start=True, stop=False)
                for cc in range(2):
                    c = 2 * g + cc
                    X("tensor", "matmul", ["Abd", "Sb%d" % c], kW, out=pW, lhsT=bd["Abd"][:, c, q, :], rhs=Sb[:, c, 0:NV], start=False, stop=(cc == 1))
                if g == 0:
                    X("scalar", "activation", kW, ["Wt%d" % g], out=Wt[:, g, 0:NV], in_=pW, func=AF.Copy)
                else:
                    X("vector", "tensor_copy", kW, ["Wt%d" % g], out=Wt[:, g, 0:NV], in_=pW)

            def st_U(g):
                pU, kU = PS(2 + g, 0, NV)
                X("tensor", "matmul", [kp("Ma", g), "Wt%d" % g], kU, out=pU, lhsT=CB("Ma", g)[:], rhs=Wt[:, g, 0:NV], start=True, stop=True)
                if g == 0:
                    X("vector", "tensor_copy", kU, ["Ut%d" % g], out=Ut[:, g, 0:NV], in_=pU)
                else:
                    X("scalar", "activation", kU, ["Ut%d" % g], out=Ut[:, g, 0:NV], in_=pU, func=AF.Copy)

            def st_Y(g):
                pY, kY = PS(0 + g, 256, 64)
                X("tensor", "matmul", [kp("Akr", g), "Vst%d_%d" % (par, g)], kY, out=pY, lhsT=CB("Akr", g)[:], rhs=Vstp[:, g, 0:64], start=True, stop=False)
                X("tensor", "matmul", [kp("Abr", g), "Ut%d" % g], kY, out=pY, lhsT=CB("Abr", g)[:], rhs=Ut[:, g, 0:64], start=False, stop=False)
                for cc in range(2):
                    c = 2 * g + cc
                    X("tensor", "matmul", ["Rbd", "Sb%d" % c], kY, out=pY, lhsT=bd["Rbd"][:, c, q, :], rhs=Sb[:, c, 0:64], start=False, stop=(cc == 1))
                X("scalar", "activation", kY, ["Yst"], out=Yst[:, g * 4 + q, :], in_=pY, func=AF.Copy)

            def st_S(c):
                g = c // 2
                pS, kS = PS(2 + (c % 2), 256 * (c // 2), NV)
                X("tensor", "matmul", ["KT%d_%d" % (par, c), "Vst%d_%d" % (par, g)], kS, out=pS, lhsT=KTp[:, c, :], rhs=Vstp[:, g, 0:NV], start=True, stop=False)
                X("tensor", "matmul", ["BT%d_%d" % (par, c), "Ut%d" % g], kS, out=pS, lhsT=BTp[:, c, :], rhs=Ut[:, g, 0:NV], start=False, stop=True)
                X("vector", "scalar_tensor_tensor", kS + ["Sf%d" % c, "gL"], ["Sf%d" % c], out=Sf[:, c, 0:NV], in0=Sf[:, c, 0:NV],
                  scalar=gL[:, c, q:q + 1], in1=pS, op0=ALU.mult, op1=ALU.add)
                X("scalar", "activation", ["Sf%d" % c], ["Sb%d" % c], out=Sb[:, c, 0:NV], in_=Sf[:, c, 0:NV], func=AF.Copy)

            mk = lambda f, a: (lambda: f(a))
            pre = [mk(st_stage1, 0), mk(st_stage1, 1), mk(st_evac1, 0), mk(st_evac1, 1), mk(st_tokmajor, 0), mk(st_tokmajor, 1)]
            i0, i1 = inv_steps(0), inv_steps(1)
            for a_, b_ in zip(i0, i1):
                pre += [a_, b_]
            chain = [mk(st_W, 0), mk(st_W, 1), mk(st_U, 0), mk(st_U, 1)]
            if phB:
                chain += [mk(st_Y, 0), mk(st_Y, 1)]
            chain += [mk(st_S, c) for c in range(4)]
            return pre, chain

        def mixer_chunks(l, phB, col0, before_chunk, after_chunk):
            pre0, _ = rwkv_steps(l, phB, 0, 0)
            for f_ in pre0:
                f_()
            for q in range(4):
                par = q % 2
                before_chunk(q)
                _, chain = rwkv_steps(l, phB, q, par)
                nxt = rwkv_steps(l, phB, q + 1, 1 - par)[0] if q < 3 else []
                hg = hgrn_chunk_parts(l, phB, q, col0)
                hg[0]()
                per = -(-len(nxt) // len(chain)) if nxt else 0
                for ci, cstep in enumerate(chain):
                    cstep()
                    if ci % 2 == 1:
                        FEED(1)
                    for _ in range(per):
                        if nxt:
                            nxt.pop(0)()
                    if ci == 1:
                        hg[1]()
                    if ci == 3:
                        hg[2]()
                while nxt:
                    nxt.pop(0)()
                after_chunk(q)

        def rwkv_post(l, col0):
            s1, s2, mean, var = (gst[n] for n in ["s1", "s2", "mean", "var"])
            X("vector", "tensor_reduce", ["Yst"], ["g_s1"], out=s1[:], in_=Yst[:], axis=AX.X, op=ALU.add)
            X("gpsimd", "tensor_tensor", ["Yst"], ["Ysq"], out=Ysq[:], in0=Yst[:], in1=Yst[:], op=ALU.mult)
            X("vector", "tensor_reduce", ["Ysq"], ["g_s2"], out=s2[:], in_=Ysq[:], axis=AX.X, op=ALU.add)
            X("vector", "tensor_scalar", ["g_s1"], ["g_mean"], out=mean[:], in0=s1[:], scalar1=1.0 / 64, scalar2=None, op0=ALU.mult)
            X("vector", "tensor_tensor", ["g_mean"], ["g_s1"], out=s1[:], in0=mean[:], in1=mean[:], op=ALU.mult)
            X("vector", "scalar_tensor_tensor", ["g_s2", "g_s1"], ["g_var"], out=var[:], in0=s2[:], scalar=1.0 / 64, in1=s1[:],
              op0=ALU.mult, op1=ALU.subtract)
            X("scalar", "activation", ["g_var", "eps_gn"], ["g_var"], out=var[:], in_=var[:], func=AF.Ln, bias=eps_gn[:])
            X("scalar", "activation", ["g_var"], ["g_var"], out=var[:], in_=var[:], func=AF.Exp, scale=-0.5)
            X("vector", "tensor_tensor", ["Yst", "g_mean"], ["Ysq"], out=Ysq[:], in0=Yst[:], in1=mean[:].unsqueeze(2).to_broadcast([128, 8, 64]),
              op=ALU.subtract)
            X("vector", "tensor_tensor", ["Ysq", "g_var"], ["Ysq"], out=Ysq[:], in0=Ysq[:], in1=var[:].unsqueeze(2).to_broadcast([128, 8, 64]),
              op=ALU.mult)
            for g in range(2):
                ys = Ysq[:, g * 4:(g + 1) * 4, :]
                X(EW(), "tensor_tensor", ["Ysq", "lnw"], ["Ysq"], out=ys, in0=ys, in1=lnw[:, l, g, :].unsqueeze(1).to_broadcast([128, 4, 64]),
                  op=ALU.mult)
                X(EW(), "tensor_tensor", ["Ysq", "lnb"], ["Yrep"], out=Yrep[:, g * 4:(g + 1) * 4, :, :],
                  in0=ys.unsqueeze(2).to_broadcast([128, 4, 2, 64]),
                  in1=lnb[:, l, g, :].unsqueeze(1).unsqueeze(1).to_broadcast([128, 4, 2, 64]), op=ALU.add)
            t1 = m4["t1"]
            for g in range(2):
                for q in range(4):
                    pT, kT = QSLOT()
                    X("tensor", "transpose", ["Yrep", "c_ident"], kT, out=pT, in_=Yrep[:, g * 4 + q, :, :].rearrange("p r v -> p (r v)"),
                      identity=cst["ident"][:])
                    for hh in range(2):
                        X("vector", "tensor_tensor", kT + ["bv"], ["t1"], out=t1[hh * 64:(hh + 1) * 64, 2 * g:2 * g + 2, q * L:(q + 1) * L],
                          in0=pT[hh * 64:(hh + 1) * 64, :].rearrange("p (c h t) -> p c h t", c=2, h=2)[:, :, hh, :],
                          in1=m4["bv"][hh * 64:(hh + 1) * 64, 2 * g:2 * g + 2, q * L:(q + 1) * L], op=ALU.add)
            X("gpsimd", "tensor_tensor", ["t1", "gfm"], ["yg_b"], out=yg_b[:, :, col0:col0 + T], in0=t1[:], in1=m4["gfm"][:], op=ALU.mult)

        def hgrn_prep(l, phB, col0):
            f, lf, khh, G2, d1, E2, hA = (h4[n] for n in ["f", "lf", "khh", "G2", "d1", "E2", "hA"])
            fl = lambda t: t[:].rearrange("p c t -> p (c t)")
            sig = hsig[:, :, col0:col0 + T]
            X("vector", "tensor_tensor", ["hsig", "oml"], ["sg"], out=f[:], in0=sig, in1=oml[:, l, :].unsqueeze(2).to_broadcast([128, 4, T]),
              op=ALU.mult)
            X("vector", "tensor_tensor", ["sg", "lbv"], ["sg"], out=f[:], in0=f[:], in1=lbv[:, l, :].unsqueeze(2).to_broadcast([128, 4, T]),
              op=ALU.add)
            X("scalar", "activation", ["sg"], ["aa"], out=fl(lf), in_=fl(f), func=AF.Ln)
            X("gpsimd", "tensor_scalar", ["sg"], ["kkn"], out=fl(khh), in0=fl(f), scalar1=-1.0, scalar2=1.0, op0=ALU.mult, op1=ALU.add)
            X("vector", "tensor_tensor_scan", ["aa", "c_m_reset"], ["Gw"], out=fl(G2), data0=cst["m_reset"][:], data1=fl(lf), initial=0.0,
              op0=ALU.mult, op1=ALU.add)
            GLv = v4(G2[:])[:, :, :, L - 1:L]
            X("scalar", "activation", ["Gw"], ["dL"], out=dL[:].unsqueeze(3), in_=GLv, func=AF.Exp)
            if phB:
                hqv = hq[:, :, col0:col0 + T]
                Gm = v4(G2[:])[:, :, :, L // 2 - 1:L // 2]
                X("vector", "tensor_tensor", ["Gw"], ["tB"], out=v4(d1[:]), in0=v4(G2[:]), in1=Gm.to_broadcast([128, 4, 4, L]), op=ALU.subtract)
                X("scalar", "activation", ["tB"], ["tA"], out=fl(hA), in_=fl(d1), func=AF.Exp)
                X("vector", "tensor_tensor", ["hq", "tA"], ["hb_Qt"], out=hb["Qt"][:], in0=hqv, in1=hA[:], op=ALU.mult)
                X("scalar", "activation", ["tB"], ["tA"], out=fl(hA), in_=fl(d1), func=AF.Exp, scale=-1.0)
                X("gpsimd", "tensor_tensor", ["kkn", "tA"], ["hb_Kt"], out=hb["Kt"][:], in0=khh[:], in1=hA[:], op=ALU.mult)
                X("scalar", "activation", ["Gw"], ["Ex"], out=fl(E2), in_=fl(G2), func=AF.Exp)
                X("vector", "tensor_tensor", ["hq", "Ex"], ["hb_Q2"], out=hb["Q2"][:], in0=hqv, in1=E2[:], op=ALU.mult)
            X("vector", "tensor_tensor", ["Gw"], ["tB"], out=v4(d1[:]), in0=GLv.to_broadcast([128, 4, 4, L]), in1=v4(G2[:]), op=ALU.subtract)
            X("scalar", "activation", ["tB"], ["tA"], out=fl(hA), in_=fl(d1), func=AF.Exp)
            X("gpsimd", "tensor_tensor", ["kkn", "tA"], ["hb_Kh"], out=hb["Kh"][:], in0=khh[:], in1=hA[:], op=ALU.mult)

        def hgrn_chunk_parts(l, phB, q, col0):
            cs = slice(q * L, (q + 1) * L)

            def part_pre():
                if phB:
                    pat, kat = PS(6, 384, 128)
                    for c in range(4):
                        X("tensor", "matmul", ["hb_Kt", "hb_Qt"], kat, out=pat[0:32, c * 32:(c + 1) * 32], lhsT=hb["Kt"][:, c, cs], rhs=hb["Qt"][:, c, cs],
                          start=True, stop=True)
                    X("vector", "tensor_tensor", kat + ["c_m_att"], ["att_b"], out=att_b[:], in0=pat[0:32, :].rearrange("p (c t) -> p c t", c=4),
                      in1=cst["m_att"][:].unsqueeze(1).to_broadcast([32, 4, 32]), op=ALU.mult)
                pvt, kvt = PSB(0, 512)
                pkt, kkt = PSB(512, 512)
                for c in range(4):
                    X("tensor", "transpose", ["hv_b", "ident_b"], kvt, out=pvt[0:32, c * 128:(c + 1) * 128],
                      in_=hv_b[:, c, col0 + q * L: col0 + (q + 1) * L], identity=ident_b[:])
                    X("tensor", "transpose", ["hb_Kh", "ident_b"], kkt, out=pkt[0:32, c * 128:(c + 1) * 128], in_=hb["Kh"][:, c, cs], identity=ident_b[:])
                X("scalar", "activation", kvt, ["VTh"], out=VTh[:].rearrange("p c v -> p (c v)"), in_=pvt[0:32, :], func=AF.Copy)
                X("vector", "tensor_copy", kkt, ["KTh"], out=KTh[:].rearrange("p c v -> p (c v)"), in_=pkt[0:32, :])

            def part_o():
                if phB:
                    po, ko = PS(4, 384, 128)
                    for c in range(4):
                        X("tensor", "matmul", ["Shb", "hb_Q2"], ko, out=po[:, c * 32:(c + 1) * 32], lhsT=Shb[:, c, :], rhs=hb["Q2"][:, c, cs], start=True, stop=False)
                        X("tensor", "matmul", ["VTh", "att_b"], ko, out=po[:, c * 32:(c + 1) * 32], lhsT=VTh[:, c, :], rhs=att_b[:, c, :], start=False, stop=True)
                    X("scalar", "activation", ko, ["h_o"], out=h4["o"][:, :, cs], in_=po.rearrange("p (c t) -> p c t", c=4), func=AF.Copy)

            def part_s():
                pss_, kss_ = PS(1, 0, 512)
                for c in range(4):
                    X("tensor", "matmul", ["KTh", "VTh"], kss_, out=pss_[:, c * 128:(c + 1) * 128], lhsT=KTh[:, c, :], rhs=VTh[:, c, :], start=True, stop=True)
                for c in range(4):
                    X("vector", "scalar_tensor_tensor", kss_ + ["Shf", "dL"], ["Shf"], out=Shf[:, c, :], in0=Shf[:, c, :], scalar=dL[:, c, q:q + 1],
                      in1=pss_[:, c * 128:(c + 1) * 128], op0=ALU.mult, op1=ALU.add)
                X("scalar", "activation", ["Shf"], ["Shb"], out=Shb[:].rearrange("p c v -> p (c v)"), in_=Shf[:].rearrange("p c v -> p (c v)"), func=AF.Copy)
                if not phB:
                    X("gpsimd", "tensor_tensor", ["Dtot", "dL"], ["Dtot"], out=Dtot[:], in0=Dtot[:], in1=dL[:, :, q], op=ALU.mult)
            return [part_pre, part_o, part_s]

        def hgrn_post(l, col0):
            o, o2, rs, hA = (h4[n] for n in ["o", "o2", "rs", "hA"])
            fl = lambda t: t[:].rearrange("p c t -> p (c t)")
            X("gpsimd", "tensor_tensor", ["h_o"], ["kh"], out=o2[:], in0=o[:], in1=o[:], op=ALU.mult)
            pn, kn_ = PS(4, 0, 512)
            for c in range(4):
                X("tensor", "matmul", ["kh", "c_ones_f"], kn_, out=pn[:, c * T:(c + 1) * T], lhsT=cst["ones_f"][:], rhs=o2[:, c, :], start=True, stop=True)
            X("scalar", "activation", kn_ + ["eps_rms"], ["brec"], out=fl(rs), in_=pn, func=AF.Ln, scale=1.0 / 128, bias=eps_rms[:])
            X("scalar", "activation", ["brec"], ["brec"], out=fl(rs), in_=fl(rs), func=AF.Exp, scale=-0.5)
            X("vector", "tensor_tensor", ["h_o", "brec"], ["h_o"], out=o[:], in0=o[:], in1=rs[:], op=ALU.mult)
            X("gpsimd", "tensor_tensor", ["hog", "pvs"], ["tA"], out=hA[:], in0=hog[:, :, col0:col0 + T], in1=pbc(l, "hnw", 4, T), op=ALU.mult)
            X("vector", "tensor_tensor", ["h_o", "tA"], ["ob_b"], out=ob_b[:, :, col0:col0 + T], in0=o[:], in1=hA[:], op=ALU.mult)

        shst = sb("shst", [128, RW_BLK, 4])
        shout = sb("shout", [128, RW_BLK, 4])
        shoutp = sb("shoutp", [128, RW_BLK])
        halo_prev = sb("halo_prev", [128, RW_BLK])
        hraw = sb("hraw", [128, 16])
        hall = sb("hall", [128, 4, 16])
        exb = sb("exb", [128, SW])
        exall = big[:, 0:SW]
        Xr = sb("Xr", [128, 4, 64])
        Xh = sb("Xh", [128, 4, 128])
        PTbd = sb("PTbd", [128, 128])
        lhsTf = sb("lhsTf", [128, 128])
        ftmp = sb("ftmp", [128, 128])
        X("vector", "memset", [], ["PTbd"], ap=PTbd[:], constant=0.0)
        X("vector", "memset", [], ["hraw"], ap=hraw[:], constant=0.0)
        groups4 = [[0, 1, 2, 3], [4, 5, 6, 7]]

        def make_handler(is_s, N):
            def handler(b, pr, pk):
                if b < 15:
                    if is_s:
                        dst = raw[:, b, 0:132].rearrange("p (s t) -> p s t", t=33)[:, :, 1:33]
                        src = pr.rearrange("p (s t) -> p s t", t=32)
                    else:
                        dst, src = raw[:, b, 1:N + 1], pr
                    X("scalar", "activation", pk, ["raw"], out=dst, in_=src, func=AF.Copy)
                elif b < 19:
                    X("scalar", "activation", pk, ["hq"], out=hq[:, b - 15, 0:N], in_=pr, func=AF.Silu)
                elif b < 23:
                    X("scalar", "activation", pk, ["hsig"], out=hsig[:, b - 19, 0:N], in_=pr, func=AF.Sigmoid)
                elif b < 27:
                    X("scalar", "activation", pk, ["hv_b"], out=hv_b[:, b - 23, 0:N], in_=pr, func=AF.Copy)
                elif b < 31:
                    X("scalar", "activation", pk, ["hog"], out=hog[:, b - 27, 0:N], in_=pr, func=AF.Silu)
                else:
                    X("scalar", "activation", pk, ["gates"], out=gates[:, b - 31, 0:N], in_=pr, func=AF.Sigmoid)
            return handler

        class Feeder:
            def __init__(self, l, blocks, N, handler):
                self.l, self.q, self.N, self.h = l, list(blocks), N, handler

            def feed(self, n=2):
                if self.q:
                    take, self.q = self.q[:n], self.q[n:]
                    emit_proj(self.l, take, self.N, self.h)

            def until(self, b):
                while self.q and self.q[0] <= b:
                    self.feed(2)

            def flush(self):
                while self.q:
                    self.feed(2)

        _feeder = [None]

        def FEED(n=2):
            if _feeder[0] is not None:
                _feeder[0].feed(n)

        def xsrc(l):
            return (xT, []) if l == 0 else (xs1, ["xs1"])

        def emit_halo(l):
            if l > 0:
                conv_pump(l, 10 ** 6)
            src, sk = xsrc(l)
            DMA("sync", "x_ld", xt[:, :, 0:1], src[:, :, NPT - 1:NPT], sk, ["xt"], allow_slow_non_contiguous=True)
            emit_norm(l, xt, "xt", 1, "nmix")

            def hh_(b, pr, pk):
                X("scalar", "activation", pk, ["hraw"], out=hraw[:, b:b + 1], in_=pr, func=AF.Copy)
            emit_proj(l, list(range(RW_BLK)), 1, hh_)
            DMA("gpsimd", "ex_h", cin_h[:, :], hraw[:], ["hraw"], ["cin_h"])
            p.op("gpsimd", lambda e: e.collective_compute("AllGather", ALU.bypass, replica_groups=groups4, ins=[cin_h[:, :]], outs=[cout_h[:, :]]),
                 ["cin_h"], ["cout_h"])
            DMA("gpsimd", "ex_h", hall[:], cout_h.rearrange("(r p) c -> p r c", p=128), ["cout_h"], ["hall"])
            X("vector", "tensor_scalar", ["hall", "c_halom"], ["halo_prev"], out=halo_prev[:], in0=hall[:, 0, 0:RW_BLK], scalar1=cst["halom"][:, 0:1],
              scalar2=None, op0=ALU.mult)
            for r in range(1, 4):
                X("vector", "scalar_tensor_tensor", ["hall", "c_halom", "halo_prev"], ["halo_prev"], out=halo_prev[:], in0=hall[:, r, 0:RW_BLK],
                  scalar=cst["halom"][:, r:r + 1], in1=halo_prev[:], op0=ALU.mult, op1=ALU.add)
            if l == 0:
                conv_pump(0, 8)

        def emit_exchange(l):
            conv_pump(l, 10 ** 6)
            X("vector", "tensor_copy", ["Sf0", "Sf1", "Sf2", "Sf3"], ["exb"], out=exb[:, 0:512], in_=Sf[:].rearrange("p c v -> p (c v)"))
            X("vector", "tensor_copy", ["Shf"], ["exb"], out=exb[:, 512:1024], in_=Shf[:].rearrange("p c v -> p (c v)"))
            X("vector", "tensor_copy", ["Dtot"], ["exb"], out=exb[:, 1024:1028], in_=Dtot[:])
            X("vector", "memset", [], ["exb"], ap=exb[:, 1028:SW], constant=0.0)
            DMA("gpsimd", "ex_s", cin_s[:, :], exb[:], ["exb"], ["cin_s"])
            p.op("gpsimd", lambda e: e.collective_compute("AllGather", ALU.bypass, replica_groups=groups4, ins=[cin_s[:, :]], outs=[cout_s[:, :]]),
                 ["cin_s"], ["cout_s"])
            X("vector", "memset", [], ["Xr"], ap=Xr[:], constant=0.0)
            X("vector", "memset", [], ["Xh"], ap=Xh[:], constant=0.0)
            fm = cst["foldm"]
            for r in range(3):
                DMA("gpsimd", "ex_s", exall, cout_s[r * 128:(r + 1) * 128, :], ["cout_s"], ["gates"])
                for c in range(4):
                    for hh in range(2):
                        X("vector", "tensor_copy", ["gates"], ["PTbd"], out=PTbd[hh * 64:(hh + 1) * 64, hh * 64:(hh + 1) * 64],
                          in_=exall[hh * 64:(hh + 1) * 64, c * 128 + 64:c * 128 + 128])
                    pT, kT = QSLOT()
                    X("tensor", "transpose", ["PTbd", "c_ident"], kT, out=pT, in_=PTbd[:], identity=cst["ident"][:])
                    X("vector", "tensor_copy", kT, ["lhsTf"], out=lhsTf[:], in_=pT)
                    pm, km = QSLOT()
                    X("tensor", "matmul", ["lhsTf", "Xr"], km, out=pm[:, 0:64], lhsT=lhsTf[:], rhs=Xr[:, c, :], start=True, stop=True)
                    X("vector", "tensor_tensor", km + ["gates"], ["ftmp"], out=ftmp[:, 0:64], in0=pm[:, 0:64], in1=exall[:, c * 128:c * 128 + 64], op=ALU.add)
                    X("vector", "tensor_tensor", ["ftmp", "Xr"], ["ftmp"], out=ftmp[:, 0:64], in0=ftmp[:, 0:64], in1=Xr[:, c, :], op=ALU.subtract)
                    X("vector", "scalar_tensor_tensor", ["ftmp", "Xr", "c_foldm"], ["Xr"], out=Xr[:, c, :], in0=ftmp[:, 0:64], scalar=fm[:, r:r + 1],
                      in1=Xr[:, c, :], op0=ALU.mult, op1=ALU.add)
                for c in range(4):
                    X("vector", "scalar_tensor_tensor", ["Xh", "gates"], ["ftmp"], out=ftmp[:], in0=Xh[:, c, :], scalar=exall[:, 1024 + c:1025 + c],
                      in1=exall[:, 512 + c * 128:512 + (c + 1) * 128], op0=ALU.mult, op1=ALU.add)
                    X("vector", "tensor_tensor", ["ftmp", "Xh"], ["ftmp"], out=ftmp[:], in0=ftmp[:], in1=Xh[:, c, :], op=ALU.subtract)
                    X("vector", "scalar_tensor_tensor", ["ftmp", "Xh", "c_foldm"], ["Xh"], out=Xh[:, c, :], in0=ftmp[:], scalar=fm[:, r:r + 1],
                      in1=Xh[:, c, :], op0=ALU.mult, op1=ALU.add)

        SFK = ["Sf0", "Sf1", "Sf2", "Sf3"]
        SBK = ["Sb0", "Sb1", "Sb2", "Sb3"]

        def shadows():
            X("scalar", "activation", SFK, SBK, out=Sb[:].rearrange("p c v -> p (c v)"), in_=Sf[:].rearrange("p c v -> p (c v)"), func=AF.Copy)
            X("scalar", "activation", ["Shf"], ["Shb"], out=Shb[:].rearrange("p c v -> p (c v)"), in_=Shf[:].rearrange("p c v -> p (c v)"), func=AF.Copy)

        def init_states_A():
            X("vector", "memset", [], SFK, ap=Sf[:], constant=0.0)
            for c in range(4):
                X("vector", "tensor_copy", ["c_isel"], SFK, out=Sf[:, c, 64:128], in_=cst["isel"][:])
            X("vector", "memset", [], ["Shf"], ap=Shf[:], constant=0.0)
            X("vector", "memset", [], ["Dtot"], ap=Dtot[:], constant=1.0)
            shadows()

        def init_states_B():
            X("vector", "tensor_copy", ["Xr"], SFK, out=Sf[:, :, 0:64], in_=Xr[:])
            X("vector", "tensor_copy", ["Xh"], ["Shf"], out=Shf[:], in_=Xh[:])
            shadows()

        def layer_tile(l, phB, ti):
            conv_pump(l + 1 if phB else l, 6)
            is_s = (ti == NPS)
            N = 128 if is_s else W
            tok0 = NPT if is_s else ti * W
            last_prompt = (ti == NPS - 1)
            src, sk = xsrc(l)
            DMA("sync", "x_ld", xt[:, :, 0:N], src[:, :, tok0:tok0 + N], sk, ["xt"])
            emit_norm(l, xt, "xt", N, "nmix")
            if is_s:
                DMA("sync", "sh_ld", shst[:], st_shift[l], [], ["shst"])
                X("vector", "tensor_copy", ["shst"], ["raw"], out=raw[:, :, 0:132].rearrange("p b (s t) -> p b s t", t=33)[:, :, :, 0:1],
                  in_=shst[:].unsqueeze(3))
            blocks = list(range(NBLK)) if phB else (list(range(4, 13)) + list(range(19, 27)))
            fd = Feeder(l, blocks, N, make_handler(is_s, N))
            _feeder[0] = fd
            fd.until(14)
            nb = (0, RW_BLK) if phB else (4, 13)
            for j in range(N // T):
                col0 = j * T
                if is_s:
                    rv = raw[:, nb[0]:nb[1], 0:132].rearrange("p b (s t) -> p b s t", t=33)
                    cur, prv = rv[:, :, :, 1:33], rv[:, :, :, 0:32]
                    mxv = mx[:, nb[0]:nb[1], :].rearrange("p b (s t) -> p b s t", t=32)
                else:
                    cur, prv = raw[:, nb[0]:nb[1], 1 + col0:1 + col0 + T], raw[:, nb[0]:nb[1], col0:col0 + T]
                    mxv = mx[:, nb[0]:nb[1], :]
                rwkv_prep(l, phB, cur, prv, mxv, tok0 + col0, nb)
                fd.until(22)
                hgrn_prep(l, phB, col0)
                fd.until(26)
                def before_chunk(q, l=l, is_s=is_s):
                    if is_s:
                        DMA("sync", "st_ld", Sf[:, :, 0:64], st_rwkv[l, q], [], SFK)
                        DMA("sync", "st_ld", Shf[:], st_hgrn[l, q], [], ["Shf"])
                        shadows()

                def after_chunk(q, l=l, is_s=is_s):
                    if is_s:
                        DMA("gpsimd", "st_out", o_rwkv_s[l, q], Sf[:, :, 0:64], SFK, ["o_rwkv_s"], is_output=True)
                        DMA("gpsimd", "st_out", o_hgrn_s[l, q], Shf[:], ["Shf"], ["o_hgrn_s"], is_output=True)
                mixer_chunks(l, phB, col0, before_chunk, after_chunk)
                if phB:
                    rwkv_post(l, col0)
                    fd.until(30)
                    hgrn_post(l, col0)
            fd.flush()
            _feeder[0] = None
            if is_s:
                if phB:
                    X("vector", "tensor_copy", ["raw"], ["shout"], out=shout[:].unsqueeze(3),
                      in_=raw[:, :, 0:132].rearrange("p b (s t) -> p b s t", t=33)[:, :, :, 32:33])
                    DMA("gpsimd", "st_out", o_shift_s[l], shout[:], ["shout"], ["o_shift_s"], is_output=True)
            else:
                if phB and last_prompt:
                    X("vector", "tensor_copy", ["raw"], ["shoutp"], out=shoutp[:].unsqueeze(2), in_=raw[:, :, W:W + 1])
                    DMA("gpsimd", "st_out", o_shift_p[l], shoutp[:], ["shoutp"], ["o_shift_p"], is_output=True)
                    DMA("gpsimd", "st_out", o_rwkv_p[l], Sf[:, :, 0:64], SFK, ["o_rwkv_p"], is_output=True)
                    DMA("gpsimd", "st_out", o_hgrn_p[l], Shf[:], ["Shf"], ["o_hgrn_p"], is_output=True)
                X("vector", "tensor_copy", ["raw"], ["raw"], out=raw[:, :, 0:1], in_=raw[:, :, W:W + 1])
            if not phB:
                return
            for o8 in range(8):
                s_ = _wsl[0] % WSL
                _wsl[0] += 1
                DMA("sync", "wsl%d" % s_, wslot[s_][:, 0, :], wb_oab[l, o8], ["wb_o%d" % l], ["wslot%d" % s_])
                sa = _psl[0] % 4
                _psl[0] += 1
                pa_, ka_ = PS(sa, 0, N)
                for c in range(4):
                    X("tensor", "matmul", ["wslot%d" % s_, "yg_b"], ka_, out=pa_, lhsT=wslot[s_][:, 0, c * 128:(c + 1) * 128], rhs=yg_b[:, c, 0:N],
                      start=(c == 0), stop=(c == 3))
                sb_ = _psl[0] % 4
                _psl[0] += 1
                pb_, kb_ = PS(sb_, 0, N)
                for c in range(4):
                    X("tensor", "matmul", ["wslot%d" % s_, "ob_b"], kb_, out=pb_, lhsT=wslot[s_][:, 0, (4 + c) * 128:(5 + c) * 128], rhs=ob_b[:, c, 0:N],
                      start=(c == 0), stop=(c == 3))
                X("vector", "tensor_tensor", ka_ + ["gates"], ["mtmp"], out=mtmp[:, 0:N], in0=pa_, in1=gates[:, o8, 0:N], op=ALU.mult)
                X("vector", "tensor_tensor", kb_ + ["gates"], ["relu0"], out=relu_t[0][:, 0:N], in0=pb_, in1=gates[:, 8 + o8, 0:N], op=ALU.mult)
                X("gpsimd", "tensor_tensor", ["mtmp", "relu0"], ["mixin"], out=mixin[:, o8, 0:N], in0=mtmp[:, 0:N], in1=relu_t[0][:, 0:N], op=ALU.add)
            for o8 in range(8):
                s_ = _wsl[0] % WSL
                _wsl[0] += 1
                DMA("sync", "wsl%d" % s_, wslot[s_][:, 0, :], wb_o[l, o8], ["wb_o%d" % l], ["wslot%d" % s_])
                sm = _psl[0] % 4
                _psl[0] += 1
                pm_, km_ = PS(sm, 0, N)
                for kc in range(KC):
                    X("tensor", "matmul", ["wslot%d" % s_, "mixin"], km_, out=pm_, lhsT=wslot[s_][:, 0, kc * 128:(kc + 1) * 128], rhs=mixin[:, kc, 0:N],
                      start=(kc == 0), stop=(kc == KC - 1))
                X("vector", "tensor_tensor", km_ + ["xt"], ["x1"], out=x1[:, o8, 0:N], in0=pm_, in1=xt[:, o8, 0:N], op=ALU.add)
            DMA("gpsimd", "x1_st", x1s[:, :, tok0:tok0 + N], x1[:, :, 0:N], ["x1"], ["x1s"])

        F_x = aalloc([128, KC, 512], F32, reset=True)
        F_h = aalloc([128, KC, 512], BF16)
        F_sq = [aalloc([128, 512]) for _ in range(2)]
        F_rstd = aalloc([128, 512])
        F_relu = F_sq
        F_act = aalloc([128, 16, 512], BF16)
        F_up = [aalloc([128, KC, 256], BF16) for _ in range(3)]
        F_dn = [aalloc([128, 16, 128], BF16) for _ in range(3)]
        _fs = [0, 0]

        def stage_F(l, tok0, N):
            DMA("sync", "f_ld", F_x[:, :, 0:N], x1s[:, :, tok0:tok0 + N], ["x1s"], ["F_x"])
            emit_norm(l, F_x, "F_x", N, "nffn", outbuf=F_h, outkey="F_h", sqk=F_sq, rstd=F_rstd, pfx="F_")
            for h in range(2):
                for fg in range(8):
                    su = _fs[0] % 3
                    _fs[0] += 1
                    DMA("sync", "up%d" % su, F_up[su][:].rearrange("p k c -> p (k c)"), wb_up[l, h * 8 + fg], ["wb_f%d" % l], ["F_up%d" % su])
                    for fb in range(2):
                        pu, ku = PS(4 + (fb % 2), 0, N)
                        for kc in range(KC):
                            X("tensor", "matmul", ["F_up%d" % su, "F_h"], ku, out=pu, lhsT=F_up[su][:, kc, fb * 128:(fb + 1) * 128], rhs=F_h[:, kc, 0:N],
                              start=(kc == 0), stop=(kc == KC - 1))
                        rt = F_relu[fb % 2]
                        X("scalar", "activation", ku, ["F_sqk%d" % (fb % 2)], out=rt[:, 0:N], in_=pu, func=AF.Relu)
                        X("gpsimd", "tensor_tensor", ["F_sqk%d" % (fb % 2)], ["F_act"], out=F_act[:, fg * 2 + fb, 0:N], in0=rt[:, 0:N], in1=rt[:, 0:N], op=ALU.mult)
                for o8 in range(8):
                    sd = _fs[1] % 3
                    _fs[1] += 1
                    DMA("sync", "dn%d" % sd, F_dn[sd][:].rearrange("p k c -> p (k c)"), wb_dn[l, h * 8 + o8], ["wb_f%d" % l], ["F_dn%d" % sd])
                    pd_, kd_ = PS(o8 % 4, 0, N)
                    for fc in range(16):
                        X("tensor", "matmul", ["F_dn%d" % sd, "F_act"], kd_, out=pd_, lhsT=F_dn[sd][:, fc, :], rhs=F_act[:, fc, 0:N],
                          start=(fc == 0), stop=(fc == 15))
                    X("vector", "tensor_tensor", kd_ + ["F_x"], ["F_x"], out=F_x[:, o8, 0:N], in0=pd_, in1=F_x[:, o8, 0:N], op=ALU.add)
            if l < NL - 1:
                DMA("gpsimd", "x_st", xs1[:, :, tok0:tok0 + N], F_x[:, :, 0:N], ["F_x"], ["xs1"])
            else:
                emit_norm(0, F_x, "F_x", N, "nfin", outbuf=F_x, outkey="F_x", sqk=F_sq, rstd=F_rstd, pfx="F_")
                DMA("gpsimd", "y_st", yT[:, :, tok0:tok0 + N], F_x[:, :, 0:N], ["F_x"], ["yT"], is_output=True)

        _step = [0]

        def step(fn, *a):
            _step[0] += 1
            if cfg.stop is None or _step[0] <= cfg.stop:
                fn(*a)

        def set_prev():
            X("vector", "tensor_copy", ["halo_prev"], ["raw"], out=raw[:, :, 0:1], in_=halo_prev[:].unsqueeze(2))

        for l in range(NL):
            step(emit_halo, l)
            step(set_prev)
            step(init_states_A)
            for ti in range(NPS):
                step(layer_tile, l, False, ti)
            step(emit_exchange, l)
            step(set_prev)
            step(init_states_B)
            GT = max(1, 512 // W)
            for g0 in range(0, NPS, GT):
                g1 = min(NPS, g0 + GT)
                for ti in range(g0, g1):
                    step(layer_tile, l, True, ti)
                step(p.fence)
                step(stage_F, l, g0 * W, (g1 - g0) * W)
                step(p.fence)
            step(layer_tile, l, True, NPS)
            step(p.fence)
            step(stage_F, l, NPT, 128)
            step(p.fence)

        with nc.Block() as block:
            p.emit(block)
    return nc


_NC_CACHE = {}


def _prep_shared(inp):
    f = np.float32
    NL = inp["w_in"].shape[0]
    idx = _win_cols()
    sh = {}
    w_in = _take_cols(np.asarray(inp["w_in"], f), idx)
    sh["w_in_r"] = np.ascontiguousarray(w_in.reshape(NL, KC, 128, NBLK, 128).transpose(0, 3, 2, 1, 4))
    w_up = np.asarray(inp["w_ffn_up"], f)
    sh["w_up_r"] = np.ascontiguousarray(w_up.reshape(NL, KC, 128, 16, 256).transpose(0, 3, 2, 1, 4))
    w_dn = np.asarray(inp["w_ffn_down"], f)
    sh["w_dn_r"] = np.ascontiguousarray(w_dn.reshape(NL, 2, 16, 128, 8, 128).transpose(0, 1, 4, 3, 2, 5).reshape(NL, 16, 128, 16, 128))
    woa = np.asarray(inp["w_out_a"], f).reshape(NL, 4, 128, 8, 128).transpose(0, 3, 2, 1, 4)
    wob = np.asarray(inp["w_out_b"], f).reshape(NL, 4, 128, 8, 128).transpose(0, 3, 2, 1, 4)
    sh["w_oab_r"] = np.ascontiguousarray(np.concatenate([woa, wob], axis=3))
    sh["w_o_r"] = np.ascontiguousarray(np.asarray(inp["w_out"], f).reshape(NL, 8, 128, 8, 128).transpose(0, 3, 2, 1, 4))
    w2 = np.zeros((NL, 128, 512), f)
    w2[:, 0:64] = inp["rwkv_w2"]
    a2 = np.zeros((NL, 128, 512), f)
    a2[:, 64:128] = inp["rwkv_a2"]
    g2 = np.zeros((NL, 256, 512), f)
    g2[:, 0:160] = inp["rwkv_g2"]
    sh["w2pad"], sh["a2pad"] = w2, a2
    sh["g2pad"] = np.ascontiguousarray(g2.reshape(NL, 2, 128, 512).transpose(0, 2, 1, 3))
    sh["vw1"] = np.ascontiguousarray(np.asarray(inp["rwkv_vres_w1"], f)[0].reshape(4, 128, 32).transpose(1, 0, 2))
    sh["vw2"] = np.ascontiguousarray(np.asarray(inp["rwkv_vres_w2"], f)[0])
    pv = np.zeros((NL, 128, NPV), f)
    mu = _take_cols(np.asarray(inp["rwkv_mu"], f), idx[:RW_BLK * 128])
    for l in range(NL):
        pv[l, :, PV["nmix"]:PV["nmix"] + 8] = _pk(np.asarray(inp["norm_mix"], f)[l], 8)
        pv[l, :, PV["nffn"]:PV["nffn"] + 8] = _pk(np.asarray(inp["norm_ffn"], f)[l], 8)
        pv[l, :, PV["mu"]:PV["mu"] + 15] = _pk(mu[l], 15)
        pv[l, :, PV["w0"]:PV["w0"] + 4] = _pk(np.asarray(inp["rwkv_w0"], f)[l], 4)
        pv[l, :, PV["a0"]:PV["a0"] + 4] = _pk(np.asarray(inp["rwkv_a0"], f)[l], 4)
        pv[l, :, PV["v0"]:PV["v0"] + 4] = _pk(np.asarray(inp["rwkv_v0"], f)[0], 4)
        pv[l, :, PV["kk"]:PV["kk"] + 4] = _pk(np.asarray(inp["rwkv_k_k"], f)[l], 4)
        pv[l, :, PV["ka"]:PV["ka"] + 4] = _pk(np.asarray(inp["rwkv_k_a"], f)[l], 4)
        pv[l, :, PV["rk"]:PV["rk"] + 4] = _pk(np.asarray(inp["rwkv_r_k"], f)[l].reshape(-1), 4)
        pv[l, :, PV["hnw"]:PV["hnw"] + 4] = _pk(np.asarray(inp["hgrn_norm_w"], f)[l], 4)
        pv[l, :, PV["lbz"]:PV["lbz"] + 4] = _pk(np.asarray(inp["hgrn_lb_logits"], f)[l], 4)
        pv[l, :, PV["nfin"]:PV["nfin"] + 8] = _pk(np.asarray(inp["norm_final"], f), 8)
    sh["pvec"] = pv
    for nm, key in (("lnw_st", "rwkv_ln_w"), ("lnb_st", "rwkv_ln_b")):
        a = np.asarray(inp[key], f).reshape(NL, 2, 2, 2, 64)
        a = np.broadcast_to(a[:, :, :, :, None, :], (NL, 2, 2, 2, 32, 64))
        sh[nm] = np.ascontiguousarray(a.transpose(0, 2, 3, 4, 1, 5).reshape(NL, 128, 2, 64))
    return sh


def _run(inp, npt, w, dbg=(), stop=None):
    f = np.float32
    cfg = Cfg(npt=npt, w=w, nlayer=int(inp["w_in"].shape[0]), dbg=dbg, stop=stop)
    key = (npt, w, cfg.NL, tuple(dbg), stop)
    if key not in _NC_CACHE:
        _NC_CACHE[key] = build(cfg)
    nc = _NC_CACHE[key]
    NL, NT = cfg.NL, cfg.NT
    sh = _prep_shared(inp)
    xp = np.asarray(inp["x_prompt"], f)
    xs = np.asarray(inp["x_sample"], f)
    idx = _win_cols()
    sshift = _take_cols(np.asarray(inp["state_shift"], f), idx[:RW_BLK * 128])
    srw = np.asarray(inp["state_rwkv"], f)
    shg = np.asarray(inp["state_hgrn"], f)
    in_maps = []
    for c in range(NCORE):
        b, seg = c // 4, c % 4
        xtok = np.concatenate([xp[b, seg * npt:(seg + 1) * npt], xs[4 * c:4 * c + 4].reshape(4 * L, D)], axis=0)
        m = dict(sh)
        m["xT"] = np.ascontiguousarray(xtok.reshape(NT, KC, 128).transpose(2, 1, 0))
        ss = sshift[:, 4 * c:4 * c + 4]
        m["st_shift"] = np.ascontiguousarray(ss.reshape(NL, 4, RW_BLK, 128).transpose(0, 3, 2, 1))
        r = srw[:, 4 * c:4 * c + 4].reshape(NL, 4, 4, 2, 64, 64)
        m["st_rwkv"] = np.ascontiguousarray(r.transpose(0, 1, 3, 5, 2, 4).reshape(NL, 4, 128, 4, 64))
        h = shg[:, 4 * c:4 * c + 4]
        m["st_hgrn"] = np.ascontiguousarray(h.transpose(0, 1, 3, 2, 4))
        for n, v in _consts(seg).items():
            m["c_" + n] = v
        in_maps.append(m)
    res = run_bass_kernel_spmd(nc, in_maps, core_ids=list(range(NCORE)))
    R = res.results
    B = xp.shape[0]
    y_p = np.zeros((B, 4 * npt, D), f)
    y_s = np.zeros((4 * NCORE, L, D), f)
    for c in range(NCORE):
        yt = R[c]["yT"].transpose(2, 1, 0).reshape(NT, D)
        y_p[c // 4, (c % 4) * npt:(c % 4 + 1) * npt] = yt[:npt]
        y_s[4 * c:4 * c + 4] = yt[npt:].reshape(4, L, D)

    def unshift(a):
        a = np.moveaxis(a, -2, -1)
        return a.reshape(a.shape[:-2] + (RW_BLK * 128,))[..., :1824]

    def unrw(a):
        lead = a.shape[:-3]
        a = a.reshape(lead + (2, 64, 4, 64))
        n = len(lead)
        a = a.transpose(tuple(range(n)) + (n + 2, n + 0, n + 3, n + 1))
        return a.reshape(lead + (8, 64, 64))

    def unhg(a):
        n = a.ndim - 3
        return a.transpose(tuple(range(n)) + (n + 1, n + 0, n + 2))

    lastc = [4 * bb + 3 for bb in range(B)]
    shift_p = np.stack([unshift(R[c]["o_shift_p"]) for c in lastc], axis=1)
    rwkv_p = np.stack([unrw(R[c]["o_rwkv_p"]) for c in lastc], axis=1)
    hgrn_p = np.stack([unhg(R[c]["o_hgrn_p"]) for c in lastc], axis=1)
    shift_s = np.concatenate([np.moveaxis(unshift(np.moveaxis(R[c]["o_shift_s"], -1, 1)), 1, 1) for c in range(NCORE)], axis=1)
    rwkv_s = np.concatenate([unrw(R[c]["o_rwkv_s"]) for c in range(NCORE)], axis=1)
    hgrn_s = np.concatenate([unhg(R[c]["o_hgrn_s"]) for c in range(NCORE)], axis=1)
    outs = (y_p, y_s, shift_p, rwkv_p, hgrn_p, shift_s, rwkv_s, hgrn_s)
    return tuple(np.ascontiguousarray(o, dtype=f) for o in outs), R


def kernel(**inputs):
    outs, _ = _run(inputs, 2048, 128)
    return outs
```

```python
import numpy as np
from contextlib import ExitStack
import concourse.bass as bass
import concourse.mybir as mybir
from concourse.bass_utils import run_bass_kernel_spmd

F32 = mybir.dt.float32
BF16 = mybir.dt.bfloat16
AF = mybir.ActivationFunctionType
ALU = mybir.AluOpType
AX = mybir.AxisListType

D = 1024
KC = 8
NCORE = 8
L = 32
NBLK = 47
RW_BLK = 15
DFF = 4096
RMS_EPS = 1e-6
GN_EPS = 64e-5
C0 = -float(np.exp(-0.5))


class Prog:
    ENGS = ["sync", "scalar", "vector", "gpsimd", "tensor"]

    def __init__(self, nc, stack):
        self.nc = nc
        self.stack = stack
        self.ops = {e: [] for e in self.ENGS}
        self.esem = {e: stack.enter_context(nc.semaphore("s_" + e)) for e in self.ENGS}
        self.ecnt = {e: 0 for e in self.ENGS}
        self.dsem = {}
        self.dcnt = {}
        self.writer = {}
        self.readers = {}
        self.waited = {e: {} for e in self.ENGS}
        self.out_tokens = []
        self.pending = {e: {} for e in self.ENGS}

    def _dsem(self, name):
        if name not in self.dsem:
            self.dsem[name] = self.stack.enter_context(self.nc.semaphore("d_" + name))
            self.dcnt[name] = 0
        return self.dsem[name]

    def _deps(self, eng, reads, writes):
        toks = []
        for k in reads:
            w = self.writer.get(k)
            if w is not None:
                toks.append(w)
        for k in writes:
            w = self.writer.get(k)
            if w is not None:
                toks.append(w)
            toks.extend(self.readers.get(k, []))
        need = {}
        for (s, sid, v) in toks:
            if sid == ("e", "tensor") and eng == "tensor":
                continue
            if sid[0] == "e" and sid[1] == eng and eng == "sync":
                continue
            if sid[0] == "d":
                v = max(v, self.dcnt[sid[1]])
            if need.get(sid, (None, -1))[1] < v:
                need[sid] = (s, v)
        for sid, (s, v) in self.pending[eng].items():
            if need.get(sid, (None, -1))[1] < v:
                need[sid] = (s, v)
        self.pending[eng] = {}
        waits = []
        for sid, (s, v) in need.items():
            if self.waited[eng].get(sid, -1) >= v:
                continue
            self.waited[eng][sid] = v
            waits.append((s, v))
        return waits

    def fence(self):
        allt = {}
        for e in self.ENGS:
            if self.ecnt[e] > 0:
                allt[("e", e)] = (self.esem[e], self.ecnt[e])
        for n, c in self.dcnt.items():
            if c > 0:
                allt[("d", n)] = (self.dsem[n], c)
        for e in self.ENGS:
            for sid, sv in allt.items():
                if sid == ("e", e):
                    continue
                self.pending[e][sid] = sv
        self.writer.clear()
        self.readers.clear()

    def _record(self, tok, reads, writes):
        for k in reads:
            self.readers.setdefault(k, []).append(tok)
        for k in writes:
            self.writer[k] = tok
            self.readers[k] = []

    def op(self, eng, fn, reads=(), writes=()):
        pk = [k for k in reads if k.startswith("psbank")]
        if pk:
            reads = [k for k in reads if not k.startswith("psbank")]
            writes = list(writes) + [k for k in pk if k not in writes]
        waits = self._deps(eng, reads, writes)
        self.ecnt[eng] += 1
        tok = (self.esem[eng], ("e", eng), self.ecnt[eng])
        self.ops[eng].append((waits, fn, self.esem[eng], 1))
        self._record(tok, reads, writes)
        return tok

    def dma(self, eng, dname, fn, reads=(), writes=(), is_output=False):
        waits = self._deps(eng, reads, writes)
        s = self._dsem(dname)
        self.dcnt[dname] += 16
        tok = (s, ("d", dname), self.dcnt[dname])
        self.ops[eng].append((waits, fn, s, 16))
        self._record(tok, reads, writes)
        if is_output:
            self.out_tokens.append(tok)
        return tok

    def emit(self, block):
        prog = self

        def mk(ename):
            def body(e):
                for (waits, fn, s, inc) in prog.ops[ename]:
                    for (ws, wv) in waits:
                        e.wait_ge(ws, wv)
                    fn(e).then_inc(s, inc)
                if ename == "gpsimd":
                    last = {}
                    for (s2, sid, v) in prog.out_tokens:
                        if last.get(sid, (None, -1))[1] < v:
                            last[sid] = (s2, v)
                    for sid, (s2, v) in last.items():
                        e.wait_ge(s2, v)
            return body
        block.sync(mk("sync"))
        block.scalar(mk("scalar"))
        block.vector(mk("vector"))
        block.gpsimd(mk("gpsimd"))
        block.tensor(mk("tensor"))


def _win_cols():
    idx = list(range(0, 1664))
    idx += list(range(1664, 1824)) + [-1] * 96
    idx += list(range(1824, 5920))
    assert len(idx) == NBLK * 128
    return np.array(idx)


def _take_cols(a, idx, axis=-1):
    a = np.moveaxis(a, axis, -1)
    out = np.zeros(a.shape[:-1] + (len(idx),), a.dtype)
    m = idx >= 0
    out[..., m] = a[..., idx[m]]
    return np.moveaxis(out, -1, axis)


def _pk(v, nchunk):
    return np.ascontiguousarray(v.reshape(nchunk, 128).T)


def _consts(rank4):
    c = {}
    c["ident"] = np.eye(128, dtype=np.float32)
    ob = np.zeros((128, 128), np.float32)
    ob[:64, :64] = 1
    ob[64:, 64:] = 1
    c["ones_bd"] = ob
    c["ones_f"] = np.ones((128, 128), np.float32)
    isel = np.zeros((128, 64), np.float32)
    isel[np.arange(128), np.arange(128) % 64] = 1
    c["isel"] = isel
    rho = np.arange(128)
    blk, pos = rho // 32, rho % 32
    same = blk[:, None] == blk[None, :]
    c["m_strict"] = (same & (pos[:, None] < pos[None, :])).astype(np.float32)
    c["m_incl"] = (same & (pos[:, None] <= pos[None, :])).astype(np.float32)
    c["m_lower"] = (same & (pos[:, None] > pos[None, :])).astype(np.float32)
    s = np.arange(32)
    c["m_att"] = (s[:, None] <= s[None, :]).astype(np.float32)
    rst = np.ones((128, 4 * 128), np.float32)
    rst[:, ::32] = 0
    c["m_reset"] = rst
    fm = np.zeros((128, 4), np.float32)
    fm[:, :3] = (np.arange(3) < rank4).astype(np.float32)[None, :]
    c["foldm"] = fm
    hm = np.zeros((128, 4), np.float32)
    if rank4 > 0:
        hm[:, rank4 - 1] = 1
    c["halom"] = hm
    return c


class Cfg:
    def __init__(self, npt=2048, w=256, nlayer=2, dbg=(), stop=None):
        self.stop = stop
        self.NPT = npt
        self.W = w
        self.NS = 4
        self.NT = npt + 4 * L
        self.NL = nlayer
        self.dbg = tuple(dbg)


PV = dict(nmix=0, nffn=8, mu=16, w0=31, a0=35, v0=39, kk=43, ka=47, rk=51, hnw=55, lbz=59, nfin=63)
NPV = 71


def build(cfg):
    nc = bass.Bass("TRN2", target_bir_lowering=False)
    NPT, W, NT, NL = cfg.NPT, cfg.W, cfg.NT, cfg.NL
    NPS = NPT // W

    def din(name, shape, dt=F32):
        return nc.dram_tensor(name, list(shape), dt, kind="ExternalInput").ap()

    def dout(name, shape, dt=F32):
        return nc.dram_tensor(name, list(shape), dt, kind="ExternalOutput").ap()

    def dint(name, shape, dt=F32):
        return nc.dram_tensor(name, list(shape), dt, kind="Internal").ap()

    xT = din("xT", [128, KC, NT])
    st_shift = din("st_shift", [NL, 128, RW_BLK, 4])
    st_rwkv = din("st_rwkv", [NL, 4, 128, 4, 64])
    st_hgrn = din("st_hgrn", [NL, 4, 128, 4, 128])
    w_in_r = din("w_in_r", [NL, NBLK, 128, KC, 128])
    w_up_r = din("w_up_r", [NL, 16, 128, KC, 256])
    w_dn_r = din("w_dn_r", [NL, 16, 128, 16, 128])
    w_oab_r = din("w_oab_r", [NL, 8, 128, 8, 128])
    w_o_r = din("w_o_r", [NL, 8, 128, 8, 128])
    w2pad = din("w2pad", [NL, 128, 512])
    a2pad = din("a2pad", [NL, 128, 512])
    g2pad = din("g2pad", [NL, 128, 2, 512])
    vw1 = din("vw1", [128, 4, 32])
    vw2 = din("vw2", [32, 512])
    pvec = din("pvec", [NL, 128, NPV])
    lnw_st = din("lnw_st", [NL, 128, 2, 64])
    lnb_st = din("lnb_st", [NL, 128, 2, 64])
    cnames = ["ident", "ones_bd", "ones_f", "isel", "m_strict", "m_incl", "m_lower", "m_att", "m_reset", "foldm", "halom"]
    cshape = dict(ident=[128, 128], ones_bd=[128, 128], ones_f=[128, 128], isel=[128, 64], m_strict=[128, 128],
                  m_incl=[128, 128], m_lower=[128, 128], m_att=[32, 32], m_reset=[128, 512], foldm=[128, 4], halom=[128, 4])
    cdram = {n: din("c_" + n, cshape[n]) for n in cnames}

    yT = dout("yT", [128, KC, NT])
    o_shift_p = dout("o_shift_p", [NL, 128, RW_BLK])
    o_rwkv_p = dout("o_rwkv_p", [NL, 128, 4, 64])
    o_hgrn_p = dout("o_hgrn_p", [NL, 128, 4, 128])
    o_shift_s = dout("o_shift_s", [NL, 128, RW_BLK, 4])
    o_rwkv_s = dout("o_rwkv_s", [NL, 4, 128, 4, 64])
    o_hgrn_s = dout("o_hgrn_s", [NL, 4, 128, 4, 128])
    dbg_out = {n: dout("dbg_" + n, shp) for (n, shp) in cfg.dbg}

    xs1 = dint("xs1", [128, KC, NT])
    vfirst_d = dint("vfirst_d", [128, 4, NT])
    cin_h = dint("cin_h", [128, 16])
    cout_h = dint("cout_h", [512, 16])
    SW = 4 * 128 + 4 * 128 + 8
    cin_s = dint("cin_s", [128, SW])
    cout_s = dint("cout_s", [512, SW])
    wb_in = dint("wb_in", [NL, NBLK, 128, KC * 128], BF16)
    wb_up = dint("wb_up", [NL, 16, 128, KC * 256], BF16)
    wb_dn = dint("wb_dn", [NL, 16, 128, 16 * 128], BF16)
    x1s = dint("x1s", [128, KC, NT])
    wb_oab = dint("wb_oab", [NL, 8, 128, 8 * 128], BF16)
    wb_o = dint("wb_o", [NL, 8, 128, 8 * 128], BF16)

    with ExitStack() as st:
        p = Prog(nc, st)

        def sb(name, shape, dt=F32):
            return st.enter_context(nc.sbuf_tensor(name, list(shape), dt))

        def X(eng, method, reads, writes, **kw):
            return p.op(eng, lambda e: getattr(e, method)(**kw), reads, writes)

        def DMA(eng, dname, out, in_, reads, writes, is_output=False, **kw):
            return p.dma(eng, dname, lambda e: e.dma_start(out=out, in_=in_, **kw), reads, writes, is_output)

        _rr = [0]

        def EW():
            _rr[0] ^= 1
            return "vector" if _rr[0] else "gpsimd"

        ps = st.enter_context(nc.psum_tensor("ps", [128, 7 * 512], F32))
        psb = st.enter_context(nc.psum_tensor("psb", [128, 1024], BF16))

        def PS(bank, off, n):
            assert off + n <= 512
            keys = ["psbank%d" % bank]
            return ps[:, bank * 512 + off: bank * 512 + off + n], keys

        def PSB(off, n):
            keys = ["psbankB"]
            return psb[:, off:off + n], keys

        cst = {}
        for n in cnames:
            cst[n] = sb("k_" + n, cshape[n])
            DMA("sync", "cst", cst[n][:], cdram[n][:], [], ["c_" + n])
        ident_b = sb("ident_b", [128, 128], BF16)
        isel_b = sb("isel_b", [128, 64], BF16)
        X("vector", "tensor_copy", ["c_ident"], ["ident_b"], out=ident_b[:], in_=cst["ident"][:])
        X("vector", "tensor_copy", ["c_isel"], ["isel_b"], out=isel_b[:], in_=cst["isel"][:])
        eps_rms = sb("eps_rms", [128, 1])
        eps_gn = sb("eps_gn", [128, 1])
        X("vector", "memset", [], ["eps_rms"], ap=eps_rms[:], constant=RMS_EPS)
        X("vector", "memset", [], ["eps_gn"], ap=eps_gn[:], constant=GN_EPS)

        def wcls(b):
            return "a" if b < RW_BLK else ("b" if 19 <= b < 27 else "c")

        def wkey(l, b):
            return "wb_in%d%s" % (l, wcls(b))

        conv_q = {l: [] for l in range(NL)}

        def conv_layer(l):
            order = list(range(0, RW_BLK)) + list(range(19, 27)) + list(range(RW_BLK, 19)) + list(range(27, NBLK))
            mk = lambda *a: (lambda: DMA(*a))
            for b in order:
                conv_q[l].append(mk("gpsimd", "cv_in%d%s" % (l, wcls(b)), wb_in[l, b], w_in_r[l, b].rearrange("p k c -> p (k c)"), [], [wkey(l, b)]))
            for g in range(8):
                conv_q[l].append(mk("gpsimd", "cv_o%d" % l, wb_oab[l, g], w_oab_r[l, g].rearrange("p k c -> p (k c)"), [], ["wb_o%d" % l]))
                conv_q[l].append(mk("gpsimd", "cv_o%d" % l, wb_o[l, g], w_o_r[l, g].rearrange("p k c -> p (k c)"), [], ["wb_o%d" % l]))
            for g in range(16):
                conv_q[l].append(mk("gpsimd", "cv_f%d" % l, wb_up[l, g], w_up_r[l, g].rearrange("p k c -> p (k c)"), [], ["wb_f%d" % l]))
                conv_q[l].append(mk("gpsimd", "cv_f%d" % l, wb_dn[l, g], w_dn_r[l, g].rearrange("p k c -> p (k c)"), [], ["wb_f%d" % l]))

        def conv_pump(l, n):
            if l < NL:
                for _ in range(n):
                    if conv_q[l]:
                        conv_q[l].pop(0)()

        def conv_pump_any(n):
            for _ in range(n):
                for ll in range(NL):
                    if conv_q[ll]:
                        conv_q[ll].pop(0)()
                        break

        for l in range(NL):
            conv_layer(l)
        conv_pump(0, RW_BLK)

        pvs = sb("pvs", [128, NL, NPV])
        DMA("sync", "cstp", pvs[:], pvec.rearrange("l p n -> p l n"), [], ["pvs"])
        lnw = sb("lnw", [128, NL, 2, 64])
        lnb = sb("lnb", [128, NL, 2, 64])
        DMA("sync", "cstp", lnw[:], lnw_st.rearrange("l p g v -> p l g v"), [], ["lnw"])
        DMA("sync", "cstp", lnb[:], lnb_st.rearrange("l p g v -> p l g v"), [], ["lnb"])
        w2p_b = sb("w2p_b", [128, NL, 512], BF16)
        a2p_b = sb("a2p_b", [128, NL, 512], BF16)
        g2p_b = sb("g2p_b", [128, NL, 2, 512], BF16)
        vw1_b = sb("vw1_b", [128, 4, 32], BF16)
        vw2_b = sb("vw2_b", [32, 512], BF16)
        DMA("gpsimd", "cst2", w2p_b[:], w2pad.rearrange("l p n -> p l n"), [], ["w2p_b"])
        DMA("gpsimd", "cst2", a2p_b[:], a2pad.rearrange("l p n -> p l n"), [], ["a2p_b"])
        DMA("gpsimd", "cst2", g2p_b[:], g2pad.rearrange("l p j n -> p l j n"), [], ["g2p_b"])
        DMA("gpsimd", "cst2", vw1_b[:], vw1[:], [], ["vw1_b"])
        DMA("gpsimd", "cst2", vw2_b[:], vw2[:], [], ["vw2_b"])
        lbe = sb("lbe", [128, NL, 4])
        lbs = sb("lbs", [128, 4])
        lbv = sb("lbv", [128, NL, 4])
        oml = sb("oml", [128, NL, 4])
        X("scalar", "activation", ["pvs"], ["lbe"], out=lbe[:], in_=pvs[:, :, PV["lbz"]:PV["lbz"] + 4], func=AF.Exp)
        X("vector", "tensor_copy", ["lbe"], ["lbs"], out=lbs[:], in_=lbe[:, 0, :])
        for l in range(1, NL):
            X("vector", "tensor_tensor", ["lbe", "lbs"], ["lbs"], out=lbs[:], in0=lbs[:], in1=lbe[:, l, :], op=ALU.add)
        X("vector", "reciprocal", ["lbs"], ["lbs"], out=lbs[:], in_=lbs[:])
        for l in range(NL):
            X("vector", "tensor_tensor", ["lbe", "lbs"], ["lbe"], out=lbe[:, l, :], in0=lbe[:, l, :], in1=lbs[:], op=ALU.mult)
        X("vector", "tensor_tensor", ["lbe"], ["lbv"], out=lbv[:, 0, :], in0=lbe[:, 0, :], in1=lbe[:, 0, :], op=ALU.subtract)
        for l in range(1, NL):
            X("vector", "tensor_tensor", ["lbe", "lbv"], ["lbv"], out=lbv[:, l, :], in0=lbv[:, l - 1, :], in1=lbe[:, l, :], op=ALU.add)
        X("vector", "tensor_scalar", ["lbv"], ["oml"], out=oml[:], in0=lbv[:], scalar1=-1.0, scalar2=1.0, op0=ALU.mult, op1=ALU.add)

        def pcol(l, name, i=0, n=1):
            return pvs[:, l, PV[name] + i: PV[name] + i + n]

        def pbc(l, name, n, T):
            return pvs[:, l, PV[name]: PV[name] + n].unsqueeze(2).to_broadcast([128, n, T])

        WSL = 8
        wslot = [sb("wslot%d" % i, [128, 2, KC * 128], BF16) for i in range(WSL)]
        xt = sb("xt", [128, KC, W])
        x1 = sb("x1", [128, KC, W])
        hT = sb("hT", [128, KC, W], BF16)
        sqk_d = [sb("sqk%d" % i, [128, W]) for i in range(2)]
        rstd_d = sb("rstd", [128, W])
        RAWW = W + 4
        raw = sb("raw", [128, RW_BLK, RAWW])
        X("gpsimd", "memset", [], ["raw"], ap=raw[:], constant=0.0)
        hq = sb("hq", [128, 4, W])
        hsig = sb("hsig", [128, 4, W])
        hv_b = sb("hv_b", [128, 4, W], BF16)
        hog = sb("hog", [128, 4, W])
        big = sb("big", [128, max(16 * W, SW)])
        gates = big[:, 0:16 * W].rearrange("p (j w) -> p j w", w=W)
        yg_b = sb("yg_b", [128, 4, W], BF16)
        ob_b = sb("ob_b", [128, 4, W], BF16)
        mixin = sb("mixin", [128, KC, W], BF16)
        mtmp = sb("mtmp", [128, W])
        relu_t = [sb("relu%d" % i, [128, W]) for i in range(1)]
        _wsl = [0]

        def emit_norm(l, xbuf, xkey, N, pname, outbuf=None, outkey="hT", sqk=None, rstd=None, pfx=""):
            if outbuf is None:
                outbuf = hT
            if sqk is None:
                sqk, rstd = sqk_d, rstd_d
            nps, nkeys = PS(6, 0, N)
            for kc in range(KC):
                sq = sqk[kc % 2]
                X("scalar", "activation", [xkey], [pfx + "sqk%d" % (kc % 2)], out=sq[:, 0:N], in_=xbuf[:, kc, 0:N], func=AF.Square)
                X("tensor", "matmul", [pfx + "sqk%d" % (kc % 2), "c_ones_f"], nkeys, out=nps, lhsT=cst["ones_f"][:], rhs=sq[:, 0:N],
                  start=(kc == 0), stop=(kc == KC - 1))
            X("scalar", "activation", nkeys + ["eps_rms"], [pfx + "rstd"], out=rstd[:, 0:N], in_=nps, func=AF.Ln, scale=1.0 / D, bias=eps_rms[:])
            X("scalar", "activation", [pfx + "rstd"], [pfx + "rstd"], out=rstd[:, 0:N], in_=rstd[:, 0:N], func=AF.Exp, scale=-0.5)
            for kc in range(KC):
                X("vector", "scalar_tensor_tensor", [xkey, pfx + "rstd", "pvs"], [outkey], out=outbuf[:, kc, 0:N], in0=xbuf[:, kc, 0:N],
                  scalar=pcol(l, pname, kc), in1=rstd[:, 0:N], op0=ALU.mult, op1=ALU.mult)

        _psl = [0]

        def emit_proj(l, blocks, N, handler):
            groups = [blocks[i:i + 2] for i in range(0, len(blocks), 2)]
            for grp in groups:
                s = _wsl[0] % WSL
                _wsl[0] += 1
                if len(grp) == 2 and grp[1] == grp[0] + 1:
                    DMA("sync", "wsl%d" % s, wslot[s][:, 0:2, :], wb_in[l, grp[0]:grp[0] + 2].rearrange("j p n -> p j n"),
                        sorted(set([wkey(l, grp[0]), wkey(l, grp[1])])), ["wslot%d" % s])
                else:
                    for j, b in enumerate(grp):
                        DMA("sync", "wsl%d" % s, wslot[s][:, j, :], wb_in[l, b], [wkey(l, b)], ["wslot%d" % s])
                for j, b in enumerate(grp):
                    slot = _psl[0] % 4
                    _psl[0] += 1
                    pr, pk = PS(slot, 0, N)
                    for kc in range(KC):
                        X("tensor", "matmul", ["wslot%d" % s, "hT"], pk, out=pr, lhsT=wslot[s][:, j, kc * 128:(kc + 1) * 128],
                          rhs=hT[:, kc, 0:N], start=(kc == 0), stop=(kc == KC - 1))
                    handler(b, pr, pk)

        T = 128
        AW = 17920
        arena = sb("arena", [128, AW])
        _ao = [0]

        def aalloc(shape, dt=F32, reset=False):
            if reset:
                _ao[0] = 0
            n = int(np.prod(shape[1:]))
            n32 = n if dt == F32 else (n + 1) // 2
            a = _ao[0]
            _ao[0] += n32
            assert _ao[0] <= AW, ("arena overflow", _ao[0])
            v = arena[:, a:a + n32]
            if dt != F32:
                v = v.bitcast(dt)[:, 0:n]
            if len(shape) == 3:
                v = v.rearrange("p (a b) -> p a b", a=shape[1])
            elif len(shape) == 4:
                v = v.rearrange("p (a b c) -> p a b c", a=shape[1], b=shape[2])
            return v[0:shape[0]] if shape[0] < 128 else v

        def sbm(name, shape, dt=F32):
            return aalloc(list(shape), dt)
        mx = sbm("mx", [128, RW_BLK, T])
        lw_in = sbm("lw_in", [128, T], BF16)
        siggl = sbm("siggl", [128, 2, T], BF16)
        names4 = ["sg", "aa", "gfm", "vv", "kkn", "kh", "brec", "Gw", "tA", "tB", "Ex", "bv", "t1", "vf"]
        m4 = {n: sbm("m_" + n, [128, 4, T]) for n in names4}
        vb16 = sbm("vb16", [128, 4, T], BF16)
        t32b = sbm("t32b", [32, T], BF16)
        bdn = ["Kbd", "Bbd", "Abd", "Rbd", "KHbd", "BHbd", "Vbd"]
        bd = {n: sb(n, [128, 4, 4, 128], BF16) for n in bdn}
        for n in bdn:
            X("gpsimd", "memset", [], [n], ap=bd[n][:], constant=0.0)
        AR = sbm("AR", [128, 4, 4, 64], BF16)
        Bp = sbm("Bp", [128, 4, 4, 32], BF16)
        gL = sb("gL", [128, 4, 4])
        chn = ["X1", "X1T", "Mb", "Xa", "XaT", "Xb", "XbT"]
        chb = {n: [sbm("%s%d" % (n, g), [128, 128], BF16) for g in range(2)] for n in chn}
        chb2 = {n: [[sbm("%s_%d_%d" % (n, pp, g), [128, 128], BF16) for g in range(2)] for pp in range(2)] for n in ["Aka", "Akr", "Abr", "Ma"]}
        KT2 = [sbm("KT_%d" % pp, [128, 4, 128], BF16) for pp in range(2)]
        BT2 = [sbm("BT_%d" % pp, [128, 4, 128], BF16) for pp in range(2)]
        Vst2 = [sb("Vst_%d" % pp, [128, 2, 128], BF16) for pp in range(2)]
        for pp in range(2):
            X("gpsimd", "memset", [], ["Vst%d_0" % pp, "Vst%d_1" % pp], ap=Vst2[pp][:], constant=0.0)
        Wt = sbm("Wt", [128, 2, 128], BF16)
        Ut = sbm("Ut", [128, 2, 128], BF16)
        Sf = sb("Sf", [128, 4, 128])
        Sb = sb("Sb", [128, 4, 128], BF16)
        Yst = sbm("Yst", [128, 8, 64])
        Ysq = sbm("Ysq", [128, 8, 64])
        Yrep = sbm("Yrep", [128, 8, 2, 64])
        gst = {n: sb("gst_" + n, [128, 8]) for n in ["s1", "s2", "mean", "var"]}
        h4 = {"o": sbm("h_o", [128, 4, T])}
        for hn_, mn_ in {'f': 'sg', 'lf': 'aa', 'khh': 'kkn', 'G2': 'Gw', 'd1': 'tB', 'E2': 'Ex', 'hA': 'tA', 'o2': 'kh', 'rs': 'brec'}.items():
            h4[hn_] = m4[mn_]
        hb = {n: sbm("hb_" + n, [128, 4, T], BF16) for n in ["Qt", "Q2", "Kt", "Kh"]}
        dL = sb("dL", [128, 4, 4])
        att_b = sbm("att_b", [32, 4, 32], BF16)
        VTh = sbm("VTh", [32, 4, 128], BF16)
        KTh = sbm("KTh", [32, 4, 128], BF16)
        Shf = sb("Shf", [128, 4, 128])
        Shb = sb("Shb", [128, 4, 128], BF16)
        Dtot = sb("Dtot", [128, 4])

        def v4(ap):
            return ap.rearrange("p c (q t) -> p c q t", t=L)

        def bd_write(name, in0, in1, op, keys_r):
            for hh in range(2):
                for cc in range(2):
                    out = bd[name][hh * 64:(hh + 1) * 64, cc::2, :, 64 * cc + 32 * hh: 64 * cc + 32 * hh + 32]
                    a = v4(in0)[hh * 64:(hh + 1) * 64, cc::2]
                    if in1 is None:
                        X(EW(), "tensor_copy", keys_r, [name], out=out, in_=a)
                    else:
                        b = v4(in1)[hh * 64:(hh + 1) * 64, cc::2]
                        X(EW(), "tensor_tensor", keys_r, [name], out=out, in0=a, in1=b, op=op)

        def bc4(ap, shape):
            return ap.to_broadcast(shape)

        def rwkv_prep(l, phB, cur, prv, mxv, tok0, nvalid_blocks):
            b0, b1 = nvalid_blocks
            shp = list(cur.shape)
            mu = pvs[:, l, PV["mu"] + b0: PV["mu"] + b1]
            mu_bc = (mu.unsqueeze(2) if len(shp) == 3 else mu.unsqueeze(2).unsqueeze(3)).to_broadcast(shp)
            X("vector", "tensor_tensor", ["raw"], ["mx"], out=mxv, in0=prv, in1=cur, op=ALU.subtract)
            X("vector", "tensor_tensor", ["mx", "pvs"], ["mx"], out=mxv, in0=mxv, in1=mu_bc, op=ALU.mult)
            X("vector", "tensor_tensor", ["mx", "raw"], ["mx"], out=mxv, in0=mxv, in1=cur, op=ALU.add)
            r, k, v = mx[:, 0:4, :], mx[:, 4:8, :], mx[:, 8:12, :]
            X("scalar", "activation", ["mx"], ["lw_in"], out=lw_in[0:64, :], in_=mx[0:64, 12, :], func=AF.Tanh)
            X("scalar", "activation", ["mx"], ["lw_in"], out=lw_in[64:128, :], in_=mx[64:128, 12, :], func=AF.Copy)
            pw, kw = PS(4, 0, 512)
            pa, ka = PS(5, 0, 512)
            pg, kg = PS(6, 0, 512)
            for c in range(4):
                X("tensor", "matmul", ["lw_in", "w2p_b"], kw, out=pw[:, c * T:(c + 1) * T], lhsT=w2p_b[:, l, c * 128:(c + 1) * 128],
                  rhs=lw_in[:], start=True, stop=True)
                X("tensor", "matmul", ["lw_in", "a2p_b"], ka, out=pa[:, c * T:(c + 1) * T], lhsT=a2p_b[:, l, c * 128:(c + 1) * 128],
                  rhs=lw_in[:], start=True, stop=True)
            if phB:
                X("scalar", "activation", ["mx"], ["siggl"], out=siggl[:], in_=mx[:, 13:15, :], func=AF.Sigmoid)
                for c in range(4):
                    for j in range(2):
                        X("tensor", "matmul", ["siggl", "g2p_b"], kg, out=pg[:, c * T:(c + 1) * T],
                          lhsT=g2p_b[:, l, j, c * 128:(c + 1) * 128], rhs=siggl[:, j, :], start=(j == 0), stop=(j == 1))
            for c in range(4):
                X("scalar", "activation", kw + ["pvs"], ["sg"], out=m4["sg"][:, c, :], in_=pw[:, c * T:(c + 1) * T], func=AF.Sigmoid,
                  bias=pcol(l, "w0", c))
                X("scalar", "activation", ka + ["pvs"], ["aa"], out=m4["aa"][:, c, :], in_=pa[:, c * T:(c + 1) * T], func=AF.Sigmoid,
                  bias=pcol(l, "a0", c))
            if phB:
                X("scalar", "activation", kg, ["gfm"], out=m4["gfm"][:].rearrange("p c t -> p (c t)"), in_=pg, func=AF.Copy)
            FEED(2)
            vv = m4["vv"]
            if l == 0:
                X(EW(), "tensor_copy", ["mx"], ["vv"], out=vv[:], in_=v)
                if phB:
                    DMA("sync", "vf_st", vfirst_d[:, :, tok0:tok0 + T], vv[:], ["vv"], ["vfirst_d"])
            else:
                DMA("sync", "vf_ld", m4["vf"][:], vfirst_d[:, :, tok0:tok0 + T], ["vfirst_d"], ["vf"])
                X("gpsimd", "tensor_copy", ["mx"], ["vb16"], out=vb16[:], in_=v)
                p32, k32 = PS(4, 0, T)
                for c in range(4):
                    X("tensor", "matmul", ["vb16", "vw1_b"], k32, out=p32[0:32, :], lhsT=vw1_b[:, c, :], rhs=vb16[:, c, :],
                      start=(c == 0), stop=(c == 3))
                X("scalar", "activation", k32, ["t32b"], out=t32b[:], in_=p32[0:32, :], func=AF.Copy)
                pv_, kv_ = PS(5, 0, 512)
                for c in range(4):
                    X("tensor", "matmul", ["t32b", "vw2_b"], kv_, out=pv_[:, c * T:(c + 1) * T], lhsT=vw2_b[:, c * 128:(c + 1) * 128],
                      rhs=t32b[:], start=True, stop=True)
                for c in range(4):
                    X("scalar", "activation", kv_ + ["pvs"], ["tA"], out=m4["tA"][:, c, :], in_=pv_[:, c * T:(c + 1) * T],
                      func=AF.Sigmoid, bias=pcol(0, "v0", c))
                X("vector", "tensor_tensor", ["vf", "mx"], ["tB"], out=m4["tB"][:], in0=m4["vf"][:], in1=v, op=ALU.subtract)
                X("vector", "tensor_tensor", ["tB", "tA"], ["tB"], out=m4["tB"][:], in0=m4["tB"][:], in1=m4["tA"][:], op=ALU.mult)
                X("vector", "tensor_tensor", ["tB", "mx"], ["vv"], out=vv[:], in0=m4["tB"][:], in1=v, op=ALU.add)
            kkn, kh, brec, Gw, tA, tB, Ex = (m4[n] for n in ["kkn", "kh", "brec", "Gw", "tA", "tB", "Ex"])
            X("vector", "tensor_tensor", ["mx", "pvs"], ["kkn"], out=kkn[:], in0=k, in1=pbc(l, "kk", 4, T), op=ALU.mult)
            X("gpsimd", "tensor_tensor", ["kkn"], ["tA"], out=tA[:], in0=kkn[:], in1=kkn[:], op=ALU.mult)
            pss, kss = PS(6, 0, 512)
            for c in range(4):
                X("tensor", "matmul", ["tA", "c_ones_bd"], kss, out=pss[:, c * T:(c + 1) * T], lhsT=cst["ones_bd"][:], rhs=tA[:, c, :],
                  start=True, stop=True)
            tAf = tA[:].rearrange("p c t -> p (c t)")
            X("vector", "tensor_scalar", kss, ["tA"], out=tAf, in0=pss, scalar1=1e-24, scalar2=None, op0=ALU.max)
            X("scalar", "activation", ["tA"], ["tA"], out=tAf, in_=tAf, func=AF.Ln)
            X("scalar", "activation", ["tA"], ["tA"], out=tAf, in_=tAf, func=AF.Exp, scale=-0.5)
            X("vector", "tensor_tensor", ["kkn", "tA"], ["kkn"], out=kkn[:], in0=kkn[:], in1=tA[:], op=ALU.mult)
            FEED(2)
            X("vector", "scalar_tensor_tensor", ["aa", "pvs"], ["tB"], out=tB[:], in0=m4["aa"][:], scalar=-1.0, in1=pbc(l, "ka", 4, T),
              op0=ALU.add, op1=ALU.mult)
            X("vector", "scalar_tensor_tensor", ["tB", "mx"], ["kh"], out=kh[:], in0=tB[:], scalar=1.0, in1=k, op0=ALU.add, op1=ALU.mult)
            X("gpsimd", "tensor_tensor", ["kkn", "aa"], ["brec"], out=brec[:], in0=kkn[:], in1=m4["aa"][:], op=ALU.mult)
            sgf = m4["sg"][:].rearrange("p c t -> p (c t)")
            Gwf = Gw[:].rearrange("p c t -> p (c t)")
            X("scalar", "mul", ["sg"], ["sg"], out=sgf, in_=sgf, mul=C0)
            X("vector", "tensor_tensor_scan", ["sg", "c_m_reset"], ["Gw"], out=Gwf, data0=cst["m_reset"][:], data1=sgf, initial=0.0,
              op0=ALU.mult, op1=ALU.add)
            X("gpsimd", "tensor_tensor", ["Gw", "sg"], ["tB"], out=tB[:], in0=Gw[:], in1=m4["sg"][:], op=ALU.subtract)
            Exf = Ex[:].rearrange("p c t -> p (c t)")
            X("scalar", "activation", ["tB"], ["Ex"], out=Exf, in_=tB[:].rearrange("p c t -> p (c t)"), func=AF.Exp)
            X("vector", "scalar_tensor_tensor", ["kkn", "Ex"], ["tA"], out=tA[:], in0=kkn[:], scalar=-1.0, in1=Ex[:],
              op0=ALU.mult, op1=ALU.mult)
            bd_write("Abd", tA[:], None, None, ["tA"])
            X(EW(), "tensor_copy", ["tA"], ["AR"], out=AR[:, :, :, 0:32], in_=v4(tA[:]))
            if phB:
                X("scalar", "activation", ["Gw"], ["Ex"], out=Exf, in_=Gwf, func=AF.Exp)
                bd_write("Rbd", r, Ex[:], ALU.mult, ["mx", "Ex"])
                X(EW(), "tensor_tensor", ["mx", "Ex"], ["AR"], out=AR[:, :, :, 32:64], in0=v4(r), in1=v4(Ex[:]), op=ALU.mult)
            FEED(2)
            X("scalar", "activation", ["Gw"], ["Ex"], out=Exf, in_=Gwf, func=AF.Exp, scale=-1.0)
            bd_write("Kbd", kh[:], Ex[:], ALU.mult, ["kh", "Ex"])
            bd_write("Bbd", brec[:], Ex[:], ALU.mult, ["brec", "Ex"])
            X(EW(), "tensor_tensor", ["brec", "Ex"], ["Bp"], out=Bp[:], in0=v4(brec[:]), in1=v4(Ex[:]), op=ALU.mult)
            FEED(2)
            GL = v4(Gw[:])[:, :, :, L - 1:L]
            X("scalar", "activation", ["Gw"], ["gL"], out=gL[:].unsqueeze(3), in_=GL, func=AF.Exp)
            X("vector", "tensor_tensor", ["Gw"], ["tB"], out=v4(tB[:]), in0=GL.to_broadcast([128, 4, 4, L]), in1=v4(Gw[:]), op=ALU.subtract)
            X("scalar", "activation", ["tB"], ["Ex"], out=Exf, in_=tB[:].rearrange("p c t -> p (c t)"), func=AF.Exp)
            bd_write("KHbd", kh[:], Ex[:], ALU.mult, ["kh", "Ex"])
            bd_write("BHbd", brec[:], Ex[:], ALU.mult, ["brec", "Ex"])
            bd_write("Vbd", vv[:], None, None, ["vv"])
            if phB:
                X("vector", "tensor_tensor", ["mx", "kh"], ["tA"], out=tA[:], in0=r, in1=kh[:], op=ALU.mult)
                X("gpsimd", "tensor_tensor", ["tA", "pvs"], ["tA"], out=tA[:], in0=tA[:], in1=pbc(l, "rk", 4, T), op=ALU.mult)
                pbn, kbn = PS(4, 0, 512)
                for c in range(4):
                    X("tensor", "matmul", ["tA", "c_ones_bd"], kbn, out=pbn[:, c * T:(c + 1) * T], lhsT=cst["ones_bd"][:], rhs=tA[:, c, :],
                      start=True, stop=True)
                X("vector", "tensor_tensor", kbn + ["vv"], ["bv"], out=m4["bv"][:].rearrange("p c t -> p (c t)"), in0=pbn,
                  in1=vv[:].rearrange("p c t -> p (c t)"), op=ALU.mult)

        QS = [(4, 256), (5, 0), (6, 0)]
        _qs = [0]

        def QSLOT():
            b, o = QS[_qs[0] % 3]
            _qs[0] += 1
            return PS(b, o, 128)

        def mm_evac_copy(lhs, lk, rhs, rk, dst, dk, eng):
            pr, pk = QSLOT()
            X("tensor", "matmul", [lk, rk], pk, out=pr, lhsT=lhs, rhs=rhs, start=True, stop=True)
            if eng == "scalar":
                X("scalar", "activation", pk, [dk], out=dst, in_=pr, func=AF.Copy)
            else:
                X("vector", "tensor_copy", pk, [dk], out=dst, in_=pr)

        def mm_evac_add(lhs, lk, rhs, rk, addend, ak, dst, dk):
            pr, pk = QSLOT()
            X("tensor", "matmul", [lk, rk], pk, out=pr, lhsT=lhs, rhs=rhs, start=True, stop=True)
            X("vector", "tensor_tensor", pk + [ak], [dk], out=dst, in0=pr, in1=addend, op=ALU.add)

        def rwkv_steps(l, phB, q, par):
            NV = 64 if phB else 128
            ncol = 64 if phB else 32

            def kn(n, g):
                return "%s%d" % (n, g)

            def kp(n, g):
                return "%s%d_%d" % (n, par, g)

            def CB(n, g):
                return chb2[n][par][g]

            p1s = [PS(4, 0, 160), PS(5, 0, 160)]

            def st_stage1(g):
                p1, k1 = p1s[g]
                for (lf, rf, rk_, c0, cw) in (("Kbd", AR, "AR", 0, ncol), ("Bbd", AR, "AR", 64, ncol), ("Abd", Bp, "Bp", 128, 32)):
                    for cc in range(2):
                        c = 2 * g + cc
                        rhs = rf[:, c, q, 0:cw] if rk_ == "AR" else rf[:, c, q, :]
                        X("tensor", "matmul", [lf, rk_], k1, out=p1[:, c0:c0 + cw], lhsT=bd[lf][:, c, q, :], rhs=rhs,
                          start=(cc == 0), stop=(cc == 1))

            def st_evac1(g):
                p1, k1 = p1s[g]

                def mask_evac(dst, dkey, col, mname, eng):
                    X(eng, "tensor_tensor", k1 + ["c_" + mname], [dkey], out=dst[:].rearrange("p (b t) -> p b t", t=32),
                      in0=p1[:, col:col + 32].unsqueeze(1).to_broadcast([128, 4, 32]),
                      in1=cst[mname][:].rearrange("p (b t) -> p b t", t=32), op=ALU.mult)
                mask_evac(chb["X1"][g], kn("X1", g), 64, "m_strict", "vector")
                mask_evac(chb["X1T"][g], kn("X1T", g), 128, "m_lower", "vector")
                mask_evac(CB("Aka", g), kp("Aka", g), 0, "m_strict", "vector")
                if phB:
                    mask_evac(CB("Akr", g), kp("Akr", g), 32, "m_incl", "vector")
                    mask_evac(CB("Abr", g), kp("Abr", g), 96, "m_incl", "vector")
                X("gpsimd", "tensor_tensor", [kn("X1", g), "ident_b"], [kp("Ma", g)], out=CB("Ma", g)[:], in0=chb["X1"][g][:], in1=ident_b[:], op=ALU.add)

            def B(n, g):
                if n == "Ma":
                    return CB("Ma", g)[:], kp("Ma", g)
                return chb[n][g][:], kn(n, g)

            def inv_steps(g):
                cp = lambda lh, rh, ds, eng: (lambda: mm_evac_copy(B(lh, g)[0], B(lh, g)[1], B(rh, g)[0], B(rh, g)[1], B(ds, g)[0], B(ds, g)[1], eng))
                ad = lambda lh, rh, ds: (lambda: mm_evac_add(B(lh, g)[0], B(lh, g)[1], B(rh, g)[0], B(rh, g)[1], B(rh, g)[0], B(rh, g)[1], B(ds, g)[0], B(ds, g)[1]))
                return [cp("X1T", "X1", "Xa", "scalar"), cp("X1", "X1T", "XaT", "vector"), ad("XaT", "Ma", "Mb"),
                        cp("XaT", "Xa", "Xb", "scalar"), cp("Xa", "XaT", "XbT", "vector"), ad("XbT", "Mb", "Ma"),
                        cp("XbT", "Xb", "Xa", "scalar"), cp("Xb", "XbT", "XaT", "vector"), ad("XaT", "Ma", "Mb"),
                        cp("Xa", "XaT", "XbT", "vector"), ad("XbT", "Mb", "Ma")]

            KTp, BTp, Vstp = KT2[par], BT2[par], Vst2[par]

            def st_tokmajor(g):
                for cc in range(2):
                    c = 2 * g + cc
                    pk_, kk_ = PSB(c * 128, 128)
                    X("tensor", "transpose", ["KHbd", "ident_b"], kk_, out=pk_, in_=bd["KHbd"][:, c, q, :], identity=ident_b[:])
                    pb_, kb_ = PSB(512 + c * 128, 128)
                    X("tensor", "transpose", ["BHbd", "ident_b"], kb_, out=pb_, in_=bd["BHbd"][:, c, q, :], identity=ident_b[:])
                pv_, kv_ = PS(6, 256 + 64 * g, 64)
                for cc in range(2):
                    c = 2 * g + cc
                    X("tensor", "matmul", ["Vbd", "isel_b"], kv_, out=pv_, lhsT=bd["Vbd"][:, c, q, :], rhs=isel_b[:], start=(cc == 0), stop=(cc == 1))
                pk2, kk2 = PSB(2 * g * 128, 256)
                X("scalar", "activation", kk2, ["KT%d_%d" % (par, 2 * g), "KT%d_%d" % (par, 2 * g + 1)],
                  out=KTp[:, 2 * g:2 * g + 2, :].rearrange("p c k -> p (c k)"), in_=pk2, func=AF.Copy)
                pb2, kb2 = PSB(512 + 2 * g * 128, 256)
                X("vector", "tensor_copy", kb2, ["BT%d_%d" % (par, 2 * g), "BT%d_%d" % (par, 2 * g + 1)],
                  out=BTp[:, 2 * g:2 * g + 2, :].rearrange("p c k -> p (c k)"), in_=pb2)
                X("scalar", "activation", kv_, ["Vst%d_%d" % (par, g)], out=Vstp[:, g, 0:64], in_=pv_, func=AF.Copy)

            def st_W(g):
                pW, kW = PS(0 + g, 0, NV)
                X("tensor", "matmul", [kp("Aka", g), "Vst%d_%d" % (par, g)], kW, out=pW, lhsT=CB("Aka", g)[:], rhs=Vstp[:, g, 0:NV], start=True, stop=False)
                for cc in range(2):
                    c = 2 * g + cc
                    X("tensor", "matmul", ["Abd", "Sb%d" % c], kW, out=pW, lhsT=bd["Abd"][:, c, q, :], rhs=Sb[:, c, 0:NV], start=False, stop=(cc == 1))
                if g == 0:
                    X("scalar", "activation", kW, ["Wt%d" % g], out=Wt[:, g, 0:NV], in_=pW, func=AF.Copy)
                else:
                    X("vector", "tensor_copy", kW, ["Wt%d" % g], out=Wt[:, g, 0:NV], in_=pW)

            def st_U(g):
                pU, kU = PS(2 + g, 0, NV)
                X("tensor", "matmul", [kp("Ma", g), "Wt%d" % g], kU, out=pU, lhsT=CB("Ma", g)[:], rhs=Wt[:, g, 0:NV], start=True, stop=True)
                if g == 0:
                    X("vector", "tensor_copy", kU, ["Ut%d" % g], out=Ut[:, g, 0:NV], in_=pU)
                else:
                    X("scalar", "activation", kU, ["Ut%d" % g], out=Ut[:, g, 0:NV], in_=pU, func=AF.Copy)

            def st_Y(g):
                pY, kY = PS(0 + g, 256, 64)
                X("tensor", "matmul", [kp("Akr", g), "Vst%d_%d" % (par, g)], kY, out=pY, lhsT=CB("Akr", g)[:], rhs=Vstp[:, g, 0:64], start=True, stop=False)
                X("tensor", "matmul", [kp("Abr", g), "Ut%d" % g], kY, out=pY, lhsT=CB("Abr", g)[:], rhs=Ut[:, g, 0:64], start=False, stop=False)
                for cc in range(2):
                    c = 2 * g + cc
                    X("tensor", "matmul", ["Rbd", "Sb%d" % c], kY, out=pY, lhsT=bd["Rbd"][:, c, q, :], rhs=Sb[:, c, 0:64], start=False, stop=(cc == 1))
                X("scalar", "activation", kY, ["Yst"], out=Yst[:, g * 4 + q, :], in_=pY, func=AF.Copy)

            def st_S(c):
                g = c // 2
                pS, kS = PS(2 + (c % 2), 256 * (c // 2), NV)
                X("tensor", "matmul", ["KT%d_%d" % (par, c), "Vst%d_%d" % (par, g)], kS, out=pS, lhsT=KTp[:, c, :], rhs=Vstp[:, g, 0:NV], start=True, stop=False)
                X("tensor", "matmul", ["BT%d_%d" % (par, c), "Ut%d" % g], kS, out=pS, lhsT=BTp[:, c, :], rhs=Ut[:, g, 0:NV], start=False, stop=True)
                X("vector", "scalar_tensor_tensor", kS + ["Sf%d" % c, "gL"], ["Sf%d" % c], out=Sf[:, c, 0:NV], in0=Sf[:, c, 0:NV],
                  scalar=gL[:, c, q:q + 1], in1=pS, op0=ALU.mult, op1=ALU.add)
                X("scalar", "activation", ["Sf%d" % c], ["Sb%d" % c], out=Sb[:, c, 0:NV], in_=Sf[:, c, 0:NV], func=AF.Copy)

            mk = lambda f, a: (lambda: f(a))
            pre = [mk(st_stage1, 0), mk(st_stage1, 1), mk(st_evac1, 0), mk(st_evac1, 1), mk(st_tokmajor, 0), mk(st_tokmajor, 1)]
            i0, i1 = inv_steps(0), inv_steps(1)
            for a_, b_ in zip(i0, i1):
                pre += [a_, b_]
            chain = [mk(st_W, 0), mk(st_W, 1), mk(st_U, 0), mk(st_U, 1)]
            if phB:
                chain += [mk(st_Y, 0), mk(st_Y, 1)]
            chain += [mk(st_S, c) for c in range(4)]
            return pre, chain

        def mixer_chunks(l, phB, col0, before_chunk, after_chunk):
            pre0, _ = rwkv_steps(l, phB, 0, 0)
            for f_ in pre0:
                f_()
            for q in range(4):
                par = q % 2
                before_chunk(q)
                _, chain = rwkv_steps(l, phB, q, par)
                nxt = rwkv_steps(l, phB, q + 1, 1 - par)[0] if q < 3 else []
                hg = hgrn_chunk_parts(l, phB, q, col0)
                hg[0]()
                per = -(-len(nxt) // len(chain)) if nxt else 0
                for ci, cstep in enumerate(chain):
                    cstep()
                    if ci % 2 == 1:
                        FEED(1)
                    for _ in range(per):
                        if nxt:
                            nxt.pop(0)()
                    if ci == 1:
                        hg[1]()
                    if ci == 3:
                        hg[2]()
                while nxt:
                    nxt.pop(0)()
                after_chunk(q)

        def rwkv_post(l, col0):
            s1, s2, mean, var = (gst[n] for n in ["s1", "s2", "mean", "var"])
            X("vector", "tensor_reduce", ["Yst"], ["g_s1"], out=s1[:], in_=Yst[:], axis=AX.X, op=ALU.add)
            X("gpsimd", "tensor_tensor", ["Yst"], ["Ysq"], out=Ysq[:], in0=Yst[:], in1=Yst[:], op=ALU.mult)
            X("vector", "tensor_reduce", ["Ysq"], ["g_s2"], out=s2[:], in_=Ysq[:], axis=AX.X, op=ALU.add)
            X("vector", "tensor_scalar", ["g_s1"], ["g_mean"], out=mean[:], in0=s1[:], scalar1=1.0 / 64, scalar2=None, op0=ALU.mult)
            X("vector", "tensor_tensor", ["g_mean"], ["g_s1"], out=s1[:], in0=mean[:], in1=mean[:], op=ALU.mult)
            X("vector", "scalar_tensor_tensor", ["g_s2", "g_s1"], ["g_var"], out=var[:], in0=s2[:], scalar=1.0 / 64, in1=s1[:],
              op0=ALU.mult, op1=ALU.subtract)
            X("scalar", "activation", ["g_var", "eps_gn"], ["g_var"], out=var[:], in_=var[:], func=AF.Ln, bias=eps_gn[:])
            X("scalar", "activation", ["g_var"], ["g_var"], out=var[:], in_=var[:], func=AF.Exp, scale=-0.5)
            X("vector", "tensor_tensor", ["Yst", "g_mean"], ["Ysq"], out=Ysq[:], in0=Yst[:], in1=mean[:].unsqueeze(2).to_broadcast([128, 8, 64]),
              op=ALU.subtract)
            X("vector", "tensor_tensor", ["Ysq", "g_var"], ["Ysq"], out=Ysq[:], in0=Ysq[:], in1=var[:].unsqueeze(2).to_broadcast([128, 8, 64]),
              op=ALU.mult)
            for g in range(2):
                ys = Ysq[:, g * 4:(g + 1) * 4, :]
                X(EW(), "tensor_tensor", ["Ysq", "lnw"], ["Ysq"], out=ys, in0=ys, in1=lnw[:, l, g, :].unsqueeze(1).to_broadcast([128, 4, 64]),
                  op=ALU.mult)
                X(EW(), "tensor_tensor", ["Ysq", "lnb"], ["Yrep"], out=Yrep[:, g * 4:(g + 1) * 4, :, :],
                  in0=ys.unsqueeze(2).to_broadcast([128, 4, 2, 64]),
                  in1=lnb[:, l, g, :].unsqueeze(1).unsqueeze(1).to_broadcast([128, 4, 2, 64]), op=ALU.add)
            t1 = m4["t1"]
            for g in range(2):
                for q in range(4):
                    pT, kT = QSLOT()
                    X("tensor", "transpose", ["Yrep", "c_ident"], kT, out=pT, in_=Yrep[:, g * 4 + q, :, :].rearrange("p r v -> p (r v)"),
                      identity=cst["ident"][:])
                    for hh in range(2):
                        X("vector", "tensor_tensor", kT + ["bv"], ["t1"], out=t1[hh * 64:(hh + 1) * 64, 2 * g:2 * g + 2, q * L:(q + 1) * L],
                          in0=pT[hh * 64:(hh + 1) * 64, :].rearrange("p (c h t) -> p c h t", c=2, h=2)[:, :, hh, :],
                          in1=m4["bv"][hh * 64:(hh + 1) * 64, 2 * g:2 * g + 2, q * L:(q + 1) * L], op=ALU.add)
            X("gpsimd", "tensor_tensor", ["t1", "gfm"], ["yg_b"], out=yg_b[:, :, col0:col0 + T], in0=t1[:], in1=m4["gfm"][:], op=ALU.mult)

        def hgrn_prep(l, phB, col0):
            f, lf, khh, G2, d1, E2, hA = (h4[n] for n in ["f", "lf", "khh", "G2", "d1", "E2", "hA"])
            fl = lambda t: t[:].rearrange("p c t -> p (c t)")
            sig = hsig[:, :, col0:col0 + T]
            X("vector", "tensor_tensor", ["hsig", "oml"], ["sg"], out=f[:], in0=sig, in1=oml[:, l, :].unsqueeze(2).to_broadcast([128, 4, T]),
              op=ALU.mult)
            X("vector", "tensor_tensor", ["sg", "lbv"], ["sg"], out=f[:], in0=f[:], in1=lbv[:, l, :].unsqueeze(2).to_broadcast([128, 4, T]),
              op=ALU.add)
            X("scalar", "activation", ["sg"], ["aa"], out=fl(lf), in_=fl(f), func=AF.Ln)
            X("gpsimd", "tensor_scalar", ["sg"], ["kkn"], out=fl(khh), in0=fl(f), scalar1=-1.0, scalar2=1.0, op0=ALU.mult, op1=ALU.add)
            X("vector", "tensor_tensor_scan", ["aa", "c_m_reset"], ["Gw"], out=fl(G2), data0=cst["m_reset"][:], data1=fl(lf), initial=0.0,
              op0=ALU.mult, op1=ALU.add)
            GLv = v4(G2[:])[:, :, :, L - 1:L]
            X("scalar", "activation", ["Gw"], ["dL"], out=dL[:].unsqueeze(3), in_=GLv, func=AF.Exp)
            if phB:
                hqv = hq[:, :, col0:col0 + T]
                Gm = v4(G2[:])[:, :, :, L // 2 - 1:L // 2]
                X("vector", "tensor_tensor", ["Gw"], ["tB"], out=v4(d1[:]), in0=v4(G2[:]), in1=Gm.to_broadcast([128, 4, 4, L]), op=ALU.subtract)
                X("scalar", "activation", ["tB"], ["tA"], out=fl(hA), in_=fl(d1), func=AF.Exp)
                X("vector", "tensor_tensor", ["hq", "tA"], ["hb_Qt"], out=hb["Qt"][:], in0=hqv, in1=hA[:], op=ALU.mult)
                X("scalar", "activation", ["tB"], ["tA"], out=fl(hA), in_=fl(d1), func=AF.Exp, scale=-1.0)
                X("gpsimd", "tensor_tensor", ["kkn", "tA"], ["hb_Kt"], out=hb["Kt"][:], in0=khh[:], in1=hA[:], op=ALU.mult)
                X("scalar", "activation", ["Gw"], ["Ex"], out=fl(E2), in_=fl(G2), func=AF.Exp)
                X("vector", "tensor_tensor", ["hq", "Ex"], ["hb_Q2"], out=hb["Q2"][:], in0=hqv, in1=E2[:], op=ALU.mult)
            X("vector", "tensor_tensor", ["Gw"], ["tB"], out=v4(d1[:]), in0=GLv.to_broadcast([128, 4, 4, L]), in1=v4(G2[:]), op=ALU.subtract)
            X("scalar", "activation", ["tB"], ["tA"], out=fl(hA), in_=fl(d1), func=AF.Exp)
            X("gpsimd", "tensor_tensor", ["kkn", "tA"], ["hb_Kh"], out=hb["Kh"][:], in0=khh[:], in1=hA[:], op=ALU.mult)

        def hgrn_chunk_parts(l, phB, q, col0):
            cs = slice(q * L, (q + 1) * L)

            def part_pre():
                if phB:
                    pat, kat = PS(6, 384, 128)
                    for c in range(4):
                        X("tensor", "matmul", ["hb_Kt", "hb_Qt"], kat, out=pat[0:32, c * 32:(c + 1) * 32], lhsT=hb["Kt"][:, c, cs], rhs=hb["Qt"][:, c, cs],
                          start=True, stop=True)
                    X("vector", "tensor_tensor", kat + ["c_m_att"], ["att_b"], out=att_b[:], in0=pat[0:32, :].rearrange("p (c t) -> p c t", c=4),
                      in1=cst["m_att"][:].unsqueeze(1).to_broadcast([32, 4, 32]), op=ALU.mult)
                pvt, kvt = PSB(0, 512)
                pkt, kkt = PSB(512, 512)
                for c in range(4):
                    X("tensor", "transpose", ["hv_b", "ident_b"], kvt, out=pvt[0:32, c * 128:(c + 1) * 128],
                      in_=hv_b[:, c, col0 + q * L: col0 + (q + 1) * L], identity=ident_b[:])
                    X("tensor", "transpose", ["hb_Kh", "ident_b"], kkt, out=pkt[0:32, c * 128:(c + 1) * 128], in_=hb["Kh"][:, c, cs], identity=ident_b[:])
                X("scalar", "activation", kvt, ["VTh"], out=VTh[:].rearrange("p c v -> p (c v)"), in_=pvt[0:32, :], func=AF.Copy)
                X("vector", "tensor_copy", kkt, ["KTh"], out=KTh[:].rearrange("p c v -> p (c v)"), in_=pkt[0:32, :])

            def part_o():
                if phB:
                    po, ko = PS(4, 384, 128)
                    for c in range(4):
                        X("tensor", "matmul", ["Shb", "hb_Q2"], ko, out=po[:, c * 32:(c + 1) * 32], lhsT=Shb[:, c, :], rhs=hb["Q2"][:, c, cs], start=True, stop=False)
                        X("tensor", "matmul", ["VTh", "att_b"], ko, out=po[:, c * 32:(c + 1) * 32], lhsT=VTh[:, c, :], rhs=att_b[:, c, :], start=False, stop=True)
                    X("scalar", "activation", ko, ["h_o"], out=h4["o"][:, :, cs], in_=po.rearrange("p (c t) -> p c t", c=4), func=AF.Copy)

            def part_s():
                pss_, kss_ = PS(1, 0, 512)
                for c in range(4):
                    X("tensor", "matmul", ["KTh", "VTh"], kss_, out=pss_[:, c * 128:(c + 1) * 128], lhsT=KTh[:, c, :], rhs=VTh[:, c, :], start=True, stop=True)
                for c in range(4):
                    X("vector", "scalar_tensor_tensor", kss_ + ["Shf", "dL"], ["Shf"], out=Shf[:, c, :], in0=Shf[:, c, :], scalar=dL[:, c, q:q + 1],
                      in1=pss_[:, c * 128:(c + 1) * 128], op0=ALU.mult, op1=ALU.add)
                X("scalar", "activation", ["Shf"], ["Shb"], out=Shb[:].rearrange("p c v -> p (c v)"), in_=Shf[:].rearrange("p c v -> p (c v)"), func=AF.Copy)
                if not phB:
                    X("gpsimd", "tensor_tensor", ["Dtot", "dL"], ["Dtot"], out=Dtot[:], in0=Dtot[:], in1=dL[:, :, q], op=ALU.mult)
            return [part_pre, part_o, part_s]

        def hgrn_post(l, col0):
            o, o2, rs, hA = (h4[n] for n in ["o", "o2", "rs", "hA"])
            fl = lambda t: t[:].rearrange("p c t -> p (c t)")
            X("gpsimd", "tensor_tensor", ["h_o"], ["kh"], out=o2[:], in0=o[:], in1=o[:], op=ALU.mult)
            pn, kn_ = PS(4, 0, 512)
            for c in range(4):
                X("tensor", "matmul", ["kh", "c_ones_f"], kn_, out=pn[:, c * T:(c + 1) * T], lhsT=cst["ones_f"][:], rhs=o2[:, c, :], start=True, stop=True)
            X("scalar", "activation", kn_ + ["eps_rms"], ["brec"], out=fl(rs), in_=pn, func=AF.Ln, scale=1.0 / 128, bias=eps_rms[:])
            X("scalar", "activation", ["brec"], ["brec"], out=fl(rs), in_=fl(rs), func=AF.Exp, scale=-0.5)
            X("vector", "tensor_tensor", ["h_o", "brec"], ["h_o"], out=o[:], in0=o[:], in1=rs[:], op=ALU.mult)
            X("gpsimd", "tensor_tensor", ["hog", "pvs"], ["tA"], out=hA[:], in0=hog[:, :, col0:col0 + T], in1=pbc(l, "hnw", 4, T), op=ALU.mult)
            X("vector", "tensor_tensor", ["h_o", "tA"], ["ob_b"], out=ob_b[:, :, col0:col0 + T], in0=o[:], in1=hA[:], op=ALU.mult)

        shst = sb("shst", [128, RW_BLK, 4])
        shout = sb("shout", [128, RW_BLK, 4])
        shoutp = sb("shoutp", [128, RW_BLK])
        halo_prev = sb("halo_prev", [128, RW_BLK])
        hraw = sb("hraw", [128, 16])
        hall = sb("hall", [128, 4, 16])
        exb = sb("exb", [128, SW])
        exall = big[:, 0:SW]
        Xr = sb("Xr", [128, 4, 64])
        Xh = sb("Xh", [128, 4, 128])
        PTbd = sb("PTbd", [128, 128])
        lhsTf = sb("lhsTf", [128, 128])
        ftmp = sb("ftmp", [128, 128])
        X("vector", "memset", [], ["PTbd"], ap=PTbd[:], constant=0.0)
        X("vector", "memset", [], ["hraw"], ap=hraw[:], constant=0.0)
        groups4 = [[0, 1, 2, 3], [4, 5, 6, 7]]

        def make_handler(is_s, N):
            def handler(b, pr, pk):
                if b < 15:
                    if is_s:
                        dst = raw[:, b, 0:132].rearrange("p (s t) -> p s t", t=33)[:, :, 1:33]
                        src = pr.rearrange("p (s t) -> p s t", t=32)
                    else:
                        dst, src = raw[:, b, 1:N + 1], pr
                    X("scalar", "activation", pk, ["raw"], out=dst, in_=src, func=AF.Copy)
                elif b < 19:
                    X("scalar", "activation", pk, ["hq"], out=hq[:, b - 15, 0:N], in_=pr, func=AF.Silu)
                elif b < 23:
                    X("scalar", "activation", pk, ["hsig"], out=hsig[:, b - 19, 0:N], in_=pr, func=AF.Sigmoid)
                elif b < 27:
                    X("scalar", "activation", pk, ["hv_b"], out=hv_b[:, b - 23, 0:N], in_=pr, func=AF.Copy)
                elif b < 31:
                    X("scalar", "activation", pk, ["hog"], out=hog[:, b - 27, 0:N], in_=pr, func=AF.Silu)
                else:
                    X("scalar", "activation", pk, ["gates"], out=gates[:, b - 31, 0:N], in_=pr, func=AF.Sigmoid)
            return handler

        class Feeder:
            def __init__(self, l, blocks, N, handler):
                self.l, self.q, self.N, self.h = l, list(blocks), N, handler

            def feed(self, n=2):
                if self.q:
                    take, self.q = self.q[:n], self.q[n:]
                    emit_proj(self.l, take, self.N, self.h)

            def until(self, b):
                while self.q and self.q[0] <= b:
                    self.feed(2)

            def flush(self):
                while self.q:
                    self.feed(2)

        _feeder = [None]

        def FEED(n=2):
            if _feeder[0] is not None:
                _feeder[0].feed(n)

        def xsrc(l):
            return (xT, []) if l == 0 else (xs1, ["xs1"])

        def emit_halo(l):
            if l > 0:
                conv_pump(l, 10 ** 6)
            src, sk = xsrc(l)
            DMA("sync", "x_ld", xt[:, :, 0:1], src[:, :, NPT - 1:NPT], sk, ["xt"], allow_slow_non_contiguous=True)
            emit_norm(l, xt, "xt", 1, "nmix")

            def hh_(b, pr, pk):
                X("scalar", "activation", pk, ["hraw"], out=hraw[:, b:b + 1], in_=pr, func=AF.Copy)
            emit_proj(l, list(range(RW_BLK)), 1, hh_)
            DMA("gpsimd", "ex_h", cin_h[:, :], hraw[:], ["hraw"], ["cin_h"])
            p.op("gpsimd", lambda e: e.collective_compute("AllGather", ALU.bypass, replica_groups=groups4, ins=[cin_h[:, :]], outs=[cout_h[:, :]]),
                 ["cin_h"], ["cout_h"])
            DMA("gpsimd", "ex_h", hall[:], cout_h.rearrange("(r p) c -> p r c", p=128), ["cout_h"], ["hall"])
            X("vector", "tensor_scalar", ["hall", "c_halom"], ["halo_prev"], out=halo_prev[:], in0=hall[:, 0, 0:RW_BLK], scalar1=cst["halom"][:, 0:1],
              scalar2=None, op0=ALU.mult)
            for r in range(1, 4):
                X("vector", "scalar_tensor_tensor", ["hall", "c_halom", "halo_prev"], ["halo_prev"], out=halo_prev[:], in0=hall[:, r, 0:RW_BLK],
                  scalar=cst["halom"][:, r:r + 1], in1=halo_prev[:], op0=ALU.mult, op1=ALU.add)
            if l == 0:
                conv_pump(0, 8)

        def emit_exchange(l):
            conv_pump(l, 10 ** 6)
            X("vector", "tensor_copy", ["Sf0", "Sf1", "Sf2", "Sf3"], ["exb"], out=exb[:, 0:512], in_=Sf[:].rearrange("p c v -> p (c v)"))
            X("vector", "tensor_copy", ["Shf"], ["exb"], out=exb[:, 512:1024], in_=Shf[:].rearrange("p c v -> p (c v)"))
            X("vector", "tensor_copy", ["Dtot"], ["exb"], out=exb[:, 1024:1028], in_=Dtot[:])
            X("vector", "memset", [], ["exb"], ap=exb[:, 1028:SW], constant=0.0)
            DMA("gpsimd", "ex_s", cin_s[:, :], exb[:], ["exb"], ["cin_s"])
            p.op("gpsimd", lambda e: e.collective_compute("AllGather", ALU.bypass, replica_groups=groups4, ins=[cin_s[:, :]], outs=[cout_s[:, :]]),
                 ["cin_s"], ["cout_s"])
            X("vector", "memset", [], ["Xr"], ap=Xr[:], constant=0.0)
            X("vector", "memset", [], ["Xh"], ap=Xh[:], constant=0.0)
            fm = cst["foldm"]
            for r in range(3):
                DMA("gpsimd", "ex_s", exall, cout_s[r * 128:(r + 1) * 128, :], ["cout_s"], ["gates"])
                for c in range(4):
                    for hh in range(2):
                        X("vector", "tensor_copy", ["gates"], ["PTbd"], out=PTbd[hh * 64:(hh + 1) * 64, hh * 64:(hh + 1) * 64],
                          in_=exall[hh * 64:(hh + 1) * 64, c * 128 + 64:c * 128 + 128])
                    pT, kT = QSLOT()
                    X("tensor", "transpose", ["PTbd", "c_ident"], kT, out=pT, in_=PTbd[:], identity=cst["ident"][:])
                    X("vector", "tensor_copy", kT, ["lhsTf"], out=lhsTf[:], in_=pT)
                    pm, km = QSLOT()
                    X("tensor", "matmul", ["lhsTf", "Xr"], km, out=pm[:, 0:64], lhsT=lhsTf[:], rhs=Xr[:, c, :], start=True, stop=True)
                    X("vector", "tensor_tensor", km + ["gates"], ["ftmp"], out=ftmp[:, 0:64], in0=pm[:, 0:64], in1=exall[:, c * 128:c * 128 + 64], op=ALU.add)
                    X("vector", "tensor_tensor", ["ftmp", "Xr"], ["ftmp"], out=ftmp[:, 0:64], in0=ftmp[:, 0:64], in1=Xr[:, c, :], op=ALU.subtract)
                    X("vector", "scalar_tensor_tensor", ["ftmp", "Xr", "c_foldm"], ["Xr"], out=Xr[:, c, :], in0=ftmp[:, 0:64], scalar=fm[:, r:r + 1],
                      in1=Xr[:, c, :], op0=ALU.mult, op1=ALU.add)
                for c in range(4):
                    X("vector", "scalar_tensor_tensor", ["Xh", "gates"], ["ftmp"], out=ftmp[:], in0=Xh[:, c, :], scalar=exall[:, 1024 + c:1025 + c],
                      in1=exall[:, 512 + c * 128:512 + (c + 1) * 128], op0=ALU.mult, op1=ALU.add)
                    X("vector", "tensor_tensor", ["ftmp", "Xh"], ["ftmp"], out=ftmp[:], in0=ftmp[:], in1=Xh[:, c, :], op=ALU.subtract)
                    X("vector", "scalar_tensor_tensor", ["ftmp", "Xh", "c_foldm"], ["Xh"], out=Xh[:, c, :], in0=ftmp[:], scalar=fm[:, r:r + 1],
                      in1=Xh[:, c, :], op0=ALU.mult, op1=ALU.add)

        SFK = ["Sf0", "Sf1", "Sf2", "Sf3"]
        SBK = ["Sb0", "Sb1", "Sb2", "Sb3"]

        def shadows():
            X("scalar", "activation", SFK, SBK, out=Sb[:].rearrange("p c v -> p (c v)"), in_=Sf[:].rearrange("p c v -> p (c v)"), func=AF.Copy)
            X("scalar", "activation", ["Shf"], ["Shb"], out=Shb[:].rearrange("p c v -> p (c v)"), in_=Shf[:].rearrange("p c v -> p (c v)"), func=AF.Copy)

        def init_states_A():
            X("vector", "memset", [], SFK, ap=Sf[:], constant=0.0)
            for c in range(4):
                X("vector", "tensor_copy", ["c_isel"], SFK, out=Sf[:, c, 64:128], in_=cst["isel"][:])
            X("vector", "memset", [], ["Shf"], ap=Shf[:], constant=0.0)
            X("vector", "memset", [], ["Dtot"], ap=Dtot[:], constant=1.0)
            shadows()

        def init_states_B():
            X("vector", "tensor_copy", ["Xr"], SFK, out=Sf[:, :, 0:64], in_=Xr[:])
            X("vector", "tensor_copy", ["Xh"], ["Shf"], out=Shf[:], in_=Xh[:])
            shadows()

        def layer_tile(l, phB, ti):
            conv_pump_any(5)
            is_s = (ti == NPS)
            N = 128 if is_s else W
            tok0 = NPT if is_s else ti * W
            last_prompt = (ti == NPS - 1)
            src, sk = xsrc(l)
            DMA("sync", "x_ld", xt[:, :, 0:N], src[:, :, tok0:tok0 + N], sk, ["xt"])
            emit_norm(l, xt, "xt", N, "nmix")
            if is_s:
                DMA("sync", "sh_ld", shst[:], st_shift[l], [], ["shst"])
                X("vector", "tensor_copy", ["shst"], ["raw"], out=raw[:, :, 0:132].rearrange("p b (s t) -> p b s t", t=33)[:, :, :, 0:1],
                  in_=shst[:].unsqueeze(3))
            blocks = list(range(NBLK)) if phB else (list(range(4, 13)) + list(range(19, 27)))
            fd = Feeder(l, blocks, N, make_handler(is_s, N))
            _feeder[0] = fd
            fd.until(14)
            nb = (0, RW_BLK) if phB else (4, 13)
            for j in range(N // T):
                col0 = j * T
                if is_s:
                    rv = raw[:, nb[0]:nb[1], 0:132].rearrange("p b (s t) -> p b s t", t=33)
                    cur, prv = rv[:, :, :, 1:33], rv[:, :, :, 0:32]
                    mxv = mx[:, nb[0]:nb[1], :].rearrange("p b (s t) -> p b s t", t=32)
                else:
                    cur, prv = raw[:, nb[0]:nb[1], 1 + col0:1 + col0 + T], raw[:, nb[0]:nb[1], col0:col0 + T]
                    mxv = mx[:, nb[0]:nb[1], :]
                rwkv_prep(l, phB, cur, prv, mxv, tok0 + col0, nb)
                fd.until(22)
                hgrn_prep(l, phB, col0)
                fd.until(26)
                def before_chunk(q, l=l, is_s=is_s):
                    if is_s:
                        DMA("sync", "st_ld", Sf[:, :, 0:64], st_rwkv[l, q], [], SFK)
                        DMA("sync", "st_ld", Shf[:], st_hgrn[l, q], [], ["Shf"])
                        shadows()

                def after_chunk(q, l=l, is_s=is_s):
                    if is_s:
                        DMA("gpsimd", "st_out", o_rwkv_s[l, q], Sf[:, :, 0:64], SFK, ["o_rwkv_s"], is_output=True)
                        DMA("gpsimd", "st_out", o_hgrn_s[l, q], Shf[:], ["Shf"], ["o_hgrn_s"], is_output=True)
                mixer_chunks(l, phB, col0, before_chunk, after_chunk)
                if phB:
                    rwkv_post(l, col0)
                    fd.until(30)
                    hgrn_post(l, col0)
            fd.flush()
            _feeder[0] = None
            if is_s:
                if phB:
                    X("vector", "tensor_copy", ["raw"], ["shout"], out=shout[:].unsqueeze(3),
                      in_=raw[:, :, 0:132].rearrange("p b (s t) -> p b s t", t=33)[:, :, :, 32:33])
                    DMA("gpsimd", "st_out", o_shift_s[l], shout[:], ["shout"], ["o_shift_s"], is_output=True)
            else:
                if phB and last_prompt:
                    X("vector", "tensor_copy", ["raw"], ["shoutp"], out=shoutp[:].unsqueeze(2), in_=raw[:, :, W:W + 1])
                    DMA("gpsimd", "st_out", o_shift_p[l], shoutp[:], ["shoutp"], ["o_shift_p"], is_output=True)
                    DMA("gpsimd", "st_out", o_rwkv_p[l], Sf[:, :, 0:64], SFK, ["o_rwkv_p"], is_output=True)
                    DMA("gpsimd", "st_out", o_hgrn_p[l], Shf[:], ["Shf"], ["o_hgrn_p"], is_output=True)
                X("vector", "tensor_copy", ["raw"], ["raw"], out=raw[:, :, 0:1], in_=raw[:, :, W:W + 1])
            if not phB:
                return
            for o8 in range(8):
                s_ = _wsl[0] % WSL
                _wsl[0] += 1
                DMA("sync", "wsl%d" % s_, wslot[s_][:, 0, :], wb_oab[l, o8], ["wb_o%d" % l], ["wslot%d" % s_])
                sa = _psl[0] % 4
                _psl[0] += 1
                pa_, ka_ = PS(sa, 0, N)
                for c in range(4):
                    X("tensor", "matmul", ["wslot%d" % s_, "yg_b"], ka_, out=pa_, lhsT=wslot[s_][:, 0, c * 128:(c + 1) * 128], rhs=yg_b[:, c, 0:N],
                      start=(c == 0), stop=(c == 3))
                sb_ = _psl[0] % 4
                _psl[0] += 1
                pb_, kb_ = PS(sb_, 0, N)
                for c in range(4):
                    X("tensor", "matmul", ["wslot%d" % s_, "ob_b"], kb_, out=pb_, lhsT=wslot[s_][:, 0, (4 + c) * 128:(5 + c) * 128], rhs=ob_b[:, c, 0:N],
                      start=(c == 0), stop=(c == 3))
                X("vector", "tensor_tensor", ka_ + ["gates"], ["mtmp"], out=mtmp[:, 0:N], in0=pa_, in1=gates[:, o8, 0:N], op=ALU.mult)
                X("vector", "tensor_tensor", kb_ + ["gates"], ["relu0"], out=relu_t[0][:, 0:N], in0=pb_, in1=gates[:, 8 + o8, 0:N], op=ALU.mult)
                X("gpsimd", "tensor_tensor", ["mtmp", "relu0"], ["mixin"], out=mixin[:, o8, 0:N], in0=mtmp[:, 0:N], in1=relu_t[0][:, 0:N], op=ALU.add)
            for o8 in range(8):
                s_ = _wsl[0] % WSL
                _wsl[0] += 1
                DMA("sync", "wsl%d" % s_, wslot[s_][:, 0, :], wb_o[l, o8], ["wb_o%d" % l], ["wslot%d" % s_])
                sm = _psl[0] % 4
                _psl[0] += 1
                pm_, km_ = PS(sm, 0, N)
                for kc in range(KC):
                    X("tensor", "matmul", ["wslot%d" % s_, "mixin"], km_, out=pm_, lhsT=wslot[s_][:, 0, kc * 128:(kc + 1) * 128], rhs=mixin[:, kc, 0:N],
                      start=(kc == 0), stop=(kc == KC - 1))
                X("vector", "tensor_tensor", km_ + ["xt"], ["x1"], out=x1[:, o8, 0:N], in0=pm_, in1=xt[:, o8, 0:N], op=ALU.add)
            DMA("gpsimd", "x1_st", x1s[:, :, tok0:tok0 + N], x1[:, :, 0:N], ["x1"], ["x1s"])

        F_x = aalloc([128, KC, 512], F32, reset=True)
        F_h = aalloc([128, KC, 512], BF16)
        F_sq = [aalloc([128, 512]) for _ in range(2)]
        F_rstd = aalloc([128, 512])
        F_relu = F_sq
        F_act = aalloc([128, 16, 512], BF16)
        F_up = [aalloc([128, KC, 256], BF16) for _ in range(3)]
        F_dn = [aalloc([128, 16, 128], BF16) for _ in range(3)]
        _fs = [0, 0]

        def stage_F(l, tok0, N):
            DMA("sync", "f_ld", F_x[:, :, 0:N], x1s[:, :, tok0:tok0 + N], ["x1s"], ["F_x"])
            emit_norm(l, F_x, "F_x", N, "nffn", outbuf=F_h, outkey="F_h", sqk=F_sq, rstd=F_rstd, pfx="F_")
            for h in range(2):
                for fg in range(8):
                    su = _fs[0] % 3
                    _fs[0] += 1
                    DMA("sync", "up%d" % su, F_up[su][:].rearrange("p k c -> p (k c)"), wb_up[l, h * 8 + fg], ["wb_f%d" % l], ["F_up%d" % su])
                    for fb in range(2):
                        pu, ku = PS(4 + (fb % 2), 0, N)
                        for kc in range(KC):
                            X("tensor", "matmul", ["F_up%d" % su, "F_h"], ku, out=pu, lhsT=F_up[su][:, kc, fb * 128:(fb + 1) * 128], rhs=F_h[:, kc, 0:N],
                              start=(kc == 0), stop=(kc == KC - 1))
                        rt = F_relu[fb % 2]
                        X("scalar", "activation", ku, ["F_sqk%d" % (fb % 2)], out=rt[:, 0:N], in_=pu, func=AF.Relu)
                        X("gpsimd", "tensor_tensor", ["F_sqk%d" % (fb % 2)], ["F_act"], out=F_act[:, fg * 2 + fb, 0:N], in0=rt[:, 0:N], in1=rt[:, 0:N], op=ALU.mult)
                for o8 in range(8):
                    sd = _fs[1] % 3
                    _fs[1] += 1
                    DMA("sync", "dn%d" % sd, F_dn[sd][:].rearrange("p k c -> p (k c)"), wb_dn[l, h * 8 + o8], ["wb_f%d" % l], ["F_dn%d" % sd])
                    pd_, kd_ = PS(o8 % 4, 0, N)
                    for fc in range(16):
                        X("tensor", "matmul", ["F_dn%d" % sd, "F_act"], kd_, out=pd_, lhsT=F_dn[sd][:, fc, :], rhs=F_act[:, fc, 0:N],
                          start=(fc == 0), stop=(fc == 15))
                    X("vector", "tensor_tensor", kd_ + ["F_x"], ["F_x"], out=F_x[:, o8, 0:N], in0=pd_, in1=F_x[:, o8, 0:N], op=ALU.add)
            if l < NL - 1:
                DMA("gpsimd", "x_st", xs1[:, :, tok0:tok0 + N], F_x[:, :, 0:N], ["F_x"], ["xs1"])
            else:
                emit_norm(0, F_x, "F_x", N, "nfin", outbuf=F_x, outkey="F_x", sqk=F_sq, rstd=F_rstd, pfx="F_")
                DMA("gpsimd", "y_st", yT[:, :, tok0:tok0 + N], F_x[:, :, 0:N], ["F_x"], ["yT"], is_output=True)

        _step = [0]

        def step(fn, *a):
            _step[0] += 1
            if cfg.stop is None or _step[0] <= cfg.stop:
                fn(*a)

        def set_prev():
            X("vector", "tensor_copy", ["halo_prev"], ["raw"], out=raw[:, :, 0:1], in_=halo_prev[:].unsqueeze(2))

        for l in range(NL):
            step(emit_halo, l)
            step(set_prev)
            step(init_states_A)
            for ti in range(NPS):
                step(layer_tile, l, False, ti)
            step(emit_exchange, l)
            step(set_prev)
            step(init_states_B)
            GT = max(1, 512 // W)
            for g0 in range(0, NPS, GT):
                g1 = min(NPS, g0 + GT)
                for ti in range(g0, g1):
                    step(layer_tile, l, True, ti)
                step(p.fence)
                step(stage_F, l, g0 * W, (g1 - g0) * W)
                step(p.fence)
            step(layer_tile, l, True, NPS)
            step(p.fence)
            step(stage_F, l, NPT, 128)
            step(p.fence)

        with nc.Block() as block:
            p.emit(block)
    return nc


_NC_CACHE = {}


def _prep_shared(inp):
    f = np.float32
    NL = inp["w_in"].shape[0]
    idx = _win_cols()
    sh = {}
    w_in = _take_cols(np.asarray(inp["w_in"], f), idx)
    sh["w_in_r"] = np.ascontiguousarray(w_in.reshape(NL, KC, 128, NBLK, 128).transpose(0, 3, 2, 1, 4))
    w_up = np.asarray(inp["w_ffn_up"], f)
    sh["w_up_r"] = np.ascontiguousarray(w_up.reshape(NL, KC, 128, 16, 256).transpose(0, 3, 2, 1, 4))
    w_dn = np.asarray(inp["w_ffn_down"], f)
    sh["w_dn_r"] = np.ascontiguousarray(w_dn.reshape(NL, 2, 16, 128, 8, 128).transpose(0, 1, 4, 3, 2, 5).reshape(NL, 16, 128, 16, 128))
    woa = np.asarray(inp["w_out_a"], f).reshape(NL, 4, 128, 8, 128).transpose(0, 3, 2, 1, 4)
    wob = np.asarray(inp["w_out_b"], f).reshape(NL, 4, 128, 8, 128).transpose(0, 3, 2, 1, 4)
    sh["w_oab_r"] = np.ascontiguousarray(np.concatenate([woa, wob], axis=3))
    sh["w_o_r"] = np.ascontiguousarray(np.asarray(inp["w_out"], f).reshape(NL, 8, 128, 8, 128).transpose(0, 3, 2, 1, 4))
    w2 = np.zeros((NL, 128, 512), f)
    w2[:, 0:64] = inp["rwkv_w2"]
    a2 = np.zeros((NL, 128, 512), f)
    a2[:, 64:128] = inp["rwkv_a2"]
    g2 = np.zeros((NL, 256, 512), f)
    g2[:, 0:160] = inp["rwkv_g2"]
    sh["w2pad"], sh["a2pad"] = w2, a2
    sh["g2pad"] = np.ascontiguousarray(g2.reshape(NL, 2, 128, 512).transpose(0, 2, 1, 3))
    sh["vw1"] = np.ascontiguousarray(np.asarray(inp["rwkv_vres_w1"], f)[0].reshape(4, 128, 32).transpose(1, 0, 2))
    sh["vw2"] = np.ascontiguousarray(np.asarray(inp["rwkv_vres_w2"], f)[0])
    pv = np.zeros((NL, 128, NPV), f)
    mu = _take_cols(np.asarray(inp["rwkv_mu"], f), idx[:RW_BLK * 128])
    for l in range(NL):
        pv[l, :, PV["nmix"]:PV["nmix"] + 8] = _pk(np.asarray(inp["norm_mix"], f)[l], 8)
        pv[l, :, PV["nffn"]:PV["nffn"] + 8] = _pk(np.asarray(inp["norm_ffn"], f)[l], 8)
        pv[l, :, PV["mu"]:PV["mu"] + 15] = _pk(mu[l], 15)
        pv[l, :, PV["w0"]:PV["w0"] + 4] = _pk(np.asarray(inp["rwkv_w0"], f)[l], 4)
        pv[l, :, PV["a0"]:PV["a0"] + 4] = _pk(np.asarray(inp["rwkv_a0"], f)[l], 4)
        pv[l, :, PV["v0"]:PV["v0"] + 4] = _pk(np.asarray(inp["rwkv_v0"], f)[0], 4)
        pv[l, :, PV["kk"]:PV["kk"] + 4] = _pk(np.asarray(inp["rwkv_k_k"], f)[l], 4)
        pv[l, :, PV["ka"]:PV["ka"] + 4] = _pk(np.asarray(inp["rwkv_k_a"], f)[l], 4)
        pv[l, :, PV["rk"]:PV["rk"] + 4] = _pk(np.asarray(inp["rwkv_r_k"], f)[l].reshape(-1), 4)
        pv[l, :, PV["hnw"]:PV["hnw"] + 4] = _pk(np.asarray(inp["hgrn_norm_w"], f)[l], 4)
        pv[l, :, PV["lbz"]:PV["lbz"] + 4] = _pk(np.asarray(inp["hgrn_lb_logits"], f)[l], 4)
        pv[l, :, PV["nfin"]:PV["nfin"] + 8] = _pk(np.asarray(inp["norm_final"], f), 8)
    sh["pvec"] = pv
    for nm, key in (("lnw_st", "rwkv_ln_w"), ("lnb_st", "rwkv_ln_b")):
        a = np.asarray(inp[key], f).reshape(NL, 2, 2, 2, 64)
        a = np.broadcast_to(a[:, :, :, :, None, :], (NL, 2, 2, 2, 32, 64))
        sh[nm] = np.ascontiguousarray(a.transpose(0, 2, 3, 4, 1, 5).reshape(NL, 128, 2, 64))
    return sh


def _run(inp, npt, w, dbg=(), stop=None):
    f = np.float32
    cfg = Cfg(npt=npt, w=w, nlayer=int(inp["w_in"].shape[0]), dbg=dbg, stop=stop)
    key = (npt, w, cfg.NL, tuple(dbg), stop)
    if key not in _NC_CACHE:
        _NC_CACHE[key] = build(cfg)
    nc = _NC_CACHE[key]
    NL, NT = cfg.NL, cfg.NT
    sh = _prep_shared(inp)
    xp = np.asarray(inp["x_prompt"], f)
    xs = np.asarray(inp["x_sample"], f)
    idx = _win_cols()
    sshift = _take_cols(np.asarray(inp["state_shift"], f), idx[:RW_BLK * 128])
    srw = np.asarray(inp["state_rwkv"], f)
    shg = np.asarray(inp["state_hgrn"], f)
    in_maps = []
    for c in range(NCORE):
        b, seg = c // 4, c % 4
        xtok = np.concatenate([xp[b, seg * npt:(seg + 1) * npt], xs[4 * c:4 * c + 4].reshape(4 * L, D)], axis=0)
        m = dict(sh)
        m["xT"] = np.ascontiguousarray(xtok.reshape(NT, KC, 128).transpose(2, 1, 0))
        ss = sshift[:, 4 * c:4 * c + 4]
        m["st_shift"] = np.ascontiguousarray(ss.reshape(NL, 4, RW_BLK, 128).transpose(0, 3, 2, 1))
        r = srw[:, 4 * c:4 * c + 4].reshape(NL, 4, 4, 2, 64, 64)
        m["st_rwkv"] = np.ascontiguousarray(r.transpose(0, 1, 3, 5, 2, 4).reshape(NL, 4, 128, 4, 64))
        h = shg[:, 4 * c:4 * c + 4]
        m["st_hgrn"] = np.ascontiguousarray(h.transpose(0, 1, 3, 2, 4))
        for n, v in _consts(seg).items():
            m["c_" + n] = v
        in_maps.append(m)
    res = run_bass_kernel_spmd(nc, in_maps, core_ids=list(range(NCORE)))
    R = res.results
    B = xp.shape[0]
    y_p = np.zeros((B, 4 * npt, D), f)
    y_s = np.zeros((4 * NCORE, L, D), f)
    for c in range(NCORE):
        yt = R[c]["yT"].transpose(2, 1, 0).reshape(NT, D)
        y_p[c // 4, (c % 4) * npt:(c % 4 + 1) * npt] = yt[:npt]
        y_s[4 * c:4 * c + 4] = yt[npt:].reshape(4, L, D)

    def unshift(a):
        a = np.moveaxis(a, -2, -1)
        return a.reshape(a.shape[:-2] + (RW_BLK * 128,))[..., :1824]

    def unrw(a):
        lead = a.shape[:-3]
        a = a.reshape(lead + (2, 64, 4, 64))
        n = len(lead)
        a = a.transpose(tuple(range(n)) + (n + 2, n + 0, n + 3, n + 1))
        return a.reshape(lead + (8, 64, 64))

    def unhg(a):
        n = a.ndim - 3
        return a.transpose(tuple(range(n)) + (n + 1, n + 0, n + 2))

    lastc = [4 * bb + 3 for bb in range(B)]
    shift_p = np.stack([unshift(R[c]["o_shift_p"]) for c in lastc], axis=1)
    rwkv_p = np.stack([unrw(R[c]["o_rwkv_p"]) for c in lastc], axis=1)
    hgrn_p = np.stack([unhg(R[c]["o_hgrn_p"]) for c in lastc], axis=1)
    shift_s = np.concatenate([np.moveaxis(unshift(np.moveaxis(R[c]["o_shift_s"], -1, 1)), 1, 1) for c in range(NCORE)], axis=1)
    rwkv_s = np.concatenate([unrw(R[c]["o_rwkv_s"]) for c in range(NCORE)], axis=1)
    hgrn_s = np.concatenate([unhg(R[c]["o_hgrn_s"]) for c in range(NCORE)], axis=1)
    outs = (y_p, y_s, shift_p, rwkv_p, hgrn_p, shift_s, rwkv_s, hgrn_s)
    return tuple(np.ascontiguousarray(o, dtype=f) for o in outs), R


def kernel(**inputs):
    outs, _ = _run(inputs, 2048, 128)
    return outs
```
